# Optimizing a Trainium2 kernel written in Bass

```python
import math
import jax
import jax.numpy as jnp
from jax import lax
import numpy as np

D_MODEL = 1024
BATCH = 8
SEQ = 4096
DEPTH = 2
DEC_BATCH = 32
DEC_SEQ = 16
PAST_LEN = 4096

CHUNK = 64
N_META = 16
Q_BLOCK = 128
EPS = 1e-6
ROPE_BASE = 10000.0

MLA_HEADS = 8
MLA_NOPE = 64
MLA_ROPE = 32
MLA_V = 64
MLA_Q_LORA = 384
MLA_KV_LORA = 256
MLA_WIDTH = MLA_HEADS * MLA_V

M_HEADS = 4
M_DH = 128
M_WIDTH = M_HEADS * M_DH

G_HEADS = 4
G_DK = 128
G_DV = 128
G_WIDTH = G_HEADS * G_DV
G_QKV = G_HEADS * (2 * G_DK + G_DV)
CONV_W = 4

D_MIX = MLA_WIDTH + M_WIDTH + G_WIDTH

IN_SPLITS = (
    MLA_Q_LORA, MLA_KV_LORA, MLA_ROPE, MLA_WIDTH,
    M_WIDTH, M_WIDTH, M_WIDTH, M_HEADS, M_HEADS, M_WIDTH, M_WIDTH,
    G_QKV, G_HEADS, G_HEADS, G_WIDTH,
)
IN_COLS = sum(IN_SPLITS)

kernel_name = "hybrid_mla_mlstm_gdn_stream_step"


def rms_norm(x, w):
    xf = x.astype(jnp.float32)
    y = xf * lax.rsqrt(jnp.mean(xf * xf, axis=-1, keepdims=True) + EPS)
    return (y * w.astype(jnp.float32)).astype(x.dtype)


def l2_norm(x):
    xf = x.astype(jnp.float32)
    return (xf * lax.rsqrt(jnp.sum(xf * xf, axis=-1, keepdims=True) + EPS)).astype(x.dtype)


def apply_rope(x, pos):
    half = x.shape[-1] // 2
    freq = ROPE_BASE ** (-jnp.arange(half, dtype=jnp.float32) / half)
    ang = pos.astype(jnp.float32)[:, None] * freq[None, :]
    ang = ang.reshape((ang.shape[0],) + (1,) * (x.ndim - 3) + (half,))
    cos = jnp.cos(ang).astype(x.dtype)
    sin = jnp.sin(ang).astype(x.dtype)
    x1, x2 = x[..., :half], x[..., half:]
    return jnp.concatenate([x1 * cos - x2 * sin, x2 * cos + x1 * sin], axis=-1)


def to_chunks(a, blk):
    b, l = a.shape[:2]
    a = a.reshape((b, l // blk, blk) + a.shape[2:])
    return jnp.swapaxes(jnp.moveaxis(a, 1, 0), 2, 3).astype(jnp.float32)


def from_chunks(o):
    n, b, h, t, d = o.shape
    return jnp.transpose(o, (1, 0, 3, 2, 4)).reshape(b, n * t, h, d)


def mla_attention(q_nope, q_rope, c_kv, k_rope, w_uk, w_uv, n_prefix, absorbed):
    L = q_nope.shape[1]
    scale = 1.0 / math.sqrt(MLA_NOPE + MLA_ROPE)
    if absorbed:
        q_lat = jnp.einsum('blhd,rhd->blhr', q_nope, w_uk)
    else:
        k_nope = jnp.einsum('bkr,rhd->bkhd', c_kv, w_uk)
        v = jnp.einsum('bkr,rhd->bkhd', c_kv, w_uv)
    outs = []
    for s in range(0, L, Q_BLOCK):
        e = min(s + Q_BLOCK, L)
        n_own = min(L, -(-e // CHUNK) * CHUNK)
        n_keys = n_prefix + n_own
        q_chunk = jnp.arange(s, e) // CHUNK
        k_chunk = jnp.arange(n_own) // CHUNK
        mask = jnp.concatenate([jnp.ones((e - s, n_prefix), dtype=bool),
                                k_chunk[None, :] <= q_chunk[:, None]], axis=1)
        sc = jnp.einsum('blhd,bkd->bhlk', q_rope[:, s:e], k_rope[:, :n_keys])
        if absorbed:
            sc = sc + jnp.einsum('blhr,bkr->bhlk', q_lat[:, s:e], c_kv[:, :n_keys])
        else:
            sc = sc + jnp.einsum('blhd,bkhd->bhlk', q_nope[:, s:e], k_nope[:, :n_keys])
        p = jax.nn.softmax(jnp.where(mask, sc.astype(jnp.float32) * scale, -jnp.inf), axis=-1).astype(c_kv.dtype)
        if absorbed:
            o = jnp.einsum('blhr,rhd->blhd', jnp.einsum('bhlk,bkr->blhr', p, c_kv[:, :n_keys]), w_uv)
        else:
            o = jnp.einsum('bhlk,bkhd->blhd', p, v[:, :n_keys])
        outs.append(o)
    return jnp.concatenate(outs, axis=1)


def mlstm_chunkwise(q, k, v, i_pre, log_f, C0, n0, m0, blk):
    dt = q.dtype
    xs = tuple(to_chunks(a, blk) for a in (q, k, v, i_pre, log_f))
    causal = jnp.tril(jnp.ones((blk, blk), dtype=bool))

    def step(carry, xs_c):
        C, n, m = carry
        qc, kc, vc, ic, fc = xs_c
        b = jnp.cumsum(fc, axis=-1)
        dmat = jnp.where(causal, b[..., :, None] - b[..., None, :] + ic[..., None, :], -jnp.inf)
        inter = b + m[..., None]
        m_t = jnp.maximum(inter, jnp.max(dmat, axis=-1))
        w_inter = jnp.exp(inter - m_t)
        qk = jnp.einsum('bhtd,bhsd->bhts', qc, kc) * jnp.exp(dmat - m_t[..., None])
        num = w_inter[..., None] * jnp.einsum('bhed,bhtd->bhte', C, qc) + jnp.einsum('bhts,bhse->bhte', qk, vc)
        den = w_inter * jnp.einsum('bhd,bhtd->bht', n, qc) + jnp.sum(qk, axis=-1)
        h = num / jnp.maximum(jnp.abs(den), jnp.exp(-m_t))[..., None]
        m_new = m_t[..., -1]
        g_state = jnp.exp(inter[..., -1] - m_new)
        g_tok = jnp.exp(b[..., -1:] - b + ic - m_new[..., None])
        C_new = g_state[..., None, None] * C + jnp.einsum('bhs,bhse,bhsd->bhed', g_tok, vc, kc)
        n_new = g_state[..., None] * n + jnp.einsum('bhs,bhsd->bhd', g_tok, kc)
        return (C_new, n_new, m_new), h

    carry0 = (C0.astype(jnp.float32), n0.astype(jnp.float32), m0.astype(jnp.float32))
    (C, n, m), h = lax.scan(step, carry0, xs)
    return from_chunks(h).astype(dt), C.astype(dt), n.astype(dt), m.astype(dt)


def gdn_chunkwise(q, k, v, log_a, beta, S0, blk):
    dt = q.dtype
    xs = tuple(to_chunks(a, blk) for a in (q, k, v, log_a, beta))
    incl = jnp.tril(jnp.ones((blk, blk), dtype=bool))
    strict = jnp.tril(jnp.ones((blk, blk), dtype=bool), -1)
    eye = jnp.eye(blk, dtype=jnp.float32)

    def step(S, xs_c):
        qc, kc, vc, gc, bc = xs_c
        G = jnp.cumsum(gc, axis=-1)
        gam = jnp.exp(jnp.where(incl, G[..., :, None] - G[..., None, :], -jnp.inf))
        A = jnp.where(strict, bc[..., :, None] * jnp.einsum('bhtd,bhsd->bhts', kc, kc) * gam, 0.0)
        rhs = jnp.concatenate([bc[..., None] * vc, (bc * jnp.exp(G))[..., None] * kc], axis=-1)
        sol = lax.linalg.triangular_solve(A + eye, rhs, left_side=True, lower=True, unit_diagonal=True)
        u, w = sol[..., :G_DV], sol[..., G_DV:]
        delta = u - jnp.einsum('bhtk,bhkv->bhtv', w, S)
        o = (jnp.einsum('bhtk,bhkv->bhtv', qc * jnp.exp(G)[..., None], S)
             + jnp.einsum('bhts,bhsv->bhtv', jnp.einsum('bhtd,bhsd->bhts', qc, kc) * gam, delta))
        S = (jnp.exp(G[..., -1])[..., None, None] * S
             + jnp.einsum('bhsk,bhsv->bhkv', kc * jnp.exp(G[..., -1:] - G)[..., None], delta))
        return S, o

    S, o = lax.scan(step, S0.astype(jnp.float32), xs)
    return from_chunks(o).astype(dt), S.astype(dt)


def mixer_layer(x, pos0, prefix_c, prefix_kr, m_C, m_n, m_m, g_S, g_buf, absorbed,
                norm_w, w_in, q_norm, w_uq, kv_norm, w_uk, w_uv, m_gate_b, m_norm,
                g_conv_w, g_a_log, g_dt_bias, g_norm, w_out):
    B, L, _ = x.shape
    blk = min(CHUNK, L)
    proj = rms_norm(x, norm_w) @ w_in
    (c_q, c_kv, k_r, z_a, m_q, m_k, m_v, m_i, m_f, m_o, z_m,
     g_qkv, g_a, g_b, z_g) = jnp.split(proj, np.cumsum(IN_SPLITS)[:-1].tolist(), axis=-1)

    pos = pos0 + jnp.arange(L)
    q = (rms_norm(c_q, q_norm) @ w_uq).reshape(B, L, MLA_HEADS, MLA_NOPE + MLA_ROPE)
    q_nope, q_rope = q[..., :MLA_NOPE], apply_rope(q[..., MLA_NOPE:], pos)
    c_new = rms_norm(c_kv, kv_norm)
    kr_new = apply_rope(k_r, pos)
    if prefix_c is None:
        c_all, kr_all, n_prefix = c_new, kr_new, 0
    else:
        c_all = jnp.concatenate([prefix_c.astype(c_new.dtype), c_new], axis=1)
        kr_all = jnp.concatenate([prefix_kr.astype(kr_new.dtype), kr_new], axis=1)
        n_prefix = prefix_c.shape[1]
    o_a = mla_attention(q_nope, q_rope, c_all, kr_all, w_uk, w_uv, n_prefix, absorbed).reshape(B, L, MLA_WIDTH)

    hd = (B, L, M_HEADS, M_DH)
    h_m, m_C, m_n, m_m = mlstm_chunkwise(
        m_q.reshape(hd), m_k.reshape(hd) * (M_DH ** -0.5), m_v.reshape(hd),
        m_i + m_gate_b[0], jax.nn.log_sigmoid(m_f + m_gate_b[1]), m_C, m_n, m_m, blk)
    h_m = jax.nn.sigmoid(m_o).reshape(hd) * h_m
    o_m = rms_norm(h_m, m_norm.reshape(M_HEADS, M_DH)).reshape(B, L, M_WIDTH)

    xc = jnp.concatenate([g_buf.astype(g_qkv.dtype), g_qkv], axis=1)
    conv = xc[:, 0:L] * g_conv_w[0]
    for j in range(1, CONV_W):
        conv = conv + xc[:, j:j + L] * g_conv_w[j]
    g_buf = xc[:, L:]
    conv = jax.nn.silu(conv)
    g_q, g_k, g_v = jnp.split(conv, [G_HEADS * G_DK, 2 * G_HEADS * G_DK], axis=-1)
    g_q = l2_norm(g_q.reshape(B, L, G_HEADS, G_DK)) * (G_DK ** -0.5)
    g_k = l2_norm(g_k.reshape(B, L, G_HEADS, G_DK))
    log_decay = -jnp.exp(g_a_log) * jax.nn.softplus(g_a + g_dt_bias)
    h_g, g_S = gdn_chunkwise(g_q, g_k, g_v.reshape(B, L, G_HEADS, G_DV), log_decay,
                             jax.nn.sigmoid(g_b), g_S, blk)
    o_g = rms_norm(h_g, g_norm).reshape(B, L, G_WIDTH)

    mixed = jnp.concatenate([o_a * jax.nn.silu(z_a), o_m * jax.nn.silu(z_m), o_g * jax.nn.silu(z_g)], axis=-1)
    return x + mixed @ w_out, c_new, kr_new, m_C, m_n, m_m, g_S, g_buf


def stack_layers(rows, i):
    return jnp.stack([r[i] for r in rows])


def setup_inputs(seed: int = 0) -> dict:
    key = jax.random.key(seed)
    ks = jax.random.split(key, 26)
    f32 = jnp.float32

    def nrm(k, shape, scale=1.0):
        return scale * jax.random.normal(k, shape, f32)

    def gain(k, shape):
        return 1.0 + 0.02 * jax.random.normal(k, shape, f32)

    n_cache = N_META + PAST_LEN
    dt_init = jnp.exp(jax.random.uniform(ks[20], (DEPTH, G_HEADS), f32, math.log(1e-3), math.log(1e-1)))
    gate_b = jnp.stack([nrm(ks[17], (DEPTH, M_HEADS), 0.1),
                        jnp.linspace(3.0, 6.0, M_HEADS, dtype=f32)[None, :] + nrm(ks[18], (DEPTH, M_HEADS), 0.1)],
                       axis=1)
    return {
        'x_prompt': nrm(ks[0], (BATCH, SEQ, D_MODEL)),
        'x_sample': nrm(ks[1], (DEC_BATCH, DEC_SEQ, D_MODEL)),
        'cache_mla_latent': nrm(ks[2], (DEPTH, DEC_BATCH, n_cache, MLA_KV_LORA)),
        'cache_mla_krope': nrm(ks[3], (DEPTH, DEC_BATCH, n_cache, MLA_ROPE)),
        'state_mlstm_C': nrm(ks[4], (DEPTH, DEC_BATCH, M_HEADS, M_DH, M_DH), 0.5),
        'state_mlstm_n': nrm(ks[5], (DEPTH, DEC_BATCH, M_HEADS, M_DH), 0.5),
        'state_mlstm_m': nrm(ks[6], (DEPTH, DEC_BATCH, M_HEADS)),
        'state_gdn_S': nrm(ks[7], (DEPTH, DEC_BATCH, G_HEADS, G_DK, G_DV), 0.5),
        'state_gdn_conv': nrm(ks[8], (DEPTH, DEC_BATCH, CONV_W - 1, G_QKV)),
        'meta_tokens': nrm(ks[9], (N_META, D_MODEL)),
        'norm_w': gain(ks[10], (DEPTH, D_MODEL)),
        'w_in': nrm(ks[11], (DEPTH, D_MODEL, IN_COLS), D_MODEL ** -0.5),
        'mla_q_norm': gain(ks[12], (DEPTH, MLA_Q_LORA)),
        'mla_w_uq': nrm(ks[13], (DEPTH, MLA_Q_LORA, MLA_HEADS * (MLA_NOPE + MLA_ROPE)), MLA_Q_LORA ** -0.5),
        'mla_kv_norm': gain(ks[14], (DEPTH, MLA_KV_LORA)),
        'mla_w_uk': nrm(ks[15], (DEPTH, MLA_KV_LORA, MLA_HEADS, MLA_NOPE), MLA_KV_LORA ** -0.5),
        'mla_w_uv': nrm(ks[16], (DEPTH, MLA_KV_LORA, MLA_HEADS, MLA_V), MLA_KV_LORA ** -0.5),
        'mlstm_gate_b': gate_b,
        'mlstm_norm': gain(ks[19], (DEPTH, M_WIDTH)),
        'gdn_conv_w': nrm(ks[21], (DEPTH, CONV_W, G_QKV), CONV_W ** -0.5),
        'gdn_a_log': jnp.log(jax.random.uniform(ks[22], (DEPTH, G_HEADS), f32, 1.0, 16.0)),
        'gdn_dt_bias': dt_init + jnp.log(-jnp.expm1(-dt_init)),
        'gdn_norm': gain(ks[23], (DEPTH, G_DV)),
        'w_out': nrm(ks[24], (DEPTH, D_MIX, D_MODEL), D_MIX ** -0.5),
        'final_norm': gain(ks[25], (D_MODEL,)),
    }


def reference(x_prompt, x_sample, cache_mla_latent, cache_mla_krope, state_mlstm_C, state_mlstm_n,
              state_mlstm_m, state_gdn_S, state_gdn_conv, meta_tokens, norm_w, w_in, mla_q_norm,
              mla_w_uq, mla_kv_norm, mla_w_uk, mla_w_uv, mlstm_gate_b, mlstm_norm, gdn_conv_w,
              gdn_a_log, gdn_dt_bias, gdn_norm, w_out, final_norm):
    B = x_prompt.shape[0]
    dt = x_prompt.dtype
    sample_pos0 = cache_mla_latent.shape[2]
    h_meta = jnp.broadcast_to(meta_tokens.astype(dt)[None], (B, N_META, D_MODEL))
    h_p, h_s = x_prompt, x_sample
    z_C = jnp.zeros((B, M_HEADS, M_DH, M_DH), dt)
    z_n = jnp.zeros((B, M_HEADS, M_DH), dt)
    z_m = jnp.zeros((B, M_HEADS), dt)
    z_S = jnp.zeros((B, G_HEADS, G_DK, G_DV), dt)
    z_buf = jnp.zeros((B, CONV_W - 1, G_QKV), dt)
    p_rows, s_rows = [], []
    for l in range(DEPTH):
        w = tuple(p[l] for p in (norm_w, w_in, mla_q_norm, mla_w_uq, mla_kv_norm, mla_w_uk, mla_w_uv,
                                 mlstm_gate_b, mlstm_norm, gdn_conv_w, gdn_a_log, gdn_dt_bias, gdn_norm, w_out))
        h_meta, c_m, kr_m, mC, mn, mm, gS, gbuf = mixer_layer(
            h_meta, 0, None, None, z_C, z_n, z_m, z_S, z_buf, False, *w)
        h_p, c_p, kr_p, mC, mn, mm, gS, gbuf = mixer_layer(
            h_p, N_META, c_m, kr_m, mC, mn, mm, gS, gbuf, False, *w)
        p_rows.append((jnp.concatenate([c_m, c_p], axis=1), jnp.concatenate([kr_m, kr_p], axis=1),
                       mC, mn, mm, gS, gbuf))
        h_s, c_s, kr_s, sC, sn, sm, sS, sbuf = mixer_layer(
            h_s, sample_pos0, cache_mla_latent[l], cache_mla_krope[l], state_mlstm_C[l], state_mlstm_n[l],
            state_mlstm_m[l], state_gdn_S[l], state_gdn_conv[l], True, *w)
        s_rows.append((c_s, kr_s, sC, sn, sm, sS, sbuf))
    y_prompt = rms_norm(h_p, final_norm)
    y_sample = rms_norm(h_s, final_norm)
    p_mla_latent = stack_layers(p_rows, 0)
    p_mla_krope = stack_layers(p_rows, 1)
    p_mlstm_C = stack_layers(p_rows, 2)
    p_mlstm_n = stack_layers(p_rows, 3)
    p_mlstm_m = stack_layers(p_rows, 4)
    p_gdn_S = stack_layers(p_rows, 5)
    p_gdn_conv = stack_layers(p_rows, 6)
    s_mla_latent = stack_layers(s_rows, 0)
    s_mla_krope = stack_layers(s_rows, 1)
    s_mlstm_C = stack_layers(s_rows, 2)
    s_mlstm_n = stack_layers(s_rows, 3)
    s_mlstm_m = stack_layers(s_rows, 4)
    s_gdn_S = stack_layers(s_rows, 5)
    s_gdn_conv = stack_layers(s_rows, 6)
    return (y_prompt, y_sample, p_mla_latent, p_mla_krope, p_mlstm_C, p_mlstm_n, p_mlstm_m, p_gdn_S, p_gdn_conv,
            s_mla_latent, s_mla_krope, s_mlstm_C, s_mlstm_n, s_mlstm_m, s_gdn_S, s_gdn_conv)
```

```python
import math
import contextlib
import numpy as np
import concourse.bass as bass
import concourse.mybir as mybir
from concourse.bass_utils import run_bass_kernel_spmd

F32 = mybir.dt.float32
BF16 = mybir.dt.bfloat16
AF = mybir.ActivationFunctionType
ALU = mybir.AluOpType
AX = mybir.AxisListType

D = 1024
SEQ = 4096
NMETA = 16
DEPTH = 2
NS = 4
DS = 16
PAST = 4096
NCACHE = NMETA + PAST
EPS = 1e-6
IN_COLS = 5808
NROWS_X = 80 + SEQ
PROJ_ROWS = 3 + 16 + SEQ + NS * 19
SM_SCALE = 1.0 / math.sqrt(96.0)

C_ID, C_ONE, C_TRI, C_SL, C_INCL, C_NEG, C_S64, C_S16, C_SELR = 0, 128, 256, 320, 384, 448, 512, 640, 768
NCON = 864


def make_consts():
    c = np.zeros((128, NCON), np.float32)
    c[:, C_ID:C_ID + 128] = np.eye(128)
    c[:, C_ONE:C_ONE + 128] = 1.0
    k = np.arange(64)[:, None]
    t = np.arange(64)[None, :]
    c[:64, C_TRI:C_TRI + 64] = (k <= t)
    c[:64, C_SL:C_SL + 64] = (k > t)
    c[:64, C_INCL:C_INCL + 64] = (t <= k)
    c[:64, C_NEG:C_NEG + 64] = np.where(t <= k, 0.0, -1e30)
    c[63, C_S64:C_S64 + 128] = 1.0
    c[15, C_S16:C_S16 + 128] = 1.0
    c[:32, C_SELR + 64:C_SELR + 96] = np.eye(32)
    return c


def make_rope():
    half = 16
    freq = (np.float32(10000.0) ** (-np.arange(half, dtype=np.float32) / np.float32(half))).astype(np.float32)
    pos = np.arange(NCACHE + DS, dtype=np.float32)
    ang = (pos[:, None] * freq[None, :]).astype(np.float32)
    return np.concatenate([np.cos(ang), np.sin(ang)], axis=1).astype(np.float32)


class H:
    __slots__ = ("name", "w", "r", "excl")

    def __init__(self, name="", excl=False):
        self.name = name
        self.w = None
        self.r = []
        self.excl = excl


class Buf:
    __slots__ = ("t", "h", "busy")

    def __init__(self, t, name=""):
        self.t = t
        self.h = H(name)
        self.busy = False

    def __getitem__(self, k):
        return self.t[k]


def _hs(lst):
    out = []
    for x in lst:
        if x is None:
            continue
        out.append(x.h if isinstance(x, Buf) else x)
    return out


class Sched:
    ENGS = ("pe", "act", "dve", "pool", "sp")
    XLAT = 1.2

    def __init__(self, nc, n_dma_sems=20):
        self.nc = nc
        self.nodes = []
        self.aset = {}
        self.cur_set = 1
        self.seg_start = [0]
        self.n_dma = n_dma_sems
        self.sems = {}

    def _deps(self, reads, writes):
        d = set()
        for h in reads:
            if h.w is not None:
                d.add(h.w)
        for h in writes:
            if h.w is not None:
                d.add(h.w)
            d.update(h.r)
        return d

    def _record(self, idx, reads, writes):
        for h in reads:
            h.r.append(idx)
        for h in writes:
            h.w = idx
            h.r = []

    def emit(self, eng, fn, reads=(), writes=(), cost=0.3, aset=0):
        reads, writes = _hs(reads), _hs(writes)
        ex = [h for h in reads if h.excl]
        if ex:
            reads = [h for h in reads if not h.excl]
            writes = writes + [h for h in ex if h not in writes]
        deps = self._deps(reads, writes)
        idx = len(self.nodes)
        self.nodes.append((eng, fn, deps, False, cost))
        if aset:
            self.aset[idx] = aset
        self._record(idx, reads, writes)
        return idx

    def dma(self, q, fn, reads=(), writes=(), is_out=False, cost=2.5):
        reads, writes = _hs(reads), _hs(writes)
        deps = self._deps(reads, writes)
        idx = len(self.nodes)
        self.nodes.append((q, fn, deps, True, cost))
        self._record(idx, reads, writes)
        return idx

    def barrier(self):
        if self.seg_start[-1] != len(self.nodes):
            self.seg_start.append(len(self.nodes))

    def _schedule(self, lo, hi):
        import heapq
        nodes = self.nodes
        n = hi - lo
        succ = [[] for _ in range(n)]
        indeg = [0] * n
        for i in range(lo, hi):
            for d in nodes[i][2]:
                if d >= lo:
                    succ[d - lo].append(i - lo)
                    indeg[i - lo] += 1
        prio = [0.0] * n
        for i in range(n - 1, -1, -1):
            c = nodes[lo + i][4]
            m = 0.0
            for s in succ[i]:
                if prio[s] > m:
                    m = prio[s]
            prio[i] = c + m + self.XLAT
        future = {e: [] for e in self.ENGS}
        avail = {e: [] for e in self.ENGS}
        tfree = {e: 0.0 for e in self.ENGS}
        ready_t = [0.0] * n
        order = {e: [] for e in self.ENGS}
        for i in range(n):
            if indeg[i] == 0:
                heapq.heappush(future[nodes[lo + i][0]], (0.0, i))
        done = 0
        while done < n:
            best_e, best_t = None, None
            for e in self.ENGS:
                if avail[e]:
                    t = tfree[e]
                elif future[e]:
                    t = max(tfree[e], future[e][0][0])
                else:
                    continue
                if best_t is None or t < best_t:
                    best_e, best_t = e, t
            e, t = best_e, best_t
            fu = future[e]
            while fu and fu[0][0] <= t:
                rt, i = heapq.heappop(fu)
                heapq.heappush(avail[e], (-prio[i], i))
            if e == "act" and len(avail[e]) > 1:
                top = avail[e][0]
                ts_ = self.aset.get(lo + top[1], 0)
                if ts_ != 0 and ts_ != self.cur_set:
                    best = None
                    for cand in avail[e]:
                        cs_ = self.aset.get(lo + cand[1], 0)
                        if (cs_ == 0 or cs_ == self.cur_set) and (best is None or cand < best):
                            best = cand
                    if best is not None and (-best[0]) >= (-top[0]) - 12.0:
                        avail[e].remove(best)
                        heapq.heapify(avail[e])
                        i = best[1]
                    else:
                        _, i = heapq.heappop(avail[e])
                else:
                    _, i = heapq.heappop(avail[e])
            else:
                _, i = heapq.heappop(avail[e])
            node = nodes[lo + i]
            sw = 0.0
            if e == "act":
                s_ = self.aset.get(lo + i, 0)
                if s_ != 0 and s_ != self.cur_set:
                    self.cur_set = s_
                    sw = 1.3
            if node[3]:
                fin = t + node[4]
                tfree[e] = t + 0.15
            else:
                fin = t + node[4] + sw
                tfree[e] = fin
            order[e].append(lo + i)
            done += 1
            for s in succ[i]:
                se = nodes[lo + s][0]
                lat = 0.08 if (se == e and not node[3]) else self.XLAT
                if fin + lat > ready_t[s]:
                    ready_t[s] = fin + lat
                indeg[s] -= 1
                if indeg[s] == 0:
                    heapq.heappush(future[se], (ready_t[s], s))
        return order

    def finalize(self):
        nc = self.nc
        nodes = self.nodes
        self.barrier()
        segs = list(zip(self.seg_start[:-1], self.seg_start[1:]))
        eng_ops = {e: [] for e in self.ENGS}
        token = [None] * len(nodes)
        cnt = {e: 0 for e in self.ENGS}
        dma_i = {e: 0 for e in self.ENGS}
        dma_val = {e: [0] * self.n_dma for e in self.ENGS}
        dma_prev = {}
        for (lo, hi) in segs:
            order = self._schedule(lo, hi)
            for e in self.ENGS:
                for idx in order[e]:
                    if nodes[idx][3]:
                        i = dma_i[e]
                        dma_i[e] = (i + 1) % self.n_dma
                        key = ("dma", e, i)
                        if dma_val[e][i] > 0:
                            dma_prev[idx] = (key, dma_val[e][i])
                        dma_val[e][i] += 16
                        token[idx] = (key, dma_val[e][i], 16)
                    else:
                        cnt[e] += 1
                        token[idx] = (e, cnt[e], 1)
                    eng_ops[e].append(("node", idx))
            allw = {e: cnt[e] for e in self.ENGS if cnt[e] > 0}
            for e in self.ENGS:
                for i, v in enumerate(dma_val[e]):
                    if v > 0:
                        allw[("dma", e, i)] = v
            for e in self.ENGS:
                eng_ops[e].append(("bar", dict(allw)))
        keys = list(self.ENGS)
        for e in self.ENGS:
            for i, v in enumerate(dma_val[e]):
                if v > 0:
                    keys.append(("dma", e, i))
        with contextlib.ExitStack() as es:
            for k in keys:
                nm = k if isinstance(k, str) else "d_%s_%d" % (k[1], k[2])
                self.sems[k] = es.enter_context(nc.semaphore("s_" + nm))
            block = es.enter_context(nc.Block())
            sems = self.sems

            def run(eng_name):
                def body(e):
                    waited = {}
                    for kind, x in eng_ops[eng_name]:
                        if kind == "bar":
                            for k, v in x.items():
                                if k == eng_name:
                                    continue
                                if waited.get(k, 0) < v:
                                    waited[k] = v
                                    e.wait_ge(sems[k], v)
                            continue
                        idx = x
                        _, fn, deps, is_dma, _ = nodes[idx]
                        need = {}
                        if idx in dma_prev:
                            k, v = dma_prev[idx]
                            need[k] = v
                        for d in deps:
                            k, v, _ = token[d]
                            if k == eng_name and eng_name == "pe":
                                continue
                            if need.get(k, 0) < v:
                                need[k] = v
                        for k, v in need.items():
                            if waited.get(k, 0) < v:
                                waited[k] = v
                                e.wait_ge(sems[k], v)
                        k, v, inc = token[idx]
                        fn(e).then_inc(sems[k], inc)
                return body

            block.tensor(run("pe"))
            block.scalar(run("act"))
            block.vector(run("dve"))
            block.gpsimd(run("pool"))
            block.sync(run("sp"))


class Rot:
    def __init__(self, bufs):
        self.bufs = bufs
        self.i = 0

    def get(self):
        for _ in range(len(self.bufs)):
            b = self.bufs[self.i]
            self.i = (self.i + 1) % len(self.bufs)
            if not b.busy:
                b.busy = True
                return b
        raise AssertionError("rotating pool exhausted: all buffers leased")

    @staticmethod
    def rel(*bs):
        for b in bs:
            b.busy = False


class K:
    def __init__(self, dbg=None, nlayers=DEPTH, phases="PAR"):
        self.dbg = dbg
        self.nlayers = nlayers
        self.phases = phases
        self.nc = bass.Bass("TRN2", target_bir_lowering=False)
        self.S = Sched(self.nc)
        self.es = contextlib.ExitStack()
        self.dq = 0

    def dram(self, name, shape, dt=F32, kind=None):
        if kind is None:
            t = self.nc.dram_tensor(name, list(shape), dt)
        else:
            t = self.nc.dram_tensor(name, list(shape), dt, kind=kind)
        return Buf(t.ap(), name)

    def sb(self, es, name, shape, dt=F32):
        self.dq += 1
        name = "%s_u%d" % (name, self.dq)
        return Buf(es.enter_context(self.nc.sbuf_tensor(name, list(shape), dt)), name)

    def ps(self, es, name, shape, dt=F32):
        b = Buf(es.enter_context(self.nc.psum_tensor(name, list(shape), dt)), name)
        b.h.excl = True
        return b

    def mm(self, out, lhsT, rhs, start, stop, r, w):
        n = max(64, rhs.free_size()) * (4 if rhs.dtype == F32 else 1)
        self.S.emit("pe", lambda e: e.matmul(out, lhsT=lhsT, rhs=rhs, start=start, stop=stop), r, w, cost=0.04 + n / 1400.0)

    def tr(self, out, in_, ident, r, w):
        self.S.emit("pe", lambda e: e.transpose(out, in_, ident), r, w, cost=0.04 + max(64, in_.partition_size()) / 1400.0)

    @staticmethod
    def _c(eng, ap):
        n = ap.free_size()
        if eng == "dve":
            return 0.12 + n / 960.0
        if eng == "act":
            return 0.2 + n / 1400.0
        return 0.35 + n / 700.0

    def act(self, out, in_, func, r, w, scale=None, bias=None, accum=None):
        kw = {}
        if scale is not None:
            kw["scale"] = scale
        if bias is not None:
            kw["bias"] = bias
        if accum is not None:
            kw["accum_out"] = accum
        aset = 1 if func in (AF.Exp, AF.Ln) else (2 if func == AF.Silu else (3 if func in (AF.Sigmoid, AF.Sqrt) else 0))
        self.S.emit("act", lambda e: e.activation(out=out, in_=in_, func=func, **kw), r, w, cost=self._c("act", out) + (0.1 if accum is not None else 0), aset=aset)

    def tt(self, eng, out, in0, in1, op, r, w):
        self.S.emit(eng, lambda e: e.tensor_tensor(out=out, in0=in0, in1=in1, op=op), r, w, cost=self._c(eng, out))

    def ts(self, eng, out, in0, s1, op0, r, w, s2=None, op1=None):
        if op1 is None:
            self.S.emit(eng, lambda e: e.tensor_scalar(out=out, in0=in0, scalar1=s1, scalar2=None, op0=op0), r, w, cost=self._c(eng, out))
        else:
            self.S.emit(eng, lambda e: e.tensor_scalar(out=out, in0=in0, scalar1=s1, scalar2=s2, op0=op0, op1=op1), r, w, cost=self._c(eng, out))

    def stt(self, out, in0, scalar, in1, op0, op1, r, w):
        self.S.emit("dve", lambda e: e.scalar_tensor_tensor(out=out, in0=in0, scalar=scalar, in1=in1, op0=op0, op1=op1), r, w, cost=self._c("dve", out))

    def red(self, out, in_, op, r, w):
        self.S.emit("dve", lambda e: e.tensor_reduce(out=out, in_=in_, axis=AX.X, op=op), r, w, cost=self._c("dve", in_))

    def cp(self, eng, out, in_, r, w):
        if eng == "act":
            self.S.emit("act", lambda e: e.activation(out=out, in_=in_, func=AF.Copy), r, w, cost=self._c("act", out))
        else:
            self.S.emit(eng, lambda e: e.tensor_copy(out=out, in_=in_), r, w, cost=self._c(eng, out))

    def recip(self, out, in_, r, w):
        self.S.emit("dve", lambda e: e.reciprocal(out=out, in_=in_), r, w, cost=self._c("dve", out))

    def memset(self, eng, ap, val, w):
        self.S.emit(eng, lambda e: e.memset(ap, val), [], w, cost=self._c(eng, ap))

    def dma(self, out, in_, r, w, q=None, nonc=False, is_out=False):
        if q is None:
            q = ("sp", "act")[self.dq % 2] if False else "sp"
        c = 2.0 + in_.nbytes() / 150e3 * (2.0 if q == "pool" else 1.0)
        if nonc:
            self.S.dma(q, lambda e: e.dma_start(out=out, in_=in_, allow_slow_non_contiguous=True), r, w, is_out, cost=c)
        else:
            self.S.dma(q, lambda e: e.dma_start(out=out, in_=in_), r, w, is_out, cost=c)

    @staticmethod
    def interleave(gens):
        gens = [g for g in gens if g is not None]
        while gens:
            alive = []
            for g in gens:
                try:
                    next(g)
                    alive.append(g)
                except StopIteration:
                    pass
            gens = alive

    def rstd(self, out, ssq, n, r, w):
        self.act(out, ssq, AF.Ln, r, w, scale=1.0 / n, bias=EPS)
        self.act(out, out, AF.Exp, w, w, scale=-0.5)

    def build(self):
        nc = self.nc
        es = self.es
        dbg = self.dbg
        EI, EO = "ExternalInput", "ExternalOutput"
        self.xp = self.dram("xp", [SEQ, D], kind=EI)
        self.xs = self.dram("xs", [NS * DS, D], kind=EI)
        self.meta = self.dram("meta", [NMETA, D], kind=EI)
        self.cl = self.dram("cl", [DEPTH, NS, NCACHE, 256], kind=EI)
        self.ck = self.dram("ck", [DEPTH, NS, NCACHE, 32], kind=EI)
        self.imC = self.dram("imC", [DEPTH, NS, 4, 128, 128], kind=EI)
        self.imn = self.dram("imn", [DEPTH, NS, 4, 128], kind=EI)
        self.imm = self.dram("imm", [DEPTH, NS, 4], kind=EI)
        self.igS = self.dram("igS", [DEPTH, NS, 4, 128, 128], kind=EI)
        self.igc = self.dram("igc", [DEPTH, NS, 3, 1536], kind=EI)
        self.norm_w = self.dram("norm_w", [DEPTH, D], kind=EI)
        self.w_in = self.dram("w_in", [DEPTH, D, IN_COLS], kind=EI)
        self.q_norm = self.dram("q_norm", [DEPTH, 384], kind=EI)
        self.w_uq = self.dram("w_uq", [DEPTH, 384, 768], kind=EI)
        self.kv_norm = self.dram("kv_norm", [DEPTH, 256], kind=EI)
        self.w_uk = self.dram("w_uk", [DEPTH, 256, 512], kind=EI)
        self.w_uv = self.dram("w_uv", [DEPTH, 256, 512], kind=EI)
        self.gate_b = self.dram("gate_b", [DEPTH, 8], kind=EI)
        self.m_norm = self.dram("m_norm", [DEPTH, 512], kind=EI)
        self.conv_w = self.dram("conv_w", [DEPTH, 4 * 1536], kind=EI)
        self.a_log = self.dram("a_log", [DEPTH, 4], kind=EI)
        self.dt_bias = self.dram("dt_bias", [DEPTH, 4], kind=EI)
        self.g_norm = self.dram("g_norm", [DEPTH, 128], kind=EI)
        self.w_out = self.dram("w_out", [DEPTH, 1536, D], kind=EI)
        self.final_norm = self.dram("final_norm", [D], kind=EI)
        self.consts = self.dram("consts", [128, NCON], kind=EI)
        self.rope = self.dram("rope", [NCACHE + DS, 32], kind=EI)

        self.y_p = self.dram("y_p", [SEQ, D], kind=EO)
        self.y_s = self.dram("y_s", [NS * DS, D], kind=EO)
        self.p_lat = self.dram("p_lat", [DEPTH, NMETA + SEQ, 256], kind=EO)
        self.p_kr = self.dram("p_kr", [DEPTH, NMETA + SEQ, 32], kind=EO)
        self.p_C = self.dram("p_C", [DEPTH, 4, 128, 128], kind=EO)
        self.p_n = self.dram("p_n", [DEPTH, 4, 128], kind=EO)
        self.p_m = self.dram("p_m", [DEPTH, 4], kind=EO)
        self.p_S = self.dram("p_S", [DEPTH, 4, 128, 128], kind=EO)
        self.p_cv = self.dram("p_cv", [DEPTH, 3, 1536], kind=EO)
        self.s_lat = self.dram("s_lat", [DEPTH, NS, DS, 256], kind=EO)
        self.s_kr = self.dram("s_kr", [DEPTH, NS, DS, 32], kind=EO)
        self.s_C = self.dram("s_C", [DEPTH, NS, 4, 128, 128], kind=EO)
        self.s_n = self.dram("s_n", [DEPTH, NS, 4, 128], kind=EO)
        self.s_m = self.dram("s_m", [DEPTH, NS, 4], kind=EO)
        self.s_S = self.dram("s_S", [DEPTH, NS, 4, 128, 128], kind=EO)
        self.s_cv = self.dram("s_cv", [DEPTH, NS, 3, 1536], kind=EO)

        dk = EO if dbg else None
        self.proj = self.dram("proj_scr", [PROJ_ROWS, IN_COLS], kind=dk)
        self.mixa = self.dram("mixa_scr", [NROWS_X, 512], BF16, kind=dk)
        self.xscr = self.dram("x_scr", [NROWS_X, D], kind=dk)
        self.proj_h = {}
        self.mixa_h = {}
        self.x_h = {}

        self.con = self.sb(es, "con", [128, NCON])
        self.conb = self.sb(es, "conb", [128, 256 + 96], BF16)
        self.dma(self.con[:], self.consts.t, [], [self.con])
        self.dma(self.conb[:, 0:256], self.consts.t[:, 0:256], [], [self.conb], q="pool")
        self.dma(self.conb[:, 256:352], self.consts.t[:, C_SELR:C_SELR + 96], [], [self.conb], q="pool")
        self.psF = [self.ps(es, "psF%d" % i, [128, 512]) for i in range(6)]
        self.psB = [self.ps(es, "psB%d" % i, [128, 1024], BF16) for i in range(2)]
        self.rotB = Rot(self.psB)
        self.fin_bc = self.sb(es, "fin_bc", [128, D])
        self.dma(self.fin_bc[:], self.final_norm.t.partition_broadcast(128), [], [self.fin_bc])

        for l in range(self.nlayers):
            if "P" in self.phases:
                self.phase_P(l)
                self.S.barrier()
            if "A" in self.phases:
                self.phase_A(l)
                self.S.barrier()
            if "R" in self.phases:
                self.phase_R(l)
                self.S.barrier()
        self.S.finalize()
        es.close()
        return nc

    def ph(self, key):
        if key not in self.proj_h:
            self.proj_h[key] = H("proj%s" % (key,))
        return self.proj_h[key]

    def mh(self, key):
        if key not in self.mixa_h:
            self.mixa_h[key] = H("mixa%s" % (key,))
        return self.mixa_h[key]

    def xh(self, key):
        if key not in self.x_h:
            self.x_h[key] = H("x%s" % (key,))
        return self.x_h[key]

    def ident(self, n, bf=False):
        if bf:
            return self.conb[0:n, 0:n]
        return self.con[0:n, C_ID:C_ID + n]

    def load_x(self, l, tile, xt):
        if tile == "aux":
            if l == 0:
                self.dma(xt[0:16, :], self.meta.t, [], [xt])
                self.dma(xt[16:80, :], self.xs.t, [], [xt])
            else:
                self.dma(xt[0:80, :], self.xscr.t[0:80, :], [self.xh("aux")], [xt])
            return 80
        j = tile
        if l == 0:
            self.dma(xt[:, :], self.xp.t[j * 128:(j + 1) * 128, :], [], [xt])
        else:
            self.dma(xt[:, :], self.xscr.t[80 + j * 128:80 + (j + 1) * 128, :], [self.xh(j)], [xt])
        return 128

    def phase_P(self, l):
        con, conb = self.con, self.conb
        with contextlib.ExitStack() as es:
            w_bf = self.sb(es, "P_w", [128, 8, IN_COLS], BF16)
            wv = self.w_in.t[l].rearrange("(k p) c -> p k c", p=128)
            ngw = (IN_COLS + 511) // 512
            w_h = [H("w_in_g%d" % g_) for g_ in range(ngw)]
            for g_ in range(ngw):
                c0_ = g_ * 512
                cw_ = min(512, IN_COLS - c0_)
                self.dma(w_bf[:, :, c0_:c0_ + cw_], wv[:, :, c0_:c0_ + cw_], [], [w_h[g_]], q="pool")
            nw = self.sb(es, "P_nw", [128, D])
            self.dma(nw[:], self.norm_w.t[l].partition_broadcast(128), [], [nw])
            xts = Rot([self.sb(es, "P_x%d" % i, [128, D]) for i in range(2)])
            junk = self.sb(es, "P_junk", [128, D], BF16)
            ssq = Rot([self.sb(es, "P_ssq%d" % i, [128, 1]) for i in range(2)])
            rs = Rot([self.sb(es, "P_rs%d" % i, [128, 1]) for i in range(2)])
            xns = Rot([self.sb(es, "P_xn%d" % i, [128, D], BF16) for i in range(2)])
            xTs = Rot([self.sb(es, "P_xT%d" % i, [128, 8, 128], BF16) for i in range(2)])
            prs = Rot([self.sb(es, "P_pr%d" % i, [128, IN_COLS]) for i in range(2)])
            rotF = Rot(self.psF)
            zt = self.sb(es, "P_z", [4, 1536])
            self.memset("pool", zt[:], 0.0, [zt])
            self.dma(self.proj.t[0:3, 3752:5288], zt[0:3, :], [zt], [self.ph("hist")])
            for b in range(NS):
                r0 = 4115 + 19 * b
                self.dma(self.proj.t[r0:r0 + 3, 3752:5288], self.igc.t[l, b], [], [self.ph("hist")])
            for tile in ["aux"] + list(range(SEQ // 128)):
                xt = xts.get()
                T = self.load_x(l, tile, xt)
                sq, r_ = ssq.get(), rs.get()
                self.act(junk[0:T, :], xt[0:T, :], AF.Square, [xt], [junk, sq], accum=sq[0:T, :])
                self.rstd(r_[0:T, :], sq[0:T, :], D, [sq], [r_])
                xn = xns.get()
                self.stt(xn[0:T, :], xt[0:T, :], r_[0:T, 0:1], nw[0:T, :], ALU.mult, ALU.mult, [xt, r_, nw], [xn])
                pb = self.rotB.get()
                for k in range(8):
                    self.tr(pb[0:128, k * 128:k * 128 + T], xn[0:T, k * 128:(k + 1) * 128], self.ident(T, True), [xn, conb], [pb])
                xT = xTs.get()
                pbv = pb[:, :].rearrange("p (k t) -> p k t", k=8)
                self.cp("dve", xT[:, :, 0:T], pbv[:, :, 0:T], [pb], [xT])
                Rot.rel(pb, xt, sq, r_, xn)
                pr = prs.get()
                ng = (IN_COLS + 511) // 512
                for g in range(ng):
                    c0 = g * 512
                    cw = min(512, IN_COLS - c0)
                    pf = rotF.get()
                    for k in range(8):
                        self.mm(pf[0:T, 0:cw], xT[:, k, 0:T], w_bf[:, k, c0:c0 + cw], k == 0, k == 7, [xT, w_h[g]], [pf])
                    if g % 2 == 0:
                        self.cp("act", pr[0:T, c0:c0 + cw], pf[0:T, 0:cw], [pf], [pr])
                    else:
                        self.cp("dve", pr[0:T, c0:c0 + cw], pf[0:T, 0:cw], [pf], [pr])
                    Rot.rel(pf)
                Rot.rel(xT)
                if tile == "aux":
                    self.dma(self.proj.t[3:19, :], pr[0:16, :], [pr], [self.ph("aux")])
                    for b in range(NS):
                        r0 = 4118 + 19 * b
                        self.dma(self.proj.t[r0:r0 + 16, :], pr[16 + 16 * b:32 + 16 * b, :], [pr], [self.ph("aux")])
                else:
                    r0 = 19 + tile * 128
                    self.dma(self.proj.t[r0:r0 + 128, :], pr[:, :], [pr], [self.ph(tile)])
                Rot.rel(pr)

    def phase_A(self, l):
        con, conb = self.con, self.conb
        with contextlib.ExitStack() as es:
            A = type("NS", (), {})()
            wuq = self.sb(es, "A_wuq", [128, 3, 768], BF16)
            self.dma(wuq[:], self.w_uq.t[l].rearrange("(k p) c -> p k c", p=128), [], [wuq], q="pool")
            wuk = self.sb(es, "A_wuk", [128, 2, 8, 96], BF16)
            self.memset("pool", wuk[:], 0.0, [wuk])
            for k in range(2):
                self.dma(wuk[:, k, :, 0:64], self.w_uk.t[l, k * 128:(k + 1) * 128, :].rearrange("p (h d) -> p h d", h=8), [], [wuk], q="pool")
            wuv = self.sb(es, "A_wuv", [128, 2, 512], BF16)
            self.dma(wuv[:], self.w_uv.t[l].rearrange("(k p) c -> p k c", p=128), [], [wuv], q="pool")
            qn_bc = self.sb(es, "A_qn", [128, 384])
            self.dma(qn_bc[:], self.q_norm.t[l].partition_broadcast(128), [], [qn_bc])
            kvn_bc = self.sb(es, "A_kvn", [128, 256])
            self.dma(kvn_bc[:], self.kv_norm.t[l].partition_broadcast(128), [], [kvn_bc])
            KT = self.sb(es, "A_KT", [96, 8, NCACHE + DS], BF16)
            V = self.sb(es, "A_V", [128, 34, 8, 65], BF16)
            self.memset("pool", V[:], 1.0, [V])
            slot_h = [H("slot%d" % i) for i in range(34)]
            for i in range(34):
                slot_h[i].w = V.h.w
            A.wuq, A.wuk, A.wuv, A.qn_bc, A.kvn_bc, A.KT, A.V, A.slot_h = wuq, wuk, wuv, qn_bc, kvn_bc, KT, V, slot_h
            A.junk = self.sb(es, "A_junk", [128, 384], BF16)
            A.ssq = Rot([self.sb(es, "A_ssq%d" % i, [128, 1]) for i in range(4)])
            A.rs = Rot([self.sb(es, "A_rs%d" % i, [128, 1]) for i in range(4)])
            A.cqn = Rot([self.sb(es, "A_cqn%d" % i, [128, 384], BF16) for i in range(2)])
            A.cqT = Rot([self.sb(es, "A_cqT%d" % i, [128, 3, 128], BF16) for i in range(2)])
            A.qsb = Rot([self.sb(es, "A_qsb%d" % i, [128, 8, 96], BF16) for i in range(2)])
            A.tA = self.sb(es, "A_tA", [128, 4, 16])
            A.tB = self.sb(es, "A_tB", [128, 4, 16])
            A.cn = Rot([self.sb(es, "A_cn%d" % i, [128, 256]) for i in range(2)])
            A.kr = Rot([self.sb(es, "A_kr%d" % i, [128, 32]) for i in range(2)])
            A.cbf = Rot([self.sb(es, "A_cbf%d" % i, [128, 288], BF16) for i in range(2)])
            A.QT = Rot([self.sb(es, "A_QT%d" % i, [96, 8, 128], BF16) for i in range(2)])
            cTs = [self.sb(es, "A_cT%d" % i, [128, 3, 128], BF16) for i in range(3)]
            for c_ in cTs:
                self.memset("pool", c_[:], 0.0, [c_])
            A.cT = Rot(cTs)
            A.PT = Rot([self.sb(es, "A_PT%d" % i, [128, 4, 128], BF16) for i in range(3)])
            A.rc = Rot([self.sb(es, "A_rc%d" % i, [128, 8]) for i in range(2)])
            A.oa = Rot([self.sb(es, "A_oa%d" % i, [128, 8, 64]) for i in range(2)])
            A.sz = Rot([self.sb(es, "A_sz%d" % i, [128, 512]) for i in range(2)])
            A.mx = Rot([self.sb(es, "A_mx%d" % i, [128, 512], BF16) for i in range(2)])
            A.pr = Rot([self.sb(es, "A_pr%d" % i, [128, 1184]) for i in range(2)])
            A.tab = Rot([self.sb(es, "A_tab%d" % i, [128, 32]) for i in range(2)])
            A.zseg = Rot([self.sb(es, "A_zs%d" % i, [16, 512]) for i in range(2)])
            A.cbig = Rot([self.sb(es, "A_cbig%d" % i, [128, 4, 288], BF16) for i in range(2)])
            A.rotF = Rot(self.psF[0:4])
            A.O = (self.psF[4], self.psF[5])
            pr_aux = self.sb(es, "A_praux", [80, 1184])
            tab_aux = self.sb(es, "A_tabaux", [80, 32])
            QT_aux = self.sb(es, "A_QTaux", [96, 8, 80], BF16)
            cT_aux = self.sb(es, "A_cTaux", [128, 3, 80], BF16)
            self.memset("pool", cT_aux[:], 0.0, [cT_aux])
            self.dma(pr_aux[0:16, :], self.proj.t[3:19, 0:1184], [self.ph("aux")], [pr_aux])
            self.dma(tab_aux[0:16, :], self.rope.t[0:16, :], [], [tab_aux])
            for b in range(NS):
                r0 = 4118 + 19 * b
                self.dma(pr_aux[16 + 16 * b:32 + 16 * b, :], self.proj.t[r0:r0 + 16, 0:1184], [self.ph("aux")], [pr_aux])
                self.dma(tab_aux[16 + 16 * b:32 + 16 * b, :], self.rope.t[NCACHE:NCACHE + DS, :], [], [tab_aux])
            cn, kr = self.mla_proj(A, 80, pr_aux, tab_aux, QT_aux, cT_aux)
            self.dma(self.p_lat.t[l, 0:16, :], cn[0:16, :], [cn], [], is_out=True)
            self.dma(self.p_kr.t[l, 0:16, :], kr[0:16, :], [kr], [], is_out=True)
            for b in range(NS):
                self.dma(self.s_lat.t[l, b], cn[16 + 16 * b:32 + 16 * b, :], [cn], [], is_out=True)
                self.dma(self.s_kr.t[l, b], kr[16 + 16 * b:32 + 16 * b, :], [kr], [], is_out=True)
            Rot.rel(cn, kr)
            self.kv_build(A, cT_aux, 0, 16, 0)
            zs = A.zseg.get()
            self.dma(zs[:, :], self.proj.t[3:19, 672:1184], [self.ph("aux")], [zs])
            self.attend(A, 16, QT_aux, 0, [(0, 16)], None, zs, 0, self.mixa.t[0:16, :], self.mh("aux"))
            Rot.rel(zs)
            prepped = {}

            def prep(j):
                pr = A.pr.get()
                tab = A.tab.get()
                r0 = 19 + 128 * j
                self.dma(pr[:, :], self.proj.t[r0:r0 + 128, 0:1184], [self.ph(j)], [pr])
                self.dma(tab[:, :], self.rope.t[16 + 128 * j:16 + 128 * (j + 1), :], [], [tab])
                QT = A.QT.get()
                cT = A.cT.get()
                yield
                for _ in self.mla_proj_g(A, 128, pr, tab, QT, cT, prepped, j):
                    yield
                cn, kr = prepped[("ck", j)]
                self.dma(self.p_lat.t[l, 16 + 128 * j:16 + 128 * (j + 1), :], cn[:, :], [cn], [], is_out=True)
                self.dma(self.p_kr.t[l, 16 + 128 * j:16 + 128 * (j + 1), :], kr[:, :], [kr], [], is_out=True)
                Rot.rel(cn, kr, tab)
                yield
                for _ in self.kv_build_g(A, cT, 0, 128, 1 + j):
                    yield
                Rot.rel(cT)
                prepped[j] = (QT, pr)

            for _ in prep(0):
                pass
            NT = SEQ // 128
            for j in range(NT):
                QT, pr = prepped.pop(j)
                blocks = [(0, 16)] + [(1 + i, 128) for i in range(j + 1)]
                att = self.attend_g(A, 128, QT, 0, blocks, 1 + j, pr, 672, self.mixa.t[80 + 128 * j:80 + 128 * (j + 1), :], self.mh(j))
                self.interleave([att, prep(j + 1) if j + 1 < NT else None])
                Rot.rel(QT, pr)
            for b in range(NS):
                cb = A.cbig.get()
                self.dma(cb[0:16, 0, 0:256], self.cl.t[l, b, 0:16, :], [], [cb], q="pool")
                self.dma(cb[0:16, 0, 256:288], self.ck.t[l, b, 0:16, :], [], [cb], q="pool")
                self.cache_block(A, cb, 0, 16, 0)
                Rot.rel(cb)
                for g in range(8):
                    cb = A.cbig.get()
                    r0 = 16 + 512 * g
                    self.dma(cb[:, :, 0:256], self.cl.t[l, b, r0:r0 + 512, :].rearrange("(j p) c -> p j c", p=128), [], [cb], q="pool")
                    self.dma(cb[:, :, 256:288], self.ck.t[l, b, r0:r0 + 512, :].rearrange("(j p) c -> p j c", p=128), [], [cb], q="pool")
                    for jj in range(4):
                        self.cache_block(A, cb, jj, 128, 1 + 4 * g + jj)
                    Rot.rel(cb)
                self.kv_build(A, cT_aux, 16 + 16 * b, 16, 33)
                zs = A.zseg.get()
                r0 = 4118 + 19 * b
                self.dma(zs[:, :], self.proj.t[r0:r0 + 16, 672:1184], [self.ph("aux")], [zs])
                blocks = [(0, 16)] + [(1 + i, 128) for i in range(32)] + [(33, 16)]
                self.attend(A, 16, QT_aux, 16 + 16 * b, blocks, None, zs, 0,
                            self.mixa.t[16 + 16 * b:32 + 16 * b, :], self.mh("aux"))
                Rot.rel(zs)

    def rope_apply(self, A, T, o1, o2, x1, x2, cos, sin, tA, tB, r, w):
        self.tt("dve", tA, x1, cos, ALU.mult, r, [A.tA])
        self.tt("dve", tB, x2, sin, ALU.mult, r, [A.tB])
        self.tt("dve", o1, tA, tB, ALU.subtract, [A.tA, A.tB], w)
        self.tt("dve", tA, x2, cos, ALU.mult, r, [A.tA])
        self.tt("dve", tB, x1, sin, ALU.mult, r, [A.tB])
        self.tt("dve", o2, tA, tB, ALU.add, [A.tA, A.tB], w)

    def mla_proj(self, A, T, pr, tab, QT, cT):
        d = {}
        for _ in self.mla_proj_g(A, T, pr, tab, QT, cT, d, 0):
            pass
        return d[("ck", 0)]

    def mla_proj_g(self, A, T, pr, tab, QT, cT, outd, key):
        conb = self.conb
        idb = self.ident(T, True)
        sq, r_ = A.ssq.get(), A.rs.get()
        self.act(A.junk[0:T, 0:384], pr[0:T, 0:384], AF.Square, [pr], [A.junk, sq], accum=sq[0:T, :])
        self.rstd(r_[0:T, :], sq[0:T, :], 384, [sq], [r_])
        cqn = A.cqn.get()
        self.stt(cqn[0:T, :], pr[0:T, 0:384], r_[0:T, 0:1], A.qn_bc[0:T, :], ALU.mult, ALU.mult, [pr, r_, A.qn_bc], [cqn])
        pb = self.rotB.get()
        for k in range(3):
            self.tr(pb[0:128, k * 128:k * 128 + T], cqn[0:T, k * 128:(k + 1) * 128], idb, [cqn, conb], [pb])
        cqT = A.cqT.get()
        self.cp("dve", cqT[:, :, 0:T], pb[:, 0:384].rearrange("p (k t) -> p k t", k=3)[:, :, 0:T], [pb], [cqT])
        Rot.rel(pb, sq, r_, cqn)
        yield
        q_sb = A.qsb.get()
        cos4 = tab[0:T, 0:16].unsqueeze(1).broadcast_to([T, 4, 16])
        sin4 = tab[0:T, 16:32].unsqueeze(1).broadcast_to([T, 4, 16])
        for half in range(2):
            pf = A.rotF.get()
            for k in range(3):
                self.mm(pf[0:T, 0:384], cqT[:, k, 0:T], A.wuq[:, k, half * 384:(half + 1) * 384], k == 0, k == 2, [cqT, A.wuq], [pf])
            pv = pf[0:T, 0:384].rearrange("p (h d) -> p h d", h=4)
            hs = slice(half * 4, half * 4 + 4)
            self.cp("dve", q_sb[0:T, hs, 0:64], pv[:, :, 0:64], [pf], [q_sb])
            self.rope_apply(A, T, q_sb[0:T, hs, 64:80], q_sb[0:T, hs, 80:96], pv[:, :, 64:80], pv[:, :, 80:96],
                            cos4, sin4, A.tA[0:T, :, :], A.tB[0:T, :, :], [pf, tab], [q_sb])
            Rot.rel(pf)
            yield
        pb = self.rotB.get()
        for h in range(8):
            self.tr(pb[0:96, h * 128:h * 128 + T], q_sb[0:T, h, :], idb, [q_sb, conb], [pb])
        self.cp("dve", QT[0:96, :, 0:T], pb[0:96, :].rearrange("p (h t) -> p h t", h=8)[:, :, 0:T], [pb], [QT])
        Rot.rel(pb, q_sb, cqT)
        yield
        sq, r_ = A.ssq.get(), A.rs.get()
        self.act(A.junk[0:T, 0:256], pr[0:T, 384:640], AF.Square, [pr], [A.junk, sq], accum=sq[0:T, :])
        self.rstd(r_[0:T, :], sq[0:T, :], 256, [sq], [r_])
        cn = A.cn.get()
        self.stt(cn[0:T, :], pr[0:T, 384:640], r_[0:T, 0:1], A.kvn_bc[0:T, :], ALU.mult, ALU.mult, [pr, r_, A.kvn_bc], [cn])
        kr = A.kr.get()
        self.rope_apply(A, T, kr[0:T, 0:16], kr[0:T, 16:32], pr[0:T, 640:656], pr[0:T, 656:672],
                        tab[0:T, 0:16], tab[0:T, 16:32], A.tA[0:T, 0, :], A.tB[0:T, 0, :], [pr, tab], [kr])
        cbf = A.cbf.get()
        self.cp("pool", cbf[0:T, 0:256], cn[0:T, :], [cn], [cbf])
        self.cp("pool", cbf[0:T, 256:288], kr[0:T, :], [kr], [cbf])
        self.c_transpose(A, cbf[0:T, :], T, cT, [cbf])
        Rot.rel(sq, r_, cbf)
        outd[("ck", key)] = (cn, kr)
        yield

    def c_transpose(self, A, src, n, cT, r):
        conb = self.conb
        idb = self.ident(n, True)
        pb = self.rotB.get()
        self.tr(pb[0:128, 0:n], src[:, 0:128], idb, r + [conb], [pb])
        self.tr(pb[0:128, 128:128 + n], src[:, 128:256], idb, r + [conb], [pb])
        self.tr(pb[0:32, 256:256 + n], src[:, 256:288], idb, r + [conb], [pb])
        self.cp("dve", cT[:, 0:2, 0:n], pb[:, 0:256].rearrange("p (k t) -> p k t", k=2)[:, :, 0:n], [pb], [cT])
        self.cp("dve", cT[0:32, 2, 0:n], pb[0:32, 256:256 + n], [pb], [cT])
        Rot.rel(pb)

    def cache_block(self, A, cb, jj, n, slot):
        cT = A.cT.get()
        self.c_transpose(A, cb[0:n, jj, :], n, cT, [cb])
        self.kv_build(A, cT, 0, n, slot)
        Rot.rel(cT)

    def kv_build(self, A, cT, coff, n, slot):
        for _ in self.kv_build_g(A, cT, coff, n, slot):
            pass

    def kv_build_g(self, A, cT, coff, n, slot):
        conb = self.conb
        kcol = 0 if slot == 0 else 16 + 128 * (slot - 1)
        sh = A.slot_h[slot]
        for half in range(2):
            pf = A.rotF.get()
            for hh in range(4):
                h = half * 4 + hh
                o = pf[0:96, hh * 128:hh * 128 + n]
                self.mm(o, A.wuk[:, 0, h, :], cT[:, 0, coff:coff + n], True, False, [A.wuk, cT], [pf])
                self.mm(o, A.wuk[:, 1, h, :], cT[:, 1, coff:coff + n], False, False, [A.wuk, cT], [pf])
                self.mm(o, conb[:, 256:352], cT[:, 2, coff:coff + n], False, True, [conb, cT], [pf])
            src = pf[0:96, :].rearrange("p (h t) -> p h t", h=4)[:, :, 0:n]
            self.cp("dve", A.KT[0:96, half * 4:half * 4 + 4, kcol:kcol + n], src, [pf], [sh])
            Rot.rel(pf)
            yield
        pf = A.rotF.get()
        self.mm(pf[0:n, 0:512], cT[:, 0, coff:coff + n], A.wuv[:, 0, :], True, False, [cT, A.wuv], [pf])
        self.mm(pf[0:n, 0:512], cT[:, 1, coff:coff + n], A.wuv[:, 1, :], False, True, [cT, A.wuv], [pf])
        self.cp("dve", A.V[0:n, slot, :, 0:64], pf[0:n, 0:512].rearrange("p (h d) -> p h d", h=8), [pf], [sh])
        Rot.rel(pf)
        yield

    def attend(self, A, T, QT, qoff, blocks, diag_slot, zsrc, zoff, mix_out, mix_h):
        for _ in self.attend_g(A, T, QT, qoff, blocks, diag_slot, zsrc, zoff, mix_out, mix_h):
            pass

    def attend_g(self, A, T, QT, qoff, blocks, diag_slot, zsrc, zoff, mix_out, mix_h):
        groups = []
        for bl in blocks:
            if groups and groups[-1][0][1] == bl[1] and len(groups[-1]) < 4:
                groups[-1].append(bl)
            else:
                groups.append([bl])
        nblk = len(blocks)
        work = []
        for h in range(8):
            nb = 0
            for grp in groups:
                work.append((h, grp, nb))
                nb += len(grp)

        def finish(item):
            h, grp, nb0, pf = item
            Ob = A.O[h // 4]
            hh = h % 4
            n = grp[0][1]
            g = len(grp)
            PT = A.PT.get()
            self.act(PT[0:n, 0:g, 0:T], pf[0:n, 0:g * 128].rearrange("p (g t) -> p g t", g=g)[:, :, 0:T], AF.Exp,
                     [pf], [PT], scale=SM_SCALE)
            Rot.rel(pf)
            for i, (slot, _) in enumerate(grp):
                if diag_slot is not None and slot == diag_slot:
                    self.memset("pool", PT[64:128, i, 0:64], 0.0, [PT])
            for i, (slot, _) in enumerate(grp):
                self.mm(Ob[0:T, hh * 65:hh * 65 + 65], PT[0:n, i, 0:T], A.V[0:n, slot, h, :],
                        nb0 + i == 0, nb0 + i == nblk - 1, [PT, A.slot_h[slot]], [Ob])
            Rot.rel(PT)

        pend = None
        for (h, grp, nb0) in work:
            n = grp[0][1]
            pf = A.rotF.get()
            for i, (slot, _) in enumerate(grp):
                kcol = 0 if slot == 0 else 16 + 128 * (slot - 1)
                self.mm(pf[0:n, i * 128:i * 128 + T], A.KT[0:96, h, kcol:kcol + n], QT[0:96, h, qoff:qoff + T],
                        True, True, [A.slot_h[slot], QT], [pf])
            if pend is not None:
                finish(pend)
            pend = (h, grp, nb0, pf)
            yield
        finish(pend)
        yield
        rc = A.rc.get()
        oa = A.oa.get()
        for half in range(2):
            Ov = A.O[half][0:T, 0:260].rearrange("p (h d) -> p h d", h=4)
            rcv = rc[0:T, half * 4:half * 4 + 4].unsqueeze(2)
            self.recip(rcv, Ov[:, :, 64:65], [A.O[half]], [rc])
            self.tt("dve", oa[0:T, half * 4:half * 4 + 4, :], Ov[:, :, 0:64], rcv.broadcast_to([T, 4, 64]), ALU.mult,
                    [A.O[half], rc], [oa])
        sz = A.sz.get()
        self.act(sz[0:T, :], zsrc[0:T, zoff:zoff + 512], AF.Silu, [zsrc], [sz])
        mx = A.mx.get()
        self.tt("dve", mx[0:T, :], oa[0:T, :, :].rearrange("p h d -> p (h d)"), sz[0:T, :], ALU.mult, [oa, sz], [mx])
        self.dma(mix_out, mx[0:T, :], [mx], [mix_h])
        Rot.rel(rc, oa, sz, mx)

    def phase_R(self, l):
        con, conb = self.con, self.conb
        with contextlib.ExitStack() as es:
            R = type("NS", (), {})()
            wout = self.sb(es, "R_wout", [128, 12, D], BF16)
            wov = self.w_out.t[l].rearrange("(k p) c -> p k c", p=128)
            for k in range(0, 12, 4):
                self.dma(wout[:, k:k + 4, :], wov[:, k:k + 4, :], [], [wout], q="pool")
            cw = self.sb(es, "R_cw", [128, 4 * 1536])
            self.dma(cw[:], self.conv_w.t[l].partition_broadcast(128), [], [cw])
            mnorm = self.sb(es, "R_mnorm", [128, 512])
            self.dma(mnorm[:], self.m_norm.t[l].partition_broadcast(128), [], [mnorm])
            gnorm = self.sb(es, "R_gnorm", [128, 128])
            self.dma(gnorm[:], self.g_norm.t[l].partition_broadcast(128), [], [gnorm])
            gb = self.sb(es, "R_gb", [128, 8])
            self.dma(gb[:], self.gate_b.t[l].partition_broadcast(128), [], [gb])
            dtb = self.sb(es, "R_dtb", [128, 4])
            self.dma(dtb[:], self.dt_bias.t[l].partition_broadcast(128), [], [dtb])
            nea = self.sb(es, "R_nea", [128, 4])
            self.dma(nea[:], self.a_log.t[l].partition_broadcast(128), [], [nea])
            self.act(nea[:], nea[:], AF.Exp, [nea], [nea])
            self.ts("dve", nea[:], nea[:], -1.0, ALU.mult, [nea], [nea])
            R.wout, R.cw, R.mnorm, R.gnorm, R.gb, R.dtb, R.nea = wout, cw, mnorm, gnorm, gb, dtb, nea

            def pool(name, shape, dt, n):
                return Rot([self.sb(es, "R_%s%d" % (name, i), shape, dt) for i in range(n)])
            R.MP = pool("mp", [64, 2568], F32, 1)
            R.GA = pool("ga", [64, 520], F32, 2)
            R.F1536 = pool("f1536", [64, 1536], F32, 4)
            R.F1024 = pool("f1024", [64, 1024], F32, 2)
            R.F512 = pool("f512", [64, 512], F32, 11)
            R.F256 = pool("f256", [64, 256], F32, 16)
            R.B1024 = pool("b1024", [64, 1024], BF16, 2)
            R.B512 = pool("b512", [64, 512], BF16, 10)
            R.B256 = pool("b256", [64, 256], BF16, 4)
            R.BT = pool("bt", [128, 768], BF16, 6)
            R.SM = pool("sm", [128, 12], F32, 64)
            R.X = pool("x", [128, D], F32, 2)
            R.XN = pool("xn", [128, D], F32, 1)
            R.MIXT = pool("mixt", [128, 12, 128], BF16, 2)
            R.MA = pool("ma", [128, 512], BF16, 1)
            R.rotF = Rot(self.psF)

            def state(nm):
                st = type("NS", (), {})()
                st.CT = self.sb(es, "R_CT" + nm, [128, 4, 128])
                st.CTb = self.sb(es, "R_CTb" + nm, [128, 4, 128], BF16)
                st.nT = self.sb(es, "R_nT" + nm, [128, 4])
                st.nTb = self.sb(es, "R_nTb" + nm, [128, 4], BF16)
                st.mbc = self.sb(es, "R_mbc" + nm, [128, 4])
                st.S = self.sb(es, "R_S" + nm, [128, 4, 128])
                st.Sb = self.sb(es, "R_Sb" + nm, [128, 4, 128], BF16)
                return st
            stp = state("p")
            sts = state("s")
            R.tmpC = self.sb(es, "R_tmpC", [128, 4, 128])
            for t_ in (stp.CT, stp.CTb, stp.nT, stp.nTb, stp.mbc, stp.S, stp.Sb):
                self.memset("pool", t_[:], 0.0, [t_])

            items = []
            mixA = R.MIXT.get()
            items.append(dict(T=16, row0=3, st=stp, mixT=mixA, coff=0, before=None, after=None))
            for b in range(NS):
                r0 = 4118 + 19 * b

                def bef(b=b):
                    self.load_state(R, l, b, sts)

                def aft(b=b, r0=r0, last=(b == NS - 1)):
                    self.store_state(R, sts, self.s_C.t[l, b], self.s_n.t[l, b], self.s_m.t[l, b:b + 1, :], self.s_S.t[l, b])
                    self.dma(self.s_cv.t[l, b], self.proj.t[r0 + 13:r0 + 16, 3752:5288], [], [], is_out=True)
                    if last:
                        self.out_proj(R, l, "aux", 80, mixA)
                        Rot.rel(mixA)
                items.append(dict(T=16, row0=r0, st=sts, mixT=mixA, coff=16 + 16 * b, before=bef, after=aft))
            cur = {}
            for j in range(SEQ // 128):
                for c in range(2):
                    def bef(j=j, c=c):
                        if c == 0:
                            cur["mixT"] = R.MIXT.get()

                    def aft(j=j, c=c):
                        if c == 1:
                            self.out_proj(R, l, j, 128, cur["mixT"])
                            Rot.rel(cur["mixT"])
                    items.append(dict(T=64, row0=19 + 128 * j + 64 * c, st=stp, mixT=None, coff=64 * c, before=bef, after=aft))
            ctxs = [dict() for _ in items]
            for _ in self.gdn_pre_g(R, l, items[0]["T"], items[0]["row0"], ctxs[0]):
                pass
            for i, it in enumerate(items):
                if it["before"] is not None:
                    it["before"]()
                mixT = it["mixT"] if it["mixT"] is not None else cur["mixT"]
                gens = [self.mlstm_g(R, l, it["T"], it["row0"], it["st"], mixT, it["coff"]),
                        self.gdn_chain_g(R, l, it["T"], it["st"], mixT, it["coff"], ctxs[i])]
                if i + 1 < len(items):
                    nx = items[i + 1]
                    gens.append(self.gdn_pre_g(R, l, nx["T"], nx["row0"], ctxs[i + 1]))
                self.interleave(gens)
                if it["after"] is not None:
                    it["after"]()
            self.store_state(R, stp, self.p_C.t[l], self.p_n.t[l], self.p_m.t[l:l + 1, :], self.p_S.t[l])
            self.dma(self.p_cv.t[l], self.proj.t[19 + SEQ - 3:19 + SEQ, 3752:5288], [], [], is_out=True)

    def load_state(self, R, l, b, st):
        self.dma(R.tmpC[:], self.imC.t[l, b].rearrange("h e d -> e h d"), [], [R.tmpC])
        pf = R.rotF.get()
        for h in range(4):
            self.tr(pf[:, h * 128:(h + 1) * 128], R.tmpC[:, h, :], self.ident(128), [R.tmpC, self.con], [pf])
        self.cp("dve", st.CT[:], pf[:, :].rearrange("p (h e) -> p h e", h=4), [pf], [st.CT])
        Rot.rel(pf)
        self.cp("act", st.CTb[:], st.CT[:], [st.CT], [st.CTb])
        self.dma(st.nT[:], self.imn.t[l, b].rearrange("h d -> d h"), [], [st.nT], nonc=True)
        self.cp("act", st.nTb[:], st.nT[:], [st.nT], [st.nTb])
        self.dma(st.mbc[:], self.imm.t[l, b].partition_broadcast(128), [], [st.mbc])
        self.dma(st.S[:], self.igS.t[l, b].rearrange("h k v -> k h v"), [], [st.S])
        self.cp("act", st.Sb[:], st.S[:], [st.S], [st.Sb])

    def store_state(self, R, st, oC, on, om, oS):
        pf = R.rotF.get()
        for h in range(4):
            self.tr(pf[:, h * 128:(h + 1) * 128], st.CT[:, h, :], self.ident(128), [st.CT, self.con], [pf])
        self.cp("dve", R.tmpC[:], pf[:, :].rearrange("p (h e) -> p h e", h=4), [pf], [R.tmpC])
        Rot.rel(pf)
        self.dma(oC.rearrange("h e d -> e h d"), R.tmpC[:], [R.tmpC], [], is_out=True)
        self.dma(on.rearrange("h d -> d h"), st.nT[:], [st.nT], [], nonc=True, is_out=True)
        self.dma(om, st.mbc[0:1, :], [st.mbc], [], is_out=True)
        self.dma(oS.rearrange("h k v -> k h v"), st.S[:], [st.S], [], is_out=True)

    def out_proj(self, R, l, tile, T, mixT):
        conb = self.conb
        ma = R.MA.get()
        if tile == "aux":
            self.dma(ma[0:T, :], self.mixa.t[0:80, :], [], [ma])
        else:
            self.dma(ma[0:T, :], self.mixa.t[80 + 128 * tile:80 + 128 * (tile + 1), :], [], [ma])
        pb = self.rotB.get()
        for k in range(4):
            self.tr(pb[0:128, k * 128:k * 128 + T], ma[0:T, k * 128:(k + 1) * 128], self.ident(T, True), [ma, conb], [pb])
        self.cp("act", mixT[:, 0:4, 0:T], pb[:, 0:512].rearrange("p (k t) -> p k t", k=4)[:, :, 0:T], [pb], [mixT])
        Rot.rel(pb, ma)
        xt = R.X.get()
        self.load_x(l, tile, xt)
        xn = R.XN.get()
        for half in range(2):
            pf = R.rotF.get()
            for k in range(12):
                self.mm(pf[0:T, 0:512], mixT[:, k, 0:T], R.wout[:, k, half * 512:(half + 1) * 512], k == 0, k == 11, [mixT, R.wout], [pf])
            self.tt("dve", xn[0:T, half * 512:(half + 1) * 512], pf[0:T, 0:512], xt[0:T, half * 512:(half + 1) * 512], ALU.add, [pf, xt], [xn])
            Rot.rel(pf)
        Rot.rel(xt)
        if l < DEPTH - 1:
            if tile == "aux":
                self.dma(self.xscr.t[0:80, :], xn[0:80, :], [xn], [self.xh("aux")])
            else:
                self.dma(self.xscr.t[80 + 128 * tile:80 + 128 * (tile + 1), :], xn[:, :], [xn], [self.xh(tile)])
        else:
            sq = R.SM.get()
            yo = R.X.get()
            self.act(yo[0:T, :], xn[0:T, :], AF.Square, [xn], [yo, sq], accum=sq[0:T, 0:1])
            self.rstd(sq[0:T, 0:1], sq[0:T, 0:1], D, [sq], [sq])
            self.stt(yo[0:T, :], xn[0:T, :], sq[0:T, 0:1], self.fin_bc[0:T, :], ALU.mult, ALU.mult, [xn, sq, self.fin_bc], [yo])
            if tile == "aux":
                self.dma(self.y_s.t[:, :], yo[16:80, :], [yo], [], is_out=True)
            else:
                self.dma(self.y_p.t[128 * tile:128 * (tile + 1), :], yo[:, :], [yo], [], is_out=True)
            Rot.rel(sq, yo)
        Rot.rel(xn)

    @staticmethod
    def v3(ap, a):
        return ap.rearrange("p (a b) -> p a b", a=a)

    @staticmethod
    def bch(ap, T, n):
        return ap.unsqueeze(2).broadcast_to([T, 4, n])

    @staticmethod
    def bcm(ap, T, n):
        return ap.unsqueeze(1).broadcast_to([T, 4, n])

    def mlstm_chunk(self, R, l, T, row0, st, mixT, coff):
        for _ in self.mlstm_g(R, l, T, row0, st, mixT, coff):
            pass

    def mlstm_g(self, R, l, T, row0, st, mixT, coff):
        con, conb = self.con, self.conb
        L = []

        def g(p):
            b = p.get()
            L.append(b)
            return b
        P = slice(0, T)
        T4 = 4 * T
        tri = con[0:T, C_TRI:C_TRI + T]
        SLm = con[0:T, C_SL:C_SL + T]
        onesTT = con[0:T, C_ONE:C_ONE + T]
        idT = con[0:T, C_ID:C_ID + T]
        NEG = con[0:T, C_NEG:C_NEG + T]
        cs_ = C_S64 if T == 64 else C_S16
        sel = con[0:T, cs_:cs_ + 128]
        idb = self.ident(T, True)
        v3, bch, bcm = self.v3, self.bch, self.bcm
        KS = 128.0 ** -0.5
        mp = g(R.MP)
        self.dma(mp[P, :], self.proj.t[row0:row0 + T, 1184:3752], [], [mp])
        k32 = g(R.F512)
        self.dma(k32[P, :], self.proj.t[row0:row0 + T, 1696:2208], [], [k32])
        qkb = g(R.B1024)
        self.cp("act", qkb[P, 0:512], mp[P, 0:512], [mp], [qkb])
        self.act(qkb[P, 512:1024], mp[P, 512:1024], AF.Copy, [mp], [qkb], scale=KS)
        pb = self.rotB.get()
        for i in range(8):
            self.tr(pb[0:128, i * T:(i + 1) * T], qkb[P, i * 128:(i + 1) * 128], idb, [qkb, conb], [pb])
        qkT = g(R.BT)
        self.cp("dve", qkT[:, 0:8 * T], pb[:, 0:8 * T], [pb], [qkT])
        Rot.rel(pb)
        vb = g(R.B512)
        self.cp("pool", vb[P, :], mp[P, 1024:1536], [mp], [vb])
        yield
        g8 = g(R.SM)
        self.tt("dve", g8[P, 0:8], mp[P, 1536:1544], R.gb[P, 0:8], ALU.add, [mp, R.gb], [g8])
        ipre, xf = g8[P, 0:4], g8[P, 4:8]
        s1 = g(R.SM)
        self.stt(s1[P, 0:4], xf, -1.0, xf, ALU.mult, ALU.max, [g8], [s1])
        self.act(s1[P, 0:4], s1[P, 0:4], AF.Exp, [s1], [s1], scale=-1.0)
        self.act(s1[P, 0:4], s1[P, 0:4], AF.Ln, [s1], [s1], bias=1.0)
        lf = g(R.SM)
        self.ts("dve", lf[P, 0:4], xf, 0.0, ALU.min, [g8], [lf])
        self.tt("dve", lf[P, 0:4], lf[P, 0:4], s1[P, 0:4], ALU.subtract, [lf, s1], [lf])
        yield
        R1 = g(R.F256)
        self.tt("pool", v3(R1[P, 0:T4], 4), bcm(SLm, T, T), bch(lf[P, 0:4], T, T), ALU.mult, [con, lf], [R1])
        R2 = g(R.F256)
        self.tt("pool", v3(R2[P, 0:T4], 4), bcm(idT, T, T), bch(ipre, T, T), ALU.mult, [con, g8], [R2])
        pf = R.rotF.get()
        self.mm(pf[0:T, 0:T4], tri, R1[P, 0:T4], True, False, [con, R1], [pf])
        self.mm(pf[0:T, 0:T4], onesTT, R2[P, 0:T4], False, True, [con, R2], [pf])
        self.mm(pf[0:T, T4:T4 + 4], tri, lf[P, 0:4], True, True, [con, lf], [pf])
        Dm = g(R.F256)
        self.tt("dve", v3(Dm[P, 0:T4], 4), v3(pf[0:T, 0:T4], 4), bcm(NEG, T, T), ALU.add, [pf, con], [Dm])
        MIB = g(R.SM)
        self.cp("act", MIB[P, 8:12], pf[0:T, T4:T4 + 4], [pf], [MIB])
        Rot.rel(pf)
        rmx = g(R.SM)
        self.red(rmx[P, 0:4], v3(Dm[P, 0:T4], 4), ALU.max, [Dm], [rmx])
        yield
        self.tt("dve", MIB[P, 4:8], MIB[P, 8:12], st.mbc[P, 0:4], ALU.add, [MIB, st.mbc], [MIB])
        self.tt("dve", MIB[P, 0:4], MIB[P, 4:8], rmx[P, 0:4], ALU.max, [MIB, rmx], [MIB])
        E = g(R.F256)
        self.tt("dve", v3(E[P, 0:T4], 4), v3(Dm[P, 0:T4], 4), bch(MIB[P, 0:4], T, T), ALU.subtract, [Dm, MIB], [E])
        self.act(E[P, 0:T4], E[P, 0:T4], AF.Exp, [E], [E])
        wi = g(R.SM)
        self.tt("dve", wi[P, 0:4], MIB[P, 4:8], MIB[P, 0:4], ALU.subtract, [MIB], [wi])
        self.act(wi[P, 0:4], wi[P, 0:4], AF.Exp, [wi], [wi])
        emt = g(R.SM)
        self.act(emt[P, 0:4], MIB[P, 0:4], AF.Exp, [MIB], [emt], scale=-1.0)
        yield
        pf = R.rotF.get()
        for h in range(4):
            self.mm(pf[0:T, h * T:(h + 1) * T], qkT[:, h * T:(h + 1) * T], qkT[:, (4 + h) * T:(5 + h) * T], True, True, [qkT], [pf])
        qkE = g(R.F256)
        self.tt("dve", qkE[P, 0:T4], pf[0:T, 0:T4], E[P, 0:T4], ALU.mult, [pf, E], [qkE])
        Rot.rel(pf)
        den1 = g(R.SM)
        self.red(den1[P, 0:4], v3(qkE[P, 0:T4], 4), ALU.add, [qkE], [den1])
        yield
        pf = R.rotF.get()
        for h in range(4):
            self.tr(pf[0:T, h * T:(h + 1) * T], qkE[P, h * T:(h + 1) * T], idT, [qkE, con], [pf])
        qkET = g(R.B256)
        self.cp("act", qkET[P, 0:T4], pf[0:T, 0:T4], [pf], [qkET])
        Rot.rel(pf)
        yield
        pf1 = R.rotF.get()
        for h in range(4):
            self.mm(pf1[0:T, h * 128:(h + 1) * 128], qkET[P, h * T:(h + 1) * T], vb[P, h * 128:(h + 1) * 128], True, True, [qkET, vb], [pf1])
        num1 = g(R.F512)
        self.cp("act", num1[P, :], pf1[0:T, :], [pf1], [num1])
        Rot.rel(pf1)
        yield
        pf2 = R.rotF.get()
        for h in range(4):
            self.mm(pf2[0:T, h * 128:(h + 1) * 128], qkT[:, h * T:(h + 1) * T], st.CTb[:, h, :], True, True, [qkT, st.CTb], [pf2])
        pf3 = R.rotF.get()
        for h in range(4):
            self.mm(pf3[0:T, h:h + 1], qkT[:, h * T:(h + 1) * T], st.nTb[:, h:h + 1], True, True, [qkT, st.nTb], [pf3])
        num = g(R.F512)
        self.tt("dve", v3(num[P, :], 4), v3(pf2[0:T, :], 4), bch(wi[P, 0:4], T, 128), ALU.mult, [pf2, wi], [num])
        Rot.rel(pf2)
        self.tt("dve", num[P, :], num[P, :], num1[P, :], ALU.add, [num, num1], [num])
        den = g(R.SM)
        self.tt("dve", den[P, 0:4], pf3[0:T, 0:4], wi[P, 0:4], ALU.mult, [pf3, wi], [den])
        Rot.rel(pf3)
        self.tt("dve", den[P, 0:4], den[P, 0:4], den1[P, 0:4], ALU.add, [den, den1], [den])
        self.stt(den[P, 0:4], den[P, 0:4], -1.0, den[P, 0:4], ALU.mult, ALU.max, [den], [den])
        self.tt("dve", den[P, 0:4], den[P, 0:4], emt[P, 0:4], ALU.max, [den, emt], [den])
        self.recip(den[P, 0:4], den[P, 0:4], [den], [den])
        self.tt("dve", v3(num[P, :], 4), v3(num[P, :], 4), bch(den[P, 0:4], T, 128), ALU.mult, [num, den], [num])
        yield
        pf = R.rotF.get()
        self.mm(pf[0:128, 0:12], sel, MIB[P, 0:12], True, True, [con, MIB], [pf])
        LB = g(R.SM)
        self.cp("act", LB[:, 0:12], pf[:, 0:12], [pf], [LB])
        Rot.rel(pf)
        yield
        self.cp("dve", st.mbc[:, 0:4], LB[:, 0:4], [LB], [st.mbc])
        gs = g(R.SM)
        self.tt("dve", gs[:, 0:4], LB[:, 4:8], LB[:, 0:4], ALU.subtract, [LB], [gs])
        self.act(gs[:, 0:4], gs[:, 0:4], AF.Exp, [gs], [gs])
        gt = g(R.SM)
        self.tt("dve", gt[P, 0:4], LB[P, 8:12], MIB[P, 8:12], ALU.subtract, [LB, MIB], [gt])
        self.tt("dve", gt[P, 0:4], gt[P, 0:4], ipre, ALU.add, [gt, g8], [gt])
        self.tt("dve", gt[P, 0:4], gt[P, 0:4], LB[P, 0:4], ALU.subtract, [gt, LB], [gt])
        self.act(gt[P, 0:4], gt[P, 0:4], AF.Exp, [gt], [gt])
        yield
        kg = g(R.B512)
        self.stt(v3(kg[P, :], 4), v3(k32[P, :], 4), KS, bch(gt[P, 0:4], T, 128), ALU.mult, ALU.mult, [k32, gt], [kg])
        pfC = R.rotF.get()
        for h in range(4):
            self.mm(pfC[:, h * 128:(h + 1) * 128], kg[P, h * 128:(h + 1) * 128], vb[P, h * 128:(h + 1) * 128], True, True, [kg, vb], [pfC])
        pfn = R.rotF.get()
        for h in range(4):
            self.mm(pfn[:, h:h + 1], kg[P, h * 128:(h + 1) * 128], conb[0:T, 128:129], True, True, [kg, conb], [pfn])
        self.tt("dve", st.CT[:], st.CT[:], bch(gs[:, 0:4], 128, 128), ALU.mult, [st.CT, gs], [st.CT])
        self.tt("dve", st.CT[:], st.CT[:], v3(pfC[:, :], 4), ALU.add, [st.CT, pfC], [st.CT])
        Rot.rel(pfC)
        self.cp("act", st.CTb[:], st.CT[:], [st.CT], [st.CTb])
        self.tt("dve", st.nT[:], st.nT[:], gs[:, 0:4], ALU.mult, [st.nT, gs], [st.nT])
        self.tt("dve", st.nT[:], st.nT[:], pfn[:, 0:4], ALU.add, [st.nT, pfn], [st.nT])
        Rot.rel(pfn)
        self.cp("act", st.nTb[:], st.nT[:], [st.nT], [st.nTb])
        yield
        sig = g(R.F512)
        self.act(sig[P, :], mp[P, 1544:2056], AF.Exp, [mp], [sig], scale=-1.0)
        self.ts("pool", sig[P, :], sig[P, :], 1.0, ALU.add, [sig], [sig])
        self.recip(sig[P, :], sig[P, :], [sig], [sig])
        szm = g(R.F512)
        self.act(szm[P, :], mp[P, 2056:2568], AF.Silu, [mp], [szm])
        self.tt("pool", szm[P, :], szm[P, :], R.mnorm[P, :], ALU.mult, [szm, R.mnorm], [szm])
        self.tt("dve", num[P, :], num[P, :], sig[P, :], ALU.mult, [num, sig], [num])
        self.tt("pool", sig[P, :], num[P, :], num[P, :], ALU.mult, [num], [sig])
        s4 = g(R.SM)
        self.red(s4[P, 0:4], v3(sig[P, :], 4), ALU.add, [sig], [s4])
        self.rstd(s4[P, 0:4], s4[P, 0:4], 128, [s4], [s4])
        yield
        self.tt("dve", v3(num[P, :], 4), v3(num[P, :], 4), bch(s4[P, 0:4], T, 128), ALU.mult, [num, s4], [num])
        omb = g(R.B512)
        self.tt("dve", omb[P, :], num[P, :], szm[P, :], ALU.mult, [num, szm], [omb])
        pb = self.rotB.get()
        for h in range(4):
            self.tr(pb[0:128, h * T:(h + 1) * T], omb[P, h * 128:(h + 1) * 128], idb, [omb, conb], [pb])
        self.cp("act", mixT[:, 4:8, coff:coff + T], v3(pb[:, 0:T4], 4), [pb], [mixT])
        Rot.rel(pb)
        Rot.rel(*L)
        yield

    def gdn_chunk(self, R, l, T, row0, st, mixT, coff):
        ctx = {}
        for _ in self.gdn_pre_g(R, l, T, row0, ctx):
            pass
        for _ in self.gdn_chain_g(R, l, T, st, mixT, coff, ctx):
            pass

    def gdn_pre_g(self, R, l, T, row0, ctx):
        con, conb = self.con, self.conb
        L = []

        def g(p):
            b = p.get()
            L.append(b)
            return b
        P = slice(0, T)
        T4 = 4 * T
        nst = 5 if T == 64 else 3
        tri = con[0:T, C_TRI:C_TRI + T]
        SLm = con[0:T, C_SL:C_SL + T]
        INCL = con[0:T, C_INCL:C_INCL + T]
        idT = con[0:T, C_ID:C_ID + T]
        ones128 = con[0:T, C_ONE:C_ONE + 128]
        idb = self.ident(T, True)
        v3, bch, bcm = self.v3, self.bch, self.bcm
        ga = g(R.GA)
        self.dma(ga[P, :], self.proj.t[row0:row0 + T, 5288:5808], [], [ga])
        acc = None
        for j in range(4):
            gp = g(R.F1536)
            self.dma(gp[P, :], self.proj.t[row0 - 3 + j:row0 - 3 + j + T, 3752:5288], [], [gp])
            self.tt("pool", gp[P, :], gp[P, :], R.cw[P, j * 1536:(j + 1) * 1536], ALU.mult, [gp, R.cw], [gp])
            if acc is None:
                acc = gp
            else:
                self.tt("pool" if j == 1 else "dve", acc[P, :], acc[P, :], gp[P, :], ALU.add, [acc, gp], [acc])
        cs = acc
        self.act(cs[P, :], cs[P, :], AF.Silu, [cs], [cs])
        yield
        qkn = g(R.F1024)
        self.tt("pool", qkn[P, :], cs[P, 0:1024], cs[P, 0:1024], ALU.mult, [cs], [qkn])
        s8 = g(R.SM)
        self.red(s8[P, 0:8], v3(qkn[P, :], 8), ALU.add, [qkn], [s8])
        self.act(s8[P, 0:8], s8[P, 0:8], AF.Ln, [s8], [s8], bias=EPS)
        self.act(s8[P, 0:8], s8[P, 0:8], AF.Exp, [s8], [s8], scale=-0.5)
        self.ts("dve", s8[P, 0:4], s8[P, 0:4], 128.0 ** -0.5, ALU.mult, [s8], [s8])
        self.tt("dve", v3(qkn[P, :], 8), v3(cs[P, 0:1024], 8), s8[P, 0:8].unsqueeze(2).broadcast_to([T, 8, 128]), ALU.mult, [cs, s8], [qkn])
        yield
        y = g(R.SM)
        self.tt("dve", y[P, 0:4], ga[P, 0:4], R.dtb[P, 0:4], ALU.add, [ga, R.dtb], [y])
        s1 = g(R.SM)
        self.stt(s1[P, 0:4], y[P, 0:4], -1.0, y[P, 0:4], ALU.mult, ALU.max, [y], [s1])
        self.act(s1[P, 0:4], s1[P, 0:4], AF.Exp, [s1], [s1], scale=-1.0)
        self.act(s1[P, 0:4], s1[P, 0:4], AF.Ln, [s1], [s1], bias=1.0)
        gg = g(R.SM)
        self.ts("dve", gg[P, 0:4], y[P, 0:4], 0.0, ALU.max, [y], [gg])
        self.tt("dve", gg[P, 0:4], gg[P, 0:4], s1[P, 0:4], ALU.add, [gg, s1], [gg])
        self.tt("dve", gg[P, 0:4], gg[P, 0:4], R.nea[P, 0:4], ALU.mult, [gg, R.nea], [gg])
        bt = g(R.SM)
        self.act(bt[P, 0:4], ga[P, 4:8], AF.Exp, [ga], [bt], scale=-1.0)
        self.ts("dve", bt[P, 0:4], bt[P, 0:4], 1.0, ALU.add, [bt], [bt])
        self.recip(bt[P, 0:4], bt[P, 0:4], [bt], [bt])
        self.ts("dve", bt[P, 4:8], bt[P, 0:4], -1.0, ALU.mult, [bt], [bt])
        yield
        Rg = g(R.F256)
        self.tt("pool", v3(Rg[P, 0:T4], 4), bcm(SLm, T, T), bch(gg[P, 0:4], T, T), ALU.mult, [con, gg], [Rg])
        pf = R.rotF.get()
        self.mm(pf[0:T, 0:T4], tri, Rg[P, 0:T4], True, True, [con, Rg], [pf])
        self.mm(pf[0:T, T4:T4 + 4], tri, gg[P, 0:4], True, True, [con, gg], [pf])
        pf128 = R.rotF.get()
        self.mm(pf128[0:128, 0:4], ones128, gg[P, 0:4], True, True, [con, gg], [pf128])
        gam = g(R.F256)
        self.act(gam[P, 0:T4], pf[0:T, 0:T4], AF.Exp, [pf], [gam])
        Gc = g(R.SM)
        self.cp("act", Gc[P, 0:4], pf[0:T, T4:T4 + 4], [pf], [Gc])
        Rot.rel(pf)
        GL = g(R.SM)
        self.cp("act", GL[:, 0:4], pf128[:, 0:4], [pf128], [GL])
        Rot.rel(pf128)
        yield
        gam_s = g(R.F256)
        self.tt("dve", v3(gam_s[P, 0:T4], 4), v3(gam[P, 0:T4], 4), bcm(SLm, T, T), ALU.mult, [gam, con], [gam_s])
        self.tt("dve", v3(gam[P, 0:T4], 4), v3(gam[P, 0:T4], 4), bcm(INCL, T, T), ALU.mult, [gam, con], [gam])
        eG = g(R.SM)
        self.act(eG[P, 0:4], Gc[P, 0:4], AF.Exp, [Gc], [eG])
        eGl = g(R.SM)
        self.act(eGl[:, 0:4], GL[:, 0:4], AF.Exp, [GL], [eGl])
        self.tt("dve", eG[P, 4:8], GL[P, 0:4], Gc[P, 0:4], ALU.subtract, [GL, Gc], [eG])
        self.act(eG[P, 4:8], eG[P, 4:8], AF.Exp, [eG], [eG])
        self.tt("dve", eG[P, 8:12], bt[P, 0:4], eG[P, 0:4], ALU.mult, [bt, eG], [eG])
        yield
        qkb = g(R.B1024)
        self.cp("act", qkb[P, :], qkn[P, :], [qkn], [qkb])
        qgb = g(R.B512)
        self.tt("pool", v3(qgb[P, :], 4), v3(qkn[P, 0:512], 4), bch(eG[P, 0:4], T, 128), ALU.mult, [qkn, eG], [qgb])
        bk = g(R.F512)
        self.tt("dve", v3(bk[P, :], 4), v3(qkn[P, 512:1024], 4), bch(eG[P, 8:12], T, 128), ALU.mult, [qkn, eG], [bk])
        bv = g(R.F512)
        self.tt("pool", v3(bv[P, :], 4), v3(cs[P, 1024:1536], 4), bch(bt[P, 0:4], T, 128), ALU.mult, [cs, bt], [bv])
        kd = g(R.B512)
        self.tt("pool", v3(kd[P, :], 4), v3(qkn[P, 512:1024], 4), bch(eG[P, 4:8], T, 128), ALU.mult, [qkn, eG], [kd])
        yield
        pb = self.rotB.get()
        for i in range(8):
            self.tr(pb[0:128, i * T:(i + 1) * T], qkb[P, i * 128:(i + 1) * 128], idb, [qkb, conb], [pb])
        for h in range(4):
            self.tr(pb[0:128, (8 + h) * T:(9 + h) * T], qgb[P, h * 128:(h + 1) * 128], idb, [qgb, conb], [pb])
        qkT = g(R.BT)
        self.cp("dve", qkT[:, 0:12 * T], pb[:, 0:12 * T], [pb], [qkT])
        Rot.rel(pb)
        yield
        qT = lambda h: qkT[:, h * T:(h + 1) * T]
        kT = lambda h: qkT[:, (4 + h) * T:(5 + h) * T]
        qgT = lambda h: qkT[:, (8 + h) * T:(9 + h) * T]
        pf = R.rotF.get()
        for h in range(4):
            self.mm(pf[0:T, h * T:(h + 1) * T], kT(h), kT(h), True, True, [qkT], [pf])
        for h in range(4):
            self.mm(pf[0:T, (4 + h) * T:(5 + h) * T], qT(h), kT(h), True, True, [qkT], [pf])
        N = g(R.F256)
        self.tt("dve", N[P, 0:T4], pf[0:T, 0:T4], gam_s[P, 0:T4], ALU.mult, [pf, gam_s], [N])
        self.tt("dve", v3(N[P, 0:T4], 4), v3(N[P, 0:T4], 4), bch(bt[P, 4:8], T, T), ALU.mult, [N, bt], [N])
        QG = g(R.F256)
        self.tt("dve", QG[P, 0:T4], pf[0:T, T4:2 * T4], gam[P, 0:T4], ALU.mult, [pf, gam], [QG])
        Rot.rel(pf)
        yield
        pf = R.rotF.get()
        for h in range(4):
            self.tr(pf[0:T, h * T:(h + 1) * T], N[P, h * T:(h + 1) * T], idT, [N, con], [pf])
        for h in range(4):
            self.tr(pf[0:T, (4 + h) * T:(5 + h) * T], QG[P, h * T:(h + 1) * T], idT, [QG, con], [pf])
        Q = g(R.F256)
        self.cp("act", Q[P, 0:T4], pf[0:T, 0:T4], [pf], [Q])
        QGT = g(R.B256)
        self.cp("dve", QGT[P, 0:T4], pf[0:T, T4:2 * T4], [pf], [QGT])
        Rot.rel(pf)
        Y = g(R.F256)
        self.tt("dve", v3(Y[P, 0:T4], 4), v3(Q[P, 0:T4], 4), bcm(idT, T, T), ALU.add, [Q, con], [Y])
        yield
        Pm = N
        hs = lambda b_, h: b_[P, h * T:(h + 1) * T]
        for s in range(nst):
            last = (s == nst - 1)
            pfP = R.rotF.get()
            for h in range(4):
                self.mm(pfP[0:T, h * T:(h + 1) * T], hs(Q, h), hs(Pm, h), True, True, [Q, Pm], [pfP])
            if not last:
                pfQ = R.rotF.get()
                for h in range(4):
                    self.mm(pfQ[0:T, h * T:(h + 1) * T], hs(Pm, h), hs(Q, h), True, True, [Q, Pm], [pfQ])
            Pn = g(R.F256)
            self.cp("act", Pn[P, 0:T4], pfP[0:T, 0:T4], [pfP], [Pn])
            Rot.rel(pfP)
            if not last:
                Qn = g(R.F256)
                self.cp("dve", Qn[P, 0:T4], pfQ[0:T, 0:T4], [pfQ], [Qn])
                Rot.rel(pfQ)
            pfY = R.rotF.get()
            for h in range(4):
                self.mm(pfY[0:T, h * T:(h + 1) * T], hs(Pn, h), hs(Y, h), True, True, [Pn, Y], [pfY])
            self.tt("dve", Y[P, 0:T4], Y[P, 0:T4], pfY[0:T, 0:T4], ALU.add, [Y, pfY], [Y])
            Rot.rel(pfY)
            yield
            for old in ((Pm, Q) if not last else (Pm, Q, Pn)):
                if old in L:
                    L.remove(old)
                    Rot.rel(old)
            Pm = Pn
            if not last:
                Q = Qn
        pfu = R.rotF.get()
        for h in range(4):
            self.mm(pfu[0:T, h * 128:(h + 1) * 128], hs(Y, h), bv[P, h * 128:(h + 1) * 128], True, True, [Y, bv], [pfu])
        usb = g(R.F512)
        self.cp("act", usb[P, :], pfu[0:T, :], [pfu], [usb])
        Rot.rel(pfu)
        yield
        pfw = R.rotF.get()
        for h in range(4):
            self.mm(pfw[0:128, h * T:(h + 1) * T], bk[P, h * 128:(h + 1) * 128], hs(Y, h), True, True, [bk, Y], [pfw])
        wTb = g(R.BT)
        self.cp("dve", wTb[:, 0:T4], pfw[:, 0:T4], [pfw], [wTb])
        Rot.rel(pfw)
        keep = [usb, wTb, qkT, QGT, kd, eGl, ga]
        for b_ in list(L):
            if b_ not in keep:
                Rot.rel(b_)
        ctx.update(usb=usb, wTb=wTb, qkT=qkT, QGT=QGT, kd=kd, eGl=eGl, ga=ga, keep=keep)
        yield

    def gdn_chain_g(self, R, l, T, st, mixT, coff, ctx):
        con, conb = self.con, self.conb
        L = []

        def g(p):
            b = p.get()
            L.append(b)
            return b
        P = slice(0, T)
        T4 = 4 * T
        idb = self.ident(T, True)
        v3, bch, bcm = self.v3, self.bch, self.bcm
        usb, wTb, qkT, QGT, kd, eGl, ga = (ctx[k_] for k_ in ("usb", "wTb", "qkT", "QGT", "kd", "eGl", "ga"))
        qgT = lambda h: qkT[:, (8 + h) * T:(9 + h) * T]
        hs = lambda b_, h: b_[P, h * T:(h + 1) * T]
        pf = R.rotF.get()
        for h in range(4):
            self.mm(pf[0:T, h * 128:(h + 1) * 128], wTb[:, h * T:(h + 1) * T], st.Sb[:, h, :], True, True, [wTb, st.Sb], [pf])
        dl = g(R.B512)
        self.tt("dve", dl[P, :], usb[P, :], pf[0:T, :], ALU.subtract, [usb, pf], [dl])
        Rot.rel(pf)
        yield
        pfo = R.rotF.get()
        for h in range(4):
            o = pfo[0:T, h * 128:(h + 1) * 128]
            self.mm(o, qgT(h), st.Sb[:, h, :], True, False, [qkT, st.Sb], [pfo])
            self.mm(o, hs(QGT, h), dl[P, h * 128:(h + 1) * 128], False, True, [QGT, dl], [pfo])
        pfS = R.rotF.get()
        for h in range(4):
            self.mm(pfS[:, h * 128:(h + 1) * 128], kd[P, h * 128:(h + 1) * 128], dl[P, h * 128:(h + 1) * 128], True, True, [kd, dl], [pfS])
        self.tt("dve", st.S[:], st.S[:], bch(eGl[:, 0:4], 128, 128), ALU.mult, [st.S, eGl], [st.S])
        self.tt("dve", st.S[:], st.S[:], v3(pfS[:, :], 4), ALU.add, [st.S, pfS], [st.S])
        Rot.rel(pfS)
        self.cp("act", st.Sb[:], st.S[:], [st.S], [st.Sb])
        osb = g(R.F512)
        self.cp("act", osb[P, :], pfo[0:T, :], [pfo], [osb])
        Rot.rel(pfo)
        yield
        sq = g(R.F512)
        self.tt("pool", sq[P, :], osb[P, :], osb[P, :], ALU.mult, [osb], [sq])
        s4 = g(R.SM)
        self.red(s4[P, 0:4], v3(sq[P, :], 4), ALU.add, [sq], [s4])
        self.rstd(s4[P, 0:4], s4[P, 0:4], 128, [s4], [s4])
        yield
        self.tt("dve", v3(osb[P, :], 4), v3(osb[P, :], 4), bch(s4[P, 0:4], T, 128), ALU.mult, [osb, s4], [osb])
        self.act(sq[P, :], ga[P, 8:520], AF.Silu, [ga], [sq])
        self.tt("pool", v3(sq[P, :], 4), v3(sq[P, :], 4), bcm(R.gnorm[P, :], T, 128), ALU.mult, [sq, R.gnorm], [sq])
        ogb = g(R.B512)
        self.tt("dve", ogb[P, :], osb[P, :], sq[P, :], ALU.mult, [osb, sq], [ogb])
        pb = self.rotB.get()
        for h in range(4):
            self.tr(pb[0:128, h * T:(h + 1) * T], ogb[P, h * 128:(h + 1) * 128], idb, [ogb, conb], [pb])
        self.cp("act", mixT[:, 8:12, coff:coff + T], v3(pb[:, 0:T4], 4), [pb], [mixT])
        Rot.rel(pb)
        Rot.rel(*L)
        Rot.rel(*ctx["keep"])
        yield


_CACHE = {}


def _program(dbg=None, nlayers=DEPTH, phases="PAR"):
    key = (dbg, nlayers, phases)
    if key not in _CACHE:
        _CACHE[key] = K(dbg, nlayers, phases).build()
    return _CACHE[key]


def make_in_maps(inp):
    f = lambda a: np.ascontiguousarray(np.asarray(a, dtype=np.float32))
    consts = make_consts()
    rope = make_rope()
    shared = {
        "meta": f(inp["meta_tokens"]), "norm_w": f(inp["norm_w"]), "w_in": f(inp["w_in"]),
        "q_norm": f(inp["mla_q_norm"]), "w_uq": f(inp["mla_w_uq"]), "kv_norm": f(inp["mla_kv_norm"]),
        "w_uk": f(inp["mla_w_uk"]).reshape(DEPTH, 256, 512), "w_uv": f(inp["mla_w_uv"]).reshape(DEPTH, 256, 512),
        "gate_b": f(inp["mlstm_gate_b"]).reshape(DEPTH, 8), "m_norm": f(inp["mlstm_norm"]),
        "conv_w": f(inp["gdn_conv_w"]).reshape(DEPTH, 4 * 1536), "a_log": f(inp["gdn_a_log"]),
        "dt_bias": f(inp["gdn_dt_bias"]), "g_norm": f(inp["gdn_norm"]), "w_out": f(inp["w_out"]),
        "final_norm": f(inp["final_norm"]), "consts": consts, "rope": rope,
    }
    maps = []
    for c in range(8):
        sl = slice(NS * c, NS * c + NS)
        m = dict(shared)
        m["xp"] = f(inp["x_prompt"][c])
        m["xs"] = f(inp["x_sample"][sl]).reshape(NS * DS, D)
        m["cl"] = f(inp["cache_mla_latent"][:, sl])
        m["ck"] = f(inp["cache_mla_krope"][:, sl])
        m["imC"] = f(inp["state_mlstm_C"][:, sl])
        m["imn"] = f(inp["state_mlstm_n"][:, sl])
        m["imm"] = f(inp["state_mlstm_m"][:, sl])
        m["igS"] = f(inp["state_gdn_S"][:, sl])
        m["igc"] = f(inp["state_gdn_conv"][:, sl])
        maps.append(m)
    return maps


def kernel(**inputs):
    nc = _program()
    maps = make_in_maps(inputs)
    res = run_bass_kernel_spmd(nc, maps, core_ids=list(range(8)))
    R = res.results
    st = lambda name, axis: np.stack([np.asarray(r[name], dtype=np.float32) for r in R], axis=axis)
    cat = lambda name, axis: np.concatenate([np.asarray(r[name], dtype=np.float32) for r in R], axis=axis)
    y_prompt = st("y_p", 0)
    y_sample = st("y_s", 0).reshape(8 * NS, DS, D)
    return (
        y_prompt, y_sample,
        st("p_lat", 1), st("p_kr", 1), st("p_C", 1), st("p_n", 1), st("p_m", 1), st("p_S", 1), st("p_cv", 1),
        cat("s_lat", 1), cat("s_kr", 1), cat("s_C", 1), cat("s_n", 1), cat("s_m", 1), cat("s_S", 1), cat("s_cv", 1),
    )
```

```python
import math
import contextlib
import numpy as np
import concourse.bass as bass
import concourse.mybir as mybir
from concourse.bass_utils import run_bass_kernel_spmd

F32 = mybir.dt.float32
BF16 = mybir.dt.bfloat16
AF = mybir.ActivationFunctionType
ALU = mybir.AluOpType
AX = mybir.AxisListType

D = 1024
SEQ = 4096
NMETA = 16
DEPTH = 2
NS = 4
DS = 16
PAST = 4096
NCACHE = NMETA + PAST
EPS = 1e-6
IN_COLS = 5808
NROWS_X = 80 + SEQ
PROJ_ROWS = 3 + 16 + SEQ + NS * 19
SM_SCALE = 1.0 / math.sqrt(96.0)

C_ID, C_ONE, C_TRI, C_SL, C_INCL, C_NEG, C_S64, C_S16, C_SELR = 0, 128, 256, 320, 384, 448, 512, 640, 768
NCON = 864


def make_consts():
    c = np.zeros((128, NCON), np.float32)
    c[:, C_ID:C_ID + 128] = np.eye(128)
    c[:, C_ONE:C_ONE + 128] = 1.0
    k = np.arange(64)[:, None]
    t = np.arange(64)[None, :]
    c[:64, C_TRI:C_TRI + 64] = (k <= t)
    c[:64, C_SL:C_SL + 64] = (k > t)
    c[:64, C_INCL:C_INCL + 64] = (t <= k)
    c[:64, C_NEG:C_NEG + 64] = np.where(t <= k, 0.0, -1e30)
    c[63, C_S64:C_S64 + 128] = 1.0
    c[15, C_S16:C_S16 + 128] = 1.0
    c[:32, C_SELR + 64:C_SELR + 96] = np.eye(32)
    return c


def make_rope():
    half = 16
    freq = (np.float32(10000.0) ** (-np.arange(half, dtype=np.float32) / np.float32(half))).astype(np.float32)
    pos = np.arange(NCACHE + DS, dtype=np.float32)
    ang = (pos[:, None] * freq[None, :]).astype(np.float32)
    return np.concatenate([np.cos(ang), np.sin(ang)], axis=1).astype(np.float32)


class H:
    __slots__ = ("name", "w", "r", "excl")

    def __init__(self, name="", excl=False):
        self.name = name
        self.w = None
        self.r = []
        self.excl = excl


class Buf:
    __slots__ = ("t", "h", "busy")

    def __init__(self, t, name=""):
        self.t = t
        self.h = H(name)
        self.busy = False

    def __getitem__(self, k):
        return self.t[k]


def _hs(lst):
    out = []
    for x in lst:
        if x is None:
            continue
        out.append(x.h if isinstance(x, Buf) else x)
    return out


class Sched:
    ENGS = ("pe", "act", "dve", "pool", "sp")
    XLAT = 1.2

    def __init__(self, nc, n_dma_sems=20):
        self.nc = nc
        self.nodes = []
        self.aset = {}
        self.cur_set = 1
        self.seg_start = [0]
        self.n_dma = n_dma_sems
        self.sems = {}

    def _deps(self, reads, writes):
        d = set()
        for h in reads:
            if h.w is not None:
                d.add(h.w)
        for h in writes:
            if h.w is not None:
                d.add(h.w)
            d.update(h.r)
        return d

    def _record(self, idx, reads, writes):
        for h in reads:
            h.r.append(idx)
        for h in writes:
            h.w = idx
            h.r = []

    def emit(self, eng, fn, reads=(), writes=(), cost=0.3, aset=0):
        reads, writes = _hs(reads), _hs(writes)
        ex = [h for h in reads if h.excl]
        if ex:
            reads = [h for h in reads if not h.excl]
            writes = writes + [h for h in ex if h not in writes]
        deps = self._deps(reads, writes)
        idx = len(self.nodes)
        self.nodes.append((eng, fn, deps, False, cost))
        if aset:
            self.aset[idx] = aset
        self._record(idx, reads, writes)
        return idx

    def dma(self, q, fn, reads=(), writes=(), is_out=False, cost=2.5):
        reads, writes = _hs(reads), _hs(writes)
        deps = self._deps(reads, writes)
        idx = len(self.nodes)
        self.nodes.append((q, fn, deps, True, cost))
        self._record(idx, reads, writes)
        return idx

    def barrier(self):
        if self.seg_start[-1] != len(self.nodes):
            self.seg_start.append(len(self.nodes))

    def _schedule(self, lo, hi):
        import heapq
        nodes = self.nodes
        n = hi - lo
        succ = [[] for _ in range(n)]
        indeg = [0] * n
        for i in range(lo, hi):
            for d in nodes[i][2]:
                if d >= lo:
                    succ[d - lo].append(i - lo)
                    indeg[i - lo] += 1
        prio = [0.0] * n
        for i in range(n - 1, -1, -1):
            c = nodes[lo + i][4]
            m = 0.0
            for s in succ[i]:
                if prio[s] > m:
                    m = prio[s]
            prio[i] = c + m + self.XLAT
        future = {e: [] for e in self.ENGS}
        avail = {e: [] for e in self.ENGS}
        tfree = {e: 0.0 for e in self.ENGS}
        ready_t = [0.0] * n
        order = {e: [] for e in self.ENGS}
        for i in range(n):
            if indeg[i] == 0:
                heapq.heappush(future[nodes[lo + i][0]], (0.0, i))
        done = 0
        while done < n:
            best_e, best_t = None, None
            for e in self.ENGS:
                if avail[e]:
                    t = tfree[e]
                elif future[e]:
                    t = max(tfree[e], future[e][0][0])
                else:
                    continue
                if best_t is None or t < best_t:
                    best_e, best_t = e, t
            e, t = best_e, best_t
            fu = future[e]
            while fu and fu[0][0] <= t:
                rt, i = heapq.heappop(fu)
                heapq.heappush(avail[e], (-prio[i], i))
            if e == "act" and len(avail[e]) > 1:
                top = avail[e][0]
                ts_ = self.aset.get(lo + top[1], 0)
                if ts_ != 0 and ts_ != self.cur_set:
                    best = None
                    for cand in avail[e]:
                        cs_ = self.aset.get(lo + cand[1], 0)
                        if (cs_ == 0 or cs_ == self.cur_set) and (best is None or cand < best):
                            best = cand
                    if best is not None and (-best[0]) >= (-top[0]) - 12.0:
                        avail[e].remove(best)
                        heapq.heapify(avail[e])
                        i = best[1]
                    else:
                        _, i = heapq.heappop(avail[e])
                else:
                    _, i = heapq.heappop(avail[e])
            else:
                _, i = heapq.heappop(avail[e])
            node = nodes[lo + i]
            sw = 0.0
            if e == "act":
                s_ = self.aset.get(lo + i, 0)
                if s_ != 0 and s_ != self.cur_set:
                    self.cur_set = s_
                    sw = 1.3
            if node[3]:
                fin = t + node[4]
                tfree[e] = t + 0.15
            else:
                fin = t + node[4] + sw
                tfree[e] = fin
            order[e].append(lo + i)
            done += 1
            for s in succ[i]:
                se = nodes[lo + s][0]
                lat = 0.08 if (se == e and not node[3]) else self.XLAT
                if fin + lat > ready_t[s]:
                    ready_t[s] = fin + lat
                indeg[s] -= 1
                if indeg[s] == 0:
                    heapq.heappush(future[se], (ready_t[s], s))
        return order

    def finalize(self):
        nc = self.nc
        nodes = self.nodes
        self.barrier()
        segs = list(zip(self.seg_start[:-1], self.seg_start[1:]))
        eng_ops = {e: [] for e in self.ENGS}
        token = [None] * len(nodes)
        cnt = {e: 0 for e in self.ENGS}
        dma_i = {e: 0 for e in self.ENGS}
        dma_val = {e: [0] * self.n_dma for e in self.ENGS}
        dma_prev = {}
        for (lo, hi) in segs:
            order = self._schedule(lo, hi)
            for e in self.ENGS:
                for idx in order[e]:
                    if nodes[idx][3]:
                        i = dma_i[e]
                        dma_i[e] = (i + 1) % self.n_dma
                        key = ("dma", e, i)
                        if dma_val[e][i] > 0:
                            dma_prev[idx] = (key, dma_val[e][i])
                        dma_val[e][i] += 16
                        token[idx] = (key, dma_val[e][i], 16)
                    else:
                        cnt[e] += 1
                        token[idx] = (e, cnt[e], 1)
                    eng_ops[e].append(("node", idx))
            allw = {e: cnt[e] for e in self.ENGS if cnt[e] > 0}
            for e in self.ENGS:
                for i, v in enumerate(dma_val[e]):
                    if v > 0:
                        allw[("dma", e, i)] = v
            for e in self.ENGS:
                eng_ops[e].append(("bar", dict(allw)))
        keys = list(self.ENGS)
        for e in self.ENGS:
            for i, v in enumerate(dma_val[e]):
                if v > 0:
                    keys.append(("dma", e, i))
        with contextlib.ExitStack() as es:
            for k in keys:
                nm = k if isinstance(k, str) else "d_%s_%d" % (k[1], k[2])
                self.sems[k] = es.enter_context(nc.semaphore("s_" + nm))
            block = es.enter_context(nc.Block())
            sems = self.sems

            def run(eng_name):
                def body(e):
                    waited = {}
                    for kind, x in eng_ops[eng_name]:
                        if kind == "bar":
                            for k, v in x.items():
                                if k == eng_name:
                                    continue
                                if waited.get(k, 0) < v:
                                    waited[k] = v
                                    e.wait_ge(sems[k], v)
                            continue
                        idx = x
                        _, fn, deps, is_dma, _ = nodes[idx]
                        need = {}
                        if idx in dma_prev:
                            k, v = dma_prev[idx]
                            need[k] = v
                        for d in deps:
                            k, v, _ = token[d]
                            if k == eng_name and eng_name == "pe":
                                continue
                            if need.get(k, 0) < v:
                                need[k] = v
                        for k, v in need.items():
                            if waited.get(k, 0) < v:
                                waited[k] = v
                                e.wait_ge(sems[k], v)
                        k, v, inc = token[idx]
                        fn(e).then_inc(sems[k], inc)
                return body

            block.tensor(run("pe"))
            block.scalar(run("act"))
            block.vector(run("dve"))
            block.gpsimd(run("pool"))
            block.sync(run("sp"))


class Rot:
    def __init__(self, bufs):
        self.bufs = bufs
        self.i = 0

    def get(self):
        for _ in range(len(self.bufs)):
            b = self.bufs[self.i]
            self.i = (self.i + 1) % len(self.bufs)
            if not b.busy:
                b.busy = True
                return b
        raise AssertionError("rotating pool exhausted: all buffers leased")

    @staticmethod
    def rel(*bs):
        for b in bs:
            b.busy = False


class K:
    OPT_K32 = False
    OPT_SIGEXP = False
    OPT_WSPLIT = True
    OPT_X1 = False
    OPT_EVACT = True

    def __init__(self, dbg=None, nlayers=DEPTH, phases="PAR"):
        self.dbg = dbg
        self.nlayers = nlayers
        self.phases = phases
        self.nc = bass.Bass("TRN2", target_bir_lowering=False)
        self.S = Sched(self.nc)
        self.es = contextlib.ExitStack()
        self.dq = 0

    def dram(self, name, shape, dt=F32, kind=None):
        if kind is None:
            t = self.nc.dram_tensor(name, list(shape), dt)
        else:
            t = self.nc.dram_tensor(name, list(shape), dt, kind=kind)
        return Buf(t.ap(), name)

    def sb(self, es, name, shape, dt=F32):
        self.dq += 1
        name = "%s_u%d" % (name, self.dq)
        return Buf(es.enter_context(self.nc.sbuf_tensor(name, list(shape), dt)), name)

    def ps(self, es, name, shape, dt=F32):
        b = Buf(es.enter_context(self.nc.psum_tensor(name, list(shape), dt)), name)
        b.h.excl = True
        return b

    def mm(self, out, lhsT, rhs, start, stop, r, w):
        n = max(64, rhs.free_size()) * (4 if rhs.dtype == F32 else 1)
        self.S.emit("pe", lambda e: e.matmul(out, lhsT=lhsT, rhs=rhs, start=start, stop=stop), r, w, cost=0.04 + n / 1400.0)

    def tr(self, out, in_, ident, r, w):
        self.S.emit("pe", lambda e: e.transpose(out, in_, ident), r, w, cost=0.04 + max(64, in_.partition_size()) / 1400.0)

    @staticmethod
    def _c(eng, ap):
        n = ap.free_size()
        if eng == "dve":
            return 0.12 + n / 960.0
        if eng == "act":
            return 0.2 + n / 1400.0
        return 0.35 + n / 700.0

    def act(self, out, in_, func, r, w, scale=None, bias=None, accum=None):
        kw = {}
        if scale is not None:
            kw["scale"] = scale
        if bias is not None:
            kw["bias"] = bias
        if accum is not None:
            kw["accum_out"] = accum
        aset = 1 if func in (AF.Exp, AF.Ln) else (2 if func == AF.Silu else (3 if func in (AF.Sigmoid, AF.Sqrt) else 0))
        self.S.emit("act", lambda e: e.activation(out=out, in_=in_, func=func, **kw), r, w, cost=self._c("act", out) + (0.1 if accum is not None else 0), aset=aset)

    def tt(self, eng, out, in0, in1, op, r, w):
        self.S.emit(eng, lambda e: e.tensor_tensor(out=out, in0=in0, in1=in1, op=op), r, w, cost=self._c(eng, out))

    def ts(self, eng, out, in0, s1, op0, r, w, s2=None, op1=None):
        if op1 is None:
            self.S.emit(eng, lambda e: e.tensor_scalar(out=out, in0=in0, scalar1=s1, scalar2=None, op0=op0), r, w, cost=self._c(eng, out))
        else:
            self.S.emit(eng, lambda e: e.tensor_scalar(out=out, in0=in0, scalar1=s1, scalar2=s2, op0=op0, op1=op1), r, w, cost=self._c(eng, out))

    def stt(self, out, in0, scalar, in1, op0, op1, r, w):
        self.S.emit("dve", lambda e: e.scalar_tensor_tensor(out=out, in0=in0, scalar=scalar, in1=in1, op0=op0, op1=op1), r, w, cost=self._c("dve", out))

    def red(self, out, in_, op, r, w):
        self.S.emit("dve", lambda e: e.tensor_reduce(out=out, in_=in_, axis=AX.X, op=op), r, w, cost=self._c("dve", in_))

    def cp(self, eng, out, in_, r, w):
        if eng == "act":
            self.S.emit("act", lambda e: e.activation(out=out, in_=in_, func=AF.Copy), r, w, cost=self._c("act", out))
        else:
            self.S.emit(eng, lambda e: e.tensor_copy(out=out, in_=in_), r, w, cost=self._c(eng, out))

    def recip(self, out, in_, r, w):
        self.S.emit("dve", lambda e: e.reciprocal(out=out, in_=in_), r, w, cost=self._c("dve", out))

    def memset(self, eng, ap, val, w):
        self.S.emit(eng, lambda e: e.memset(ap, val), [], w, cost=self._c(eng, ap))

    def dma(self, out, in_, r, w, q=None, nonc=False, is_out=False):
        if q is None:
            q = ("sp", "act")[self.dq % 2] if False else "sp"
        c = 2.0 + in_.nbytes() / 150e3 * (2.0 if q == "pool" else 1.0)
        if nonc:
            self.S.dma(q, lambda e: e.dma_start(out=out, in_=in_, allow_slow_non_contiguous=True), r, w, is_out, cost=c)
        else:
            self.S.dma(q, lambda e: e.dma_start(out=out, in_=in_), r, w, is_out, cost=c)

    @staticmethod
    def interleave(gens):
        gens = [g for g in gens if g is not None]
        while gens:
            alive = []
            for g in gens:
                try:
                    next(g)
                    alive.append(g)
                except StopIteration:
                    pass
            gens = alive

    def rstd(self, out, ssq, n, r, w):
        self.act(out, ssq, AF.Ln, r, w, scale=1.0 / n, bias=EPS)
        self.act(out, out, AF.Exp, w, w, scale=-0.5)

    def build(self):
        nc = self.nc
        es = self.es
        dbg = self.dbg
        EI, EO = "ExternalInput", "ExternalOutput"
        self.xp = self.dram("xp", [SEQ, D], kind=EI)
        self.xs = self.dram("xs", [NS * DS, D], kind=EI)
        self.meta = self.dram("meta", [NMETA, D], kind=EI)
        self.cl = self.dram("cl", [DEPTH, NS, NCACHE, 256], kind=EI)
        self.ck = self.dram("ck", [DEPTH, NS, NCACHE, 32], kind=EI)
        self.imC = self.dram("imC", [DEPTH, NS, 4, 128, 128], kind=EI)
        self.imn = self.dram("imn", [DEPTH, NS, 4, 128], kind=EI)
        self.imm = self.dram("imm", [DEPTH, NS, 4], kind=EI)
        self.igS = self.dram("igS", [DEPTH, NS, 4, 128, 128], kind=EI)
        self.igc = self.dram("igc", [DEPTH, NS, 3, 1536], kind=EI)
        self.norm_w = self.dram("norm_w", [DEPTH, D], kind=EI)
        self.w_in = self.dram("w_in", [DEPTH, D, IN_COLS], kind=EI)
        self.q_norm = self.dram("q_norm", [DEPTH, 384], kind=EI)
        self.w_uq = self.dram("w_uq", [DEPTH, 384, 768], kind=EI)
        self.kv_norm = self.dram("kv_norm", [DEPTH, 256], kind=EI)
        self.w_uk = self.dram("w_uk", [DEPTH, 256, 512], kind=EI)
        self.w_uv = self.dram("w_uv", [DEPTH, 256, 512], kind=EI)
        self.gate_b = self.dram("gate_b", [DEPTH, 8], kind=EI)
        self.m_norm = self.dram("m_norm", [DEPTH, 512], kind=EI)
        self.conv_w = self.dram("conv_w", [DEPTH, 4 * 1536], kind=EI)
        self.a_log = self.dram("a_log", [DEPTH, 4], kind=EI)
        self.dt_bias = self.dram("dt_bias", [DEPTH, 4], kind=EI)
        self.g_norm = self.dram("g_norm", [DEPTH, 128], kind=EI)
        self.w_out = self.dram("w_out", [DEPTH, 1536, D], kind=EI)
        self.final_norm = self.dram("final_norm", [D], kind=EI)
        self.consts = self.dram("consts", [128, NCON], kind=EI)
        self.rope = self.dram("rope", [NCACHE + DS, 32], kind=EI)

        self.y_p = self.dram("y_p", [SEQ, D], kind=EO)
        self.y_s = self.dram("y_s", [NS * DS, D], kind=EO)
        self.p_lat = self.dram("p_lat", [DEPTH, NMETA + SEQ, 256], kind=EO)
        self.p_kr = self.dram("p_kr", [DEPTH, NMETA + SEQ, 32], kind=EO)
        self.p_C = self.dram("p_C", [DEPTH, 4, 128, 128], kind=EO)
        self.p_n = self.dram("p_n", [DEPTH, 4, 128], kind=EO)
        self.p_m = self.dram("p_m", [DEPTH, 4], kind=EO)
        self.p_S = self.dram("p_S", [DEPTH, 4, 128, 128], kind=EO)
        self.p_cv = self.dram("p_cv", [DEPTH, 3, 1536], kind=EO)
        self.s_lat = self.dram("s_lat", [DEPTH, NS, DS, 256], kind=EO)
        self.s_kr = self.dram("s_kr", [DEPTH, NS, DS, 32], kind=EO)
        self.s_C = self.dram("s_C", [DEPTH, NS, 4, 128, 128], kind=EO)
        self.s_n = self.dram("s_n", [DEPTH, NS, 4, 128], kind=EO)
        self.s_m = self.dram("s_m", [DEPTH, NS, 4], kind=EO)
        self.s_S = self.dram("s_S", [DEPTH, NS, 4, 128, 128], kind=EO)
        self.s_cv = self.dram("s_cv", [DEPTH, NS, 3, 1536], kind=EO)

        dk = EO if dbg else None
        self.proj = self.dram("proj_scr", [PROJ_ROWS, IN_COLS], kind=dk)
        self.mixa = self.dram("mixa_scr", [NROWS_X, 512], BF16, kind=dk)
        self.xscr = self.dram("x_scr", [NROWS_X, D], kind=dk)
        self.proj_h = {}
        self.mixa_h = {}
        self.x_h = {}

        self.con = self.sb(es, "con", [128, NCON])
        self.conb = self.sb(es, "conb", [128, 256 + 96], BF16)
        self.dma(self.con[:], self.consts.t, [], [self.con])
        self.dma(self.conb[:, 0:256], self.consts.t[:, 0:256], [], [self.conb], q="pool")
        self.dma(self.conb[:, 256:352], self.consts.t[:, C_SELR:C_SELR + 96], [], [self.conb], q="pool")
        self.psF = [self.ps(es, "psF%d" % i, [128, 512]) for i in range(6)]
        self.psB = [self.ps(es, "psB%d" % i, [128, 1024], BF16) for i in range(2)]
        self.rotB = Rot(self.psB)
        self.fin_bc = self.sb(es, "fin_bc", [128, D])
        self.dma(self.fin_bc[:], self.final_norm.t.partition_broadcast(128), [], [self.fin_bc])

        for l in range(self.nlayers):
            if "P" in self.phases:
                self.phase_P(l)
                self.S.barrier()
            if "A" in self.phases:
                self.phase_A(l)
                self.S.barrier()
            if "R" in self.phases:
                self.phase_R(l)
                self.S.barrier()
        self.S.finalize()
        es.close()
        return nc

    def ph(self, key):
        if key not in self.proj_h:
            self.proj_h[key] = H("proj%s" % (key,))
        return self.proj_h[key]

    def mh(self, key):
        if key not in self.mixa_h:
            self.mixa_h[key] = H("mixa%s" % (key,))
        return self.mixa_h[key]

    def xh(self, key):
        if key not in self.x_h:
            self.x_h[key] = H("x%s" % (key,))
        return self.x_h[key]

    def ident(self, n, bf=False):
        if bf:
            return self.conb[0:n, 0:n]
        return self.con[0:n, C_ID:C_ID + n]

    def load_x(self, l, tile, xt):
        if tile == "aux":
            if l == 0:
                self.dma(xt[0:16, :], self.meta.t, [], [xt])
                self.dma(xt[16:80, :], self.xs.t, [], [xt])
            else:
                self.dma(xt[0:80, :], self.xscr.t[0:80, :], [self.xh("aux")], [xt])
            return 80
        j = tile
        if l == 0:
            self.dma(xt[:, :], self.xp.t[j * 128:(j + 1) * 128, :], [], [xt])
        else:
            self.dma(xt[:, :], self.xscr.t[80 + j * 128:80 + (j + 1) * 128, :], [self.xh(j)], [xt])
        return 128

    def phase_P(self, l):
        con, conb = self.con, self.conb
        with contextlib.ExitStack() as es:
            w_bf = self.sb(es, "P_w", [128, 8, IN_COLS], BF16)
            wv = self.w_in.t[l].rearrange("(k p) c -> p k c", p=128)
            ngw = (IN_COLS + 511) // 512
            w_h = [H("w_in_g%d" % g_) for g_ in range(ngw)]
            if K.OPT_WSPLIT:
                for g_ in range(ngw):
                    c0_ = g_ * 512
                    cw_ = min(512, IN_COLS - c0_)
                    self.dma(w_bf[:, :, c0_:c0_ + cw_], wv[:, :, c0_:c0_ + cw_], [], [w_h[g_]], q="pool")
            else:
                for k in range(8):
                    self.dma(w_bf[:, k, :], wv[:, k, :], [], w_h, q="pool")
            nw = self.sb(es, "P_nw", [128, D])
            self.dma(nw[:], self.norm_w.t[l].partition_broadcast(128), [], [nw])
            xts = Rot([self.sb(es, "P_x%d" % i, [128, D]) for i in range(2)])
            junk = self.sb(es, "P_junk", [128, D], BF16)
            ssq = Rot([self.sb(es, "P_ssq%d" % i, [128, 1]) for i in range(2)])
            rs = Rot([self.sb(es, "P_rs%d" % i, [128, 1]) for i in range(2)])
            xns = Rot([self.sb(es, "P_xn%d" % i, [128, D], BF16) for i in range(2)])
            xTs = Rot([self.sb(es, "P_xT%d" % i, [128, 8, 128], BF16) for i in range(2)])
            prs = Rot([self.sb(es, "P_pr%d" % i, [128, IN_COLS]) for i in range(2)])
            rotF = Rot(self.psF)
            zt = self.sb(es, "P_z", [4, 1536])
            self.memset("pool", zt[:], 0.0, [zt])
            self.dma(self.proj.t[0:3, 3752:5288], zt[0:3, :], [zt], [self.ph("hist")])
            for b in range(NS):
                r0 = 4115 + 19 * b
                self.dma(self.proj.t[r0:r0 + 3, 3752:5288], self.igc.t[l, b], [], [self.ph("hist")])
            for tile in ["aux"] + list(range(SEQ // 128)):
                xt = xts.get()
                T = self.load_x(l, tile, xt)
                sq, r_ = ssq.get(), rs.get()
                self.act(junk[0:T, :], xt[0:T, :], AF.Square, [xt], [junk, sq], accum=sq[0:T, :])
                self.rstd(r_[0:T, :], sq[0:T, :], D, [sq], [r_])
                xn = xns.get()
                self.stt(xn[0:T, :], xt[0:T, :], r_[0:T, 0:1], nw[0:T, :], ALU.mult, ALU.mult, [xt, r_, nw], [xn])
                pb = self.rotB.get()
                for k in range(8):
                    self.tr(pb[0:128, k * 128:k * 128 + T], xn[0:T, k * 128:(k + 1) * 128], self.ident(T, True), [xn, conb], [pb])
                xT = xTs.get()
                pbv = pb[:, :].rearrange("p (k t) -> p k t", k=8)
                self.cp("dve", xT[:, :, 0:T], pbv[:, :, 0:T], [pb], [xT])
                Rot.rel(pb, xt, sq, r_, xn)
                pr = prs.get()
                ng = (IN_COLS + 511) // 512
                for g in range(ng):
                    c0 = g * 512
                    cw = min(512, IN_COLS - c0)
                    pf = rotF.get()
                    for k in range(8):
                        self.mm(pf[0:T, 0:cw], xT[:, k, 0:T], w_bf[:, k, c0:c0 + cw], k == 0, k == 7, [xT, w_h[g]], [pf])
                    if g % 2 == 0:
                        self.cp("act", pr[0:T, c0:c0 + cw], pf[0:T, 0:cw], [pf], [pr])
                    else:
                        self.cp("dve", pr[0:T, c0:c0 + cw], pf[0:T, 0:cw], [pf], [pr])
                    Rot.rel(pf)
                Rot.rel(xT)
                if tile == "aux":
                    self.dma(self.proj.t[3:19, :], pr[0:16, :], [pr], [self.ph("aux")])
                    for b in range(NS):
                        r0 = 4118 + 19 * b
                        self.dma(self.proj.t[r0:r0 + 16, :], pr[16 + 16 * b:32 + 16 * b, :], [pr], [self.ph("aux")])
                else:
                    r0 = 19 + tile * 128
                    self.dma(self.proj.t[r0:r0 + 128, :], pr[:, :], [pr], [self.ph(tile)])
                Rot.rel(pr)

    def phase_A(self, l):
        con, conb = self.con, self.conb
        with contextlib.ExitStack() as es:
            A = type("NS", (), {})()
            wuq = self.sb(es, "A_wuq", [128, 3, 768], BF16)
            self.dma(wuq[:], self.w_uq.t[l].rearrange("(k p) c -> p k c", p=128), [], [wuq], q="pool")
            wuk = self.sb(es, "A_wuk", [128, 2, 8, 96], BF16)
            self.memset("pool", wuk[:], 0.0, [wuk])
            for k in range(2):
                self.dma(wuk[:, k, :, 0:64], self.w_uk.t[l, k * 128:(k + 1) * 128, :].rearrange("p (h d) -> p h d", h=8), [], [wuk], q="pool")
            wuv = self.sb(es, "A_wuv", [128, 2, 512], BF16)
            self.dma(wuv[:], self.w_uv.t[l].rearrange("(k p) c -> p k c", p=128), [], [wuv], q="pool")
            qn_bc = self.sb(es, "A_qn", [128, 384])
            self.dma(qn_bc[:], self.q_norm.t[l].partition_broadcast(128), [], [qn_bc])
            kvn_bc = self.sb(es, "A_kvn", [128, 256])
            self.dma(kvn_bc[:], self.kv_norm.t[l].partition_broadcast(128), [], [kvn_bc])
            KT = self.sb(es, "A_KT", [96, 8, NCACHE + DS], BF16)
            V = self.sb(es, "A_V", [128, 34, 8, 65], BF16)
            self.memset("pool", V[:], 1.0, [V])
            slot_h = [H("slot%d" % i) for i in range(34)]
            for i in range(34):
                slot_h[i].w = V.h.w
            A.wuq, A.wuk, A.wuv, A.qn_bc, A.kvn_bc, A.KT, A.V, A.slot_h = wuq, wuk, wuv, qn_bc, kvn_bc, KT, V, slot_h
            A.junk = self.sb(es, "A_junk", [128, 384], BF16)
            A.ssq = Rot([self.sb(es, "A_ssq%d" % i, [128, 1]) for i in range(4)])
            A.rs = Rot([self.sb(es, "A_rs%d" % i, [128, 1]) for i in range(4)])
            A.cqn = Rot([self.sb(es, "A_cqn%d" % i, [128, 384], BF16) for i in range(2)])
            A.cqT = Rot([self.sb(es, "A_cqT%d" % i, [128, 3, 128], BF16) for i in range(2)])
            A.qsb = Rot([self.sb(es, "A_qsb%d" % i, [128, 8, 96], BF16) for i in range(2)])
            A.tA = self.sb(es, "A_tA", [128, 4, 16])
            A.tB = self.sb(es, "A_tB", [128, 4, 16])
            A.cn = Rot([self.sb(es, "A_cn%d" % i, [128, 256]) for i in range(2)])
            A.kr = Rot([self.sb(es, "A_kr%d" % i, [128, 32]) for i in range(2)])
            A.cbf = Rot([self.sb(es, "A_cbf%d" % i, [128, 288], BF16) for i in range(2)])
            A.QT = Rot([self.sb(es, "A_QT%d" % i, [96, 8, 128], BF16) for i in range(2)])
            cTs = [self.sb(es, "A_cT%d" % i, [128, 3, 128], BF16) for i in range(3)]
            for c_ in cTs:
                self.memset("pool", c_[:], 0.0, [c_])
            A.cT = Rot(cTs)
            A.PT = Rot([self.sb(es, "A_PT%d" % i, [128, 4, 128], BF16) for i in range(3)])
            A.rc = Rot([self.sb(es, "A_rc%d" % i, [128, 8]) for i in range(2)])
            A.oa = Rot([self.sb(es, "A_oa%d" % i, [128, 8, 64]) for i in range(2)])
            A.sz = Rot([self.sb(es, "A_sz%d" % i, [128, 512]) for i in range(2)])
            A.mx = Rot([self.sb(es, "A_mx%d" % i, [128, 512], BF16) for i in range(2)])
            A.pr = Rot([self.sb(es, "A_pr%d" % i, [128, 1184]) for i in range(2)])
            A.tab = Rot([self.sb(es, "A_tab%d" % i, [128, 32]) for i in range(2)])
            A.zseg = Rot([self.sb(es, "A_zs%d" % i, [16, 512]) for i in range(2)])
            A.cbig = Rot([self.sb(es, "A_cbig%d" % i, [128, 4, 288], BF16) for i in range(2)])
            A.rotF = Rot(self.psF[0:4])
            A.O = (self.psF[4], self.psF[5])
            pr_aux = self.sb(es, "A_praux", [80, 1184])
            tab_aux = self.sb(es, "A_tabaux", [80, 32])
            QT_aux = self.sb(es, "A_QTaux", [96, 8, 80], BF16)
            cT_aux = self.sb(es, "A_cTaux", [128, 3, 80], BF16)
            self.memset("pool", cT_aux[:], 0.0, [cT_aux])
            self.dma(pr_aux[0:16, :], self.proj.t[3:19, 0:1184], [self.ph("aux")], [pr_aux])
            self.dma(tab_aux[0:16, :], self.rope.t[0:16, :], [], [tab_aux])
            for b in range(NS):
                r0 = 4118 + 19 * b
                self.dma(pr_aux[16 + 16 * b:32 + 16 * b, :], self.proj.t[r0:r0 + 16, 0:1184], [self.ph("aux")], [pr_aux])
                self.dma(tab_aux[16 + 16 * b:32 + 16 * b, :], self.rope.t[NCACHE:NCACHE + DS, :], [], [tab_aux])
            cn, kr = self.mla_proj(A, 80, pr_aux, tab_aux, QT_aux, cT_aux)
            self.dma(self.p_lat.t[l, 0:16, :], cn[0:16, :], [cn], [], is_out=True)
            self.dma(self.p_kr.t[l, 0:16, :], kr[0:16, :], [kr], [], is_out=True)
            for b in range(NS):
                self.dma(self.s_lat.t[l, b], cn[16 + 16 * b:32 + 16 * b, :], [cn], [], is_out=True)
                self.dma(self.s_kr.t[l, b], kr[16 + 16 * b:32 + 16 * b, :], [kr], [], is_out=True)
            Rot.rel(cn, kr)
            self.kv_build(A, cT_aux, 0, 16, 0)
            zs = A.zseg.get()
            self.dma(zs[:, :], self.proj.t[3:19, 672:1184], [self.ph("aux")], [zs])
            self.attend(A, 16, QT_aux, 0, [(0, 16)], None, zs, 0, self.mixa.t[0:16, :], self.mh("aux"))
            Rot.rel(zs)
            prepped = {}

            def prep(j):
                pr = A.pr.get()
                tab = A.tab.get()
                r0 = 19 + 128 * j
                self.dma(pr[:, :], self.proj.t[r0:r0 + 128, 0:1184], [self.ph(j)], [pr])
                self.dma(tab[:, :], self.rope.t[16 + 128 * j:16 + 128 * (j + 1), :], [], [tab])
                QT = A.QT.get()
                cT = A.cT.get()
                yield
                for _ in self.mla_proj_g(A, 128, pr, tab, QT, cT, prepped, j):
                    yield
                cn, kr = prepped[("ck", j)]
                self.dma(self.p_lat.t[l, 16 + 128 * j:16 + 128 * (j + 1), :], cn[:, :], [cn], [], is_out=True)
                self.dma(self.p_kr.t[l, 16 + 128 * j:16 + 128 * (j + 1), :], kr[:, :], [kr], [], is_out=True)
                Rot.rel(cn, kr, tab)
                yield
                for _ in self.kv_build_g(A, cT, 0, 128, 1 + j):
                    yield
                Rot.rel(cT)
                prepped[j] = (QT, pr)

            for _ in prep(0):
                pass
            NT = SEQ // 128
            for j in range(NT):
                QT, pr = prepped.pop(j)
                blocks = [(0, 16)] + [(1 + i, 128) for i in range(j + 1)]
                att = self.attend_g(A, 128, QT, 0, blocks, 1 + j, pr, 672, self.mixa.t[80 + 128 * j:80 + 128 * (j + 1), :], self.mh(j))
                self.interleave([att, prep(j + 1) if j + 1 < NT else None])
                Rot.rel(QT, pr)
            for b in range(NS):
                cb = A.cbig.get()
                self.dma(cb[0:16, 0, 0:256], self.cl.t[l, b, 0:16, :], [], [cb], q="pool")
                self.dma(cb[0:16, 0, 256:288], self.ck.t[l, b, 0:16, :], [], [cb], q="pool")
                self.cache_block(A, cb, 0, 16, 0)
                Rot.rel(cb)
                for g in range(8):
                    cb = A.cbig.get()
                    r0 = 16 + 512 * g
                    self.dma(cb[:, :, 0:256], self.cl.t[l, b, r0:r0 + 512, :].rearrange("(j p) c -> p j c", p=128), [], [cb], q="pool")
                    self.dma(cb[:, :, 256:288], self.ck.t[l, b, r0:r0 + 512, :].rearrange("(j p) c -> p j c", p=128), [], [cb], q="pool")
                    for jj in range(4):
                        self.cache_block(A, cb, jj, 128, 1 + 4 * g + jj)
                    Rot.rel(cb)
                self.kv_build(A, cT_aux, 16 + 16 * b, 16, 33)
                zs = A.zseg.get()
                r0 = 4118 + 19 * b
                self.dma(zs[:, :], self.proj.t[r0:r0 + 16, 672:1184], [self.ph("aux")], [zs])
                blocks = [(0, 16)] + [(1 + i, 128) for i in range(32)] + [(33, 16)]
                self.attend(A, 16, QT_aux, 16 + 16 * b, blocks, None, zs, 0,
                            self.mixa.t[16 + 16 * b:32 + 16 * b, :], self.mh("aux"))
                Rot.rel(zs)

    def rope_apply(self, A, T, o1, o2, x1, x2, cos, sin, tA, tB, r, w):
        self.tt("dve", tA, x1, cos, ALU.mult, r, [A.tA])
        self.tt("dve", tB, x2, sin, ALU.mult, r, [A.tB])
        self.tt("dve", o1, tA, tB, ALU.subtract, [A.tA, A.tB], w)
        self.tt("dve", tA, x2, cos, ALU.mult, r, [A.tA])
        self.tt("dve", tB, x1, sin, ALU.mult, r, [A.tB])
        self.tt("dve", o2, tA, tB, ALU.add, [A.tA, A.tB], w)

    def mla_proj(self, A, T, pr, tab, QT, cT):
        d = {}
        for _ in self.mla_proj_g(A, T, pr, tab, QT, cT, d, 0):
            pass
        return d[("ck", 0)]

    def mla_proj_g(self, A, T, pr, tab, QT, cT, outd, key):
        conb = self.conb
        idb = self.ident(T, True)
        sq, r_ = A.ssq.get(), A.rs.get()
        self.act(A.junk[0:T, 0:384], pr[0:T, 0:384], AF.Square, [pr], [A.junk, sq], accum=sq[0:T, :])
        self.rstd(r_[0:T, :], sq[0:T, :], 384, [sq], [r_])
        cqn = A.cqn.get()
        self.stt(cqn[0:T, :], pr[0:T, 0:384], r_[0:T, 0:1], A.qn_bc[0:T, :], ALU.mult, ALU.mult, [pr, r_, A.qn_bc], [cqn])
        pb = self.rotB.get()
        for k in range(3):
            self.tr(pb[0:128, k * 128:k * 128 + T], cqn[0:T, k * 128:(k + 1) * 128], idb, [cqn, conb], [pb])
        cqT = A.cqT.get()
        self.cp("dve", cqT[:, :, 0:T], pb[:, 0:384].rearrange("p (k t) -> p k t", k=3)[:, :, 0:T], [pb], [cqT])
        Rot.rel(pb, sq, r_, cqn)
        yield
        q_sb = A.qsb.get()
        cos4 = tab[0:T, 0:16].unsqueeze(1).broadcast_to([T, 4, 16])
        sin4 = tab[0:T, 16:32].unsqueeze(1).broadcast_to([T, 4, 16])
        for half in range(2):
            pf = A.rotF.get()
            for k in range(3):
                self.mm(pf[0:T, 0:384], cqT[:, k, 0:T], A.wuq[:, k, half * 384:(half + 1) * 384], k == 0, k == 2, [cqT, A.wuq], [pf])
            pv = pf[0:T, 0:384].rearrange("p (h d) -> p h d", h=4)
            hs = slice(half * 4, half * 4 + 4)
            self.cp("dve", q_sb[0:T, hs, 0:64], pv[:, :, 0:64], [pf], [q_sb])
            self.rope_apply(A, T, q_sb[0:T, hs, 64:80], q_sb[0:T, hs, 80:96], pv[:, :, 64:80], pv[:, :, 80:96],
                            cos4, sin4, A.tA[0:T, :, :], A.tB[0:T, :, :], [pf, tab], [q_sb])
            Rot.rel(pf)
            yield
        pb = self.rotB.get()
        for h in range(8):
            self.tr(pb[0:96, h * 128:h * 128 + T], q_sb[0:T, h, :], idb, [q_sb, conb], [pb])
        self.cp("dve", QT[0:96, :, 0:T], pb[0:96, :].rearrange("p (h t) -> p h t", h=8)[:, :, 0:T], [pb], [QT])
        Rot.rel(pb, q_sb, cqT)
        yield
        sq, r_ = A.ssq.get(), A.rs.get()
        self.act(A.junk[0:T, 0:256], pr[0:T, 384:640], AF.Square, [pr], [A.junk, sq], accum=sq[0:T, :])
        self.rstd(r_[0:T, :], sq[0:T, :], 256, [sq], [r_])
        cn = A.cn.get()
        self.stt(cn[0:T, :], pr[0:T, 384:640], r_[0:T, 0:1], A.kvn_bc[0:T, :], ALU.mult, ALU.mult, [pr, r_, A.kvn_bc], [cn])
        kr = A.kr.get()
        self.rope_apply(A, T, kr[0:T, 0:16], kr[0:T, 16:32], pr[0:T, 640:656], pr[0:T, 656:672],
                        tab[0:T, 0:16], tab[0:T, 16:32], A.tA[0:T, 0, :], A.tB[0:T, 0, :], [pr, tab], [kr])
        cbf = A.cbf.get()
        self.cp("pool", cbf[0:T, 0:256], cn[0:T, :], [cn], [cbf])
        self.cp("pool", cbf[0:T, 256:288], kr[0:T, :], [kr], [cbf])
        self.c_transpose(A, cbf[0:T, :], T, cT, [cbf])
        Rot.rel(sq, r_, cbf)
        outd[("ck", key)] = (cn, kr)
        yield

    def c_transpose(self, A, src, n, cT, r):
        conb = self.conb
        idb = self.ident(n, True)
        pb = self.rotB.get()
        self.tr(pb[0:128, 0:n], src[:, 0:128], idb, r + [conb], [pb])
        self.tr(pb[0:128, 128:128 + n], src[:, 128:256], idb, r + [conb], [pb])
        self.tr(pb[0:32, 256:256 + n], src[:, 256:288], idb, r + [conb], [pb])
        self.cp("dve", cT[:, 0:2, 0:n], pb[:, 0:256].rearrange("p (k t) -> p k t", k=2)[:, :, 0:n], [pb], [cT])
        self.cp("dve", cT[0:32, 2, 0:n], pb[0:32, 256:256 + n], [pb], [cT])
        Rot.rel(pb)

    def cache_block(self, A, cb, jj, n, slot):
        cT = A.cT.get()
        self.c_transpose(A, cb[0:n, jj, :], n, cT, [cb])
        self.kv_build(A, cT, 0, n, slot)
        Rot.rel(cT)

    def kv_build(self, A, cT, coff, n, slot):
        for _ in self.kv_build_g(A, cT, coff, n, slot):
            pass

    def kv_build_g(self, A, cT, coff, n, slot):
        conb = self.conb
        kcol = 0 if slot == 0 else 16 + 128 * (slot - 1)
        sh = A.slot_h[slot]
        for half in range(2):
            pf = A.rotF.get()
            for hh in range(4):
                h = half * 4 + hh
                o = pf[0:96, hh * 128:hh * 128 + n]
                self.mm(o, A.wuk[:, 0, h, :], cT[:, 0, coff:coff + n], True, False, [A.wuk, cT], [pf])
                self.mm(o, A.wuk[:, 1, h, :], cT[:, 1, coff:coff + n], False, False, [A.wuk, cT], [pf])
                self.mm(o, conb[:, 256:352], cT[:, 2, coff:coff + n], False, True, [conb, cT], [pf])
            src = pf[0:96, :].rearrange("p (h t) -> p h t", h=4)[:, :, 0:n]
            self.cp("dve", A.KT[0:96, half * 4:half * 4 + 4, kcol:kcol + n], src, [pf], [sh])
            Rot.rel(pf)
            yield
        pf = A.rotF.get()
        self.mm(pf[0:n, 0:512], cT[:, 0, coff:coff + n], A.wuv[:, 0, :], True, False, [cT, A.wuv], [pf])
        self.mm(pf[0:n, 0:512], cT[:, 1, coff:coff + n], A.wuv[:, 1, :], False, True, [cT, A.wuv], [pf])
        self.cp("dve", A.V[0:n, slot, :, 0:64], pf[0:n, 0:512].rearrange("p (h d) -> p h d", h=8), [pf], [sh])
        Rot.rel(pf)
        yield

    def attend(self, A, T, QT, qoff, blocks, diag_slot, zsrc, zoff, mix_out, mix_h):
        for _ in self.attend_g(A, T, QT, qoff, blocks, diag_slot, zsrc, zoff, mix_out, mix_h):
            pass

    def attend_g(self, A, T, QT, qoff, blocks, diag_slot, zsrc, zoff, mix_out, mix_h):
        groups = []
        for bl in blocks:
            if groups and groups[-1][0][1] == bl[1] and len(groups[-1]) < 4:
                groups[-1].append(bl)
            else:
                groups.append([bl])
        nblk = len(blocks)
        work = []
        for h in range(8):
            nb = 0
            for grp in groups:
                work.append((h, grp, nb))
                nb += len(grp)

        def finish(item):
            h, grp, nb0, pf = item
            Ob = A.O[h // 4]
            hh = h % 4
            n = grp[0][1]
            g = len(grp)
            PT = A.PT.get()
            self.act(PT[0:n, 0:g, 0:T], pf[0:n, 0:g * 128].rearrange("p (g t) -> p g t", g=g)[:, :, 0:T], AF.Exp,
                     [pf], [PT], scale=SM_SCALE)
            Rot.rel(pf)
            for i, (slot, _) in enumerate(grp):
                if diag_slot is not None and slot == diag_slot:
                    self.memset("pool", PT[64:128, i, 0:64], 0.0, [PT])
            for i, (slot, _) in enumerate(grp):
                self.mm(Ob[0:T, hh * 65:hh * 65 + 65], PT[0:n, i, 0:T], A.V[0:n, slot, h, :],
                        nb0 + i == 0, nb0 + i == nblk - 1, [PT, A.slot_h[slot]], [Ob])
            Rot.rel(PT)

        pend = None
        for (h, grp, nb0) in work:
            n = grp[0][1]
            pf = A.rotF.get()
            for i, (slot, _) in enumerate(grp):
                kcol = 0 if slot == 0 else 16 + 128 * (slot - 1)
                self.mm(pf[0:n, i * 128:i * 128 + T], A.KT[0:96, h, kcol:kcol + n], QT[0:96, h, qoff:qoff + T],
                        True, True, [A.slot_h[slot], QT], [pf])
            if pend is not None:
                finish(pend)
            pend = (h, grp, nb0, pf)
            yield
        finish(pend)
        yield
        rc = A.rc.get()
        oa = A.oa.get()
        for half in range(2):
            Ov = A.O[half][0:T, 0:260].rearrange("p (h d) -> p h d", h=4)
            rcv = rc[0:T, half * 4:half * 4 + 4].unsqueeze(2)
            self.recip(rcv, Ov[:, :, 64:65], [A.O[half]], [rc])
            self.tt("dve", oa[0:T, half * 4:half * 4 + 4, :], Ov[:, :, 0:64], rcv.broadcast_to([T, 4, 64]), ALU.mult,
                    [A.O[half], rc], [oa])
        sz = A.sz.get()
        self.act(sz[0:T, :], zsrc[0:T, zoff:zoff + 512], AF.Silu, [zsrc], [sz])
        mx = A.mx.get()
        self.tt("dve", mx[0:T, :], oa[0:T, :, :].rearrange("p h d -> p (h d)"), sz[0:T, :], ALU.mult, [oa, sz], [mx])
        self.dma(mix_out, mx[0:T, :], [mx], [mix_h])
        Rot.rel(rc, oa, sz, mx)

    def phase_R(self, l):
        con, conb = self.con, self.conb
        with contextlib.ExitStack() as es:
            R = type("NS", (), {})()
            wout = self.sb(es, "R_wout", [128, 12, D], BF16)
            wov = self.w_out.t[l].rearrange("(k p) c -> p k c", p=128)
            for k in range(0, 12, 4):
                self.dma(wout[:, k:k + 4, :], wov[:, k:k + 4, :], [], [wout], q="pool")
            cw = self.sb(es, "R_cw", [128, 4 * 1536])
            self.dma(cw[:], self.conv_w.t[l].partition_broadcast(128), [], [cw])
            mnorm = self.sb(es, "R_mnorm", [128, 512])
            self.dma(mnorm[:], self.m_norm.t[l].partition_broadcast(128), [], [mnorm])
            gnorm = self.sb(es, "R_gnorm", [128, 128])
            self.dma(gnorm[:], self.g_norm.t[l].partition_broadcast(128), [], [gnorm])
            gb = self.sb(es, "R_gb", [128, 8])
            self.dma(gb[:], self.gate_b.t[l].partition_broadcast(128), [], [gb])
            dtb = self.sb(es, "R_dtb", [128, 4])
            self.dma(dtb[:], self.dt_bias.t[l].partition_broadcast(128), [], [dtb])
            nea = self.sb(es, "R_nea", [128, 4])
            self.dma(nea[:], self.a_log.t[l].partition_broadcast(128), [], [nea])
            self.act(nea[:], nea[:], AF.Exp, [nea], [nea])
            self.ts("dve", nea[:], nea[:], -1.0, ALU.mult, [nea], [nea])
            R.wout, R.cw, R.mnorm, R.gnorm, R.gb, R.dtb, R.nea = wout, cw, mnorm, gnorm, gb, dtb, nea

            def pool(name, shape, dt, n):
                return Rot([self.sb(es, "R_%s%d" % (name, i), shape, dt) for i in range(n)])
            R.MP = pool("mp", [64, 2568], F32, 1)
            R.GA = pool("ga", [64, 520], F32, 2)
            R.F1536 = pool("f1536", [64, 1536], F32, 4)
            R.F1024 = pool("f1024", [64, 1024], F32, 2)
            R.F512 = pool("f512", [64, 512], F32, 13 if K.OPT_X1 else 11)
            R.F256 = pool("f256", [64, 256], F32, 16)
            R.B1024 = pool("b1024", [64, 1024], BF16, 2)
            R.B512 = pool("b512", [64, 512], BF16, 10)
            R.B256 = pool("b256", [64, 256], BF16, 4)
            R.BT = pool("bt", [128, 768], BF16, 6)
            R.SM = pool("sm", [128, 12], F32, 64)
            R.X = pool("x", [128, D], F32, 1 if K.OPT_X1 else 2)
            R.XN = pool("xn", [128, D], F32, 1)
            R.MIXT = pool("mixt", [128, 12, 128], BF16, 2)
            R.MA = pool("ma", [128, 512], BF16, 1)
            R.rotF = Rot(self.psF)

            def state(nm):
                st = type("NS", (), {})()
                st.CT = self.sb(es, "R_CT" + nm, [128, 4, 128])
                st.CTb = self.sb(es, "R_CTb" + nm, [128, 4, 128], BF16)
                st.nT = self.sb(es, "R_nT" + nm, [128, 4])
                st.nTb = self.sb(es, "R_nTb" + nm, [128, 4], BF16)
                st.mbc = self.sb(es, "R_mbc" + nm, [128, 4])
                st.S = self.sb(es, "R_S" + nm, [128, 4, 128])
                st.Sb = self.sb(es, "R_Sb" + nm, [128, 4, 128], BF16)
                return st
            stp = state("p")
            sts = state("s")
            R.tmpC = self.sb(es, "R_tmpC", [128, 4, 128])
            for t_ in (stp.CT, stp.CTb, stp.nT, stp.nTb, stp.mbc, stp.S, stp.Sb):
                self.memset("pool", t_[:], 0.0, [t_])

            items = []
            mixA = R.MIXT.get()
            items.append(dict(T=16, row0=3, st=stp, mixT=mixA, coff=0, before=None, after=None))
            for b in range(NS):
                r0 = 4118 + 19 * b

                def bef(b=b):
                    self.load_state(R, l, b, sts)

                def aft(b=b, r0=r0, last=(b == NS - 1)):
                    self.store_state(R, sts, self.s_C.t[l, b], self.s_n.t[l, b], self.s_m.t[l, b:b + 1, :], self.s_S.t[l, b])
                    self.dma(self.s_cv.t[l, b], self.proj.t[r0 + 13:r0 + 16, 3752:5288], [], [], is_out=True)
                    if last:
                        self.out_proj(R, l, "aux", 80, mixA)
                        Rot.rel(mixA)
                items.append(dict(T=16, row0=r0, st=sts, mixT=mixA, coff=16 + 16 * b, before=bef, after=aft))
            cur = {}
            for j in range(SEQ // 128):
                for c in range(2):
                    def bef(j=j, c=c):
                        if c == 0:
                            cur["mixT"] = R.MIXT.get()

                    def aft(j=j, c=c):
                        if c == 1:
                            self.out_proj(R, l, j, 128, cur["mixT"])
                            Rot.rel(cur["mixT"])
                    items.append(dict(T=64, row0=19 + 128 * j + 64 * c, st=stp, mixT=None, coff=64 * c, before=bef, after=aft))
            ctxs = [dict() for _ in items]
            for _ in self.gdn_pre_g(R, l, items[0]["T"], items[0]["row0"], ctxs[0]):
                pass
            for i, it in enumerate(items):
                if it["before"] is not None:
                    it["before"]()
                mixT = it["mixT"] if it["mixT"] is not None else cur["mixT"]
                gens = [self.mlstm_g(R, l, it["T"], it["row0"], it["st"], mixT, it["coff"]),
                        self.gdn_chain_g(R, l, it["T"], it["st"], mixT, it["coff"], ctxs[i])]
                if i + 1 < len(items):
                    nx = items[i + 1]
                    gens.append(self.gdn_pre_g(R, l, nx["T"], nx["row0"], ctxs[i + 1]))
                self.interleave(gens)
                if it["after"] is not None:
                    it["after"]()
            self.store_state(R, stp, self.p_C.t[l], self.p_n.t[l], self.p_m.t[l:l + 1, :], self.p_S.t[l])
            self.dma(self.p_cv.t[l], self.proj.t[19 + SEQ - 3:19 + SEQ, 3752:5288], [], [], is_out=True)

    def load_state(self, R, l, b, st):
        self.dma(R.tmpC[:], self.imC.t[l, b].rearrange("h e d -> e h d"), [], [R.tmpC])
        pf = R.rotF.get()
        for h in range(4):
            self.tr(pf[:, h * 128:(h + 1) * 128], R.tmpC[:, h, :], self.ident(128), [R.tmpC, self.con], [pf])
        self.cp("dve", st.CT[:], pf[:, :].rearrange("p (h e) -> p h e", h=4), [pf], [st.CT])
        Rot.rel(pf)
        self.cp("act", st.CTb[:], st.CT[:], [st.CT], [st.CTb])
        self.dma(st.nT[:], self.imn.t[l, b].rearrange("h d -> d h"), [], [st.nT], nonc=True)
        self.cp("act", st.nTb[:], st.nT[:], [st.nT], [st.nTb])
        self.dma(st.mbc[:], self.imm.t[l, b].partition_broadcast(128), [], [st.mbc])
        self.dma(st.S[:], self.igS.t[l, b].rearrange("h k v -> k h v"), [], [st.S])
        self.cp("act", st.Sb[:], st.S[:], [st.S], [st.Sb])

    def store_state(self, R, st, oC, on, om, oS):
        pf = R.rotF.get()
        for h in range(4):
            self.tr(pf[:, h * 128:(h + 1) * 128], st.CT[:, h, :], self.ident(128), [st.CT, self.con], [pf])
        self.cp("dve", R.tmpC[:], pf[:, :].rearrange("p (h e) -> p h e", h=4), [pf], [R.tmpC])
        Rot.rel(pf)
        self.dma(oC.rearrange("h e d -> e h d"), R.tmpC[:], [R.tmpC], [], is_out=True)
        self.dma(on.rearrange("h d -> d h"), st.nT[:], [st.nT], [], nonc=True, is_out=True)
        self.dma(om, st.mbc[0:1, :], [st.mbc], [], is_out=True)
        self.dma(oS.rearrange("h k v -> k h v"), st.S[:], [st.S], [], is_out=True)

    def out_proj(self, R, l, tile, T, mixT):
        conb = self.conb
        ma = R.MA.get()
        if tile == "aux":
            self.dma(ma[0:T, :], self.mixa.t[0:80, :], [], [ma])
        else:
            self.dma(ma[0:T, :], self.mixa.t[80 + 128 * tile:80 + 128 * (tile + 1), :], [], [ma])
        pb = self.rotB.get()
        for k in range(4):
            self.tr(pb[0:128, k * 128:k * 128 + T], ma[0:T, k * 128:(k + 1) * 128], self.ident(T, True), [ma, conb], [pb])
        self.cp("act", mixT[:, 0:4, 0:T], pb[:, 0:512].rearrange("p (k t) -> p k t", k=4)[:, :, 0:T], [pb], [mixT])
        Rot.rel(pb, ma)
        xt = R.X.get()
        self.load_x(l, tile, xt)
        xn = R.XN.get()
        for half in range(2):
            pf = R.rotF.get()
            for k in range(12):
                self.mm(pf[0:T, 0:512], mixT[:, k, 0:T], R.wout[:, k, half * 512:(half + 1) * 512], k == 0, k == 11, [mixT, R.wout], [pf])
            self.tt("dve", xn[0:T, half * 512:(half + 1) * 512], pf[0:T, 0:512], xt[0:T, half * 512:(half + 1) * 512], ALU.add, [pf, xt], [xn])
            Rot.rel(pf)
        Rot.rel(xt)
        if l < DEPTH - 1:
            if tile == "aux":
                self.dma(self.xscr.t[0:80, :], xn[0:80, :], [xn], [self.xh("aux")])
            else:
                self.dma(self.xscr.t[80 + 128 * tile:80 + 128 * (tile + 1), :], xn[:, :], [xn], [self.xh(tile)])
        else:
            sq = R.SM.get()
            yo = R.X.get()
            self.act(yo[0:T, :], xn[0:T, :], AF.Square, [xn], [yo, sq], accum=sq[0:T, 0:1])
            self.rstd(sq[0:T, 0:1], sq[0:T, 0:1], D, [sq], [sq])
            self.stt(yo[0:T, :], xn[0:T, :], sq[0:T, 0:1], self.fin_bc[0:T, :], ALU.mult, ALU.mult, [xn, sq, self.fin_bc], [yo])
            if tile == "aux":
                self.dma(self.y_s.t[:, :], yo[16:80, :], [yo], [], is_out=True)
            else:
                self.dma(self.y_p.t[128 * tile:128 * (tile + 1), :], yo[:, :], [yo], [], is_out=True)
            Rot.rel(sq, yo)
        Rot.rel(xn)

    @staticmethod
    def v3(ap, a):
        return ap.rearrange("p (a b) -> p a b", a=a)

    @staticmethod
    def bch(ap, T, n):
        return ap.unsqueeze(2).broadcast_to([T, 4, n])

    @staticmethod
    def bcm(ap, T, n):
        return ap.unsqueeze(1).broadcast_to([T, 4, n])

    def mlstm_chunk(self, R, l, T, row0, st, mixT, coff):
        for _ in self.mlstm_g(R, l, T, row0, st, mixT, coff):
            pass

    def mlstm_g(self, R, l, T, row0, st, mixT, coff):
        con, conb = self.con, self.conb
        L = []

        def g(p):
            b = p.get()
            L.append(b)
            return b
        P = slice(0, T)
        T4 = 4 * T
        tri = con[0:T, C_TRI:C_TRI + T]
        SLm = con[0:T, C_SL:C_SL + T]
        onesTT = con[0:T, C_ONE:C_ONE + T]
        idT = con[0:T, C_ID:C_ID + T]
        NEG = con[0:T, C_NEG:C_NEG + T]
        cs_ = C_S64 if T == 64 else C_S16
        sel = con[0:T, cs_:cs_ + 128]
        idb = self.ident(T, True)
        v3, bch, bcm = self.v3, self.bch, self.bcm
        KS = 128.0 ** -0.5
        mp = g(R.MP)
        self.dma(mp[P, :], self.proj.t[row0:row0 + T, 1184:3752], [], [mp])
        if K.OPT_K32:
            k32 = g(R.F512)
            self.dma(k32[P, :], self.proj.t[row0:row0 + T, 1696:2208], [], [k32])
        qkb = g(R.B1024)
        self.cp("act", qkb[P, 0:512], mp[P, 0:512], [mp], [qkb])
        self.act(qkb[P, 512:1024], mp[P, 512:1024], AF.Copy, [mp], [qkb], scale=KS)
        pb = self.rotB.get()
        for i in range(8):
            self.tr(pb[0:128, i * T:(i + 1) * T], qkb[P, i * 128:(i + 1) * 128], idb, [qkb, conb], [pb])
        qkT = g(R.BT)
        self.cp("act" if K.OPT_EVACT else "dve", qkT[:, 0:8 * T], pb[:, 0:8 * T], [pb], [qkT])
        Rot.rel(pb)
        vb = g(R.B512)
        self.cp("pool", vb[P, :], mp[P, 1024:1536], [mp], [vb])
        yield
        g8 = g(R.SM)
        self.tt("dve", g8[P, 0:8], mp[P, 1536:1544], R.gb[P, 0:8], ALU.add, [mp, R.gb], [g8])
        ipre, xf = g8[P, 0:4], g8[P, 4:8]
        s1 = g(R.SM)
        self.stt(s1[P, 0:4], xf, -1.0, xf, ALU.mult, ALU.max, [g8], [s1])
        self.act(s1[P, 0:4], s1[P, 0:4], AF.Exp, [s1], [s1], scale=-1.0)
        self.act(s1[P, 0:4], s1[P, 0:4], AF.Ln, [s1], [s1], bias=1.0)
        lf = g(R.SM)
        self.ts("dve", lf[P, 0:4], xf, 0.0, ALU.min, [g8], [lf])
        self.tt("dve", lf[P, 0:4], lf[P, 0:4], s1[P, 0:4], ALU.subtract, [lf, s1], [lf])
        yield
        R1 = g(R.F256)
        self.tt("pool", v3(R1[P, 0:T4], 4), bcm(SLm, T, T), bch(lf[P, 0:4], T, T), ALU.mult, [con, lf], [R1])
        R2 = g(R.F256)
        self.tt("pool", v3(R2[P, 0:T4], 4), bcm(idT, T, T), bch(ipre, T, T), ALU.mult, [con, g8], [R2])
        pf = R.rotF.get()
        self.mm(pf[0:T, 0:T4], tri, R1[P, 0:T4], True, False, [con, R1], [pf])
        self.mm(pf[0:T, 0:T4], onesTT, R2[P, 0:T4], False, True, [con, R2], [pf])
        self.mm(pf[0:T, T4:T4 + 4], tri, lf[P, 0:4], True, True, [con, lf], [pf])
        Dm = g(R.F256)
        self.tt("dve", v3(Dm[P, 0:T4], 4), v3(pf[0:T, 0:T4], 4), bcm(NEG, T, T), ALU.add, [pf, con], [Dm])
        MIB = g(R.SM)
        self.cp("act", MIB[P, 8:12], pf[0:T, T4:T4 + 4], [pf], [MIB])
        Rot.rel(pf)
        rmx = g(R.SM)
        self.red(rmx[P, 0:4], v3(Dm[P, 0:T4], 4), ALU.max, [Dm], [rmx])
        yield
        self.tt("dve", MIB[P, 4:8], MIB[P, 8:12], st.mbc[P, 0:4], ALU.add, [MIB, st.mbc], [MIB])
        self.tt("dve", MIB[P, 0:4], MIB[P, 4:8], rmx[P, 0:4], ALU.max, [MIB, rmx], [MIB])
        E = g(R.F256)
        self.tt("dve", v3(E[P, 0:T4], 4), v3(Dm[P, 0:T4], 4), bch(MIB[P, 0:4], T, T), ALU.subtract, [Dm, MIB], [E])
        self.act(E[P, 0:T4], E[P, 0:T4], AF.Exp, [E], [E])
        wi = g(R.SM)
        self.tt("dve", wi[P, 0:4], MIB[P, 4:8], MIB[P, 0:4], ALU.subtract, [MIB], [wi])
        self.act(wi[P, 0:4], wi[P, 0:4], AF.Exp, [wi], [wi])
        emt = g(R.SM)
        self.act(emt[P, 0:4], MIB[P, 0:4], AF.Exp, [MIB], [emt], scale=-1.0)
        yield
        pf = R.rotF.get()
        for h in range(4):
            self.mm(pf[0:T, h * T:(h + 1) * T], qkT[:, h * T:(h + 1) * T], qkT[:, (4 + h) * T:(5 + h) * T], True, True, [qkT], [pf])
        qkE = g(R.F256)
        self.tt("dve", qkE[P, 0:T4], pf[0:T, 0:T4], E[P, 0:T4], ALU.mult, [pf, E], [qkE])
        Rot.rel(pf)
        den1 = g(R.SM)
        self.red(den1[P, 0:4], v3(qkE[P, 0:T4], 4), ALU.add, [qkE], [den1])
        yield
        pf = R.rotF.get()
        for h in range(4):
            self.tr(pf[0:T, h * T:(h + 1) * T], qkE[P, h * T:(h + 1) * T], idT, [qkE, con], [pf])
        qkET = g(R.B256)
        self.cp("act", qkET[P, 0:T4], pf[0:T, 0:T4], [pf], [qkET])
        Rot.rel(pf)
        yield
        pf1 = R.rotF.get()
        for h in range(4):
            self.mm(pf1[0:T, h * 128:(h + 1) * 128], qkET[P, h * T:(h + 1) * T], vb[P, h * 128:(h + 1) * 128], True, True, [qkET, vb], [pf1])
        num1 = g(R.F512)
        self.cp("act", num1[P, :], pf1[0:T, :], [pf1], [num1])
        Rot.rel(pf1)
        yield
        pf2 = R.rotF.get()
        for h in range(4):
            self.mm(pf2[0:T, h * 128:(h + 1) * 128], qkT[:, h * T:(h + 1) * T], st.CTb[:, h, :], True, True, [qkT, st.CTb], [pf2])
        pf3 = R.rotF.get()
        for h in range(4):
            self.mm(pf3[0:T, h:h + 1], qkT[:, h * T:(h + 1) * T], st.nTb[:, h:h + 1], True, True, [qkT, st.nTb], [pf3])
        num = g(R.F512)
        self.tt("dve", v3(num[P, :], 4), v3(pf2[0:T, :], 4), bch(wi[P, 0:4], T, 128), ALU.mult, [pf2, wi], [num])
        Rot.rel(pf2)
        self.tt("dve", num[P, :], num[P, :], num1[P, :], ALU.add, [num, num1], [num])
        den = g(R.SM)
        self.tt("dve", den[P, 0:4], pf3[0:T, 0:4], wi[P, 0:4], ALU.mult, [pf3, wi], [den])
        Rot.rel(pf3)
        self.tt("dve", den[P, 0:4], den[P, 0:4], den1[P, 0:4], ALU.add, [den, den1], [den])
        self.stt(den[P, 0:4], den[P, 0:4], -1.0, den[P, 0:4], ALU.mult, ALU.max, [den], [den])
        self.tt("dve", den[P, 0:4], den[P, 0:4], emt[P, 0:4], ALU.max, [den, emt], [den])
        self.recip(den[P, 0:4], den[P, 0:4], [den], [den])
        self.tt("dve", v3(num[P, :], 4), v3(num[P, :], 4), bch(den[P, 0:4], T, 128), ALU.mult, [num, den], [num])
        yield
        pf = R.rotF.get()
        self.mm(pf[0:128, 0:12], sel, MIB[P, 0:12], True, True, [con, MIB], [pf])
        LB = g(R.SM)
        self.cp("act", LB[:, 0:12], pf[:, 0:12], [pf], [LB])
        Rot.rel(pf)
        yield
        self.cp("dve", st.mbc[:, 0:4], LB[:, 0:4], [LB], [st.mbc])
        gs = g(R.SM)
        self.tt("dve", gs[:, 0:4], LB[:, 4:8], LB[:, 0:4], ALU.subtract, [LB], [gs])
        self.act(gs[:, 0:4], gs[:, 0:4], AF.Exp, [gs], [gs])
        gt = g(R.SM)
        self.tt("dve", gt[P, 0:4], LB[P, 8:12], MIB[P, 8:12], ALU.subtract, [LB, MIB], [gt])
        self.tt("dve", gt[P, 0:4], gt[P, 0:4], ipre, ALU.add, [gt, g8], [gt])
        self.tt("dve", gt[P, 0:4], gt[P, 0:4], LB[P, 0:4], ALU.subtract, [gt, LB], [gt])
        self.act(gt[P, 0:4], gt[P, 0:4], AF.Exp, [gt], [gt])
        yield
        kg = g(R.B512)
        if K.OPT_K32:
            self.stt(v3(kg[P, :], 4), v3(k32[P, :], 4), KS, bch(gt[P, 0:4], T, 128), ALU.mult, ALU.mult, [k32, gt], [kg])
        else:
            self.stt(v3(kg[P, :], 4), v3(mp[P, 512:1024], 4), KS, bch(gt[P, 0:4], T, 128), ALU.mult, ALU.mult, [mp, gt], [kg])
        pfC = R.rotF.get()
        for h in range(4):
            self.mm(pfC[:, h * 128:(h + 1) * 128], kg[P, h * 128:(h + 1) * 128], vb[P, h * 128:(h + 1) * 128], True, True, [kg, vb], [pfC])
        pfn = R.rotF.get()
        for h in range(4):
            self.mm(pfn[:, h:h + 1], kg[P, h * 128:(h + 1) * 128], conb[0:T, 128:129], True, True, [kg, conb], [pfn])
        self.tt("dve", st.CT[:], st.CT[:], bch(gs[:, 0:4], 128, 128), ALU.mult, [st.CT, gs], [st.CT])
        self.tt("dve", st.CT[:], st.CT[:], v3(pfC[:, :], 4), ALU.add, [st.CT, pfC], [st.CT])
        Rot.rel(pfC)
        self.cp("act", st.CTb[:], st.CT[:], [st.CT], [st.CTb])
        self.tt("dve", st.nT[:], st.nT[:], gs[:, 0:4], ALU.mult, [st.nT, gs], [st.nT])
        self.tt("dve", st.nT[:], st.nT[:], pfn[:, 0:4], ALU.add, [st.nT, pfn], [st.nT])
        Rot.rel(pfn)
        self.cp("act", st.nTb[:], st.nT[:], [st.nT], [st.nTb])
        yield
        sig = g(R.F512)
        if K.OPT_SIGEXP:
            self.act(sig[P, :], mp[P, 1544:2056], AF.Exp, [mp], [sig], scale=-1.0)
            self.ts("pool", sig[P, :], sig[P, :], 1.0, ALU.add, [sig], [sig])
            self.recip(sig[P, :], sig[P, :], [sig], [sig])
        else:
            self.act(sig[P, :], mp[P, 1544:2056], AF.Sigmoid, [mp], [sig])
        szm = g(R.F512)
        self.act(szm[P, :], mp[P, 2056:2568], AF.Silu, [mp], [szm])
        self.tt("pool", szm[P, :], szm[P, :], R.mnorm[P, :], ALU.mult, [szm, R.mnorm], [szm])
        self.tt("dve", num[P, :], num[P, :], sig[P, :], ALU.mult, [num, sig], [num])
        self.tt("pool", sig[P, :], num[P, :], num[P, :], ALU.mult, [num], [sig])
        s4 = g(R.SM)
        self.red(s4[P, 0:4], v3(sig[P, :], 4), ALU.add, [sig], [s4])
        self.rstd(s4[P, 0:4], s4[P, 0:4], 128, [s4], [s4])
        yield
        self.tt("dve", v3(num[P, :], 4), v3(num[P, :], 4), bch(s4[P, 0:4], T, 128), ALU.mult, [num, s4], [num])
        omb = g(R.B512)
        self.tt("dve", omb[P, :], num[P, :], szm[P, :], ALU.mult, [num, szm], [omb])
        pb = self.rotB.get()
        for h in range(4):
            self.tr(pb[0:128, h * T:(h + 1) * T], omb[P, h * 128:(h + 1) * 128], idb, [omb, conb], [pb])
        self.cp("act", mixT[:, 4:8, coff:coff + T], v3(pb[:, 0:T4], 4), [pb], [mixT])
        Rot.rel(pb)
        Rot.rel(*L)
        yield

    def gdn_chunk(self, R, l, T, row0, st, mixT, coff):
        ctx = {}
        for _ in self.gdn_pre_g(R, l, T, row0, ctx):
            pass
        for _ in self.gdn_chain_g(R, l, T, st, mixT, coff, ctx):
            pass

    def gdn_pre_g(self, R, l, T, row0, ctx):
        con, conb = self.con, self.conb
        L = []

        def g(p):
            b = p.get()
            L.append(b)
            return b
        P = slice(0, T)
        T4 = 4 * T
        nst = 5 if T == 64 else 3
        tri = con[0:T, C_TRI:C_TRI + T]
        SLm = con[0:T, C_SL:C_SL + T]
        INCL = con[0:T, C_INCL:C_INCL + T]
        idT = con[0:T, C_ID:C_ID + T]
        ones128 = con[0:T, C_ONE:C_ONE + 128]
        idb = self.ident(T, True)
        v3, bch, bcm = self.v3, self.bch, self.bcm
        ga = g(R.GA)
        self.dma(ga[P, :], self.proj.t[row0:row0 + T, 5288:5808], [], [ga])
        acc = None
        for j in range(4):
            gp = g(R.F1536)
            self.dma(gp[P, :], self.proj.t[row0 - 3 + j:row0 - 3 + j + T, 3752:5288], [], [gp])
            self.tt("pool", gp[P, :], gp[P, :], R.cw[P, j * 1536:(j + 1) * 1536], ALU.mult, [gp, R.cw], [gp])
            if acc is None:
                acc = gp
            else:
                self.tt("pool" if j == 1 else "dve", acc[P, :], acc[P, :], gp[P, :], ALU.add, [acc, gp], [acc])
        cs = acc
        self.act(cs[P, :], cs[P, :], AF.Silu, [cs], [cs])
        yield
        qkn = g(R.F1024)
        self.tt("pool", qkn[P, :], cs[P, 0:1024], cs[P, 0:1024], ALU.mult, [cs], [qkn])
        s8 = g(R.SM)
        self.red(s8[P, 0:8], v3(qkn[P, :], 8), ALU.add, [qkn], [s8])
        self.act(s8[P, 0:8], s8[P, 0:8], AF.Ln, [s8], [s8], bias=EPS)
        self.act(s8[P, 0:8], s8[P, 0:8], AF.Exp, [s8], [s8], scale=-0.5)
        self.ts("dve", s8[P, 0:4], s8[P, 0:4], 128.0 ** -0.5, ALU.mult, [s8], [s8])
        self.tt("dve", v3(qkn[P, :], 8), v3(cs[P, 0:1024], 8), s8[P, 0:8].unsqueeze(2).broadcast_to([T, 8, 128]), ALU.mult, [cs, s8], [qkn])
        yield
        y = g(R.SM)
        self.tt("dve", y[P, 0:4], ga[P, 0:4], R.dtb[P, 0:4], ALU.add, [ga, R.dtb], [y])
        s1 = g(R.SM)
        self.stt(s1[P, 0:4], y[P, 0:4], -1.0, y[P, 0:4], ALU.mult, ALU.max, [y], [s1])
        self.act(s1[P, 0:4], s1[P, 0:4], AF.Exp, [s1], [s1], scale=-1.0)
        self.act(s1[P, 0:4], s1[P, 0:4], AF.Ln, [s1], [s1], bias=1.0)
        gg = g(R.SM)
        self.ts("dve", gg[P, 0:4], y[P, 0:4], 0.0, ALU.max, [y], [gg])
        self.tt("dve", gg[P, 0:4], gg[P, 0:4], s1[P, 0:4], ALU.add, [gg, s1], [gg])
        self.tt("dve", gg[P, 0:4], gg[P, 0:4], R.nea[P, 0:4], ALU.mult, [gg, R.nea], [gg])
        bt = g(R.SM)
        if K.OPT_SIGEXP:
            self.act(bt[P, 0:4], ga[P, 4:8], AF.Exp, [ga], [bt], scale=-1.0)
            self.ts("dve", bt[P, 0:4], bt[P, 0:4], 1.0, ALU.add, [bt], [bt])
            self.recip(bt[P, 0:4], bt[P, 0:4], [bt], [bt])
        else:
            self.act(bt[P, 0:4], ga[P, 4:8], AF.Sigmoid, [ga], [bt])
        self.ts("dve", bt[P, 4:8], bt[P, 0:4], -1.0, ALU.mult, [bt], [bt])
        yield
        Rg = g(R.F256)
        self.tt("pool", v3(Rg[P, 0:T4], 4), bcm(SLm, T, T), bch(gg[P, 0:4], T, T), ALU.mult, [con, gg], [Rg])
        pf = R.rotF.get()
        self.mm(pf[0:T, 0:T4], tri, Rg[P, 0:T4], True, True, [con, Rg], [pf])
        self.mm(pf[0:T, T4:T4 + 4], tri, gg[P, 0:4], True, True, [con, gg], [pf])
        pf128 = R.rotF.get()
        self.mm(pf128[0:128, 0:4], ones128, gg[P, 0:4], True, True, [con, gg], [pf128])
        gam = g(R.F256)
        self.act(gam[P, 0:T4], pf[0:T, 0:T4], AF.Exp, [pf], [gam])
        Gc = g(R.SM)
        self.cp("act", Gc[P, 0:4], pf[0:T, T4:T4 + 4], [pf], [Gc])
        Rot.rel(pf)
        GL = g(R.SM)
        self.cp("act", GL[:, 0:4], pf128[:, 0:4], [pf128], [GL])
        Rot.rel(pf128)
        yield
        gam_s = g(R.F256)
        self.tt("dve", v3(gam_s[P, 0:T4], 4), v3(gam[P, 0:T4], 4), bcm(SLm, T, T), ALU.mult, [gam, con], [gam_s])
        self.tt("dve", v3(gam[P, 0:T4], 4), v3(gam[P, 0:T4], 4), bcm(INCL, T, T), ALU.mult, [gam, con], [gam])
        eG = g(R.SM)
        self.act(eG[P, 0:4], Gc[P, 0:4], AF.Exp, [Gc], [eG])
        eGl = g(R.SM)
        self.act(eGl[:, 0:4], GL[:, 0:4], AF.Exp, [GL], [eGl])
        self.tt("dve", eG[P, 4:8], GL[P, 0:4], Gc[P, 0:4], ALU.subtract, [GL, Gc], [eG])
        self.act(eG[P, 4:8], eG[P, 4:8], AF.Exp, [eG], [eG])
        self.tt("dve", eG[P, 8:12], bt[P, 0:4], eG[P, 0:4], ALU.mult, [bt, eG], [eG])
        yield
        qkb = g(R.B1024)
        self.cp("act", qkb[P, :], qkn[P, :], [qkn], [qkb])
        qgb = g(R.B512)
        self.tt("pool", v3(qgb[P, :], 4), v3(qkn[P, 0:512], 4), bch(eG[P, 0:4], T, 128), ALU.mult, [qkn, eG], [qgb])
        bk = g(R.F512)
        self.tt("dve", v3(bk[P, :], 4), v3(qkn[P, 512:1024], 4), bch(eG[P, 8:12], T, 128), ALU.mult, [qkn, eG], [bk])
        bv = g(R.F512)
        self.tt("pool", v3(bv[P, :], 4), v3(cs[P, 1024:1536], 4), bch(bt[P, 0:4], T, 128), ALU.mult, [cs, bt], [bv])
        kd = g(R.B512)
        self.tt("pool", v3(kd[P, :], 4), v3(qkn[P, 512:1024], 4), bch(eG[P, 4:8], T, 128), ALU.mult, [qkn, eG], [kd])
        yield
        pb = self.rotB.get()
        for i in range(8):
            self.tr(pb[0:128, i * T:(i + 1) * T], qkb[P, i * 128:(i + 1) * 128], idb, [qkb, conb], [pb])
        for h in range(4):
            self.tr(pb[0:128, (8 + h) * T:(9 + h) * T], qgb[P, h * 128:(h + 1) * 128], idb, [qgb, conb], [pb])
        qkT = g(R.BT)
        self.cp("act" if K.OPT_EVACT else "dve", qkT[:, 0:12 * T], pb[:, 0:12 * T], [pb], [qkT])
        Rot.rel(pb)
        yield
        qT = lambda h: qkT[:, h * T:(h + 1) * T]
        kT = lambda h: qkT[:, (4 + h) * T:(5 + h) * T]
        qgT = lambda h: qkT[:, (8 + h) * T:(9 + h) * T]
        pf = R.rotF.get()
        for h in range(4):
            self.mm(pf[0:T, h * T:(h + 1) * T], kT(h), kT(h), True, True, [qkT], [pf])
        for h in range(4):
            self.mm(pf[0:T, (4 + h) * T:(5 + h) * T], qT(h), kT(h), True, True, [qkT], [pf])
        N = g(R.F256)
        self.tt("dve", N[P, 0:T4], pf[0:T, 0:T4], gam_s[P, 0:T4], ALU.mult, [pf, gam_s], [N])
        self.tt("dve", v3(N[P, 0:T4], 4), v3(N[P, 0:T4], 4), bch(bt[P, 4:8], T, T), ALU.mult, [N, bt], [N])
        QG = g(R.F256)
        self.tt("dve", QG[P, 0:T4], pf[0:T, T4:2 * T4], gam[P, 0:T4], ALU.mult, [pf, gam], [QG])
        Rot.rel(pf)
        yield
        pf = R.rotF.get()
        for h in range(4):
            self.tr(pf[0:T, h * T:(h + 1) * T], N[P, h * T:(h + 1) * T], idT, [N, con], [pf])
        for h in range(4):
            self.tr(pf[0:T, (4 + h) * T:(5 + h) * T], QG[P, h * T:(h + 1) * T], idT, [QG, con], [pf])
        Q = g(R.F256)
        self.cp("act", Q[P, 0:T4], pf[0:T, 0:T4], [pf], [Q])
        QGT = g(R.B256)
        self.cp("act" if K.OPT_EVACT else "dve", QGT[P, 0:T4], pf[0:T, T4:2 * T4], [pf], [QGT])
        Rot.rel(pf)
        Y = g(R.F256)
        self.tt("dve", v3(Y[P, 0:T4], 4), v3(Q[P, 0:T4], 4), bcm(idT, T, T), ALU.add, [Q, con], [Y])
        yield
        Pm = N
        hs = lambda b_, h: b_[P, h * T:(h + 1) * T]
        for s in range(nst):
            last = (s == nst - 1)
            pfP = R.rotF.get()
            for h in range(4):
                self.mm(pfP[0:T, h * T:(h + 1) * T], hs(Q, h), hs(Pm, h), True, True, [Q, Pm], [pfP])
            if not last:
                pfQ = R.rotF.get()
                for h in range(4):
                    self.mm(pfQ[0:T, h * T:(h + 1) * T], hs(Pm, h), hs(Q, h), True, True, [Q, Pm], [pfQ])
            Pn = g(R.F256)
            self.cp("act", Pn[P, 0:T4], pfP[0:T, 0:T4], [pfP], [Pn])
            Rot.rel(pfP)
            if not last:
                Qn = g(R.F256)
                self.cp("act" if K.OPT_EVACT else "dve", Qn[P, 0:T4], pfQ[0:T, 0:T4], [pfQ], [Qn])
                Rot.rel(pfQ)
            pfY = R.rotF.get()
            for h in range(4):
                self.mm(pfY[0:T, h * T:(h + 1) * T], hs(Pn, h), hs(Y, h), True, True, [Pn, Y], [pfY])
            self.tt("dve", Y[P, 0:T4], Y[P, 0:T4], pfY[0:T, 0:T4], ALU.add, [Y, pfY], [Y])
            Rot.rel(pfY)
            yield
            for old in ((Pm, Q) if not last else (Pm, Q, Pn)):
                if old in L:
                    L.remove(old)
                    Rot.rel(old)
            Pm = Pn
            if not last:
                Q = Qn
        pfu = R.rotF.get()
        for h in range(4):
            self.mm(pfu[0:T, h * 128:(h + 1) * 128], hs(Y, h), bv[P, h * 128:(h + 1) * 128], True, True, [Y, bv], [pfu])
        usb = g(R.F512)
        self.cp("act", usb[P, :], pfu[0:T, :], [pfu], [usb])
        Rot.rel(pfu)
        yield
        pfw = R.rotF.get()
        for h in range(4):
            self.mm(pfw[0:128, h * T:(h + 1) * T], bk[P, h * 128:(h + 1) * 128], hs(Y, h), True, True, [bk, Y], [pfw])
        wTb = g(R.BT)
        self.cp("act" if K.OPT_EVACT else "dve", wTb[:, 0:T4], pfw[:, 0:T4], [pfw], [wTb])
        Rot.rel(pfw)
        keep = [usb, wTb, qkT, QGT, kd, eGl, ga]
        for b_ in list(L):
            if b_ not in keep:
                Rot.rel(b_)
        ctx.update(usb=usb, wTb=wTb, qkT=qkT, QGT=QGT, kd=kd, eGl=eGl, ga=ga, keep=keep)
        yield

    def gdn_chain_g(self, R, l, T, st, mixT, coff, ctx):
        con, conb = self.con, self.conb
        L = []

        def g(p):
            b = p.get()
            L.append(b)
            return b
        P = slice(0, T)
        T4 = 4 * T
        idb = self.ident(T, True)
        v3, bch, bcm = self.v3, self.bch, self.bcm
        usb, wTb, qkT, QGT, kd, eGl, ga = (ctx[k_] for k_ in ("usb", "wTb", "qkT", "QGT", "kd", "eGl", "ga"))
        qgT = lambda h: qkT[:, (8 + h) * T:(9 + h) * T]
        hs = lambda b_, h: b_[P, h * T:(h + 1) * T]
        pf = R.rotF.get()
        for h in range(4):
            self.mm(pf[0:T, h * 128:(h + 1) * 128], wTb[:, h * T:(h + 1) * T], st.Sb[:, h, :], True, True, [wTb, st.Sb], [pf])
        dl = g(R.B512)
        self.tt("dve", dl[P, :], usb[P, :], pf[0:T, :], ALU.subtract, [usb, pf], [dl])
        Rot.rel(pf)
        yield
        pfo = R.rotF.get()
        for h in range(4):
            o = pfo[0:T, h * 128:(h + 1) * 128]
            self.mm(o, qgT(h), st.Sb[:, h, :], True, False, [qkT, st.Sb], [pfo])
            self.mm(o, hs(QGT, h), dl[P, h * 128:(h + 1) * 128], False, True, [QGT, dl], [pfo])
        pfS = R.rotF.get()
        for h in range(4):
            self.mm(pfS[:, h * 128:(h + 1) * 128], kd[P, h * 128:(h + 1) * 128], dl[P, h * 128:(h + 1) * 128], True, True, [kd, dl], [pfS])
        self.tt("dve", st.S[:], st.S[:], bch(eGl[:, 0:4], 128, 128), ALU.mult, [st.S, eGl], [st.S])
        self.tt("dve", st.S[:], st.S[:], v3(pfS[:, :], 4), ALU.add, [st.S, pfS], [st.S])
        Rot.rel(pfS)
        self.cp("act", st.Sb[:], st.S[:], [st.S], [st.Sb])
        osb = g(R.F512)
        self.cp("act", osb[P, :], pfo[0:T, :], [pfo], [osb])
        Rot.rel(pfo)
        yield
        sq = g(R.F512)
        self.tt("pool", sq[P, :], osb[P, :], osb[P, :], ALU.mult, [osb], [sq])
        s4 = g(R.SM)
        self.red(s4[P, 0:4], v3(sq[P, :], 4), ALU.add, [sq], [s4])
        self.rstd(s4[P, 0:4], s4[P, 0:4], 128, [s4], [s4])
        yield
        self.tt("dve", v3(osb[P, :], 4), v3(osb[P, :], 4), bch(s4[P, 0:4], T, 128), ALU.mult, [osb, s4], [osb])
        self.act(sq[P, :], ga[P, 8:520], AF.Silu, [ga], [sq])
        self.tt("pool", v3(sq[P, :], 4), v3(sq[P, :], 4), bcm(R.gnorm[P, :], T, 128), ALU.mult, [sq, R.gnorm], [sq])
        ogb = g(R.B512)
        self.tt("dve", ogb[P, :], osb[P, :], sq[P, :], ALU.mult, [osb, sq], [ogb])
        pb = self.rotB.get()
        for h in range(4):
            self.tr(pb[0:128, h * T:(h + 1) * T], ogb[P, h * 128:(h + 1) * 128], idb, [ogb, conb], [pb])
        self.cp("act", mixT[:, 8:12, coff:coff + T], v3(pb[:, 0:T4], 4), [pb], [mixT])
        Rot.rel(pb)
        Rot.rel(*L)
        Rot.rel(*ctx["keep"])
        yield


_CACHE = {}


def _program(dbg=None, nlayers=DEPTH, phases="PAR"):
    key = (dbg, nlayers, phases)
    if key not in _CACHE:
        _CACHE[key] = K(dbg, nlayers, phases).build()
    return _CACHE[key]


def make_in_maps(inp):
    f = lambda a: np.ascontiguousarray(np.asarray(a, dtype=np.float32))
    consts = make_consts()
    rope = make_rope()
    shared = {
        "meta": f(inp["meta_tokens"]), "norm_w": f(inp["norm_w"]), "w_in": f(inp["w_in"]),
        "q_norm": f(inp["mla_q_norm"]), "w_uq": f(inp["mla_w_uq"]), "kv_norm": f(inp["mla_kv_norm"]),
        "w_uk": f(inp["mla_w_uk"]).reshape(DEPTH, 256, 512), "w_uv": f(inp["mla_w_uv"]).reshape(DEPTH, 256, 512),
        "gate_b": f(inp["mlstm_gate_b"]).reshape(DEPTH, 8), "m_norm": f(inp["mlstm_norm"]),
        "conv_w": f(inp["gdn_conv_w"]).reshape(DEPTH, 4 * 1536), "a_log": f(inp["gdn_a_log"]),
        "dt_bias": f(inp["gdn_dt_bias"]), "g_norm": f(inp["gdn_norm"]), "w_out": f(inp["w_out"]),
        "final_norm": f(inp["final_norm"]), "consts": consts, "rope": rope,
    }
    maps = []
    for c in range(8):
        sl = slice(NS * c, NS * c + NS)
        m = dict(shared)
        m["xp"] = f(inp["x_prompt"][c])
        m["xs"] = f(inp["x_sample"][sl]).reshape(NS * DS, D)
        m["cl"] = f(inp["cache_mla_latent"][:, sl])
        m["ck"] = f(inp["cache_mla_krope"][:, sl])
        m["imC"] = f(inp["state_mlstm_C"][:, sl])
        m["imn"] = f(inp["state_mlstm_n"][:, sl])
        m["imm"] = f(inp["state_mlstm_m"][:, sl])
        m["igS"] = f(inp["state_gdn_S"][:, sl])
        m["igc"] = f(inp["state_gdn_conv"][:, sl])
        maps.append(m)
    return maps


def kernel(**inputs):
    nc = _program()
    maps = make_in_maps(inputs)
    res = run_bass_kernel_spmd(nc, maps, core_ids=list(range(8)))
    R = res.results
    st = lambda name, axis: np.stack([np.asarray(r[name], dtype=np.float32) for r in R], axis=axis)
    cat = lambda name, axis: np.concatenate([np.asarray(r[name], dtype=np.float32) for r in R], axis=axis)
    y_prompt = st("y_p", 0)
    y_sample = st("y_s", 0).reshape(8 * NS, DS, D)
    return (
        y_prompt, y_sample,
        st("p_lat", 1), st("p_kr", 1), st("p_C", 1), st("p_n", 1), st("p_m", 1), st("p_S", 1), st("p_cv", 1),
        cat("s_lat", 1), cat("s_kr", 1), cat("s_C", 1), cat("s_n", 1), cat("s_m", 1), cat("s_S", 1), cat("s_cv", 1),
    )
```

```python
import math
import contextlib
import numpy as np
import concourse.bass as bass
import concourse.mybir as mybir
from concourse.bass_utils import run_bass_kernel_spmd

F32 = mybir.dt.float32
BF16 = mybir.dt.bfloat16
AF = mybir.ActivationFunctionType
ALU = mybir.AluOpType
AX = mybir.AxisListType

D = 1024
SEQ = 4096
NMETA = 16
DEPTH = 2
NS = 4
DS = 16
PAST = 4096
NCACHE = NMETA + PAST
EPS = 1e-6
IN_COLS = 5808
NROWS_X = 80 + SEQ
PROJ_ROWS = 3 + 16 + SEQ + NS * 19
SM_SCALE = 1.0 / math.sqrt(96.0)

C_ID, C_ONE, C_TRI, C_SL, C_INCL, C_NEG, C_S64, C_S16, C_SELR = 0, 128, 256, 320, 384, 448, 512, 640, 768
NCON = 864


def make_consts():
    c = np.zeros((128, NCON), np.float32)
    c[:, C_ID:C_ID + 128] = np.eye(128)
    c[:, C_ONE:C_ONE + 128] = 1.0
    k = np.arange(64)[:, None]
    t = np.arange(64)[None, :]
    c[:64, C_TRI:C_TRI + 64] = (k <= t)
    c[:64, C_SL:C_SL + 64] = (k > t)
    c[:64, C_INCL:C_INCL + 64] = (t <= k)
    c[:64, C_NEG:C_NEG + 64] = np.where(t <= k, 0.0, -1e30)
    c[63, C_S64:C_S64 + 128] = 1.0
    c[15, C_S16:C_S16 + 128] = 1.0
    c[:32, C_SELR + 64:C_SELR + 96] = np.eye(32)
    return c


def make_rope():
    half = 16
    freq = (np.float32(10000.0) ** (-np.arange(half, dtype=np.float32) / np.float32(half))).astype(np.float32)
    pos = np.arange(NCACHE + DS, dtype=np.float32)
    ang = (pos[:, None] * freq[None, :]).astype(np.float32)
    return np.concatenate([np.cos(ang), np.sin(ang)], axis=1).astype(np.float32)


class H:
    __slots__ = ("name", "w", "r", "excl")

    def __init__(self, name="", excl=False):
        self.name = name
        self.w = None
        self.r = []
        self.excl = excl


class Buf:
    __slots__ = ("t", "h", "busy")

    def __init__(self, t, name=""):
        self.t = t
        self.h = H(name)
        self.busy = False

    def __getitem__(self, k):
        return self.t[k]


def _hs(lst):
    out = []
    for x in lst:
        if x is None:
            continue
        out.append(x.h if isinstance(x, Buf) else x)
    return out


class Sched:
    ENGS = ("pe", "act", "dve", "pool", "sp")
    XLAT = 1.2

    def __init__(self, nc, n_dma_sems=20):
        self.nc = nc
        self.nodes = []
        self.aset = {}
        self.cur_set = 1
        self.seg_start = [0]
        self.n_dma = n_dma_sems
        self.sems = {}

    def _deps(self, reads, writes):
        d = set()
        for h in reads:
            if h.w is not None:
                d.add(h.w)
        for h in writes:
            if h.w is not None:
                d.add(h.w)
            d.update(h.r)
        return d

    def _record(self, idx, reads, writes):
        for h in reads:
            h.r.append(idx)
        for h in writes:
            h.w = idx
            h.r = []

    def emit(self, eng, fn, reads=(), writes=(), cost=0.3, aset=0):
        reads, writes = _hs(reads), _hs(writes)
        ex = [h for h in reads if h.excl]
        if ex:
            reads = [h for h in reads if not h.excl]
            writes = writes + [h for h in ex if h not in writes]
        deps = self._deps(reads, writes)
        idx = len(self.nodes)
        self.nodes.append((eng, fn, deps, False, cost))
        if aset:
            self.aset[idx] = aset
        self._record(idx, reads, writes)
        return idx

    def dma(self, q, fn, reads=(), writes=(), is_out=False, cost=2.5):
        reads, writes = _hs(reads), _hs(writes)
        deps = self._deps(reads, writes)
        idx = len(self.nodes)
        self.nodes.append((q, fn, deps, True, cost))
        self._record(idx, reads, writes)
        return idx

    def barrier(self):
        if self.seg_start[-1] != len(self.nodes):
            self.seg_start.append(len(self.nodes))

    def _schedule(self, lo, hi):
        import heapq
        nodes = self.nodes
        n = hi - lo
        succ = [[] for _ in range(n)]
        indeg = [0] * n
        for i in range(lo, hi):
            for d in nodes[i][2]:
                if d >= lo:
                    succ[d - lo].append(i - lo)
                    indeg[i - lo] += 1
        prio = [0.0] * n
        for i in range(n - 1, -1, -1):
            c = nodes[lo + i][4]
            m = 0.0
            for s in succ[i]:
                if prio[s] > m:
                    m = prio[s]
            prio[i] = c + m + self.XLAT
        future = {e: [] for e in self.ENGS}
        avail = {e: [] for e in self.ENGS}
        tfree = {e: 0.0 for e in self.ENGS}
        ready_t = [0.0] * n
        order = {e: [] for e in self.ENGS}
        for i in range(n):
            if indeg[i] == 0:
                heapq.heappush(future[nodes[lo + i][0]], (0.0, i))
        done = 0
        while done < n:
            best_e, best_t = None, None
            for e in self.ENGS:
                if avail[e]:
                    t = tfree[e]
                elif future[e]:
                    t = max(tfree[e], future[e][0][0])
                else:
                    continue
                if best_t is None or t < best_t:
                    best_e, best_t = e, t
            e, t = best_e, best_t
            fu = future[e]
            while fu and fu[0][0] <= t:
                rt, i = heapq.heappop(fu)
                heapq.heappush(avail[e], (-prio[i], i))
            if e == "act" and len(avail[e]) > 1:
                top = avail[e][0]
                ts_ = self.aset.get(lo + top[1], 0)
                if ts_ != 0 and ts_ != self.cur_set:
                    best = None
                    for cand in avail[e]:
                        cs_ = self.aset.get(lo + cand[1], 0)
                        if (cs_ == 0 or cs_ == self.cur_set) and (best is None or cand < best):
                            best = cand
                    if best is not None and (-best[0]) >= (-top[0]) - 12.0:
                        avail[e].remove(best)
                        heapq.heapify(avail[e])
                        i = best[1]
                    else:
                        _, i = heapq.heappop(avail[e])
                else:
                    _, i = heapq.heappop(avail[e])
            else:
                _, i = heapq.heappop(avail[e])
            node = nodes[lo + i]
            sw = 0.0
            if e == "act":
                s_ = self.aset.get(lo + i, 0)
                if s_ != 0 and s_ != self.cur_set:
                    self.cur_set = s_
                    sw = 1.3
            if node[3]:
                fin = t + node[4]
                tfree[e] = t + 0.15
            else:
                fin = t + node[4] + sw
                tfree[e] = fin
            order[e].append(lo + i)
            done += 1
            for s in succ[i]:
                se = nodes[lo + s][0]
                lat = 0.08 if (se == e and not node[3]) else self.XLAT
                if fin + lat > ready_t[s]:
                    ready_t[s] = fin + lat
                indeg[s] -= 1
                if indeg[s] == 0:
                    heapq.heappush(future[se], (ready_t[s], s))
        return order

    def finalize(self):
        nc = self.nc
        nodes = self.nodes
        self.barrier()
        segs = list(zip(self.seg_start[:-1], self.seg_start[1:]))
        eng_ops = {e: [] for e in self.ENGS}
        token = [None] * len(nodes)
        cnt = {e: 0 for e in self.ENGS}
        dma_i = {e: 0 for e in self.ENGS}
        dma_val = {e: [0] * self.n_dma for e in self.ENGS}
        dma_prev = {}
        for (lo, hi) in segs:
            order = self._schedule(lo, hi)
            for e in self.ENGS:
                for idx in order[e]:
                    if nodes[idx][3]:
                        i = dma_i[e]
                        dma_i[e] = (i + 1) % self.n_dma
                        key = ("dma", e, i)
                        if dma_val[e][i] > 0:
                            dma_prev[idx] = (key, dma_val[e][i])
                        dma_val[e][i] += 16
                        token[idx] = (key, dma_val[e][i], 16)
                    else:
                        cnt[e] += 1
                        token[idx] = (e, cnt[e], 1)
                    eng_ops[e].append(("node", idx))
            allw = {e: cnt[e] for e in self.ENGS if cnt[e] > 0}
            for e in self.ENGS:
                for i, v in enumerate(dma_val[e]):
                    if v > 0:
                        allw[("dma", e, i)] = v
            for e in self.ENGS:
                eng_ops[e].append(("bar", dict(allw)))
        keys = list(self.ENGS)
        for e in self.ENGS:
            for i, v in enumerate(dma_val[e]):
                if v > 0:
                    keys.append(("dma", e, i))
        with contextlib.ExitStack() as es:
            for k in keys:
                nm = k if isinstance(k, str) else "d_%s_%d" % (k[1], k[2])
                self.sems[k] = es.enter_context(nc.semaphore("s_" + nm))
            block = es.enter_context(nc.Block())
            sems = self.sems

            def run(eng_name):
                def body(e):
                    waited = {}
                    for kind, x in eng_ops[eng_name]:
                        if kind == "bar":
                            for k, v in x.items():
                                if k == eng_name:
                                    continue
                                if waited.get(k, 0) < v:
                                    waited[k] = v
                                    e.wait_ge(sems[k], v)
                            continue
                        idx = x
                        _, fn, deps, is_dma, _ = nodes[idx]
                        need = {}
                        if idx in dma_prev:
                            k, v = dma_prev[idx]
                            need[k] = v
                        for d in deps:
                            k, v, _ = token[d]
                            if k == eng_name and eng_name == "pe":
                                continue
                            if need.get(k, 0) < v:
                                need[k] = v
                        for k, v in need.items():
                            if waited.get(k, 0) < v:
                                waited[k] = v
                                e.wait_ge(sems[k], v)
                        k, v, inc = token[idx]
                        fn(e).then_inc(sems[k], inc)
                return body

            block.tensor(run("pe"))
            block.scalar(run("act"))
            block.vector(run("dve"))
            block.gpsimd(run("pool"))
            block.sync(run("sp"))


class Rot:
    def __init__(self, bufs):
        self.bufs = bufs
        self.i = 0

    def get(self):
        for _ in range(len(self.bufs)):
            b = self.bufs[self.i]
            self.i = (self.i + 1) % len(self.bufs)
            if not b.busy:
                b.busy = True
                return b
        raise AssertionError("rotating pool exhausted: all buffers leased")

    @staticmethod
    def rel(*bs):
        for b in bs:
            b.busy = False


class K:
    OPT_K32 = False
    OPT_SIGEXP = False
    OPT_WSPLIT = True
    OPT_X1 = False
    OPT_EVACT = True

    def __init__(self, dbg=None, nlayers=DEPTH, phases="PAR"):
        self.dbg = dbg
        self.nlayers = nlayers
        self.phases = phases
        self.nc = bass.Bass("TRN2", target_bir_lowering=False)
        self.S = Sched(self.nc)
        self.es = contextlib.ExitStack()
        self.dq = 0

    def dram(self, name, shape, dt=F32, kind=None):
        if kind is None:
            t = self.nc.dram_tensor(name, list(shape), dt)
        else:
            t = self.nc.dram_tensor(name, list(shape), dt, kind=kind)
        return Buf(t.ap(), name)

    def sb(self, es, name, shape, dt=F32):
        self.dq += 1
        name = "%s_u%d" % (name, self.dq)
        return Buf(es.enter_context(self.nc.sbuf_tensor(name, list(shape), dt)), name)

    def ps(self, es, name, shape, dt=F32):
        b = Buf(es.enter_context(self.nc.psum_tensor(name, list(shape), dt)), name)
        b.h.excl = True
        return b

    def mm(self, out, lhsT, rhs, start, stop, r, w):
        n = max(64, rhs.free_size()) * (4 if rhs.dtype == F32 else 1)
        self.S.emit("pe", lambda e: e.matmul(out, lhsT=lhsT, rhs=rhs, start=start, stop=stop), r, w, cost=0.04 + n / 1400.0)

    def tr(self, out, in_, ident, r, w):
        self.S.emit("pe", lambda e: e.transpose(out, in_, ident), r, w, cost=0.04 + max(64, in_.partition_size()) / 1400.0)

    @staticmethod
    def _c(eng, ap):
        n = ap.free_size()
        if eng == "dve":
            return 0.12 + n / 960.0
        if eng == "act":
            return 0.2 + n / 1400.0
        return 0.35 + n / 700.0

    def act(self, out, in_, func, r, w, scale=None, bias=None, accum=None):
        kw = {}
        if scale is not None:
            kw["scale"] = scale
        if bias is not None:
            kw["bias"] = bias
        if accum is not None:
            kw["accum_out"] = accum
        aset = 1 if func in (AF.Exp, AF.Ln) else (2 if func == AF.Silu else (3 if func in (AF.Sigmoid, AF.Sqrt) else 0))
        self.S.emit("act", lambda e: e.activation(out=out, in_=in_, func=func, **kw), r, w, cost=self._c("act", out) + (0.1 if accum is not None else 0), aset=aset)

    def tt(self, eng, out, in0, in1, op, r, w):
        self.S.emit(eng, lambda e: e.tensor_tensor(out=out, in0=in0, in1=in1, op=op), r, w, cost=self._c(eng, out))

    def ts(self, eng, out, in0, s1, op0, r, w, s2=None, op1=None):
        if op1 is None:
            self.S.emit(eng, lambda e: e.tensor_scalar(out=out, in0=in0, scalar1=s1, scalar2=None, op0=op0), r, w, cost=self._c(eng, out))
        else:
            self.S.emit(eng, lambda e: e.tensor_scalar(out=out, in0=in0, scalar1=s1, scalar2=s2, op0=op0, op1=op1), r, w, cost=self._c(eng, out))

    def stt(self, out, in0, scalar, in1, op0, op1, r, w):
        self.S.emit("dve", lambda e: e.scalar_tensor_tensor(out=out, in0=in0, scalar=scalar, in1=in1, op0=op0, op1=op1), r, w, cost=self._c("dve", out))

    def red(self, out, in_, op, r, w):
        self.S.emit("dve", lambda e: e.tensor_reduce(out=out, in_=in_, axis=AX.X, op=op), r, w, cost=self._c("dve", in_))

    def cp(self, eng, out, in_, r, w):
        if eng == "act":
            self.S.emit("act", lambda e: e.activation(out=out, in_=in_, func=AF.Copy), r, w, cost=self._c("act", out))
        else:
            self.S.emit(eng, lambda e: e.tensor_copy(out=out, in_=in_), r, w, cost=self._c(eng, out))

    def recip(self, out, in_, r, w):
        self.S.emit("dve", lambda e: e.reciprocal(out=out, in_=in_), r, w, cost=self._c("dve", out))

    def memset(self, eng, ap, val, w):
        self.S.emit(eng, lambda e: e.memset(ap, val), [], w, cost=self._c(eng, ap))

    def dma(self, out, in_, r, w, q=None, nonc=False, is_out=False):
        if q is None:
            q = ("sp", "act")[self.dq % 2] if False else "sp"
        c = 2.0 + in_.nbytes() / 150e3 * (2.0 if q == "pool" else 1.0)
        if nonc:
            self.S.dma(q, lambda e: e.dma_start(out=out, in_=in_, allow_slow_non_contiguous=True), r, w, is_out, cost=c)
        else:
            self.S.dma(q, lambda e: e.dma_start(out=out, in_=in_), r, w, is_out, cost=c)

    @staticmethod
    def interleave(gens):
        gens = [g for g in gens if g is not None]
        while gens:
            alive = []
            for g in gens:
                try:
                    next(g)
                    alive.append(g)
                except StopIteration:
                    pass
            gens = alive

    def rstd(self, out, ssq, n, r, w):
        self.act(out, ssq, AF.Ln, r, w, scale=1.0 / n, bias=EPS)
        self.act(out, out, AF.Exp, w, w, scale=-0.5)

    def build(self):
        nc = self.nc
        es = self.es
        dbg = self.dbg
        EI, EO = "ExternalInput", "ExternalOutput"
        self.xp = self.dram("xp", [SEQ, D], kind=EI)
        self.xs = self.dram("xs", [NS * DS, D], kind=EI)
        self.meta = self.dram("meta", [NMETA, D], kind=EI)
        self.cl = self.dram("cl", [DEPTH, NS, NCACHE, 256], kind=EI)
        self.ck = self.dram("ck", [DEPTH, NS, NCACHE, 32], kind=EI)
        self.imC = self.dram("imC", [DEPTH, NS, 4, 128, 128], kind=EI)
        self.imn = self.dram("imn", [DEPTH, NS, 4, 128], kind=EI)
        self.imm = self.dram("imm", [DEPTH, NS, 4], kind=EI)
        self.igS = self.dram("igS", [DEPTH, NS, 4, 128, 128], kind=EI)
        self.igc = self.dram("igc", [DEPTH, NS, 3, 1536], kind=EI)
        self.norm_w = self.dram("norm_w", [DEPTH, D], kind=EI)
        self.w_in = self.dram("w_in", [DEPTH, D, IN_COLS], kind=EI)
        self.q_norm = self.dram("q_norm", [DEPTH, 384], kind=EI)
        self.w_uq = self.dram("w_uq", [DEPTH, 384, 768], kind=EI)
        self.kv_norm = self.dram("kv_norm", [DEPTH, 256], kind=EI)
        self.w_uk = self.dram("w_uk", [DEPTH, 256, 512], kind=EI)
        self.w_uv = self.dram("w_uv", [DEPTH, 256, 512], kind=EI)
        self.gate_b = self.dram("gate_b", [DEPTH, 8], kind=EI)
        self.m_norm = self.dram("m_norm", [DEPTH, 512], kind=EI)
        self.conv_w = self.dram("conv_w", [DEPTH, 4 * 1536], kind=EI)
        self.a_log = self.dram("a_log", [DEPTH, 4], kind=EI)
        self.dt_bias = self.dram("dt_bias", [DEPTH, 4], kind=EI)
        self.g_norm = self.dram("g_norm", [DEPTH, 128], kind=EI)
        self.w_out = self.dram("w_out", [DEPTH, 1536, D], kind=EI)
        self.final_norm = self.dram("final_norm", [D], kind=EI)
        self.consts = self.dram("consts", [128, NCON], kind=EI)
        self.rope = self.dram("rope", [NCACHE + DS, 32], kind=EI)

        self.y_p = self.dram("y_p", [SEQ, D], kind=EO)
        self.y_s = self.dram("y_s", [NS * DS, D], kind=EO)
        self.p_lat = self.dram("p_lat", [DEPTH, NMETA + SEQ, 256], kind=EO)
        self.p_kr = self.dram("p_kr", [DEPTH, NMETA + SEQ, 32], kind=EO)
        self.p_C = self.dram("p_C", [DEPTH, 4, 128, 128], kind=EO)
        self.p_n = self.dram("p_n", [DEPTH, 4, 128], kind=EO)
        self.p_m = self.dram("p_m", [DEPTH, 4], kind=EO)
        self.p_S = self.dram("p_S", [DEPTH, 4, 128, 128], kind=EO)
        self.p_cv = self.dram("p_cv", [DEPTH, 3, 1536], kind=EO)
        self.s_lat = self.dram("s_lat", [DEPTH, NS, DS, 256], kind=EO)
        self.s_kr = self.dram("s_kr", [DEPTH, NS, DS, 32], kind=EO)
        self.s_C = self.dram("s_C", [DEPTH, NS, 4, 128, 128], kind=EO)
        self.s_n = self.dram("s_n", [DEPTH, NS, 4, 128], kind=EO)
        self.s_m = self.dram("s_m", [DEPTH, NS, 4], kind=EO)
        self.s_S = self.dram("s_S", [DEPTH, NS, 4, 128, 128], kind=EO)
        self.s_cv = self.dram("s_cv", [DEPTH, NS, 3, 1536], kind=EO)

        dk = EO if dbg else None
        self.proj = self.dram("proj_scr", [PROJ_ROWS, IN_COLS], kind=dk)
        self.mixa = self.dram("mixa_scr", [NROWS_X, 512], BF16, kind=dk)
        self.xscr = self.dram("x_scr", [NROWS_X, D], kind=dk)
        self.proj_h = {}
        self.mixa_h = {}
        self.x_h = {}

        self.con = self.sb(es, "con", [128, NCON])
        self.conb = self.sb(es, "conb", [128, 256 + 96], BF16)
        self.dma(self.con[:], self.consts.t, [], [self.con])
        self.dma(self.conb[:, 0:256], self.consts.t[:, 0:256], [], [self.conb], q="pool")
        self.dma(self.conb[:, 256:352], self.consts.t[:, C_SELR:C_SELR + 96], [], [self.conb], q="pool")
        self.psF = [self.ps(es, "psF%d" % i, [128, 512]) for i in range(6)]
        self.psB = [self.ps(es, "psB%d" % i, [128, 1024], BF16) for i in range(2)]
        self.rotB = Rot(self.psB)
        self.fin_bc = self.sb(es, "fin_bc", [128, D])
        self.dma(self.fin_bc[:], self.final_norm.t.partition_broadcast(128), [], [self.fin_bc])

        for l in range(self.nlayers):
            if "P" in self.phases:
                self.phase_P(l)
                self.S.barrier()
            if "A" in self.phases:
                self.phase_A(l)
                self.S.barrier()
            if "R" in self.phases:
                self.phase_R(l)
                self.S.barrier()
        self.S.finalize()
        es.close()
        return nc

    def ph(self, key):
        if key not in self.proj_h:
            self.proj_h[key] = H("proj%s" % (key,))
        return self.proj_h[key]

    def mh(self, key):
        if key not in self.mixa_h:
            self.mixa_h[key] = H("mixa%s" % (key,))
        return self.mixa_h[key]

    def xh(self, key):
        if key not in self.x_h:
            self.x_h[key] = H("x%s" % (key,))
        return self.x_h[key]

    def ident(self, n, bf=False):
        if bf:
            return self.conb[0:n, 0:n]
        return self.con[0:n, C_ID:C_ID + n]

    def load_x(self, l, tile, xt):
        if tile == "aux":
            if l == 0:
                self.dma(xt[0:16, :], self.meta.t, [], [xt])
                self.dma(xt[16:80, :], self.xs.t, [], [xt])
            else:
                self.dma(xt[0:80, :], self.xscr.t[0:80, :], [self.xh("aux")], [xt])
            return 80
        j = tile
        if l == 0:
            self.dma(xt[:, :], self.xp.t[j * 128:(j + 1) * 128, :], [], [xt])
        else:
            self.dma(xt[:, :], self.xscr.t[80 + j * 128:80 + (j + 1) * 128, :], [self.xh(j)], [xt])
        return 128

    def phase_P(self, l):
        con, conb = self.con, self.conb
        with contextlib.ExitStack() as es:
            w_bf = self.sb(es, "P_w", [128, 8, IN_COLS], BF16)
            wv = self.w_in.t[l].rearrange("(k p) c -> p k c", p=128)
            ngw = (IN_COLS + 511) // 512
            w_h = [H("w_in_g%d" % g_) for g_ in range(ngw)]
            if K.OPT_WSPLIT:
                for g_ in range(ngw):
                    c0_ = g_ * 512
                    cw_ = min(512, IN_COLS - c0_)
                    self.dma(w_bf[:, :, c0_:c0_ + cw_], wv[:, :, c0_:c0_ + cw_], [], [w_h[g_]], q="pool")
            else:
                for k in range(8):
                    self.dma(w_bf[:, k, :], wv[:, k, :], [], w_h, q="pool")
            nw = self.sb(es, "P_nw", [128, D])
            self.dma(nw[:], self.norm_w.t[l].partition_broadcast(128), [], [nw])
            xts = Rot([self.sb(es, "P_x%d" % i, [128, D]) for i in range(2)])
            junk = self.sb(es, "P_junk", [128, D], BF16)
            ssq = Rot([self.sb(es, "P_ssq%d" % i, [128, 1]) for i in range(2)])
            rs = Rot([self.sb(es, "P_rs%d" % i, [128, 1]) for i in range(2)])
            xns = Rot([self.sb(es, "P_xn%d" % i, [128, D], BF16) for i in range(2)])
            xTs = Rot([self.sb(es, "P_xT%d" % i, [128, 8, 128], BF16) for i in range(2)])
            prs = Rot([self.sb(es, "P_pr%d" % i, [128, IN_COLS]) for i in range(2)])
            rotF = Rot(self.psF)
            zt = self.sb(es, "P_z", [4, 1536])
            self.memset("pool", zt[:], 0.0, [zt])
            self.dma(self.proj.t[0:3, 3752:5288], zt[0:3, :], [zt], [self.ph("hist")])
            for b in range(NS):
                r0 = 4115 + 19 * b
                self.dma(self.proj.t[r0:r0 + 3, 3752:5288], self.igc.t[l, b], [], [self.ph("hist")])
            for tile in ["aux"] + list(range(SEQ // 128)):
                xt = xts.get()
                T = self.load_x(l, tile, xt)
                sq, r_ = ssq.get(), rs.get()
                self.act(junk[0:T, :], xt[0:T, :], AF.Square, [xt], [junk, sq], accum=sq[0:T, :])
                self.rstd(r_[0:T, :], sq[0:T, :], D, [sq], [r_])
                xn = xns.get()
                self.stt(xn[0:T, :], xt[0:T, :], r_[0:T, 0:1], nw[0:T, :], ALU.mult, ALU.mult, [xt, r_, nw], [xn])
                pb = self.rotB.get()
                for k in range(8):
                    self.tr(pb[0:128, k * 128:k * 128 + T], xn[0:T, k * 128:(k + 1) * 128], self.ident(T, True), [xn, conb], [pb])
                xT = xTs.get()
                pbv = pb[:, :].rearrange("p (k t) -> p k t", k=8)
                self.cp("dve", xT[:, :, 0:T], pbv[:, :, 0:T], [pb], [xT])
                Rot.rel(pb, xt, sq, r_, xn)
                pr = prs.get()
                ng = (IN_COLS + 511) // 512
                for g in range(ng):
                    c0 = g * 512
                    cw = min(512, IN_COLS - c0)
                    pf = rotF.get()
                    for k in range(8):
                        self.mm(pf[0:T, 0:cw], xT[:, k, 0:T], w_bf[:, k, c0:c0 + cw], k == 0, k == 7, [xT, w_h[g]], [pf])
                    if g % 2 == 0:
                        self.cp("act", pr[0:T, c0:c0 + cw], pf[0:T, 0:cw], [pf], [pr])
                    else:
                        self.cp("dve", pr[0:T, c0:c0 + cw], pf[0:T, 0:cw], [pf], [pr])
                    Rot.rel(pf)
                Rot.rel(xT)
                if tile == "aux":
                    self.dma(self.proj.t[3:19, :], pr[0:16, :], [pr], [self.ph("aux")])
                    for b in range(NS):
                        r0 = 4118 + 19 * b
                        self.dma(self.proj.t[r0:r0 + 16, :], pr[16 + 16 * b:32 + 16 * b, :], [pr], [self.ph("aux")])
                else:
                    r0 = 19 + tile * 128
                    self.dma(self.proj.t[r0:r0 + 128, :], pr[:, :], [pr], [self.ph(tile)])
                Rot.rel(pr)

    def phase_A(self, l):
        con, conb = self.con, self.conb
        with contextlib.ExitStack() as es:
            A = type("NS", (), {})()
            wuq = self.sb(es, "A_wuq", [128, 3, 768], BF16)
            self.dma(wuq[:], self.w_uq.t[l].rearrange("(k p) c -> p k c", p=128), [], [wuq], q="pool")
            wuk = self.sb(es, "A_wuk", [128, 2, 8, 96], BF16)
            self.memset("pool", wuk[:], 0.0, [wuk])
            for k in range(2):
                self.dma(wuk[:, k, :, 0:64], self.w_uk.t[l, k * 128:(k + 1) * 128, :].rearrange("p (h d) -> p h d", h=8), [], [wuk], q="pool")
            wuv = self.sb(es, "A_wuv", [128, 2, 512], BF16)
            self.dma(wuv[:], self.w_uv.t[l].rearrange("(k p) c -> p k c", p=128), [], [wuv], q="pool")
            qn_bc = self.sb(es, "A_qn", [128, 384])
            self.dma(qn_bc[:], self.q_norm.t[l].partition_broadcast(128), [], [qn_bc])
            kvn_bc = self.sb(es, "A_kvn", [128, 256])
            self.dma(kvn_bc[:], self.kv_norm.t[l].partition_broadcast(128), [], [kvn_bc])
            KT = self.sb(es, "A_KT", [96, 8, NCACHE + DS], BF16)
            V = self.sb(es, "A_V", [128, 34, 8, 65], BF16)
            self.memset("pool", V[:], 1.0, [V])
            slot_h = [H("slot%d" % i) for i in range(34)]
            for i in range(34):
                slot_h[i].w = V.h.w
            A.wuq, A.wuk, A.wuv, A.qn_bc, A.kvn_bc, A.KT, A.V, A.slot_h = wuq, wuk, wuv, qn_bc, kvn_bc, KT, V, slot_h
            A.junk = self.sb(es, "A_junk", [128, 384], BF16)
            A.ssq = Rot([self.sb(es, "A_ssq%d" % i, [128, 1]) for i in range(4)])
            A.rs = Rot([self.sb(es, "A_rs%d" % i, [128, 1]) for i in range(4)])
            A.cqn = Rot([self.sb(es, "A_cqn%d" % i, [128, 384], BF16) for i in range(2)])
            A.cqT = Rot([self.sb(es, "A_cqT%d" % i, [128, 3, 128], BF16) for i in range(2)])
            A.qsb = Rot([self.sb(es, "A_qsb%d" % i, [128, 8, 96], BF16) for i in range(2)])
            A.tA = self.sb(es, "A_tA", [128, 4, 16])
            A.tB = self.sb(es, "A_tB", [128, 4, 16])
            A.cn = Rot([self.sb(es, "A_cn%d" % i, [128, 256]) for i in range(2)])
            A.kr = Rot([self.sb(es, "A_kr%d" % i, [128, 32]) for i in range(2)])
            A.cbf = Rot([self.sb(es, "A_cbf%d" % i, [128, 288], BF16) for i in range(2)])
            A.QT = Rot([self.sb(es, "A_QT%d" % i, [96, 8, 128], BF16) for i in range(2)])
            cTs = [self.sb(es, "A_cT%d" % i, [128, 3, 128], BF16) for i in range(3)]
            for c_ in cTs:
                self.memset("pool", c_[:], 0.0, [c_])
            A.cT = Rot(cTs)
            A.PT = Rot([self.sb(es, "A_PT%d" % i, [128, 4, 128], BF16) for i in range(3)])
            A.rc = Rot([self.sb(es, "A_rc%d" % i, [128, 8]) for i in range(2)])
            A.oa = Rot([self.sb(es, "A_oa%d" % i, [128, 8, 64]) for i in range(2)])
            A.sz = Rot([self.sb(es, "A_sz%d" % i, [128, 512]) for i in range(2)])
            A.mx = Rot([self.sb(es, "A_mx%d" % i, [128, 512], BF16) for i in range(2)])
            A.pr = Rot([self.sb(es, "A_pr%d" % i, [128, 1184]) for i in range(2)])
            A.tab = Rot([self.sb(es, "A_tab%d" % i, [128, 32]) for i in range(2)])
            A.zseg = Rot([self.sb(es, "A_zs%d" % i, [16, 512]) for i in range(2)])
            cbs = [self.sb(es, "A_cbig%d" % i, [128, 4, 289], BF16) for i in range(3)]
            for c_ in cbs:
                self.memset("pool", c_[:, :, 256:257], 1.0, [c_])
            A.cbig = Rot(cbs)
            A.rotF = Rot(self.psF[0:4])
            A.O = (self.psF[4], self.psF[5])
            pr_aux = self.sb(es, "A_praux", [80, 1184])
            tab_aux = self.sb(es, "A_tabaux", [80, 32])
            QT_aux = self.sb(es, "A_QTaux", [96, 8, 80], BF16)
            cT_aux = self.sb(es, "A_cTaux", [128, 3, 80], BF16)
            self.memset("pool", cT_aux[:], 0.0, [cT_aux])
            self.dma(pr_aux[0:16, :], self.proj.t[3:19, 0:1184], [self.ph("aux")], [pr_aux])
            self.dma(tab_aux[0:16, :], self.rope.t[0:16, :], [], [tab_aux])
            for b in range(NS):
                r0 = 4118 + 19 * b
                self.dma(pr_aux[16 + 16 * b:32 + 16 * b, :], self.proj.t[r0:r0 + 16, 0:1184], [self.ph("aux")], [pr_aux])
                self.dma(tab_aux[16 + 16 * b:32 + 16 * b, :], self.rope.t[NCACHE:NCACHE + DS, :], [], [tab_aux])
            sl_h = H("s_lat_out")
            cn, kr = self.mla_proj(A, 80, pr_aux, tab_aux, QT_aux, cT_aux)
            wukT = self.sb(es, "A_wukT", [64, 8, 256], BF16)
            for k in range(2):
                pb = self.rotB.get()
                for h in range(8):
                    self.tr(pb[0:64, h * 128:(h + 1) * 128], wuk[:, k, h, 0:64], self.ident(128, True), [wuk, conb], [pb])
                self.cp("dve", wukT[:, :, k * 128:(k + 1) * 128], pb[0:64, :].rearrange("p (h r) -> p h r", h=8), [pb], [wukT])
                Rot.rel(pb)
            qr_pad = self.sb(es, "A_qrpad", [128, 8, 80], BF16)
            self.memset("pool", qr_pad[:], 0.0, [qr_pad])
            self.dma(qr_pad[0:32, :, :], QT_aux[64:96, :, :], [QT_aux], [qr_pad])
            self.dma(self.p_lat.t[l, 0:16, :], cn[0:16, :], [cn], [], is_out=True)
            self.dma(self.p_kr.t[l, 0:16, :], kr[0:16, :], [kr], [], is_out=True)
            for b in range(NS):
                self.dma(self.s_lat.t[l, b], cn[16 + 16 * b:32 + 16 * b, :], [cn], [sl_h], is_out=True)
                self.dma(self.s_kr.t[l, b], kr[16 + 16 * b:32 + 16 * b, :], [kr], [sl_h], is_out=True)
            Rot.rel(cn, kr)
            self.kv_build(A, cT_aux, 0, 16, 0)
            zs = A.zseg.get()
            self.dma(zs[:, :], self.proj.t[3:19, 672:1184], [self.ph("aux")], [zs])
            self.attend(A, 16, QT_aux, 0, [(0, 16)], None, zs, 0, self.mixa.t[0:16, :], self.mh("aux"))
            Rot.rel(zs)
            prepped = {}

            def prep(j):
                pr = A.pr.get()
                tab = A.tab.get()
                r0 = 19 + 128 * j
                self.dma(pr[:, :], self.proj.t[r0:r0 + 128, 0:1184], [self.ph(j)], [pr])
                self.dma(tab[:, :], self.rope.t[16 + 128 * j:16 + 128 * (j + 1), :], [], [tab])
                QT = A.QT.get()
                cT = A.cT.get()
                yield
                for _ in self.mla_proj_g(A, 128, pr, tab, QT, cT, prepped, j):
                    yield
                cn, kr = prepped[("ck", j)]
                self.dma(self.p_lat.t[l, 16 + 128 * j:16 + 128 * (j + 1), :], cn[:, :], [cn], [], is_out=True)
                self.dma(self.p_kr.t[l, 16 + 128 * j:16 + 128 * (j + 1), :], kr[:, :], [kr], [], is_out=True)
                Rot.rel(cn, kr, tab)
                yield
                for _ in self.kv_build_g(A, cT, 0, 128, 1 + j):
                    yield
                Rot.rel(cT)
                prepped[j] = (QT, pr)

            for _ in prep(0):
                pass
            NT = SEQ // 128
            for j in range(NT):
                QT, pr = prepped.pop(j)
                blocks = [(0, 16)] + [(1 + i, 128) for i in range(j + 1)]
                att = self.attend_g(A, 128, QT, 0, blocks, 1 + j, pr, 672, self.mixa.t[80 + 128 * j:80 + 128 * (j + 1), :], self.mh(j))
                self.interleave([att, prep(j + 1) if j + 1 < NT else None])
                Rot.rel(QT, pr)
            qlT = Rot([self.sb(es, "A_qlT%d" % i, [128, 2, 128], BF16) for i in range(2)])
            oln = Rot([self.sb(es, "A_oln%d" % i, [128, 256], BF16) for i in range(2)])
            olT = Rot([self.sb(es, "A_olT%d" % i, [128, 2, 128], BF16) for i in range(2)])
            rcl = Rot([self.sb(es, "A_rcl%d" % i, [128, 1]) for i in range(2)])
            Olat = A.O[0]
            for b in range(NS):
                off = 16 + 16 * b
                ql = qlT.get()
                for k in range(2):
                    pf = A.rotF.get()
                    for h in range(8):
                        self.mm(pf[0:128, h * 16:(h + 1) * 16], wukT[0:64, h, k * 128:(k + 1) * 128], QT_aux[0:64, h, off:off + 16],
                                True, True, [wukT, QT_aux], [pf])
                    self.cp("dve", ql[:, k, :], pf[:, 0:128], [pf], [ql])
                    Rot.rel(pf)
                qrv = qr_pad[:, :, off:off + 16]
                blist = []
                cb0 = A.cbig.get()
                self.dma(cb0[0:16, 0, 0:256], self.cl.t[l, b, 0:16, :], [], [cb0], q="pool")
                self.dma(cb0[0:16, 0, 257:289], self.ck.t[l, b, 0:16, :], [], [cb0], q="pool")
                self.dma(cb0[0:16, 1, 0:256], self.s_lat.t[l, b], [sl_h], [cb0], q="pool")
                self.dma(cb0[0:16, 1, 257:289], self.s_kr.t[l, b], [sl_h], [cb0], q="pool")
                groups = [[(cb0, 0, 16), (cb0, 1, 16)]]
                nblk = 34
                done = 0

                def run_group(grp, done):
                    n = grp[0][2]
                    g = len(grp)
                    pf = A.rotF.get()
                    for i, (cb, jj, _) in enumerate(grp):
                        cT = A.cT.get()
                        self.c_transpose289(A, cb[0:n, jj, :], n, cT, [cb])
                        o = pf[0:n, i * 128:(i + 1) * 128]
                        self.mm(o, cT[:, 0, 0:n], ql[:, 0, :], True, False, [cT, ql], [pf])
                        self.mm(o, cT[:, 1, 0:n], ql[:, 1, :], False, False, [cT, ql], [pf])
                        self.mm(o, cT[:, 2, 0:n], qrv, False, True, [cT, qr_pad], [pf])
                        Rot.rel(cT)
                    PT = A.PT.get()
                    self.act(PT[0:n, 0:g, :], pf[0:n, 0:g * 128].rearrange("p (g t) -> p g t", g=g), AF.Exp, [pf], [PT], scale=SM_SCALE)
                    Rot.rel(pf)
                    for i, (cb, jj, _) in enumerate(grp):
                        self.mm(Olat[0:128, 0:257], PT[0:n, i, :], cb[0:n, jj, 0:257], done + i == 0, done + i == nblk - 1, [PT, cb], [Olat])
                    Rot.rel(PT)
                    return done + g

                done = run_group(groups[0], done)
                Rot.rel(cb0)
                for g_ in range(8):
                    cb = A.cbig.get()
                    r0 = 16 + 512 * g_
                    self.dma(cb[:, :, 0:256], self.cl.t[l, b, r0:r0 + 512, :].rearrange("(j p) c -> p j c", p=128), [], [cb], q="pool")
                    self.dma(cb[:, :, 257:289], self.ck.t[l, b, r0:r0 + 512, :].rearrange("(j p) c -> p j c", p=128), [], [cb], q="pool")
                    done = run_group([(cb, jj, 128) for jj in range(4)], done)
                    Rot.rel(cb)
                rc1 = rcl.get()
                self.recip(rc1[:, 0:1], Olat[:, 256:257], [Olat], [rc1])
                on = oln.get()
                self.ts("dve", on[:, :], Olat[:, 0:256], rc1[:, 0:1], ALU.mult, [Olat, rc1], [on])
                pb = self.rotB.get()
                for k in range(2):
                    self.tr(pb[0:128, k * 128:(k + 1) * 128], on[:, k * 128:(k + 1) * 128], self.ident(128, True), [on, conb], [pb])
                oT = olT.get()
                self.cp("dve", oT[:, :, :], pb[:, 0:256].rearrange("p (k t) -> p k t", k=2), [pb], [oT])
                Rot.rel(pb, on, rc1)
                pf = A.rotF.get()
                for h in range(8):
                    for k in range(2):
                        self.mm(pf[0:16, h * 64:(h + 1) * 64], oT[:, k, h * 16:(h + 1) * 16], wuv[:, k, h * 64:(h + 1) * 64],
                                k == 0, k == 1, [oT, wuv], [pf])
                zs = A.zseg.get()
                r0 = 4118 + 19 * b
                self.dma(zs[:, :], self.proj.t[r0:r0 + 16, 672:1184], [self.ph("aux")], [zs])
                sz = A.sz.get()
                self.act(sz[0:16, :], zs[0:16, :], AF.Silu, [zs], [sz])
                mx = A.mx.get()
                self.tt("dve", mx[0:16, :], pf[0:16, 0:512], sz[0:16, :], ALU.mult, [pf, sz], [mx])
                Rot.rel(pf)
                self.dma(self.mixa.t[off:off + 16, :], mx[0:16, :], [mx], [self.mh("aux")])
                Rot.rel(zs, sz, mx, oT, ql)

    def c_transpose289(self, A, src, n, cT, r):
        conb = self.conb
        idb = self.ident(n, True)
        pb = self.rotB.get()
        self.tr(pb[0:128, 0:n], src[:, 0:128], idb, r + [conb], [pb])
        self.tr(pb[0:128, 128:128 + n], src[:, 128:256], idb, r + [conb], [pb])
        self.tr(pb[0:32, 256:256 + n], src[:, 257:289], idb, r + [conb], [pb])
        self.cp("dve", cT[:, 0:2, 0:n], pb[:, 0:256].rearrange("p (k t) -> p k t", k=2)[:, :, 0:n], [pb], [cT])
        self.cp("act", cT[0:32, 2, 0:n], pb[0:32, 256:256 + n], [pb], [cT])
        Rot.rel(pb)

    def rope_apply(self, A, T, o1, o2, x1, x2, cos, sin, tA, tB, r, w):
        self.tt("dve", tA, x1, cos, ALU.mult, r, [A.tA])
        self.tt("dve", tB, x2, sin, ALU.mult, r, [A.tB])
        self.tt("dve", o1, tA, tB, ALU.subtract, [A.tA, A.tB], w)
        self.tt("dve", tA, x2, cos, ALU.mult, r, [A.tA])
        self.tt("dve", tB, x1, sin, ALU.mult, r, [A.tB])
        self.tt("dve", o2, tA, tB, ALU.add, [A.tA, A.tB], w)

    def mla_proj(self, A, T, pr, tab, QT, cT):
        d = {}
        for _ in self.mla_proj_g(A, T, pr, tab, QT, cT, d, 0):
            pass
        return d[("ck", 0)]

    def mla_proj_g(self, A, T, pr, tab, QT, cT, outd, key):
        conb = self.conb
        idb = self.ident(T, True)
        sq, r_ = A.ssq.get(), A.rs.get()
        self.act(A.junk[0:T, 0:384], pr[0:T, 0:384], AF.Square, [pr], [A.junk, sq], accum=sq[0:T, :])
        self.rstd(r_[0:T, :], sq[0:T, :], 384, [sq], [r_])
        cqn = A.cqn.get()
        self.stt(cqn[0:T, :], pr[0:T, 0:384], r_[0:T, 0:1], A.qn_bc[0:T, :], ALU.mult, ALU.mult, [pr, r_, A.qn_bc], [cqn])
        pb = self.rotB.get()
        for k in range(3):
            self.tr(pb[0:128, k * 128:k * 128 + T], cqn[0:T, k * 128:(k + 1) * 128], idb, [cqn, conb], [pb])
        cqT = A.cqT.get()
        self.cp("dve", cqT[:, :, 0:T], pb[:, 0:384].rearrange("p (k t) -> p k t", k=3)[:, :, 0:T], [pb], [cqT])
        Rot.rel(pb, sq, r_, cqn)
        yield
        q_sb = A.qsb.get()
        cos4 = tab[0:T, 0:16].unsqueeze(1).broadcast_to([T, 4, 16])
        sin4 = tab[0:T, 16:32].unsqueeze(1).broadcast_to([T, 4, 16])
        for half in range(2):
            pf = A.rotF.get()
            for k in range(3):
                self.mm(pf[0:T, 0:384], cqT[:, k, 0:T], A.wuq[:, k, half * 384:(half + 1) * 384], k == 0, k == 2, [cqT, A.wuq], [pf])
            pv = pf[0:T, 0:384].rearrange("p (h d) -> p h d", h=4)
            hs = slice(half * 4, half * 4 + 4)
            self.cp("dve", q_sb[0:T, hs, 0:64], pv[:, :, 0:64], [pf], [q_sb])
            self.rope_apply(A, T, q_sb[0:T, hs, 64:80], q_sb[0:T, hs, 80:96], pv[:, :, 64:80], pv[:, :, 80:96],
                            cos4, sin4, A.tA[0:T, :, :], A.tB[0:T, :, :], [pf, tab], [q_sb])
            Rot.rel(pf)
            yield
        pb = self.rotB.get()
        for h in range(8):
            self.tr(pb[0:96, h * 128:h * 128 + T], q_sb[0:T, h, :], idb, [q_sb, conb], [pb])
        self.cp("dve", QT[0:96, :, 0:T], pb[0:96, :].rearrange("p (h t) -> p h t", h=8)[:, :, 0:T], [pb], [QT])
        Rot.rel(pb, q_sb, cqT)
        yield
        sq, r_ = A.ssq.get(), A.rs.get()
        self.act(A.junk[0:T, 0:256], pr[0:T, 384:640], AF.Square, [pr], [A.junk, sq], accum=sq[0:T, :])
        self.rstd(r_[0:T, :], sq[0:T, :], 256, [sq], [r_])
        cn = A.cn.get()
        self.stt(cn[0:T, :], pr[0:T, 384:640], r_[0:T, 0:1], A.kvn_bc[0:T, :], ALU.mult, ALU.mult, [pr, r_, A.kvn_bc], [cn])
        kr = A.kr.get()
        self.rope_apply(A, T, kr[0:T, 0:16], kr[0:T, 16:32], pr[0:T, 640:656], pr[0:T, 656:672],
                        tab[0:T, 0:16], tab[0:T, 16:32], A.tA[0:T, 0, :], A.tB[0:T, 0, :], [pr, tab], [kr])
        cbf = A.cbf.get()
        self.cp("pool", cbf[0:T, 0:256], cn[0:T, :], [cn], [cbf])
        self.cp("pool", cbf[0:T, 256:288], kr[0:T, :], [kr], [cbf])
        self.c_transpose(A, cbf[0:T, :], T, cT, [cbf])
        Rot.rel(sq, r_, cbf)
        outd[("ck", key)] = (cn, kr)
        yield

    def c_transpose(self, A, src, n, cT, r):
        conb = self.conb
        idb = self.ident(n, True)
        pb = self.rotB.get()
        self.tr(pb[0:128, 0:n], src[:, 0:128], idb, r + [conb], [pb])
        self.tr(pb[0:128, 128:128 + n], src[:, 128:256], idb, r + [conb], [pb])
        self.tr(pb[0:32, 256:256 + n], src[:, 256:288], idb, r + [conb], [pb])
        self.cp("dve", cT[:, 0:2, 0:n], pb[:, 0:256].rearrange("p (k t) -> p k t", k=2)[:, :, 0:n], [pb], [cT])
        self.cp("dve", cT[0:32, 2, 0:n], pb[0:32, 256:256 + n], [pb], [cT])
        Rot.rel(pb)

    def cache_block(self, A, cb, jj, n, slot):
        cT = A.cT.get()
        self.c_transpose(A, cb[0:n, jj, :], n, cT, [cb])
        self.kv_build(A, cT, 0, n, slot)
        Rot.rel(cT)

    def kv_build(self, A, cT, coff, n, slot):
        for _ in self.kv_build_g(A, cT, coff, n, slot):
            pass

    def kv_build_g(self, A, cT, coff, n, slot):
        conb = self.conb
        kcol = 0 if slot == 0 else 16 + 128 * (slot - 1)
        sh = A.slot_h[slot]
        for half in range(2):
            pf = A.rotF.get()
            for hh in range(4):
                h = half * 4 + hh
                o = pf[0:96, hh * 128:hh * 128 + n]
                self.mm(o, A.wuk[:, 0, h, :], cT[:, 0, coff:coff + n], True, False, [A.wuk, cT], [pf])
                self.mm(o, A.wuk[:, 1, h, :], cT[:, 1, coff:coff + n], False, False, [A.wuk, cT], [pf])
                self.mm(o, conb[:, 256:352], cT[:, 2, coff:coff + n], False, True, [conb, cT], [pf])
            src = pf[0:96, :].rearrange("p (h t) -> p h t", h=4)[:, :, 0:n]
            self.cp("dve", A.KT[0:96, half * 4:half * 4 + 4, kcol:kcol + n], src, [pf], [sh])
            Rot.rel(pf)
            yield
        pf = A.rotF.get()
        self.mm(pf[0:n, 0:512], cT[:, 0, coff:coff + n], A.wuv[:, 0, :], True, False, [cT, A.wuv], [pf])
        self.mm(pf[0:n, 0:512], cT[:, 1, coff:coff + n], A.wuv[:, 1, :], False, True, [cT, A.wuv], [pf])
        self.cp("dve", A.V[0:n, slot, :, 0:64], pf[0:n, 0:512].rearrange("p (h d) -> p h d", h=8), [pf], [sh])
        Rot.rel(pf)
        yield

    def attend(self, A, T, QT, qoff, blocks, diag_slot, zsrc, zoff, mix_out, mix_h):
        for _ in self.attend_g(A, T, QT, qoff, blocks, diag_slot, zsrc, zoff, mix_out, mix_h):
            pass

    def attend_g(self, A, T, QT, qoff, blocks, diag_slot, zsrc, zoff, mix_out, mix_h):
        groups = []
        for bl in blocks:
            if groups and groups[-1][0][1] == bl[1] and len(groups[-1]) < 4:
                groups[-1].append(bl)
            else:
                groups.append([bl])
        nblk = len(blocks)
        work = []
        for h in range(8):
            nb = 0
            for grp in groups:
                work.append((h, grp, nb))
                nb += len(grp)

        def finish(item):
            h, grp, nb0, pf = item
            Ob = A.O[h // 4]
            hh = h % 4
            n = grp[0][1]
            g = len(grp)
            PT = A.PT.get()
            self.act(PT[0:n, 0:g, 0:T], pf[0:n, 0:g * 128].rearrange("p (g t) -> p g t", g=g)[:, :, 0:T], AF.Exp,
                     [pf], [PT], scale=SM_SCALE)
            Rot.rel(pf)
            for i, (slot, _) in enumerate(grp):
                if diag_slot is not None and slot == diag_slot:
                    self.memset("pool", PT[64:128, i, 0:64], 0.0, [PT])
            for i, (slot, _) in enumerate(grp):
                self.mm(Ob[0:T, hh * 65:hh * 65 + 65], PT[0:n, i, 0:T], A.V[0:n, slot, h, :],
                        nb0 + i == 0, nb0 + i == nblk - 1, [PT, A.slot_h[slot]], [Ob])
            Rot.rel(PT)

        pend = None
        for (h, grp, nb0) in work:
            n = grp[0][1]
            pf = A.rotF.get()
            for i, (slot, _) in enumerate(grp):
                kcol = 0 if slot == 0 else 16 + 128 * (slot - 1)
                self.mm(pf[0:n, i * 128:i * 128 + T], A.KT[0:96, h, kcol:kcol + n], QT[0:96, h, qoff:qoff + T],
                        True, True, [A.slot_h[slot], QT], [pf])
            if pend is not None:
                finish(pend)
            pend = (h, grp, nb0, pf)
            yield
        finish(pend)
        yield
        rc = A.rc.get()
        oa = A.oa.get()
        for half in range(2):
            Ov = A.O[half][0:T, 0:260].rearrange("p (h d) -> p h d", h=4)
            rcv = rc[0:T, half * 4:half * 4 + 4].unsqueeze(2)
            self.recip(rcv, Ov[:, :, 64:65], [A.O[half]], [rc])
            self.tt("dve", oa[0:T, half * 4:half * 4 + 4, :], Ov[:, :, 0:64], rcv.broadcast_to([T, 4, 64]), ALU.mult,
                    [A.O[half], rc], [oa])
        sz = A.sz.get()
        self.act(sz[0:T, :], zsrc[0:T, zoff:zoff + 512], AF.Silu, [zsrc], [sz])
        mx = A.mx.get()
        self.tt("dve", mx[0:T, :], oa[0:T, :, :].rearrange("p h d -> p (h d)"), sz[0:T, :], ALU.mult, [oa, sz], [mx])
        self.dma(mix_out, mx[0:T, :], [mx], [mix_h])
        Rot.rel(rc, oa, sz, mx)

    def phase_R(self, l):
        con, conb = self.con, self.conb
        with contextlib.ExitStack() as es:
            R = type("NS", (), {})()
            wout = self.sb(es, "R_wout", [128, 12, D], BF16)
            wov = self.w_out.t[l].rearrange("(k p) c -> p k c", p=128)
            for k in range(0, 12, 4):
                self.dma(wout[:, k:k + 4, :], wov[:, k:k + 4, :], [], [wout], q="pool")
            cw = self.sb(es, "R_cw", [128, 4 * 1536])
            self.dma(cw[:], self.conv_w.t[l].partition_broadcast(128), [], [cw])
            mnorm = self.sb(es, "R_mnorm", [128, 512])
            self.dma(mnorm[:], self.m_norm.t[l].partition_broadcast(128), [], [mnorm])
            gnorm = self.sb(es, "R_gnorm", [128, 128])
            self.dma(gnorm[:], self.g_norm.t[l].partition_broadcast(128), [], [gnorm])
            gb = self.sb(es, "R_gb", [128, 8])
            self.dma(gb[:], self.gate_b.t[l].partition_broadcast(128), [], [gb])
            dtb = self.sb(es, "R_dtb", [128, 4])
            self.dma(dtb[:], self.dt_bias.t[l].partition_broadcast(128), [], [dtb])
            nea = self.sb(es, "R_nea", [128, 4])
            self.dma(nea[:], self.a_log.t[l].partition_broadcast(128), [], [nea])
            self.act(nea[:], nea[:], AF.Exp, [nea], [nea])
            self.ts("dve", nea[:], nea[:], -1.0, ALU.mult, [nea], [nea])
            R.wout, R.cw, R.mnorm, R.gnorm, R.gb, R.dtb, R.nea = wout, cw, mnorm, gnorm, gb, dtb, nea

            def pool(name, shape, dt, n):
                return Rot([self.sb(es, "R_%s%d" % (name, i), shape, dt) for i in range(n)])
            R.MP = pool("mp", [64, 2568], F32, 1)
            R.GA = pool("ga", [64, 520], F32, 2)
            R.F1536 = pool("f1536", [64, 1536], F32, 4)
            R.F1024 = pool("f1024", [64, 1024], F32, 2)
            R.F512 = pool("f512", [64, 512], F32, 13 if K.OPT_X1 else 11)
            R.F256 = pool("f256", [64, 256], F32, 16)
            R.B1024 = pool("b1024", [64, 1024], BF16, 2)
            R.B512 = pool("b512", [64, 512], BF16, 10)
            R.B256 = pool("b256", [64, 256], BF16, 4)
            R.BT = pool("bt", [128, 768], BF16, 6)
            R.SM = pool("sm", [128, 12], F32, 64)
            R.X = pool("x", [128, D], F32, 1 if K.OPT_X1 else 2)
            R.XN = pool("xn", [128, D], F32, 1)
            R.MIXT = pool("mixt", [128, 12, 128], BF16, 2)
            R.MA = pool("ma", [128, 512], BF16, 1)
            R.rotF = Rot(self.psF)

            def state(nm):
                st = type("NS", (), {})()
                st.CT = self.sb(es, "R_CT" + nm, [128, 4, 128])
                st.CTb = self.sb(es, "R_CTb" + nm, [128, 4, 128], BF16)
                st.nT = self.sb(es, "R_nT" + nm, [128, 4])
                st.nTb = self.sb(es, "R_nTb" + nm, [128, 4], BF16)
                st.mbc = self.sb(es, "R_mbc" + nm, [128, 4])
                st.S = self.sb(es, "R_S" + nm, [128, 4, 128])
                st.Sb = self.sb(es, "R_Sb" + nm, [128, 4, 128], BF16)
                return st
            stp = state("p")
            sts = state("s")
            R.tmpC = self.sb(es, "R_tmpC", [128, 4, 128])
            for t_ in (stp.CT, stp.CTb, stp.nT, stp.nTb, stp.mbc, stp.S, stp.Sb):
                self.memset("pool", t_[:], 0.0, [t_])

            items = []
            mixA = R.MIXT.get()
            items.append(dict(T=16, row0=3, st=stp, mixT=mixA, coff=0, before=None, after=None))
            for b in range(NS):
                r0 = 4118 + 19 * b

                def bef(b=b):
                    self.load_state(R, l, b, sts)

                def aft(b=b, r0=r0, last=(b == NS - 1)):
                    self.store_state(R, sts, self.s_C.t[l, b], self.s_n.t[l, b], self.s_m.t[l, b:b + 1, :], self.s_S.t[l, b])
                    self.dma(self.s_cv.t[l, b], self.proj.t[r0 + 13:r0 + 16, 3752:5288], [], [], is_out=True)
                    if last:
                        self.out_proj(R, l, "aux", 80, mixA)
                        Rot.rel(mixA)
                items.append(dict(T=16, row0=r0, st=sts, mixT=mixA, coff=16 + 16 * b, before=bef, after=aft))
            cur = {}
            for j in range(SEQ // 128):
                for c in range(2):
                    def bef(j=j, c=c):
                        if c == 0:
                            cur["mixT"] = R.MIXT.get()

                    def aft(j=j, c=c):
                        if c == 1:
                            self.out_proj(R, l, j, 128, cur["mixT"])
                            Rot.rel(cur["mixT"])
                    items.append(dict(T=64, row0=19 + 128 * j + 64 * c, st=stp, mixT=None, coff=64 * c, before=bef, after=aft))
            ctxs = [dict() for _ in items]
            for _ in self.gdn_pre_g(R, l, items[0]["T"], items[0]["row0"], ctxs[0]):
                pass
            for i, it in enumerate(items):
                if it["before"] is not None:
                    it["before"]()
                mixT = it["mixT"] if it["mixT"] is not None else cur["mixT"]
                gens = [self.mlstm_g(R, l, it["T"], it["row0"], it["st"], mixT, it["coff"]),
                        self.gdn_chain_g(R, l, it["T"], it["st"], mixT, it["coff"], ctxs[i])]
                if i + 1 < len(items):
                    nx = items[i + 1]
                    gens.append(self.gdn_pre_g(R, l, nx["T"], nx["row0"], ctxs[i + 1]))
                self.interleave(gens)
                if it["after"] is not None:
                    it["after"]()
            self.store_state(R, stp, self.p_C.t[l], self.p_n.t[l], self.p_m.t[l:l + 1, :], self.p_S.t[l])
            self.dma(self.p_cv.t[l], self.proj.t[19 + SEQ - 3:19 + SEQ, 3752:5288], [], [], is_out=True)

    def load_state(self, R, l, b, st):
        self.dma(R.tmpC[:], self.imC.t[l, b].rearrange("h e d -> e h d"), [], [R.tmpC])
        pf = R.rotF.get()
        for h in range(4):
            self.tr(pf[:, h * 128:(h + 1) * 128], R.tmpC[:, h, :], self.ident(128), [R.tmpC, self.con], [pf])
        self.cp("dve", st.CT[:], pf[:, :].rearrange("p (h e) -> p h e", h=4), [pf], [st.CT])
        Rot.rel(pf)
        self.cp("act", st.CTb[:], st.CT[:], [st.CT], [st.CTb])
        self.dma(st.nT[:], self.imn.t[l, b].rearrange("h d -> d h"), [], [st.nT], nonc=True)
        self.cp("act", st.nTb[:], st.nT[:], [st.nT], [st.nTb])
        self.dma(st.mbc[:], self.imm.t[l, b].partition_broadcast(128), [], [st.mbc])
        self.dma(st.S[:], self.igS.t[l, b].rearrange("h k v -> k h v"), [], [st.S])
        self.cp("act", st.Sb[:], st.S[:], [st.S], [st.Sb])

    def store_state(self, R, st, oC, on, om, oS):
        pf = R.rotF.get()
        for h in range(4):
            self.tr(pf[:, h * 128:(h + 1) * 128], st.CT[:, h, :], self.ident(128), [st.CT, self.con], [pf])
        self.cp("dve", R.tmpC[:], pf[:, :].rearrange("p (h e) -> p h e", h=4), [pf], [R.tmpC])
        Rot.rel(pf)
        self.dma(oC.rearrange("h e d -> e h d"), R.tmpC[:], [R.tmpC], [], is_out=True)
        self.dma(on.rearrange("h d -> d h"), st.nT[:], [st.nT], [], nonc=True, is_out=True)
        self.dma(om, st.mbc[0:1, :], [st.mbc], [], is_out=True)
        self.dma(oS.rearrange("h k v -> k h v"), st.S[:], [st.S], [], is_out=True)

    def out_proj(self, R, l, tile, T, mixT):
        conb = self.conb
        ma = R.MA.get()
        if tile == "aux":
            self.dma(ma[0:T, :], self.mixa.t[0:80, :], [], [ma])
        else:
            self.dma(ma[0:T, :], self.mixa.t[80 + 128 * tile:80 + 128 * (tile + 1), :], [], [ma])
        pb = self.rotB.get()
        for k in range(4):
            self.tr(pb[0:128, k * 128:k * 128 + T], ma[0:T, k * 128:(k + 1) * 128], self.ident(T, True), [ma, conb], [pb])
        self.cp("act", mixT[:, 0:4, 0:T], pb[:, 0:512].rearrange("p (k t) -> p k t", k=4)[:, :, 0:T], [pb], [mixT])
        Rot.rel(pb, ma)
        xt = R.X.get()
        self.load_x(l, tile, xt)
        xn = R.XN.get()
        for half in range(2):
            pf = R.rotF.get()
            for k in range(12):
                self.mm(pf[0:T, 0:512], mixT[:, k, 0:T], R.wout[:, k, half * 512:(half + 1) * 512], k == 0, k == 11, [mixT, R.wout], [pf])
            self.tt("dve", xn[0:T, half * 512:(half + 1) * 512], pf[0:T, 0:512], xt[0:T, half * 512:(half + 1) * 512], ALU.add, [pf, xt], [xn])
            Rot.rel(pf)
        Rot.rel(xt)
        if l < DEPTH - 1:
            if tile == "aux":
                self.dma(self.xscr.t[0:80, :], xn[0:80, :], [xn], [self.xh("aux")])
            else:
                self.dma(self.xscr.t[80 + 128 * tile:80 + 128 * (tile + 1), :], xn[:, :], [xn], [self.xh(tile)])
        else:
            sq = R.SM.get()
            yo = R.X.get()
            self.act(yo[0:T, :], xn[0:T, :], AF.Square, [xn], [yo, sq], accum=sq[0:T, 0:1])
            self.rstd(sq[0:T, 0:1], sq[0:T, 0:1], D, [sq], [sq])
            self.stt(yo[0:T, :], xn[0:T, :], sq[0:T, 0:1], self.fin_bc[0:T, :], ALU.mult, ALU.mult, [xn, sq, self.fin_bc], [yo])
            if tile == "aux":
                self.dma(self.y_s.t[:, :], yo[16:80, :], [yo], [], is_out=True)
            else:
                self.dma(self.y_p.t[128 * tile:128 * (tile + 1), :], yo[:, :], [yo], [], is_out=True)
            Rot.rel(sq, yo)
        Rot.rel(xn)

    @staticmethod
    def v3(ap, a):
        return ap.rearrange("p (a b) -> p a b", a=a)

    @staticmethod
    def bch(ap, T, n):
        return ap.unsqueeze(2).broadcast_to([T, 4, n])

    @staticmethod
    def bcm(ap, T, n):
        return ap.unsqueeze(1).broadcast_to([T, 4, n])

    def mlstm_chunk(self, R, l, T, row0, st, mixT, coff):
        for _ in self.mlstm_g(R, l, T, row0, st, mixT, coff):
            pass

    def mlstm_g(self, R, l, T, row0, st, mixT, coff):
        con, conb = self.con, self.conb
        L = []

        def g(p):
            b = p.get()
            L.append(b)
            return b
        P = slice(0, T)
        T4 = 4 * T
        tri = con[0:T, C_TRI:C_TRI + T]
        SLm = con[0:T, C_SL:C_SL + T]
        onesTT = con[0:T, C_ONE:C_ONE + T]
        idT = con[0:T, C_ID:C_ID + T]
        NEG = con[0:T, C_NEG:C_NEG + T]
        cs_ = C_S64 if T == 64 else C_S16
        sel = con[0:T, cs_:cs_ + 128]
        idb = self.ident(T, True)
        v3, bch, bcm = self.v3, self.bch, self.bcm
        KS = 128.0 ** -0.5
        mp = g(R.MP)
        self.dma(mp[P, :], self.proj.t[row0:row0 + T, 1184:3752], [], [mp])
        if K.OPT_K32:
            k32 = g(R.F512)
            self.dma(k32[P, :], self.proj.t[row0:row0 + T, 1696:2208], [], [k32])
        qkb = g(R.B1024)
        self.cp("act", qkb[P, 0:512], mp[P, 0:512], [mp], [qkb])
        self.act(qkb[P, 512:1024], mp[P, 512:1024], AF.Copy, [mp], [qkb], scale=KS)
        pb = self.rotB.get()
        for i in range(8):
            self.tr(pb[0:128, i * T:(i + 1) * T], qkb[P, i * 128:(i + 1) * 128], idb, [qkb, conb], [pb])
        qkT = g(R.BT)
        self.cp("act" if K.OPT_EVACT else "dve", qkT[:, 0:8 * T], pb[:, 0:8 * T], [pb], [qkT])
        Rot.rel(pb)
        vb = g(R.B512)
        self.cp("pool", vb[P, :], mp[P, 1024:1536], [mp], [vb])
        yield
        g8 = g(R.SM)
        self.tt("dve", g8[P, 0:8], mp[P, 1536:1544], R.gb[P, 0:8], ALU.add, [mp, R.gb], [g8])
        ipre, xf = g8[P, 0:4], g8[P, 4:8]
        s1 = g(R.SM)
        self.stt(s1[P, 0:4], xf, -1.0, xf, ALU.mult, ALU.max, [g8], [s1])
        self.act(s1[P, 0:4], s1[P, 0:4], AF.Exp, [s1], [s1], scale=-1.0)
        self.act(s1[P, 0:4], s1[P, 0:4], AF.Ln, [s1], [s1], bias=1.0)
        lf = g(R.SM)
        self.ts("dve", lf[P, 0:4], xf, 0.0, ALU.min, [g8], [lf])
        self.tt("dve", lf[P, 0:4], lf[P, 0:4], s1[P, 0:4], ALU.subtract, [lf, s1], [lf])
        yield
        R1 = g(R.F256)
        self.tt("pool", v3(R1[P, 0:T4], 4), bcm(SLm, T, T), bch(lf[P, 0:4], T, T), ALU.mult, [con, lf], [R1])
        R2 = g(R.F256)
        self.tt("pool", v3(R2[P, 0:T4], 4), bcm(idT, T, T), bch(ipre, T, T), ALU.mult, [con, g8], [R2])
        pf = R.rotF.get()
        self.mm(pf[0:T, 0:T4], tri, R1[P, 0:T4], True, False, [con, R1], [pf])
        self.mm(pf[0:T, 0:T4], onesTT, R2[P, 0:T4], False, True, [con, R2], [pf])
        self.mm(pf[0:T, T4:T4 + 4], tri, lf[P, 0:4], True, True, [con, lf], [pf])
        Dm = g(R.F256)
        self.tt("dve", v3(Dm[P, 0:T4], 4), v3(pf[0:T, 0:T4], 4), bcm(NEG, T, T), ALU.add, [pf, con], [Dm])
        MIB = g(R.SM)
        self.cp("act", MIB[P, 8:12], pf[0:T, T4:T4 + 4], [pf], [MIB])
        Rot.rel(pf)
        rmx = g(R.SM)
        self.red(rmx[P, 0:4], v3(Dm[P, 0:T4], 4), ALU.max, [Dm], [rmx])
        yield
        self.tt("dve", MIB[P, 4:8], MIB[P, 8:12], st.mbc[P, 0:4], ALU.add, [MIB, st.mbc], [MIB])
        self.tt("dve", MIB[P, 0:4], MIB[P, 4:8], rmx[P, 0:4], ALU.max, [MIB, rmx], [MIB])
        E = g(R.F256)
        self.tt("dve", v3(E[P, 0:T4], 4), v3(Dm[P, 0:T4], 4), bch(MIB[P, 0:4], T, T), ALU.subtract, [Dm, MIB], [E])
        self.act(E[P, 0:T4], E[P, 0:T4], AF.Exp, [E], [E])
        wi = g(R.SM)
        self.tt("dve", wi[P, 0:4], MIB[P, 4:8], MIB[P, 0:4], ALU.subtract, [MIB], [wi])
        self.act(wi[P, 0:4], wi[P, 0:4], AF.Exp, [wi], [wi])
        emt = g(R.SM)
        self.act(emt[P, 0:4], MIB[P, 0:4], AF.Exp, [MIB], [emt], scale=-1.0)
        yield
        pf = R.rotF.get()
        for h in range(4):
            self.mm(pf[0:T, h * T:(h + 1) * T], qkT[:, h * T:(h + 1) * T], qkT[:, (4 + h) * T:(5 + h) * T], True, True, [qkT], [pf])
        qkE = g(R.F256)
        self.tt("dve", qkE[P, 0:T4], pf[0:T, 0:T4], E[P, 0:T4], ALU.mult, [pf, E], [qkE])
        Rot.rel(pf)
        den1 = g(R.SM)
        self.red(den1[P, 0:4], v3(qkE[P, 0:T4], 4), ALU.add, [qkE], [den1])
        yield
        pf = R.rotF.get()
        for h in range(4):
            self.tr(pf[0:T, h * T:(h + 1) * T], qkE[P, h * T:(h + 1) * T], idT, [qkE, con], [pf])
        qkET = g(R.B256)
        self.cp("act", qkET[P, 0:T4], pf[0:T, 0:T4], [pf], [qkET])
        Rot.rel(pf)
        yield
        pf1 = R.rotF.get()
        for h in range(4):
            self.mm(pf1[0:T, h * 128:(h + 1) * 128], qkET[P, h * T:(h + 1) * T], vb[P, h * 128:(h + 1) * 128], True, True, [qkET, vb], [pf1])
        num1 = g(R.F512)
        self.cp("act", num1[P, :], pf1[0:T, :], [pf1], [num1])
        Rot.rel(pf1)
        yield
        pf2 = R.rotF.get()
        for h in range(4):
            self.mm(pf2[0:T, h * 128:(h + 1) * 128], qkT[:, h * T:(h + 1) * T], st.CTb[:, h, :], True, True, [qkT, st.CTb], [pf2])
        pf3 = R.rotF.get()
        for h in range(4):
            self.mm(pf3[0:T, h:h + 1], qkT[:, h * T:(h + 1) * T], st.nTb[:, h:h + 1], True, True, [qkT, st.nTb], [pf3])
        num = g(R.F512)
        self.tt("dve", v3(num[P, :], 4), v3(pf2[0:T, :], 4), bch(wi[P, 0:4], T, 128), ALU.mult, [pf2, wi], [num])
        Rot.rel(pf2)
        self.tt("dve", num[P, :], num[P, :], num1[P, :], ALU.add, [num, num1], [num])
        den = g(R.SM)
        self.tt("dve", den[P, 0:4], pf3[0:T, 0:4], wi[P, 0:4], ALU.mult, [pf3, wi], [den])
        Rot.rel(pf3)
        self.tt("dve", den[P, 0:4], den[P, 0:4], den1[P, 0:4], ALU.add, [den, den1], [den])
        self.stt(den[P, 0:4], den[P, 0:4], -1.0, den[P, 0:4], ALU.mult, ALU.max, [den], [den])
        self.tt("dve", den[P, 0:4], den[P, 0:4], emt[P, 0:4], ALU.max, [den, emt], [den])
        self.recip(den[P, 0:4], den[P, 0:4], [den], [den])
        self.tt("dve", v3(num[P, :], 4), v3(num[P, :], 4), bch(den[P, 0:4], T, 128), ALU.mult, [num, den], [num])
        yield
        pf = R.rotF.get()
        self.mm(pf[0:128, 0:12], sel, MIB[P, 0:12], True, True, [con, MIB], [pf])
        LB = g(R.SM)
        self.cp("act", LB[:, 0:12], pf[:, 0:12], [pf], [LB])
        Rot.rel(pf)
        yield
        self.cp("dve", st.mbc[:, 0:4], LB[:, 0:4], [LB], [st.mbc])
        gs = g(R.SM)
        self.tt("dve", gs[:, 0:4], LB[:, 4:8], LB[:, 0:4], ALU.subtract, [LB], [gs])
        self.act(gs[:, 0:4], gs[:, 0:4], AF.Exp, [gs], [gs])
        gt = g(R.SM)
        self.tt("dve", gt[P, 0:4], LB[P, 8:12], MIB[P, 8:12], ALU.subtract, [LB, MIB], [gt])
        self.tt("dve", gt[P, 0:4], gt[P, 0:4], ipre, ALU.add, [gt, g8], [gt])
        self.tt("dve", gt[P, 0:4], gt[P, 0:4], LB[P, 0:4], ALU.subtract, [gt, LB], [gt])
        self.act(gt[P, 0:4], gt[P, 0:4], AF.Exp, [gt], [gt])
        yield
        kg = g(R.B512)
        if K.OPT_K32:
            self.stt(v3(kg[P, :], 4), v3(k32[P, :], 4), KS, bch(gt[P, 0:4], T, 128), ALU.mult, ALU.mult, [k32, gt], [kg])
        else:
            self.stt(v3(kg[P, :], 4), v3(mp[P, 512:1024], 4), KS, bch(gt[P, 0:4], T, 128), ALU.mult, ALU.mult, [mp, gt], [kg])
        pfC = R.rotF.get()
        for h in range(4):
            self.mm(pfC[:, h * 128:(h + 1) * 128], kg[P, h * 128:(h + 1) * 128], vb[P, h * 128:(h + 1) * 128], True, True, [kg, vb], [pfC])
        pfn = R.rotF.get()
        for h in range(4):
            self.mm(pfn[:, h:h + 1], kg[P, h * 128:(h + 1) * 128], conb[0:T, 128:129], True, True, [kg, conb], [pfn])
        self.tt("dve", st.CT[:], st.CT[:], bch(gs[:, 0:4], 128, 128), ALU.mult, [st.CT, gs], [st.CT])
        self.tt("dve", st.CT[:], st.CT[:], v3(pfC[:, :], 4), ALU.add, [st.CT, pfC], [st.CT])
        Rot.rel(pfC)
        self.cp("act", st.CTb[:], st.CT[:], [st.CT], [st.CTb])
        self.tt("dve", st.nT[:], st.nT[:], gs[:, 0:4], ALU.mult, [st.nT, gs], [st.nT])
        self.tt("dve", st.nT[:], st.nT[:], pfn[:, 0:4], ALU.add, [st.nT, pfn], [st.nT])
        Rot.rel(pfn)
        self.cp("act", st.nTb[:], st.nT[:], [st.nT], [st.nTb])
        yield
        sig = g(R.F512)
        if K.OPT_SIGEXP:
            self.act(sig[P, :], mp[P, 1544:2056], AF.Exp, [mp], [sig], scale=-1.0)
            self.ts("pool", sig[P, :], sig[P, :], 1.0, ALU.add, [sig], [sig])
            self.recip(sig[P, :], sig[P, :], [sig], [sig])
        else:
            self.act(sig[P, :], mp[P, 1544:2056], AF.Sigmoid, [mp], [sig])
        szm = g(R.F512)
        self.act(szm[P, :], mp[P, 2056:2568], AF.Silu, [mp], [szm])
        self.tt("pool", szm[P, :], szm[P, :], R.mnorm[P, :], ALU.mult, [szm, R.mnorm], [szm])
        self.tt("dve", num[P, :], num[P, :], sig[P, :], ALU.mult, [num, sig], [num])
        self.tt("pool", sig[P, :], num[P, :], num[P, :], ALU.mult, [num], [sig])
        s4 = g(R.SM)
        self.red(s4[P, 0:4], v3(sig[P, :], 4), ALU.add, [sig], [s4])
        self.rstd(s4[P, 0:4], s4[P, 0:4], 128, [s4], [s4])
        yield
        self.tt("dve", v3(num[P, :], 4), v3(num[P, :], 4), bch(s4[P, 0:4], T, 128), ALU.mult, [num, s4], [num])
        omb = g(R.B512)
        self.tt("dve", omb[P, :], num[P, :], szm[P, :], ALU.mult, [num, szm], [omb])
        pb = self.rotB.get()
        for h in range(4):
            self.tr(pb[0:128, h * T:(h + 1) * T], omb[P, h * 128:(h + 1) * 128], idb, [omb, conb], [pb])
        self.cp("act", mixT[:, 4:8, coff:coff + T], v3(pb[:, 0:T4], 4), [pb], [mixT])
        Rot.rel(pb)
        Rot.rel(*L)
        yield

    def gdn_chunk(self, R, l, T, row0, st, mixT, coff):
        ctx = {}
        for _ in self.gdn_pre_g(R, l, T, row0, ctx):
            pass
        for _ in self.gdn_chain_g(R, l, T, st, mixT, coff, ctx):
            pass

    def gdn_pre_g(self, R, l, T, row0, ctx):
        con, conb = self.con, self.conb
        L = []

        def g(p):
            b = p.get()
            L.append(b)
            return b
        P = slice(0, T)
        T4 = 4 * T
        nst = 5 if T == 64 else 3
        tri = con[0:T, C_TRI:C_TRI + T]
        SLm = con[0:T, C_SL:C_SL + T]
        INCL = con[0:T, C_INCL:C_INCL + T]
        idT = con[0:T, C_ID:C_ID + T]
        ones128 = con[0:T, C_ONE:C_ONE + 128]
        idb = self.ident(T, True)
        v3, bch, bcm = self.v3, self.bch, self.bcm
        ga = g(R.GA)
        self.dma(ga[P, :], self.proj.t[row0:row0 + T, 5288:5808], [], [ga])
        acc = None
        for j in range(4):
            gp = g(R.F1536)
            self.dma(gp[P, :], self.proj.t[row0 - 3 + j:row0 - 3 + j + T, 3752:5288], [], [gp])
            self.tt("pool", gp[P, :], gp[P, :], R.cw[P, j * 1536:(j + 1) * 1536], ALU.mult, [gp, R.cw], [gp])
            if acc is None:
                acc = gp
            else:
                self.tt("pool" if j == 1 else "dve", acc[P, :], acc[P, :], gp[P, :], ALU.add, [acc, gp], [acc])
        cs = acc
        self.act(cs[P, :], cs[P, :], AF.Silu, [cs], [cs])
        yield
        qkn = g(R.F1024)
        self.tt("pool", qkn[P, :], cs[P, 0:1024], cs[P, 0:1024], ALU.mult, [cs], [qkn])
        s8 = g(R.SM)
        self.red(s8[P, 0:8], v3(qkn[P, :], 8), ALU.add, [qkn], [s8])
        self.act(s8[P, 0:8], s8[P, 0:8], AF.Ln, [s8], [s8], bias=EPS)
        self.act(s8[P, 0:8], s8[P, 0:8], AF.Exp, [s8], [s8], scale=-0.5)
        self.ts("dve", s8[P, 0:4], s8[P, 0:4], 128.0 ** -0.5, ALU.mult, [s8], [s8])
        self.tt("dve", v3(qkn[P, :], 8), v3(cs[P, 0:1024], 8), s8[P, 0:8].unsqueeze(2).broadcast_to([T, 8, 128]), ALU.mult, [cs, s8], [qkn])
        yield
        y = g(R.SM)
        self.tt("dve", y[P, 0:4], ga[P, 0:4], R.dtb[P, 0:4], ALU.add, [ga, R.dtb], [y])
        s1 = g(R.SM)
        self.stt(s1[P, 0:4], y[P, 0:4], -1.0, y[P, 0:4], ALU.mult, ALU.max, [y], [s1])
        self.act(s1[P, 0:4], s1[P, 0:4], AF.Exp, [s1], [s1], scale=-1.0)
        self.act(s1[P, 0:4], s1[P, 0:4], AF.Ln, [s1], [s1], bias=1.0)
        gg = g(R.SM)
        self.ts("dve", gg[P, 0:4], y[P, 0:4], 0.0, ALU.max, [y], [gg])
        self.tt("dve", gg[P, 0:4], gg[P, 0:4], s1[P, 0:4], ALU.add, [gg, s1], [gg])
        self.tt("dve", gg[P, 0:4], gg[P, 0:4], R.nea[P, 0:4], ALU.mult, [gg, R.nea], [gg])
        bt = g(R.SM)
        if K.OPT_SIGEXP:
            self.act(bt[P, 0:4], ga[P, 4:8], AF.Exp, [ga], [bt], scale=-1.0)
            self.ts("dve", bt[P, 0:4], bt[P, 0:4], 1.0, ALU.add, [bt], [bt])
            self.recip(bt[P, 0:4], bt[P, 0:4], [bt], [bt])
        else:
            self.act(bt[P, 0:4], ga[P, 4:8], AF.Sigmoid, [ga], [bt])
        self.ts("dve", bt[P, 4:8], bt[P, 0:4], -1.0, ALU.mult, [bt], [bt])
        yield
        Rg = g(R.F256)
        self.tt("pool", v3(Rg[P, 0:T4], 4), bcm(SLm, T, T), bch(gg[P, 0:4], T, T), ALU.mult, [con, gg], [Rg])
        pf = R.rotF.get()
        self.mm(pf[0:T, 0:T4], tri, Rg[P, 0:T4], True, True, [con, Rg], [pf])
        self.mm(pf[0:T, T4:T4 + 4], tri, gg[P, 0:4], True, True, [con, gg], [pf])
        pf128 = R.rotF.get()
        self.mm(pf128[0:128, 0:4], ones128, gg[P, 0:4], True, True, [con, gg], [pf128])
        gam = g(R.F256)
        self.act(gam[P, 0:T4], pf[0:T, 0:T4], AF.Exp, [pf], [gam])
        Gc = g(R.SM)
        self.cp("act", Gc[P, 0:4], pf[0:T, T4:T4 + 4], [pf], [Gc])
        Rot.rel(pf)
        GL = g(R.SM)
        self.cp("act", GL[:, 0:4], pf128[:, 0:4], [pf128], [GL])
        Rot.rel(pf128)
        yield
        gam_s = g(R.F256)
        self.tt("dve", v3(gam_s[P, 0:T4], 4), v3(gam[P, 0:T4], 4), bcm(SLm, T, T), ALU.mult, [gam, con], [gam_s])
        self.tt("dve", v3(gam[P, 0:T4], 4), v3(gam[P, 0:T4], 4), bcm(INCL, T, T), ALU.mult, [gam, con], [gam])
        eG = g(R.SM)
        self.act(eG[P, 0:4], Gc[P, 0:4], AF.Exp, [Gc], [eG])
        eGl = g(R.SM)
        self.act(eGl[:, 0:4], GL[:, 0:4], AF.Exp, [GL], [eGl])
        self.tt("dve", eG[P, 4:8], GL[P, 0:4], Gc[P, 0:4], ALU.subtract, [GL, Gc], [eG])
        self.act(eG[P, 4:8], eG[P, 4:8], AF.Exp, [eG], [eG])
        self.tt("dve", eG[P, 8:12], bt[P, 0:4], eG[P, 0:4], ALU.mult, [bt, eG], [eG])
        yield
        qkb = g(R.B1024)
        self.cp("act", qkb[P, :], qkn[P, :], [qkn], [qkb])
        qgb = g(R.B512)
        self.tt("pool", v3(qgb[P, :], 4), v3(qkn[P, 0:512], 4), bch(eG[P, 0:4], T, 128), ALU.mult, [qkn, eG], [qgb])
        bk = g(R.F512)
        self.tt("dve", v3(bk[P, :], 4), v3(qkn[P, 512:1024], 4), bch(eG[P, 8:12], T, 128), ALU.mult, [qkn, eG], [bk])
        bv = g(R.F512)
        self.tt("pool", v3(bv[P, :], 4), v3(cs[P, 1024:1536], 4), bch(bt[P, 0:4], T, 128), ALU.mult, [cs, bt], [bv])
        kd = g(R.B512)
        self.tt("pool", v3(kd[P, :], 4), v3(qkn[P, 512:1024], 4), bch(eG[P, 4:8], T, 128), ALU.mult, [qkn, eG], [kd])
        yield
        pb = self.rotB.get()
        for i in range(8):
            self.tr(pb[0:128, i * T:(i + 1) * T], qkb[P, i * 128:(i + 1) * 128], idb, [qkb, conb], [pb])
        for h in range(4):
            self.tr(pb[0:128, (8 + h) * T:(9 + h) * T], qgb[P, h * 128:(h + 1) * 128], idb, [qgb, conb], [pb])
        qkT = g(R.BT)
        self.cp("act" if K.OPT_EVACT else "dve", qkT[:, 0:12 * T], pb[:, 0:12 * T], [pb], [qkT])
        Rot.rel(pb)
        yield
        qT = lambda h: qkT[:, h * T:(h + 1) * T]
        kT = lambda h: qkT[:, (4 + h) * T:(5 + h) * T]
        qgT = lambda h: qkT[:, (8 + h) * T:(9 + h) * T]
        pf = R.rotF.get()
        for h in range(4):
            self.mm(pf[0:T, h * T:(h + 1) * T], kT(h), kT(h), True, True, [qkT], [pf])
        for h in range(4):
            self.mm(pf[0:T, (4 + h) * T:(5 + h) * T], qT(h), kT(h), True, True, [qkT], [pf])
        N = g(R.F256)
        self.tt("dve", N[P, 0:T4], pf[0:T, 0:T4], gam_s[P, 0:T4], ALU.mult, [pf, gam_s], [N])
        self.tt("dve", v3(N[P, 0:T4], 4), v3(N[P, 0:T4], 4), bch(bt[P, 4:8], T, T), ALU.mult, [N, bt], [N])
        QG = g(R.F256)
        self.tt("dve", QG[P, 0:T4], pf[0:T, T4:2 * T4], gam[P, 0:T4], ALU.mult, [pf, gam], [QG])
        Rot.rel(pf)
        yield
        pf = R.rotF.get()
        for h in range(4):
            self.tr(pf[0:T, h * T:(h + 1) * T], N[P, h * T:(h + 1) * T], idT, [N, con], [pf])
        for h in range(4):
            self.tr(pf[0:T, (4 + h) * T:(5 + h) * T], QG[P, h * T:(h + 1) * T], idT, [QG, con], [pf])
        Q = g(R.F256)
        self.cp("act", Q[P, 0:T4], pf[0:T, 0:T4], [pf], [Q])
        QGT = g(R.B256)
        self.cp("act" if K.OPT_EVACT else "dve", QGT[P, 0:T4], pf[0:T, T4:2 * T4], [pf], [QGT])
        Rot.rel(pf)
        Y = g(R.F256)
        self.tt("dve", v3(Y[P, 0:T4], 4), v3(Q[P, 0:T4], 4), bcm(idT, T, T), ALU.add, [Q, con], [Y])
        yield
        Pm = N
        hs = lambda b_, h: b_[P, h * T:(h + 1) * T]
        for s in range(nst):
            last = (s == nst - 1)
            pfP = R.rotF.get()
            for h in range(4):
                self.mm(pfP[0:T, h * T:(h + 1) * T], hs(Q, h), hs(Pm, h), True, True, [Q, Pm], [pfP])
            if not last:
                pfQ = R.rotF.get()
                for h in range(4):
                    self.mm(pfQ[0:T, h * T:(h + 1) * T], hs(Pm, h), hs(Q, h), True, True, [Q, Pm], [pfQ])
            Pn = g(R.F256)
            self.cp("act", Pn[P, 0:T4], pfP[0:T, 0:T4], [pfP], [Pn])
            Rot.rel(pfP)
            if not last:
                Qn = g(R.F256)
                self.cp("act" if K.OPT_EVACT else "dve", Qn[P, 0:T4], pfQ[0:T, 0:T4], [pfQ], [Qn])
                Rot.rel(pfQ)
            pfY = R.rotF.get()
            for h in range(4):
                self.mm(pfY[0:T, h * T:(h + 1) * T], hs(Pn, h), hs(Y, h), True, True, [Pn, Y], [pfY])
            self.tt("dve", Y[P, 0:T4], Y[P, 0:T4], pfY[0:T, 0:T4], ALU.add, [Y, pfY], [Y])
            Rot.rel(pfY)
            yield
            for old in ((Pm, Q) if not last else (Pm, Q, Pn)):
                if old in L:
                    L.remove(old)
                    Rot.rel(old)
            Pm = Pn
            if not last:
                Q = Qn
        pfu = R.rotF.get()
        for h in range(4):
            self.mm(pfu[0:T, h * 128:(h + 1) * 128], hs(Y, h), bv[P, h * 128:(h + 1) * 128], True, True, [Y, bv], [pfu])
        usb = g(R.F512)
        self.cp("act", usb[P, :], pfu[0:T, :], [pfu], [usb])
        Rot.rel(pfu)
        yield
        pfw = R.rotF.get()
        for h in range(4):
            self.mm(pfw[0:128, h * T:(h + 1) * T], bk[P, h * 128:(h + 1) * 128], hs(Y, h), True, True, [bk, Y], [pfw])
        wTb = g(R.BT)
        self.cp("act" if K.OPT_EVACT else "dve", wTb[:, 0:T4], pfw[:, 0:T4], [pfw], [wTb])
        Rot.rel(pfw)
        keep = [usb, wTb, qkT, QGT, kd, eGl, ga]
        for b_ in list(L):
            if b_ not in keep:
                Rot.rel(b_)
        ctx.update(usb=usb, wTb=wTb, qkT=qkT, QGT=QGT, kd=kd, eGl=eGl, ga=ga, keep=keep)
        yield

    def gdn_chain_g(self, R, l, T, st, mixT, coff, ctx):
        con, conb = self.con, self.conb
        L = []

        def g(p):
            b = p.get()
            L.append(b)
            return b
        P = slice(0, T)
        T4 = 4 * T
        idb = self.ident(T, True)
        v3, bch, bcm = self.v3, self.bch, self.bcm
        usb, wTb, qkT, QGT, kd, eGl, ga = (ctx[k_] for k_ in ("usb", "wTb", "qkT", "QGT", "kd", "eGl", "ga"))
        qgT = lambda h: qkT[:, (8 + h) * T:(9 + h) * T]
        hs = lambda b_, h: b_[P, h * T:(h + 1) * T]
        pf = R.rotF.get()
        for h in range(4):
            self.mm(pf[0:T, h * 128:(h + 1) * 128], wTb[:, h * T:(h + 1) * T], st.Sb[:, h, :], True, True, [wTb, st.Sb], [pf])
        dl = g(R.B512)
        self.tt("dve", dl[P, :], usb[P, :], pf[0:T, :], ALU.subtract, [usb, pf], [dl])
        Rot.rel(pf)
        yield
        pfo = R.rotF.get()
        for h in range(4):
            o = pfo[0:T, h * 128:(h + 1) * 128]
            self.mm(o, qgT(h), st.Sb[:, h, :], True, False, [qkT, st.Sb], [pfo])
            self.mm(o, hs(QGT, h), dl[P, h * 128:(h + 1) * 128], False, True, [QGT, dl], [pfo])
        pfS = R.rotF.get()
        for h in range(4):
            self.mm(pfS[:, h * 128:(h + 1) * 128], kd[P, h * 128:(h + 1) * 128], dl[P, h * 128:(h + 1) * 128], True, True, [kd, dl], [pfS])
        self.tt("dve", st.S[:], st.S[:], bch(eGl[:, 0:4], 128, 128), ALU.mult, [st.S, eGl], [st.S])
        self.tt("dve", st.S[:], st.S[:], v3(pfS[:, :], 4), ALU.add, [st.S, pfS], [st.S])
        Rot.rel(pfS)
        self.cp("act", st.Sb[:], st.S[:], [st.S], [st.Sb])
        osb = g(R.F512)
        self.cp("act", osb[P, :], pfo[0:T, :], [pfo], [osb])
        Rot.rel(pfo)
        yield
        sq = g(R.F512)
        self.tt("pool", sq[P, :], osb[P, :], osb[P, :], ALU.mult, [osb], [sq])
        s4 = g(R.SM)
        self.red(s4[P, 0:4], v3(sq[P, :], 4), ALU.add, [sq], [s4])
        self.rstd(s4[P, 0:4], s4[P, 0:4], 128, [s4], [s4])
        yield
        self.tt("dve", v3(osb[P, :], 4), v3(osb[P, :], 4), bch(s4[P, 0:4], T, 128), ALU.mult, [osb, s4], [osb])
        self.act(sq[P, :], ga[P, 8:520], AF.Silu, [ga], [sq])
        self.tt("pool", v3(sq[P, :], 4), v3(sq[P, :], 4), bcm(R.gnorm[P, :], T, 128), ALU.mult, [sq, R.gnorm], [sq])
        ogb = g(R.B512)
        self.tt("dve", ogb[P, :], osb[P, :], sq[P, :], ALU.mult, [osb, sq], [ogb])
        pb = self.rotB.get()
        for h in range(4):
            self.tr(pb[0:128, h * T:(h + 1) * T], ogb[P, h * 128:(h + 1) * 128], idb, [ogb, conb], [pb])
        self.cp("act", mixT[:, 8:12, coff:coff + T], v3(pb[:, 0:T4], 4), [pb], [mixT])
        Rot.rel(pb)
        Rot.rel(*L)
        Rot.rel(*ctx["keep"])
        yield


_CACHE = {}


def _program(dbg=None, nlayers=DEPTH, phases="PAR"):
    key = (dbg, nlayers, phases)
    if key not in _CACHE:
        _CACHE[key] = K(dbg, nlayers, phases).build()
    return _CACHE[key]


def make_in_maps(inp):
    f = lambda a: np.ascontiguousarray(np.asarray(a, dtype=np.float32))
    consts = make_consts()
    rope = make_rope()
    shared = {
        "meta": f(inp["meta_tokens"]), "norm_w": f(inp["norm_w"]), "w_in": f(inp["w_in"]),
        "q_norm": f(inp["mla_q_norm"]), "w_uq": f(inp["mla_w_uq"]), "kv_norm": f(inp["mla_kv_norm"]),
        "w_uk": f(inp["mla_w_uk"]).reshape(DEPTH, 256, 512), "w_uv": f(inp["mla_w_uv"]).reshape(DEPTH, 256, 512),
        "gate_b": f(inp["mlstm_gate_b"]).reshape(DEPTH, 8), "m_norm": f(inp["mlstm_norm"]),
        "conv_w": f(inp["gdn_conv_w"]).reshape(DEPTH, 4 * 1536), "a_log": f(inp["gdn_a_log"]),
        "dt_bias": f(inp["gdn_dt_bias"]), "g_norm": f(inp["gdn_norm"]), "w_out": f(inp["w_out"]),
        "final_norm": f(inp["final_norm"]), "consts": consts, "rope": rope,
    }
    maps = []
    for c in range(8):
        sl = slice(NS * c, NS * c + NS)
        m = dict(shared)
        m["xp"] = f(inp["x_prompt"][c])
        m["xs"] = f(inp["x_sample"][sl]).reshape(NS * DS, D)
        m["cl"] = f(inp["cache_mla_latent"][:, sl])
        m["ck"] = f(inp["cache_mla_krope"][:, sl])
        m["imC"] = f(inp["state_mlstm_C"][:, sl])
        m["imn"] = f(inp["state_mlstm_n"][:, sl])
        m["imm"] = f(inp["state_mlstm_m"][:, sl])
        m["igS"] = f(inp["state_gdn_S"][:, sl])
        m["igc"] = f(inp["state_gdn_conv"][:, sl])
        maps.append(m)
    return maps


def kernel(**inputs):
    nc = _program()
    maps = make_in_maps(inputs)
    res = run_bass_kernel_spmd(nc, maps, core_ids=list(range(8)))
    R = res.results
    st = lambda name, axis: np.stack([np.asarray(r[name], dtype=np.float32) for r in R], axis=axis)
    cat = lambda name, axis: np.concatenate([np.asarray(r[name], dtype=np.float32) for r in R], axis=axis)
    y_prompt = st("y_p", 0)
    y_sample = st("y_s", 0).reshape(8 * NS, DS, D)
    return (
        y_prompt, y_sample,
        st("p_lat", 1), st("p_kr", 1), st("p_C", 1), st("p_n", 1), st("p_m", 1), st("p_S", 1), st("p_cv", 1),
        cat("s_lat", 1), cat("s_kr", 1), cat("s_C", 1), cat("s_n", 1), cat("s_m", 1), cat("s_S", 1), cat("s_cv", 1),
    )
```

```python
import math
import contextlib
import numpy as np
import concourse.bass as bass
import concourse.mybir as mybir
from concourse.bass_utils import run_bass_kernel_spmd

F32 = mybir.dt.float32
BF16 = mybir.dt.bfloat16
AF = mybir.ActivationFunctionType
ALU = mybir.AluOpType
AX = mybir.AxisListType

D = 1024
SEQ = 4096
NMETA = 16
DEPTH = 2
NS = 4
DS = 16
PAST = 4096
NCACHE = NMETA + PAST
EPS = 1e-6
IN_COLS = 5808
NROWS_X = 80 + SEQ
PROJ_ROWS = 3 + 16 + SEQ + NS * 19
SM_SCALE = 1.0 / math.sqrt(96.0)

C_ID, C_ONE, C_TRI, C_SL, C_INCL, C_NEG, C_S128, C_S16, C_SELR = 0, 128, 256, 384, 512, 640, 768, 896, 1024
NCON = 1120
RCHUNK = 128


def make_consts():
    c = np.zeros((128, NCON), np.float32)
    c[:, C_ID:C_ID + 128] = np.eye(128)
    c[:, C_ONE:C_ONE + 128] = 1.0
    k = np.arange(128)[:, None]
    t = np.arange(128)[None, :]
    c[:, C_TRI:C_TRI + 128] = (k <= t)
    c[:, C_SL:C_SL + 128] = (k > t)
    c[:, C_INCL:C_INCL + 128] = (t <= k)
    c[:, C_NEG:C_NEG + 128] = np.where(t <= k, 0.0, -1e30)
    c[127, C_S128:C_S128 + 128] = 1.0
    c[15, C_S16:C_S16 + 128] = 1.0
    c[:32, C_SELR + 64:C_SELR + 96] = np.eye(32)
    return c


def make_rope():
    half = 16
    freq = (np.float32(10000.0) ** (-np.arange(half, dtype=np.float32) / np.float32(half))).astype(np.float32)
    pos = np.arange(NCACHE + DS, dtype=np.float32)
    ang = (pos[:, None] * freq[None, :]).astype(np.float32)
    return np.concatenate([np.cos(ang), np.sin(ang)], axis=1).astype(np.float32)


class H:
    __slots__ = ("name", "w", "r", "excl")

    def __init__(self, name="", excl=False):
        self.name = name
        self.w = None
        self.r = []
        self.excl = excl


class Buf:
    __slots__ = ("t", "h", "busy")

    def __init__(self, t, name=""):
        self.t = t
        self.h = H(name)
        self.busy = False

    def __getitem__(self, k):
        return self.t[k]


def _hs(lst):
    out = []
    for x in lst:
        if x is None:
            continue
        out.append(x.h if isinstance(x, Buf) else x)
    return out


class Sched:
    ENGS = ("pe", "act", "dve", "pool", "sp")
    XLAT = 1.2

    def __init__(self, nc, n_dma_sems=20):
        self.nc = nc
        self.nodes = []
        self.aset = {}
        self.cur_set = 1
        self.seg_start = [0]
        self.n_dma = n_dma_sems
        self.sems = {}

    def _deps(self, reads, writes):
        d = set()
        for h in reads:
            if h.w is not None:
                d.add(h.w)
        for h in writes:
            if h.w is not None:
                d.add(h.w)
            d.update(h.r)
        return d

    def _record(self, idx, reads, writes):
        for h in reads:
            h.r.append(idx)
        for h in writes:
            h.w = idx
            h.r = []

    def emit(self, eng, fn, reads=(), writes=(), cost=0.3, aset=0):
        reads, writes = _hs(reads), _hs(writes)
        ex = [h for h in reads if h.excl]
        if ex:
            reads = [h for h in reads if not h.excl]
            writes = writes + [h for h in ex if h not in writes]
        deps = self._deps(reads, writes)
        idx = len(self.nodes)
        self.nodes.append((eng, fn, deps, False, cost))
        if aset:
            self.aset[idx] = aset
        self._record(idx, reads, writes)
        return idx

    def dma(self, q, fn, reads=(), writes=(), is_out=False, cost=2.5):
        reads, writes = _hs(reads), _hs(writes)
        deps = self._deps(reads, writes)
        idx = len(self.nodes)
        self.nodes.append((q, fn, deps, True, cost))
        self._record(idx, reads, writes)
        return idx

    def barrier(self):
        if self.seg_start[-1] != len(self.nodes):
            self.seg_start.append(len(self.nodes))

    def _schedule(self, lo, hi):
        import heapq
        nodes = self.nodes
        n = hi - lo
        succ = [[] for _ in range(n)]
        indeg = [0] * n
        for i in range(lo, hi):
            for d in nodes[i][2]:
                if d >= lo:
                    succ[d - lo].append(i - lo)
                    indeg[i - lo] += 1
        prio = [0.0] * n
        for i in range(n - 1, -1, -1):
            c = nodes[lo + i][4]
            m = 0.0
            for s in succ[i]:
                if prio[s] > m:
                    m = prio[s]
            prio[i] = c + m + self.XLAT
        future = {e: [] for e in self.ENGS}
        avail = {e: [] for e in self.ENGS}
        tfree = {e: 0.0 for e in self.ENGS}
        ready_t = [0.0] * n
        order = {e: [] for e in self.ENGS}
        for i in range(n):
            if indeg[i] == 0:
                heapq.heappush(future[nodes[lo + i][0]], (0.0, i))
        done = 0
        while done < n:
            best_e, best_t = None, None
            for e in self.ENGS:
                if avail[e]:
                    t = tfree[e]
                elif future[e]:
                    t = max(tfree[e], future[e][0][0])
                else:
                    continue
                if best_t is None or t < best_t:
                    best_e, best_t = e, t
            e, t = best_e, best_t
            fu = future[e]
            while fu and fu[0][0] <= t:
                rt, i = heapq.heappop(fu)
                heapq.heappush(avail[e], (-prio[i], i))
            if e == "act" and len(avail[e]) > 1:
                top = avail[e][0]
                ts_ = self.aset.get(lo + top[1], 0)
                if ts_ != 0 and ts_ != self.cur_set:
                    best = None
                    for cand in avail[e]:
                        cs_ = self.aset.get(lo + cand[1], 0)
                        if (cs_ == 0 or cs_ == self.cur_set) and (best is None or cand < best):
                            best = cand
                    if best is not None and (-best[0]) >= (-top[0]) - 12.0:
                        avail[e].remove(best)
                        heapq.heapify(avail[e])
                        i = best[1]
                    else:
                        _, i = heapq.heappop(avail[e])
                else:
                    _, i = heapq.heappop(avail[e])
            else:
                _, i = heapq.heappop(avail[e])
            node = nodes[lo + i]
            sw = 0.0
            if e == "act":
                s_ = self.aset.get(lo + i, 0)
                if s_ != 0 and s_ != self.cur_set:
                    self.cur_set = s_
                    sw = 1.3
            if node[3]:
                fin = t + node[4]
                tfree[e] = t + 0.15
            else:
                fin = t + node[4] + sw
                tfree[e] = fin
            order[e].append(lo + i)
            done += 1
            for s in succ[i]:
                se = nodes[lo + s][0]
                lat = 0.08 if (se == e and not node[3]) else self.XLAT
                if fin + lat > ready_t[s]:
                    ready_t[s] = fin + lat
                indeg[s] -= 1
                if indeg[s] == 0:
                    heapq.heappush(future[se], (ready_t[s], s))
        return order

    def finalize(self):
        nc = self.nc
        nodes = self.nodes
        self.barrier()
        segs = list(zip(self.seg_start[:-1], self.seg_start[1:]))
        eng_ops = {e: [] for e in self.ENGS}
        token = [None] * len(nodes)
        cnt = {e: 0 for e in self.ENGS}
        dma_i = {e: 0 for e in self.ENGS}
        dma_val = {e: [0] * self.n_dma for e in self.ENGS}
        dma_prev = {}
        for (lo, hi) in segs:
            order = self._schedule(lo, hi)
            for e in self.ENGS:
                for idx in order[e]:
                    if nodes[idx][3]:
                        i = dma_i[e]
                        dma_i[e] = (i + 1) % self.n_dma
                        key = ("dma", e, i)
                        if dma_val[e][i] > 0:
                            dma_prev[idx] = (key, dma_val[e][i])
                        dma_val[e][i] += 16
                        token[idx] = (key, dma_val[e][i], 16)
                    else:
                        cnt[e] += 1
                        token[idx] = (e, cnt[e], 1)
                    eng_ops[e].append(("node", idx))
            allw = {e: cnt[e] for e in self.ENGS if cnt[e] > 0}
            for e in self.ENGS:
                for i, v in enumerate(dma_val[e]):
                    if v > 0:
                        allw[("dma", e, i)] = v
            for e in self.ENGS:
                eng_ops[e].append(("bar", dict(allw)))
        keys = list(self.ENGS)
        for e in self.ENGS:
            for i, v in enumerate(dma_val[e]):
                if v > 0:
                    keys.append(("dma", e, i))
        with contextlib.ExitStack() as es:
            for k in keys:
                nm = k if isinstance(k, str) else "d_%s_%d" % (k[1], k[2])
                self.sems[k] = es.enter_context(nc.semaphore("s_" + nm))
            block = es.enter_context(nc.Block())
            sems = self.sems

            def run(eng_name):
                def body(e):
                    waited = {}
                    for kind, x in eng_ops[eng_name]:
                        if kind == "bar":
                            for k, v in x.items():
                                if k == eng_name:
                                    continue
                                if waited.get(k, 0) < v:
                                    waited[k] = v
                                    e.wait_ge(sems[k], v)
                            continue
                        idx = x
                        _, fn, deps, is_dma, _ = nodes[idx]
                        need = {}
                        if idx in dma_prev:
                            k, v = dma_prev[idx]
                            need[k] = v
                        for d in deps:
                            k, v, _ = token[d]
                            if k == eng_name and eng_name == "pe":
                                continue
                            if need.get(k, 0) < v:
                                need[k] = v
                        for k, v in need.items():
                            if waited.get(k, 0) < v:
                                waited[k] = v
                                e.wait_ge(sems[k], v)
                        k, v, inc = token[idx]
                        fn(e).then_inc(sems[k], inc)
                return body

            block.tensor(run("pe"))
            block.scalar(run("act"))
            block.vector(run("dve"))
            block.gpsimd(run("pool"))
            block.sync(run("sp"))


class Rot:
    def __init__(self, bufs):
        self.bufs = bufs
        self.i = 0

    def get(self):
        for _ in range(len(self.bufs)):
            b = self.bufs[self.i]
            self.i = (self.i + 1) % len(self.bufs)
            if not b.busy:
                b.busy = True
                return b
        raise AssertionError("rotating pool exhausted: all buffers leased")

    @staticmethod
    def rel(*bs):
        for b in bs:
            b.busy = False


class K:
    OPT_K32 = False
    OPT_SIGEXP = False
    OPT_WSPLIT = True
    OPT_X1 = False
    OPT_EVACT = True

    def __init__(self, dbg=None, nlayers=DEPTH, phases="PAR"):
        self.dbg = dbg
        self.nlayers = nlayers
        self.phases = phases
        self.nc = bass.Bass("TRN2", target_bir_lowering=False)
        self.S = Sched(self.nc)
        self.es = contextlib.ExitStack()
        self.dq = 0

    def dram(self, name, shape, dt=F32, kind=None):
        if kind is None:
            t = self.nc.dram_tensor(name, list(shape), dt)
        else:
            t = self.nc.dram_tensor(name, list(shape), dt, kind=kind)
        return Buf(t.ap(), name)

    def sb(self, es, name, shape, dt=F32):
        self.dq += 1
        name = "%s_u%d" % (name, self.dq)
        return Buf(es.enter_context(self.nc.sbuf_tensor(name, list(shape), dt)), name)

    def ps(self, es, name, shape, dt=F32):
        b = Buf(es.enter_context(self.nc.psum_tensor(name, list(shape), dt)), name)
        b.h.excl = True
        return b

    def mm(self, out, lhsT, rhs, start, stop, r, w):
        n = max(64, rhs.free_size()) * (4 if rhs.dtype == F32 else 1)
        self.S.emit("pe", lambda e: e.matmul(out, lhsT=lhsT, rhs=rhs, start=start, stop=stop), r, w, cost=0.04 + n / 1400.0)

    def tr(self, out, in_, ident, r, w):
        self.S.emit("pe", lambda e: e.transpose(out, in_, ident), r, w, cost=0.04 + max(64, in_.partition_size()) / 1400.0)

    @staticmethod
    def _c(eng, ap):
        n = ap.free_size()
        if eng == "dve":
            return 0.12 + n / 960.0
        if eng == "act":
            return 0.2 + n / 1400.0
        return 0.35 + n / 700.0

    def act(self, out, in_, func, r, w, scale=None, bias=None, accum=None):
        kw = {}
        if scale is not None:
            kw["scale"] = scale
        if bias is not None:
            kw["bias"] = bias
        if accum is not None:
            kw["accum_out"] = accum
        aset = 1 if func in (AF.Exp, AF.Ln) else (2 if func == AF.Silu else (3 if func in (AF.Sigmoid, AF.Sqrt) else 0))
        self.S.emit("act", lambda e: e.activation(out=out, in_=in_, func=func, **kw), r, w, cost=self._c("act", out) + (0.1 if accum is not None else 0), aset=aset)

    def tt(self, eng, out, in0, in1, op, r, w):
        self.S.emit(eng, lambda e: e.tensor_tensor(out=out, in0=in0, in1=in1, op=op), r, w, cost=self._c(eng, out))

    def ts(self, eng, out, in0, s1, op0, r, w, s2=None, op1=None):
        if op1 is None:
            self.S.emit(eng, lambda e: e.tensor_scalar(out=out, in0=in0, scalar1=s1, scalar2=None, op0=op0), r, w, cost=self._c(eng, out))
        else:
            self.S.emit(eng, lambda e: e.tensor_scalar(out=out, in0=in0, scalar1=s1, scalar2=s2, op0=op0, op1=op1), r, w, cost=self._c(eng, out))

    def stt(self, out, in0, scalar, in1, op0, op1, r, w):
        self.S.emit("dve", lambda e: e.scalar_tensor_tensor(out=out, in0=in0, scalar=scalar, in1=in1, op0=op0, op1=op1), r, w, cost=self._c("dve", out))

    def red(self, out, in_, op, r, w):
        self.S.emit("dve", lambda e: e.tensor_reduce(out=out, in_=in_, axis=AX.X, op=op), r, w, cost=self._c("dve", in_))

    def cp(self, eng, out, in_, r, w):
        if eng == "act":
            self.S.emit("act", lambda e: e.activation(out=out, in_=in_, func=AF.Copy), r, w, cost=self._c("act", out))
        else:
            self.S.emit(eng, lambda e: e.tensor_copy(out=out, in_=in_), r, w, cost=self._c(eng, out))

    def recip(self, out, in_, r, w):
        self.S.emit("dve", lambda e: e.reciprocal(out=out, in_=in_), r, w, cost=self._c("dve", out))

    def memset(self, eng, ap, val, w):
        self.S.emit(eng, lambda e: e.memset(ap, val), [], w, cost=self._c(eng, ap))

    def dma(self, out, in_, r, w, q=None, nonc=False, is_out=False):
        if q is None:
            q = ("sp", "act")[self.dq % 2] if False else "sp"
        c = 2.0 + in_.nbytes() / 150e3 * (2.0 if q == "pool" else 1.0)
        if nonc:
            self.S.dma(q, lambda e: e.dma_start(out=out, in_=in_, allow_slow_non_contiguous=True), r, w, is_out, cost=c)
        else:
            self.S.dma(q, lambda e: e.dma_start(out=out, in_=in_), r, w, is_out, cost=c)

    @staticmethod
    def interleave(gens):
        gens = [g for g in gens if g is not None]
        while gens:
            alive = []
            for g in gens:
                try:
                    next(g)
                    alive.append(g)
                except StopIteration:
                    pass
            gens = alive

    def rstd(self, out, ssq, n, r, w):
        self.act(out, ssq, AF.Ln, r, w, scale=1.0 / n, bias=EPS)
        self.act(out, out, AF.Exp, w, w, scale=-0.5)

    def build(self):
        nc = self.nc
        es = self.es
        dbg = self.dbg
        EI, EO = "ExternalInput", "ExternalOutput"
        self.xp = self.dram("xp", [SEQ, D], kind=EI)
        self.xs = self.dram("xs", [NS * DS, D], kind=EI)
        self.meta = self.dram("meta", [NMETA, D], kind=EI)
        self.cl = self.dram("cl", [DEPTH, NS, NCACHE, 256], kind=EI)
        self.ck = self.dram("ck", [DEPTH, NS, NCACHE, 32], kind=EI)
        self.imC = self.dram("imC", [DEPTH, NS, 4, 128, 128], kind=EI)
        self.imn = self.dram("imn", [DEPTH, NS, 4, 128], kind=EI)
        self.imm = self.dram("imm", [DEPTH, NS, 4], kind=EI)
        self.igS = self.dram("igS", [DEPTH, NS, 4, 128, 128], kind=EI)
        self.igc = self.dram("igc", [DEPTH, NS, 3, 1536], kind=EI)
        self.norm_w = self.dram("norm_w", [DEPTH, D], kind=EI)
        self.w_in = self.dram("w_in", [DEPTH, D, IN_COLS], kind=EI)
        self.q_norm = self.dram("q_norm", [DEPTH, 384], kind=EI)
        self.w_uq = self.dram("w_uq", [DEPTH, 384, 768], kind=EI)
        self.kv_norm = self.dram("kv_norm", [DEPTH, 256], kind=EI)
        self.w_uk = self.dram("w_uk", [DEPTH, 256, 512], kind=EI)
        self.w_uv = self.dram("w_uv", [DEPTH, 256, 512], kind=EI)
        self.gate_b = self.dram("gate_b", [DEPTH, 8], kind=EI)
        self.m_norm = self.dram("m_norm", [DEPTH, 512], kind=EI)
        self.conv_w = self.dram("conv_w", [DEPTH, 4 * 1536], kind=EI)
        self.a_log = self.dram("a_log", [DEPTH, 4], kind=EI)
        self.dt_bias = self.dram("dt_bias", [DEPTH, 4], kind=EI)
        self.g_norm = self.dram("g_norm", [DEPTH, 128], kind=EI)
        self.w_out = self.dram("w_out", [DEPTH, 1536, D], kind=EI)
        self.final_norm = self.dram("final_norm", [D], kind=EI)
        self.consts = self.dram("consts", [128, NCON], kind=EI)
        self.rope = self.dram("rope", [NCACHE + DS, 32], kind=EI)

        self.y_p = self.dram("y_p", [SEQ, D], kind=EO)
        self.y_s = self.dram("y_s", [NS * DS, D], kind=EO)
        self.p_lat = self.dram("p_lat", [DEPTH, NMETA + SEQ, 256], kind=EO)
        self.p_kr = self.dram("p_kr", [DEPTH, NMETA + SEQ, 32], kind=EO)
        self.p_C = self.dram("p_C", [DEPTH, 4, 128, 128], kind=EO)
        self.p_n = self.dram("p_n", [DEPTH, 4, 128], kind=EO)
        self.p_m = self.dram("p_m", [DEPTH, 4], kind=EO)
        self.p_S = self.dram("p_S", [DEPTH, 4, 128, 128], kind=EO)
        self.p_cv = self.dram("p_cv", [DEPTH, 3, 1536], kind=EO)
        self.s_lat = self.dram("s_lat", [DEPTH, NS, DS, 256], kind=EO)
        self.s_kr = self.dram("s_kr", [DEPTH, NS, DS, 32], kind=EO)
        self.s_C = self.dram("s_C", [DEPTH, NS, 4, 128, 128], kind=EO)
        self.s_n = self.dram("s_n", [DEPTH, NS, 4, 128], kind=EO)
        self.s_m = self.dram("s_m", [DEPTH, NS, 4], kind=EO)
        self.s_S = self.dram("s_S", [DEPTH, NS, 4, 128, 128], kind=EO)
        self.s_cv = self.dram("s_cv", [DEPTH, NS, 3, 1536], kind=EO)

        dk = EO if dbg else None
        self.proj = self.dram("proj_scr", [PROJ_ROWS, IN_COLS], kind=dk)
        self.mixa = self.dram("mixa_scr", [NROWS_X, 512], BF16, kind=dk)
        self.xscr = self.dram("x_scr", [NROWS_X, D], kind=dk)
        self.proj_h = {}
        self.mixa_h = {}
        self.x_h = {}

        self.con = self.sb(es, "con", [128, NCON])
        self.conb = self.sb(es, "conb", [128, 256 + 96], BF16)
        self.dma(self.con[:], self.consts.t, [], [self.con])
        self.dma(self.conb[:, 0:256], self.consts.t[:, 0:256], [], [self.conb], q="pool")
        self.dma(self.conb[:, 256:352], self.consts.t[:, C_SELR:C_SELR + 96], [], [self.conb], q="pool")
        self.psF = [self.ps(es, "psF%d" % i, [128, 512]) for i in range(6)]
        self.psB = [self.ps(es, "psB%d" % i, [128, 1024], BF16) for i in range(2)]
        self.rotB = Rot(self.psB)
        self.fin_bc = self.sb(es, "fin_bc", [128, D])
        self.dma(self.fin_bc[:], self.final_norm.t.partition_broadcast(128), [], [self.fin_bc])

        for l in range(self.nlayers):
            if "P" in self.phases:
                self.phase_P(l)
                self.S.barrier()
            if "A" in self.phases:
                self.phase_A(l)
                self.S.barrier()
            if "R" in self.phases:
                self.phase_R(l)
                self.S.barrier()
        self.S.finalize()
        es.close()
        return nc

    def ph(self, key):
        if key not in self.proj_h:
            self.proj_h[key] = H("proj%s" % (key,))
        return self.proj_h[key]

    def mh(self, key):
        if key not in self.mixa_h:
            self.mixa_h[key] = H("mixa%s" % (key,))
        return self.mixa_h[key]

    def xh(self, key):
        if key not in self.x_h:
            self.x_h[key] = H("x%s" % (key,))
        return self.x_h[key]

    def ident(self, n, bf=False):
        if bf:
            return self.conb[0:n, 0:n]
        return self.con[0:n, C_ID:C_ID + n]

    def load_x(self, l, tile, xt):
        if tile == "aux":
            if l == 0:
                self.dma(xt[0:16, :], self.meta.t, [], [xt])
                self.dma(xt[16:80, :], self.xs.t, [], [xt])
            else:
                self.dma(xt[0:80, :], self.xscr.t[0:80, :], [self.xh("aux")], [xt])
            return 80
        j = tile
        if l == 0:
            self.dma(xt[:, :], self.xp.t[j * 128:(j + 1) * 128, :], [], [xt])
        else:
            self.dma(xt[:, :], self.xscr.t[80 + j * 128:80 + (j + 1) * 128, :], [self.xh(j)], [xt])
        return 128

    def phase_P(self, l):
        con, conb = self.con, self.conb
        with contextlib.ExitStack() as es:
            w_bf = self.sb(es, "P_w", [128, 8, IN_COLS], BF16)
            wv = self.w_in.t[l].rearrange("(k p) c -> p k c", p=128)
            ngw = (IN_COLS + 511) // 512
            w_h = [H("w_in_g%d" % g_) for g_ in range(ngw)]
            if K.OPT_WSPLIT:
                for g_ in range(ngw):
                    c0_ = g_ * 512
                    cw_ = min(512, IN_COLS - c0_)
                    self.dma(w_bf[:, :, c0_:c0_ + cw_], wv[:, :, c0_:c0_ + cw_], [], [w_h[g_]], q="pool")
            else:
                for k in range(8):
                    self.dma(w_bf[:, k, :], wv[:, k, :], [], w_h, q="pool")
            nw = self.sb(es, "P_nw", [128, D])
            self.dma(nw[:], self.norm_w.t[l].partition_broadcast(128), [], [nw])
            xts = Rot([self.sb(es, "P_x%d" % i, [128, D]) for i in range(2)])
            junk = self.sb(es, "P_junk", [128, D], BF16)
            ssq = Rot([self.sb(es, "P_ssq%d" % i, [128, 1]) for i in range(2)])
            rs = Rot([self.sb(es, "P_rs%d" % i, [128, 1]) for i in range(2)])
            xns = Rot([self.sb(es, "P_xn%d" % i, [128, D], BF16) for i in range(2)])
            xTs = Rot([self.sb(es, "P_xT%d" % i, [128, 8, 128], BF16) for i in range(2)])
            prs = Rot([self.sb(es, "P_pr%d" % i, [128, IN_COLS]) for i in range(2)])
            rotF = Rot(self.psF)
            zt = self.sb(es, "P_z", [4, 1536])
            self.memset("pool", zt[:], 0.0, [zt])
            self.dma(self.proj.t[0:3, 3752:5288], zt[0:3, :], [zt], [self.ph("hist")])
            for b in range(NS):
                r0 = 4115 + 19 * b
                self.dma(self.proj.t[r0:r0 + 3, 3752:5288], self.igc.t[l, b], [], [self.ph("hist")])
            for tile in ["aux"] + list(range(SEQ // 128)):
                xt = xts.get()
                T = self.load_x(l, tile, xt)
                sq, r_ = ssq.get(), rs.get()
                self.act(junk[0:T, :], xt[0:T, :], AF.Square, [xt], [junk, sq], accum=sq[0:T, :])
                self.rstd(r_[0:T, :], sq[0:T, :], D, [sq], [r_])
                xn = xns.get()
                self.stt(xn[0:T, :], xt[0:T, :], r_[0:T, 0:1], nw[0:T, :], ALU.mult, ALU.mult, [xt, r_, nw], [xn])
                pb = self.rotB.get()
                for k in range(8):
                    self.tr(pb[0:128, k * 128:k * 128 + T], xn[0:T, k * 128:(k + 1) * 128], self.ident(T, True), [xn, conb], [pb])
                xT = xTs.get()
                pbv = pb[:, :].rearrange("p (k t) -> p k t", k=8)
                self.cp("dve", xT[:, :, 0:T], pbv[:, :, 0:T], [pb], [xT])
                Rot.rel(pb, xt, sq, r_, xn)
                pr = prs.get()
                ng = (IN_COLS + 511) // 512
                for g in range(ng):
                    c0 = g * 512
                    cw = min(512, IN_COLS - c0)
                    pf = rotF.get()
                    for k in range(8):
                        self.mm(pf[0:T, 0:cw], xT[:, k, 0:T], w_bf[:, k, c0:c0 + cw], k == 0, k == 7, [xT, w_h[g]], [pf])
                    if g % 2 == 0:
                        self.cp("act", pr[0:T, c0:c0 + cw], pf[0:T, 0:cw], [pf], [pr])
                    else:
                        self.cp("dve", pr[0:T, c0:c0 + cw], pf[0:T, 0:cw], [pf], [pr])
                    Rot.rel(pf)
                Rot.rel(xT)
                if tile == "aux":
                    self.dma(self.proj.t[3:19, :], pr[0:16, :], [pr], [self.ph("aux")])
                    for b in range(NS):
                        r0 = 4118 + 19 * b
                        self.dma(self.proj.t[r0:r0 + 16, :], pr[16 + 16 * b:32 + 16 * b, :], [pr], [self.ph("aux")])
                else:
                    r0 = 19 + tile * 128
                    self.dma(self.proj.t[r0:r0 + 128, :], pr[:, :], [pr], [self.ph(tile)])
                Rot.rel(pr)

    def phase_A(self, l):
        con, conb = self.con, self.conb
        with contextlib.ExitStack() as es:
            A = type("NS", (), {})()
            wuq = self.sb(es, "A_wuq", [128, 3, 768], BF16)
            self.dma(wuq[:], self.w_uq.t[l].rearrange("(k p) c -> p k c", p=128), [], [wuq], q="pool")
            wuk = self.sb(es, "A_wuk", [128, 2, 8, 96], BF16)
            self.memset("pool", wuk[:], 0.0, [wuk])
            for k in range(2):
                self.dma(wuk[:, k, :, 0:64], self.w_uk.t[l, k * 128:(k + 1) * 128, :].rearrange("p (h d) -> p h d", h=8), [], [wuk], q="pool")
            wuv = self.sb(es, "A_wuv", [128, 2, 512], BF16)
            self.dma(wuv[:], self.w_uv.t[l].rearrange("(k p) c -> p k c", p=128), [], [wuv], q="pool")
            qn_bc = self.sb(es, "A_qn", [128, 384])
            self.dma(qn_bc[:], self.q_norm.t[l].partition_broadcast(128), [], [qn_bc])
            kvn_bc = self.sb(es, "A_kvn", [128, 256])
            self.dma(kvn_bc[:], self.kv_norm.t[l].partition_broadcast(128), [], [kvn_bc])
            KT = self.sb(es, "A_KT", [96, 8, NCACHE + DS], BF16)
            V = self.sb(es, "A_V", [128, 34, 8, 65], BF16)
            self.memset("pool", V[:], 1.0, [V])
            slot_h = [H("slot%d" % i) for i in range(34)]
            for i in range(34):
                slot_h[i].w = V.h.w
            A.wuq, A.wuk, A.wuv, A.qn_bc, A.kvn_bc, A.KT, A.V, A.slot_h = wuq, wuk, wuv, qn_bc, kvn_bc, KT, V, slot_h
            A.junk = self.sb(es, "A_junk", [128, 384], BF16)
            A.ssq = Rot([self.sb(es, "A_ssq%d" % i, [128, 1]) for i in range(4)])
            A.rs = Rot([self.sb(es, "A_rs%d" % i, [128, 1]) for i in range(4)])
            A.cqn = Rot([self.sb(es, "A_cqn%d" % i, [128, 384], BF16) for i in range(2)])
            A.cqT = Rot([self.sb(es, "A_cqT%d" % i, [128, 3, 128], BF16) for i in range(2)])
            A.qsb = Rot([self.sb(es, "A_qsb%d" % i, [128, 8, 96], BF16) for i in range(2)])
            A.tA = self.sb(es, "A_tA", [128, 4, 16])
            A.tB = self.sb(es, "A_tB", [128, 4, 16])
            A.cn = Rot([self.sb(es, "A_cn%d" % i, [128, 256]) for i in range(2)])
            A.kr = Rot([self.sb(es, "A_kr%d" % i, [128, 32]) for i in range(2)])
            A.cbf = Rot([self.sb(es, "A_cbf%d" % i, [128, 288], BF16) for i in range(2)])
            A.QT = Rot([self.sb(es, "A_QT%d" % i, [96, 8, 128], BF16) for i in range(2)])
            cTs = [self.sb(es, "A_cT%d" % i, [128, 3, 128], BF16) for i in range(3)]
            for c_ in cTs:
                self.memset("pool", c_[:], 0.0, [c_])
            A.cT = Rot(cTs)
            A.PT = Rot([self.sb(es, "A_PT%d" % i, [128, 4, 128], BF16) for i in range(3)])
            A.rc = Rot([self.sb(es, "A_rc%d" % i, [128, 8]) for i in range(2)])
            A.oa = Rot([self.sb(es, "A_oa%d" % i, [128, 8, 64]) for i in range(2)])
            A.sz = Rot([self.sb(es, "A_sz%d" % i, [128, 512]) for i in range(2)])
            A.mx = Rot([self.sb(es, "A_mx%d" % i, [128, 512], BF16) for i in range(2)])
            A.pr = Rot([self.sb(es, "A_pr%d" % i, [128, 1184]) for i in range(2)])
            A.tab = Rot([self.sb(es, "A_tab%d" % i, [128, 32]) for i in range(2)])
            A.zseg = Rot([self.sb(es, "A_zs%d" % i, [16, 512]) for i in range(2)])
            cbs = [self.sb(es, "A_cbig%d" % i, [128, 4, 289], BF16) for i in range(3)]
            for c_ in cbs:
                self.memset("pool", c_[:, :, 256:257], 1.0, [c_])
            A.cbig = Rot(cbs)
            A.rotF = Rot(self.psF[0:4])
            A.O = (self.psF[4], self.psF[5])
            pr_aux = self.sb(es, "A_praux", [80, 1184])
            tab_aux = self.sb(es, "A_tabaux", [80, 32])
            QT_aux = self.sb(es, "A_QTaux", [96, 8, 80], BF16)
            cT_aux = self.sb(es, "A_cTaux", [128, 3, 80], BF16)
            self.memset("pool", cT_aux[:], 0.0, [cT_aux])
            self.dma(pr_aux[0:16, :], self.proj.t[3:19, 0:1184], [self.ph("aux")], [pr_aux])
            self.dma(tab_aux[0:16, :], self.rope.t[0:16, :], [], [tab_aux])
            for b in range(NS):
                r0 = 4118 + 19 * b
                self.dma(pr_aux[16 + 16 * b:32 + 16 * b, :], self.proj.t[r0:r0 + 16, 0:1184], [self.ph("aux")], [pr_aux])
                self.dma(tab_aux[16 + 16 * b:32 + 16 * b, :], self.rope.t[NCACHE:NCACHE + DS, :], [], [tab_aux])
            sl_h = H("s_lat_out")
            cn, kr = self.mla_proj(A, 80, pr_aux, tab_aux, QT_aux, cT_aux)
            wukT = self.sb(es, "A_wukT", [64, 8, 256], BF16)
            for k in range(2):
                pb = self.rotB.get()
                for h in range(8):
                    self.tr(pb[0:64, h * 128:(h + 1) * 128], wuk[:, k, h, 0:64], self.ident(128, True), [wuk, conb], [pb])
                self.cp("dve", wukT[:, :, k * 128:(k + 1) * 128], pb[0:64, :].rearrange("p (h r) -> p h r", h=8), [pb], [wukT])
                Rot.rel(pb)
            qr_pad = self.sb(es, "A_qrpad", [128, 8, 80], BF16)
            self.memset("pool", qr_pad[:], 0.0, [qr_pad])
            self.dma(qr_pad[0:32, :, :], QT_aux[64:96, :, :], [QT_aux], [qr_pad])
            self.dma(self.p_lat.t[l, 0:16, :], cn[0:16, :], [cn], [], is_out=True)
            self.dma(self.p_kr.t[l, 0:16, :], kr[0:16, :], [kr], [], is_out=True)
            for b in range(NS):
                self.dma(self.s_lat.t[l, b], cn[16 + 16 * b:32 + 16 * b, :], [cn], [sl_h], is_out=True)
                self.dma(self.s_kr.t[l, b], kr[16 + 16 * b:32 + 16 * b, :], [kr], [sl_h], is_out=True)
            Rot.rel(cn, kr)
            self.kv_build(A, cT_aux, 0, 16, 0)
            zs = A.zseg.get()
            self.dma(zs[:, :], self.proj.t[3:19, 672:1184], [self.ph("aux")], [zs])
            self.attend(A, 16, QT_aux, 0, [(0, 16)], None, zs, 0, self.mixa.t[0:16, :], self.mh("aux"))
            Rot.rel(zs)
            prepped = {}

            def prep(j):
                pr = A.pr.get()
                tab = A.tab.get()
                r0 = 19 + 128 * j
                self.dma(pr[:, :], self.proj.t[r0:r0 + 128, 0:1184], [self.ph(j)], [pr])
                self.dma(tab[:, :], self.rope.t[16 + 128 * j:16 + 128 * (j + 1), :], [], [tab])
                QT = A.QT.get()
                cT = A.cT.get()
                yield
                for _ in self.mla_proj_g(A, 128, pr, tab, QT, cT, prepped, j):
                    yield
                cn, kr = prepped[("ck", j)]
                self.dma(self.p_lat.t[l, 16 + 128 * j:16 + 128 * (j + 1), :], cn[:, :], [cn], [], is_out=True)
                self.dma(self.p_kr.t[l, 16 + 128 * j:16 + 128 * (j + 1), :], kr[:, :], [kr], [], is_out=True)
                Rot.rel(cn, kr, tab)
                yield
                for _ in self.kv_build_g(A, cT, 0, 128, 1 + j):
                    yield
                Rot.rel(cT)
                prepped[j] = (QT, pr)

            for _ in prep(0):
                pass
            NT = SEQ // 128
            for j in range(NT):
                QT, pr = prepped.pop(j)
                blocks = [(0, 16)] + [(1 + i, 128) for i in range(j + 1)]
                att = self.attend_g(A, 128, QT, 0, blocks, 1 + j, pr, 672, self.mixa.t[80 + 128 * j:80 + 128 * (j + 1), :], self.mh(j))
                self.interleave([att, prep(j + 1) if j + 1 < NT else None])
                Rot.rel(QT, pr)
            qlT = Rot([self.sb(es, "A_qlT%d" % i, [128, 2, 128], BF16) for i in range(2)])
            oln = Rot([self.sb(es, "A_oln%d" % i, [128, 256], BF16) for i in range(2)])
            olT = Rot([self.sb(es, "A_olT%d" % i, [128, 2, 128], BF16) for i in range(2)])
            rcl = Rot([self.sb(es, "A_rcl%d" % i, [128, 1]) for i in range(2)])
            Olat = A.O[0]
            for b in range(NS):
                off = 16 + 16 * b
                ql = qlT.get()
                for k in range(2):
                    pf = A.rotF.get()
                    for h in range(8):
                        self.mm(pf[0:128, h * 16:(h + 1) * 16], wukT[0:64, h, k * 128:(k + 1) * 128], QT_aux[0:64, h, off:off + 16],
                                True, True, [wukT, QT_aux], [pf])
                    self.cp("dve", ql[:, k, :], pf[:, 0:128], [pf], [ql])
                    Rot.rel(pf)
                qrv = qr_pad[:, :, off:off + 16]
                blist = []
                cb0 = A.cbig.get()
                self.dma(cb0[0:16, 0, 0:256], self.cl.t[l, b, 0:16, :], [], [cb0], q="pool")
                self.dma(cb0[0:16, 0, 257:289], self.ck.t[l, b, 0:16, :], [], [cb0], q="pool")
                self.dma(cb0[0:16, 1, 0:256], self.s_lat.t[l, b], [sl_h], [cb0], q="pool")
                self.dma(cb0[0:16, 1, 257:289], self.s_kr.t[l, b], [sl_h], [cb0], q="pool")
                groups = [[(cb0, 0, 16), (cb0, 1, 16)]]
                nblk = 34
                done = 0

                def run_group(grp, done):
                    n = grp[0][2]
                    g = len(grp)
                    pf = A.rotF.get()
                    for i, (cb, jj, _) in enumerate(grp):
                        cT = A.cT.get()
                        self.c_transpose289(A, cb[0:n, jj, :], n, cT, [cb])
                        o = pf[0:n, i * 128:(i + 1) * 128]
                        self.mm(o, cT[:, 0, 0:n], ql[:, 0, :], True, False, [cT, ql], [pf])
                        self.mm(o, cT[:, 1, 0:n], ql[:, 1, :], False, False, [cT, ql], [pf])
                        self.mm(o, cT[:, 2, 0:n], qrv, False, True, [cT, qr_pad], [pf])
                        Rot.rel(cT)
                    PT = A.PT.get()
                    self.act(PT[0:n, 0:g, :], pf[0:n, 0:g * 128].rearrange("p (g t) -> p g t", g=g), AF.Exp, [pf], [PT], scale=SM_SCALE)
                    Rot.rel(pf)
                    for i, (cb, jj, _) in enumerate(grp):
                        self.mm(Olat[0:128, 0:257], PT[0:n, i, :], cb[0:n, jj, 0:257], done + i == 0, done + i == nblk - 1, [PT, cb], [Olat])
                    Rot.rel(PT)
                    return done + g

                done = run_group(groups[0], done)
                Rot.rel(cb0)
                for g_ in range(8):
                    cb = A.cbig.get()
                    r0 = 16 + 512 * g_
                    self.dma(cb[:, :, 0:256], self.cl.t[l, b, r0:r0 + 512, :].rearrange("(j p) c -> p j c", p=128), [], [cb], q="pool")
                    self.dma(cb[:, :, 257:289], self.ck.t[l, b, r0:r0 + 512, :].rearrange("(j p) c -> p j c", p=128), [], [cb], q="pool")
                    done = run_group([(cb, jj, 128) for jj in range(4)], done)
                    Rot.rel(cb)
                rc1 = rcl.get()
                self.recip(rc1[:, 0:1], Olat[:, 256:257], [Olat], [rc1])
                on = oln.get()
                self.ts("dve", on[:, :], Olat[:, 0:256], rc1[:, 0:1], ALU.mult, [Olat, rc1], [on])
                pb = self.rotB.get()
                for k in range(2):
                    self.tr(pb[0:128, k * 128:(k + 1) * 128], on[:, k * 128:(k + 1) * 128], self.ident(128, True), [on, conb], [pb])
                oT = olT.get()
                self.cp("dve", oT[:, :, :], pb[:, 0:256].rearrange("p (k t) -> p k t", k=2), [pb], [oT])
                Rot.rel(pb, on, rc1)
                pf = A.rotF.get()
                for h in range(8):
                    for k in range(2):
                        self.mm(pf[0:16, h * 64:(h + 1) * 64], oT[:, k, h * 16:(h + 1) * 16], wuv[:, k, h * 64:(h + 1) * 64],
                                k == 0, k == 1, [oT, wuv], [pf])
                zs = A.zseg.get()
                r0 = 4118 + 19 * b
                self.dma(zs[:, :], self.proj.t[r0:r0 + 16, 672:1184], [self.ph("aux")], [zs])
                sz = A.sz.get()
                self.act(sz[0:16, :], zs[0:16, :], AF.Silu, [zs], [sz])
                mx = A.mx.get()
                self.tt("dve", mx[0:16, :], pf[0:16, 0:512], sz[0:16, :], ALU.mult, [pf, sz], [mx])
                Rot.rel(pf)
                self.dma(self.mixa.t[off:off + 16, :], mx[0:16, :], [mx], [self.mh("aux")])
                Rot.rel(zs, sz, mx, oT, ql)

    def c_transpose289(self, A, src, n, cT, r):
        conb = self.conb
        idb = self.ident(n, True)
        pb = self.rotB.get()
        self.tr(pb[0:128, 0:n], src[:, 0:128], idb, r + [conb], [pb])
        self.tr(pb[0:128, 128:128 + n], src[:, 128:256], idb, r + [conb], [pb])
        self.tr(pb[0:32, 256:256 + n], src[:, 257:289], idb, r + [conb], [pb])
        self.cp("dve", cT[:, 0:2, 0:n], pb[:, 0:256].rearrange("p (k t) -> p k t", k=2)[:, :, 0:n], [pb], [cT])
        self.cp("act", cT[0:32, 2, 0:n], pb[0:32, 256:256 + n], [pb], [cT])
        Rot.rel(pb)

    def rope_apply(self, A, T, o1, o2, x1, x2, cos, sin, tA, tB, r, w):
        self.tt("dve", tA, x1, cos, ALU.mult, r, [A.tA])
        self.tt("dve", tB, x2, sin, ALU.mult, r, [A.tB])
        self.tt("dve", o1, tA, tB, ALU.subtract, [A.tA, A.tB], w)
        self.tt("dve", tA, x2, cos, ALU.mult, r, [A.tA])
        self.tt("dve", tB, x1, sin, ALU.mult, r, [A.tB])
        self.tt("dve", o2, tA, tB, ALU.add, [A.tA, A.tB], w)

    def mla_proj(self, A, T, pr, tab, QT, cT):
        d = {}
        for _ in self.mla_proj_g(A, T, pr, tab, QT, cT, d, 0):
            pass
        return d[("ck", 0)]

    def mla_proj_g(self, A, T, pr, tab, QT, cT, outd, key):
        conb = self.conb
        idb = self.ident(T, True)
        sq, r_ = A.ssq.get(), A.rs.get()
        self.act(A.junk[0:T, 0:384], pr[0:T, 0:384], AF.Square, [pr], [A.junk, sq], accum=sq[0:T, :])
        self.rstd(r_[0:T, :], sq[0:T, :], 384, [sq], [r_])
        cqn = A.cqn.get()
        self.stt(cqn[0:T, :], pr[0:T, 0:384], r_[0:T, 0:1], A.qn_bc[0:T, :], ALU.mult, ALU.mult, [pr, r_, A.qn_bc], [cqn])
        pb = self.rotB.get()
        for k in range(3):
            self.tr(pb[0:128, k * 128:k * 128 + T], cqn[0:T, k * 128:(k + 1) * 128], idb, [cqn, conb], [pb])
        cqT = A.cqT.get()
        self.cp("dve", cqT[:, :, 0:T], pb[:, 0:384].rearrange("p (k t) -> p k t", k=3)[:, :, 0:T], [pb], [cqT])
        Rot.rel(pb, sq, r_, cqn)
        yield
        q_sb = A.qsb.get()
        cos4 = tab[0:T, 0:16].unsqueeze(1).broadcast_to([T, 4, 16])
        sin4 = tab[0:T, 16:32].unsqueeze(1).broadcast_to([T, 4, 16])
        for half in range(2):
            pf = A.rotF.get()
            for k in range(3):
                self.mm(pf[0:T, 0:384], cqT[:, k, 0:T], A.wuq[:, k, half * 384:(half + 1) * 384], k == 0, k == 2, [cqT, A.wuq], [pf])
            pv = pf[0:T, 0:384].rearrange("p (h d) -> p h d", h=4)
            hs = slice(half * 4, half * 4 + 4)
            self.cp("dve", q_sb[0:T, hs, 0:64], pv[:, :, 0:64], [pf], [q_sb])
            self.rope_apply(A, T, q_sb[0:T, hs, 64:80], q_sb[0:T, hs, 80:96], pv[:, :, 64:80], pv[:, :, 80:96],
                            cos4, sin4, A.tA[0:T, :, :], A.tB[0:T, :, :], [pf, tab], [q_sb])
            Rot.rel(pf)
            yield
        pb = self.rotB.get()
        for h in range(8):
            self.tr(pb[0:96, h * 128:h * 128 + T], q_sb[0:T, h, :], idb, [q_sb, conb], [pb])
        self.cp("dve", QT[0:96, :, 0:T], pb[0:96, :].rearrange("p (h t) -> p h t", h=8)[:, :, 0:T], [pb], [QT])
        Rot.rel(pb, q_sb, cqT)
        yield
        sq, r_ = A.ssq.get(), A.rs.get()
        self.act(A.junk[0:T, 0:256], pr[0:T, 384:640], AF.Square, [pr], [A.junk, sq], accum=sq[0:T, :])
        self.rstd(r_[0:T, :], sq[0:T, :], 256, [sq], [r_])
        cn = A.cn.get()
        self.stt(cn[0:T, :], pr[0:T, 384:640], r_[0:T, 0:1], A.kvn_bc[0:T, :], ALU.mult, ALU.mult, [pr, r_, A.kvn_bc], [cn])
        kr = A.kr.get()
        self.rope_apply(A, T, kr[0:T, 0:16], kr[0:T, 16:32], pr[0:T, 640:656], pr[0:T, 656:672],
                        tab[0:T, 0:16], tab[0:T, 16:32], A.tA[0:T, 0, :], A.tB[0:T, 0, :], [pr, tab], [kr])
        cbf = A.cbf.get()
        self.cp("pool", cbf[0:T, 0:256], cn[0:T, :], [cn], [cbf])
        self.cp("pool", cbf[0:T, 256:288], kr[0:T, :], [kr], [cbf])
        self.c_transpose(A, cbf[0:T, :], T, cT, [cbf])
        Rot.rel(sq, r_, cbf)
        outd[("ck", key)] = (cn, kr)
        yield

    def c_transpose(self, A, src, n, cT, r):
        conb = self.conb
        idb = self.ident(n, True)
        pb = self.rotB.get()
        self.tr(pb[0:128, 0:n], src[:, 0:128], idb, r + [conb], [pb])
        self.tr(pb[0:128, 128:128 + n], src[:, 128:256], idb, r + [conb], [pb])
        self.tr(pb[0:32, 256:256 + n], src[:, 256:288], idb, r + [conb], [pb])
        self.cp("dve", cT[:, 0:2, 0:n], pb[:, 0:256].rearrange("p (k t) -> p k t", k=2)[:, :, 0:n], [pb], [cT])
        self.cp("dve", cT[0:32, 2, 0:n], pb[0:32, 256:256 + n], [pb], [cT])
        Rot.rel(pb)

    def cache_block(self, A, cb, jj, n, slot):
        cT = A.cT.get()
        self.c_transpose(A, cb[0:n, jj, :], n, cT, [cb])
        self.kv_build(A, cT, 0, n, slot)
        Rot.rel(cT)

    def kv_build(self, A, cT, coff, n, slot):
        for _ in self.kv_build_g(A, cT, coff, n, slot):
            pass

    def kv_build_g(self, A, cT, coff, n, slot):
        conb = self.conb
        kcol = 0 if slot == 0 else 16 + 128 * (slot - 1)
        sh = A.slot_h[slot]
        for half in range(2):
            pf = A.rotF.get()
            for hh in range(4):
                h = half * 4 + hh
                o = pf[0:96, hh * 128:hh * 128 + n]
                self.mm(o, A.wuk[:, 0, h, :], cT[:, 0, coff:coff + n], True, False, [A.wuk, cT], [pf])
                self.mm(o, A.wuk[:, 1, h, :], cT[:, 1, coff:coff + n], False, False, [A.wuk, cT], [pf])
                self.mm(o, conb[:, 256:352], cT[:, 2, coff:coff + n], False, True, [conb, cT], [pf])
            src = pf[0:96, :].rearrange("p (h t) -> p h t", h=4)[:, :, 0:n]
            self.cp("dve", A.KT[0:96, half * 4:half * 4 + 4, kcol:kcol + n], src, [pf], [sh])
            Rot.rel(pf)
            yield
        pf = A.rotF.get()
        self.mm(pf[0:n, 0:512], cT[:, 0, coff:coff + n], A.wuv[:, 0, :], True, False, [cT, A.wuv], [pf])
        self.mm(pf[0:n, 0:512], cT[:, 1, coff:coff + n], A.wuv[:, 1, :], False, True, [cT, A.wuv], [pf])
        self.cp("dve", A.V[0:n, slot, :, 0:64], pf[0:n, 0:512].rearrange("p (h d) -> p h d", h=8), [pf], [sh])
        Rot.rel(pf)
        yield

    def attend(self, A, T, QT, qoff, blocks, diag_slot, zsrc, zoff, mix_out, mix_h):
        for _ in self.attend_g(A, T, QT, qoff, blocks, diag_slot, zsrc, zoff, mix_out, mix_h):
            pass

    def attend_g(self, A, T, QT, qoff, blocks, diag_slot, zsrc, zoff, mix_out, mix_h):
        groups = []
        for bl in blocks:
            if groups and groups[-1][0][1] == bl[1] and len(groups[-1]) < 4:
                groups[-1].append(bl)
            else:
                groups.append([bl])
        nblk = len(blocks)
        work = []
        for h in range(8):
            nb = 0
            for grp in groups:
                work.append((h, grp, nb))
                nb += len(grp)

        def finish(item):
            h, grp, nb0, pf = item
            Ob = A.O[h // 4]
            hh = h % 4
            n = grp[0][1]
            g = len(grp)
            PT = A.PT.get()
            self.act(PT[0:n, 0:g, 0:T], pf[0:n, 0:g * 128].rearrange("p (g t) -> p g t", g=g)[:, :, 0:T], AF.Exp,
                     [pf], [PT], scale=SM_SCALE)
            Rot.rel(pf)
            for i, (slot, _) in enumerate(grp):
                if diag_slot is not None and slot == diag_slot:
                    self.memset("pool", PT[64:128, i, 0:64], 0.0, [PT])
            for i, (slot, _) in enumerate(grp):
                self.mm(Ob[0:T, hh * 65:hh * 65 + 65], PT[0:n, i, 0:T], A.V[0:n, slot, h, :],
                        nb0 + i == 0, nb0 + i == nblk - 1, [PT, A.slot_h[slot]], [Ob])
            Rot.rel(PT)

        pend = None
        for (h, grp, nb0) in work:
            n = grp[0][1]
            pf = A.rotF.get()
            for i, (slot, _) in enumerate(grp):
                kcol = 0 if slot == 0 else 16 + 128 * (slot - 1)
                self.mm(pf[0:n, i * 128:i * 128 + T], A.KT[0:96, h, kcol:kcol + n], QT[0:96, h, qoff:qoff + T],
                        True, True, [A.slot_h[slot], QT], [pf])
            if pend is not None:
                finish(pend)
            pend = (h, grp, nb0, pf)
            yield
        finish(pend)
        yield
        rc = A.rc.get()
        oa = A.oa.get()
        for half in range(2):
            Ov = A.O[half][0:T, 0:260].rearrange("p (h d) -> p h d", h=4)
            rcv = rc[0:T, half * 4:half * 4 + 4].unsqueeze(2)
            self.recip(rcv, Ov[:, :, 64:65], [A.O[half]], [rc])
            self.tt("dve", oa[0:T, half * 4:half * 4 + 4, :], Ov[:, :, 0:64], rcv.broadcast_to([T, 4, 64]), ALU.mult,
                    [A.O[half], rc], [oa])
        sz = A.sz.get()
        self.act(sz[0:T, :], zsrc[0:T, zoff:zoff + 512], AF.Silu, [zsrc], [sz])
        mx = A.mx.get()
        self.tt("dve", mx[0:T, :], oa[0:T, :, :].rearrange("p h d -> p (h d)"), sz[0:T, :], ALU.mult, [oa, sz], [mx])
        self.dma(mix_out, mx[0:T, :], [mx], [mix_h])
        Rot.rel(rc, oa, sz, mx)

    def phase_R(self, l):
        con, conb = self.con, self.conb
        with contextlib.ExitStack() as es:
            R = type("NS", (), {})()
            wout = self.sb(es, "R_wout", [128, 12, D], BF16)
            wov = self.w_out.t[l].rearrange("(k p) c -> p k c", p=128)
            for k in range(0, 12, 4):
                self.dma(wout[:, k:k + 4, :], wov[:, k:k + 4, :], [], [wout], q="pool")
            cw = self.sb(es, "R_cw", [128, 4 * 1536])
            self.dma(cw[:], self.conv_w.t[l].partition_broadcast(128), [], [cw])
            mnorm = self.sb(es, "R_mnorm", [128, 512])
            self.dma(mnorm[:], self.m_norm.t[l].partition_broadcast(128), [], [mnorm])
            gnorm = self.sb(es, "R_gnorm", [128, 128])
            self.dma(gnorm[:], self.g_norm.t[l].partition_broadcast(128), [], [gnorm])
            gb = self.sb(es, "R_gb", [128, 8])
            self.dma(gb[:], self.gate_b.t[l].partition_broadcast(128), [], [gb])
            dtb = self.sb(es, "R_dtb", [128, 4])
            self.dma(dtb[:], self.dt_bias.t[l].partition_broadcast(128), [], [dtb])
            nea = self.sb(es, "R_nea", [128, 4])
            self.dma(nea[:], self.a_log.t[l].partition_broadcast(128), [], [nea])
            self.act(nea[:], nea[:], AF.Exp, [nea], [nea])
            self.ts("dve", nea[:], nea[:], -1.0, ALU.mult, [nea], [nea])
            R.wout, R.cw, R.mnorm, R.gnorm, R.gb, R.dtb, R.nea = wout, cw, mnorm, gnorm, gb, dtb, nea

            def pool(name, shape, dt, n):
                return Rot([self.sb(es, "R_%s%d" % (name, i), shape, dt) for i in range(n)])
            R.MP = pool("mp", [128, 2568], F32, 1)
            R.GA = pool("ga", [128, 520], F32, 2)
            R.F1536 = pool("f1536", [128, 1536], F32, 4)
            R.F1024 = pool("f1024", [128, 1024], F32, 1)
            R.F512 = pool("f512", [128, 512], F32, 22)
            R.F256 = R.F512
            R.B1024 = pool("b1024", [128, 1024], BF16, 2)
            R.B512 = pool("b512", [128, 512], BF16, 12)
            R.B256 = R.B512
            R.BT = pool("bt", [128, 1536], BF16, 4)
            R.SM = pool("sm", [128, 12], F32, 64)
            R.X = pool("x", [128, D], F32, 1)
            R.XN = pool("xn", [128, D], F32, 1)
            R.MIXT = pool("mixt", [128, 12, 128], BF16, 2)
            R.MA = pool("ma", [128, 512], BF16, 1)
            R.rotF = Rot(self.psF)

            def state(nm):
                st = type("NS", (), {})()
                st.CT = self.sb(es, "R_CT" + nm, [128, 4, 128])
                st.CTb = self.sb(es, "R_CTb" + nm, [128, 4, 128], BF16)
                st.nT = self.sb(es, "R_nT" + nm, [128, 4])
                st.nTb = self.sb(es, "R_nTb" + nm, [128, 4], BF16)
                st.mbc = self.sb(es, "R_mbc" + nm, [128, 4])
                st.S = self.sb(es, "R_S" + nm, [128, 4, 128])
                st.Sb = self.sb(es, "R_Sb" + nm, [128, 4, 128], BF16)
                return st
            stp = state("p")
            sts = state("s")
            R.tmpC = self.sb(es, "R_tmpC", [128, 4, 128])
            for t_ in (stp.CT, stp.CTb, stp.nT, stp.nTb, stp.mbc, stp.S, stp.Sb):
                self.memset("pool", t_[:], 0.0, [t_])

            items = []
            mixA = R.MIXT.get()
            items.append(dict(T=16, row0=3, st=stp, mixT=mixA, coff=0, before=None, after=None))
            for b in range(NS):
                r0 = 4118 + 19 * b

                def bef(b=b):
                    self.load_state(R, l, b, sts)

                def aft(b=b, r0=r0, last=(b == NS - 1)):
                    self.store_state(R, sts, self.s_C.t[l, b], self.s_n.t[l, b], self.s_m.t[l, b:b + 1, :], self.s_S.t[l, b])
                    self.dma(self.s_cv.t[l, b], self.proj.t[r0 + 13:r0 + 16, 3752:5288], [], [], is_out=True)
                    if last:
                        self.out_proj(R, l, "aux", 80, mixA)
                        Rot.rel(mixA)
                items.append(dict(T=16, row0=r0, st=sts, mixT=mixA, coff=16 + 16 * b, before=bef, after=aft))
            cur = {}
            NCH = 128 // RCHUNK
            for j in range(SEQ // 128):
                for c in range(NCH):
                    def bef(j=j, c=c):
                        if c == 0:
                            cur["mixT"] = R.MIXT.get()

                    def aft(j=j, c=c):
                        if c == NCH - 1:
                            self.out_proj(R, l, j, 128, cur["mixT"])
                            Rot.rel(cur["mixT"])
                    items.append(dict(T=RCHUNK, row0=19 + 128 * j + RCHUNK * c, st=stp, mixT=None, coff=RCHUNK * c, before=bef, after=aft))
            ctxs = [dict() for _ in items]
            for _ in self.gdn_pre_g(R, l, items[0]["T"], items[0]["row0"], ctxs[0]):
                pass
            for i, it in enumerate(items):
                if it["before"] is not None:
                    it["before"]()
                mixT = it["mixT"] if it["mixT"] is not None else cur["mixT"]
                gens = [self.mlstm_g(R, l, it["T"], it["row0"], it["st"], mixT, it["coff"]),
                        self.gdn_chain_g(R, l, it["T"], it["st"], mixT, it["coff"], ctxs[i])]
                if i + 1 < len(items):
                    nx = items[i + 1]
                    gens.append(self.gdn_pre_g(R, l, nx["T"], nx["row0"], ctxs[i + 1]))
                self.interleave(gens)
                if it["after"] is not None:
                    it["after"]()
            self.store_state(R, stp, self.p_C.t[l], self.p_n.t[l], self.p_m.t[l:l + 1, :], self.p_S.t[l])
            self.dma(self.p_cv.t[l], self.proj.t[19 + SEQ - 3:19 + SEQ, 3752:5288], [], [], is_out=True)

    def load_state(self, R, l, b, st):
        self.dma(R.tmpC[:], self.imC.t[l, b].rearrange("h e d -> e h d"), [], [R.tmpC])
        pf = R.rotF.get()
        for h in range(4):
            self.tr(pf[:, h * 128:(h + 1) * 128], R.tmpC[:, h, :], self.ident(128), [R.tmpC, self.con], [pf])
        self.cp("dve", st.CT[:], pf[:, :].rearrange("p (h e) -> p h e", h=4), [pf], [st.CT])
        Rot.rel(pf)
        self.cp("act", st.CTb[:], st.CT[:], [st.CT], [st.CTb])
        self.dma(st.nT[:], self.imn.t[l, b].rearrange("h d -> d h"), [], [st.nT], nonc=True)
        self.cp("act", st.nTb[:], st.nT[:], [st.nT], [st.nTb])
        self.dma(st.mbc[:], self.imm.t[l, b].partition_broadcast(128), [], [st.mbc])
        self.dma(st.S[:], self.igS.t[l, b].rearrange("h k v -> k h v"), [], [st.S])
        self.cp("act", st.Sb[:], st.S[:], [st.S], [st.Sb])

    def store_state(self, R, st, oC, on, om, oS):
        pf = R.rotF.get()
        for h in range(4):
            self.tr(pf[:, h * 128:(h + 1) * 128], st.CT[:, h, :], self.ident(128), [st.CT, self.con], [pf])
        self.cp("dve", R.tmpC[:], pf[:, :].rearrange("p (h e) -> p h e", h=4), [pf], [R.tmpC])
        Rot.rel(pf)
        self.dma(oC.rearrange("h e d -> e h d"), R.tmpC[:], [R.tmpC], [], is_out=True)
        self.dma(on.rearrange("h d -> d h"), st.nT[:], [st.nT], [], nonc=True, is_out=True)
        self.dma(om, st.mbc[0:1, :], [st.mbc], [], is_out=True)
        self.dma(oS.rearrange("h k v -> k h v"), st.S[:], [st.S], [], is_out=True)

    def out_proj(self, R, l, tile, T, mixT):
        conb = self.conb
        ma = R.MA.get()
        if tile == "aux":
            self.dma(ma[0:T, :], self.mixa.t[0:80, :], [], [ma])
        else:
            self.dma(ma[0:T, :], self.mixa.t[80 + 128 * tile:80 + 128 * (tile + 1), :], [], [ma])
        pb = self.rotB.get()
        for k in range(4):
            self.tr(pb[0:128, k * 128:k * 128 + T], ma[0:T, k * 128:(k + 1) * 128], self.ident(T, True), [ma, conb], [pb])
        self.cp("act", mixT[:, 0:4, 0:T], pb[:, 0:512].rearrange("p (k t) -> p k t", k=4)[:, :, 0:T], [pb], [mixT])
        Rot.rel(pb, ma)
        xt = R.X.get()
        self.load_x(l, tile, xt)
        xn = R.XN.get()
        for half in range(2):
            pf = R.rotF.get()
            for k in range(12):
                self.mm(pf[0:T, 0:512], mixT[:, k, 0:T], R.wout[:, k, half * 512:(half + 1) * 512], k == 0, k == 11, [mixT, R.wout], [pf])
            self.tt("dve", xn[0:T, half * 512:(half + 1) * 512], pf[0:T, 0:512], xt[0:T, half * 512:(half + 1) * 512], ALU.add, [pf, xt], [xn])
            Rot.rel(pf)
        Rot.rel(xt)
        if l < DEPTH - 1:
            if tile == "aux":
                self.dma(self.xscr.t[0:80, :], xn[0:80, :], [xn], [self.xh("aux")])
            else:
                self.dma(self.xscr.t[80 + 128 * tile:80 + 128 * (tile + 1), :], xn[:, :], [xn], [self.xh(tile)])
        else:
            sq = R.SM.get()
            yo = R.X.get()
            self.act(yo[0:T, :], xn[0:T, :], AF.Square, [xn], [yo, sq], accum=sq[0:T, 0:1])
            self.rstd(sq[0:T, 0:1], sq[0:T, 0:1], D, [sq], [sq])
            self.stt(yo[0:T, :], xn[0:T, :], sq[0:T, 0:1], self.fin_bc[0:T, :], ALU.mult, ALU.mult, [xn, sq, self.fin_bc], [yo])
            if tile == "aux":
                self.dma(self.y_s.t[:, :], yo[16:80, :], [yo], [], is_out=True)
            else:
                self.dma(self.y_p.t[128 * tile:128 * (tile + 1), :], yo[:, :], [yo], [], is_out=True)
            Rot.rel(sq, yo)
        Rot.rel(xn)

    @staticmethod
    def v3(ap, a):
        return ap.rearrange("p (a b) -> p a b", a=a)

    @staticmethod
    def bch(ap, T, n):
        return ap.unsqueeze(2).broadcast_to([T, 4, n])

    @staticmethod
    def bcm(ap, T, n):
        return ap.unsqueeze(1).broadcast_to([T, 4, n])

    def mlstm_chunk(self, R, l, T, row0, st, mixT, coff):
        for _ in self.mlstm_g(R, l, T, row0, st, mixT, coff):
            pass

    def mlstm_g(self, R, l, T, row0, st, mixT, coff):
        con, conb = self.con, self.conb
        L = []

        def g(p):
            b = p.get()
            L.append(b)
            return b
        P = slice(0, T)
        T4 = 4 * T
        tri = con[0:T, C_TRI:C_TRI + T]
        SLm = con[0:T, C_SL:C_SL + T]
        onesTT = con[0:T, C_ONE:C_ONE + T]
        idT = con[0:T, C_ID:C_ID + T]
        NEG = con[0:T, C_NEG:C_NEG + T]
        cs_ = C_S128 if T == 128 else C_S16
        sel = con[0:T, cs_:cs_ + 128]
        idb = self.ident(T, True)
        v3, bch, bcm = self.v3, self.bch, self.bcm
        KS = 128.0 ** -0.5
        mp = g(R.MP)
        self.dma(mp[P, :], self.proj.t[row0:row0 + T, 1184:3752], [], [mp])
        if K.OPT_K32:
            k32 = g(R.F512)
            self.dma(k32[P, :], self.proj.t[row0:row0 + T, 1696:2208], [], [k32])
        qkb = g(R.B1024)
        self.cp("act", qkb[P, 0:512], mp[P, 0:512], [mp], [qkb])
        self.act(qkb[P, 512:1024], mp[P, 512:1024], AF.Copy, [mp], [qkb], scale=KS)
        pb = self.rotB.get()
        for i in range(8):
            self.tr(pb[0:128, i * T:(i + 1) * T], qkb[P, i * 128:(i + 1) * 128], idb, [qkb, conb], [pb])
        qkT = g(R.BT)
        self.cp("act" if K.OPT_EVACT else "dve", qkT[:, 0:8 * T], pb[:, 0:8 * T], [pb], [qkT])
        Rot.rel(pb)
        vb = g(R.B512)
        self.cp("pool", vb[P, :], mp[P, 1024:1536], [mp], [vb])
        yield
        g8 = g(R.SM)
        self.tt("dve", g8[P, 0:8], mp[P, 1536:1544], R.gb[P, 0:8], ALU.add, [mp, R.gb], [g8])
        ipre, xf = g8[P, 0:4], g8[P, 4:8]
        s1 = g(R.SM)
        self.stt(s1[P, 0:4], xf, -1.0, xf, ALU.mult, ALU.max, [g8], [s1])
        self.act(s1[P, 0:4], s1[P, 0:4], AF.Exp, [s1], [s1], scale=-1.0)
        self.act(s1[P, 0:4], s1[P, 0:4], AF.Ln, [s1], [s1], bias=1.0)
        lf = g(R.SM)
        self.ts("dve", lf[P, 0:4], xf, 0.0, ALU.min, [g8], [lf])
        self.tt("dve", lf[P, 0:4], lf[P, 0:4], s1[P, 0:4], ALU.subtract, [lf, s1], [lf])
        yield
        R1 = g(R.F256)
        self.tt("pool", v3(R1[P, 0:T4], 4), bcm(SLm, T, T), bch(lf[P, 0:4], T, T), ALU.mult, [con, lf], [R1])
        R2 = g(R.F256)
        self.tt("pool", v3(R2[P, 0:T4], 4), bcm(idT, T, T), bch(ipre, T, T), ALU.mult, [con, g8], [R2])
        pf = R.rotF.get()
        self.mm(pf[0:T, 0:T4], tri, R1[P, 0:T4], True, False, [con, R1], [pf])
        self.mm(pf[0:T, 0:T4], onesTT, R2[P, 0:T4], False, True, [con, R2], [pf])
        pfb = R.rotF.get()
        self.mm(pfb[0:T, 0:4], tri, lf[P, 0:4], True, True, [con, lf], [pfb])
        Dm = g(R.F256)
        self.tt("dve", v3(Dm[P, 0:T4], 4), v3(pf[0:T, 0:T4], 4), bcm(NEG, T, T), ALU.add, [pf, con], [Dm])
        MIB = g(R.SM)
        self.cp("act", MIB[P, 8:12], pfb[0:T, 0:4], [pfb], [MIB])
        Rot.rel(pf, pfb)
        rmx = g(R.SM)
        self.red(rmx[P, 0:4], v3(Dm[P, 0:T4], 4), ALU.max, [Dm], [rmx])
        yield
        self.tt("dve", MIB[P, 4:8], MIB[P, 8:12], st.mbc[P, 0:4], ALU.add, [MIB, st.mbc], [MIB])
        self.tt("dve", MIB[P, 0:4], MIB[P, 4:8], rmx[P, 0:4], ALU.max, [MIB, rmx], [MIB])
        E = g(R.F256)
        self.tt("dve", v3(E[P, 0:T4], 4), v3(Dm[P, 0:T4], 4), bch(MIB[P, 0:4], T, T), ALU.subtract, [Dm, MIB], [E])
        self.act(E[P, 0:T4], E[P, 0:T4], AF.Exp, [E], [E])
        wi = g(R.SM)
        self.tt("dve", wi[P, 0:4], MIB[P, 4:8], MIB[P, 0:4], ALU.subtract, [MIB], [wi])
        self.act(wi[P, 0:4], wi[P, 0:4], AF.Exp, [wi], [wi])
        emt = g(R.SM)
        self.act(emt[P, 0:4], MIB[P, 0:4], AF.Exp, [MIB], [emt], scale=-1.0)
        yield
        pf = R.rotF.get()
        for h in range(4):
            self.mm(pf[0:T, h * T:(h + 1) * T], qkT[:, h * T:(h + 1) * T], qkT[:, (4 + h) * T:(5 + h) * T], True, True, [qkT], [pf])
        qkE = g(R.F256)
        self.tt("dve", qkE[P, 0:T4], pf[0:T, 0:T4], E[P, 0:T4], ALU.mult, [pf, E], [qkE])
        Rot.rel(pf)
        den1 = g(R.SM)
        self.red(den1[P, 0:4], v3(qkE[P, 0:T4], 4), ALU.add, [qkE], [den1])
        yield
        pf = R.rotF.get()
        for h in range(4):
            self.tr(pf[0:T, h * T:(h + 1) * T], qkE[P, h * T:(h + 1) * T], idT, [qkE, con], [pf])
        qkET = g(R.B256)
        self.cp("act", qkET[P, 0:T4], pf[0:T, 0:T4], [pf], [qkET])
        Rot.rel(pf)
        yield
        pf1 = R.rotF.get()
        for h in range(4):
            self.mm(pf1[0:T, h * 128:(h + 1) * 128], qkET[P, h * T:(h + 1) * T], vb[P, h * 128:(h + 1) * 128], True, True, [qkET, vb], [pf1])
        num1 = g(R.F512)
        self.cp("act", num1[P, :], pf1[0:T, :], [pf1], [num1])
        Rot.rel(pf1)
        yield
        pf2 = R.rotF.get()
        for h in range(4):
            self.mm(pf2[0:T, h * 128:(h + 1) * 128], qkT[:, h * T:(h + 1) * T], st.CTb[:, h, :], True, True, [qkT, st.CTb], [pf2])
        pf3 = R.rotF.get()
        for h in range(4):
            self.mm(pf3[0:T, h:h + 1], qkT[:, h * T:(h + 1) * T], st.nTb[:, h:h + 1], True, True, [qkT, st.nTb], [pf3])
        num = g(R.F512)
        self.tt("dve", v3(num[P, :], 4), v3(pf2[0:T, :], 4), bch(wi[P, 0:4], T, 128), ALU.mult, [pf2, wi], [num])
        Rot.rel(pf2)
        self.tt("dve", num[P, :], num[P, :], num1[P, :], ALU.add, [num, num1], [num])
        den = g(R.SM)
        self.tt("dve", den[P, 0:4], pf3[0:T, 0:4], wi[P, 0:4], ALU.mult, [pf3, wi], [den])
        Rot.rel(pf3)
        self.tt("dve", den[P, 0:4], den[P, 0:4], den1[P, 0:4], ALU.add, [den, den1], [den])
        self.stt(den[P, 0:4], den[P, 0:4], -1.0, den[P, 0:4], ALU.mult, ALU.max, [den], [den])
        self.tt("dve", den[P, 0:4], den[P, 0:4], emt[P, 0:4], ALU.max, [den, emt], [den])
        self.recip(den[P, 0:4], den[P, 0:4], [den], [den])
        self.tt("dve", v3(num[P, :], 4), v3(num[P, :], 4), bch(den[P, 0:4], T, 128), ALU.mult, [num, den], [num])
        yield
        pf = R.rotF.get()
        self.mm(pf[0:128, 0:12], sel, MIB[P, 0:12], True, True, [con, MIB], [pf])
        LB = g(R.SM)
        self.cp("act", LB[:, 0:12], pf[:, 0:12], [pf], [LB])
        Rot.rel(pf)
        yield
        self.cp("dve", st.mbc[:, 0:4], LB[:, 0:4], [LB], [st.mbc])
        gs = g(R.SM)
        self.tt("dve", gs[:, 0:4], LB[:, 4:8], LB[:, 0:4], ALU.subtract, [LB], [gs])
        self.act(gs[:, 0:4], gs[:, 0:4], AF.Exp, [gs], [gs])
        gt = g(R.SM)
        self.tt("dve", gt[P, 0:4], LB[P, 8:12], MIB[P, 8:12], ALU.subtract, [LB, MIB], [gt])
        self.tt("dve", gt[P, 0:4], gt[P, 0:4], ipre, ALU.add, [gt, g8], [gt])
        self.tt("dve", gt[P, 0:4], gt[P, 0:4], LB[P, 0:4], ALU.subtract, [gt, LB], [gt])
        self.act(gt[P, 0:4], gt[P, 0:4], AF.Exp, [gt], [gt])
        yield
        kg = g(R.B512)
        if K.OPT_K32:
            self.stt(v3(kg[P, :], 4), v3(k32[P, :], 4), KS, bch(gt[P, 0:4], T, 128), ALU.mult, ALU.mult, [k32, gt], [kg])
        else:
            self.stt(v3(kg[P, :], 4), v3(mp[P, 512:1024], 4), KS, bch(gt[P, 0:4], T, 128), ALU.mult, ALU.mult, [mp, gt], [kg])
        pfC = R.rotF.get()
        for h in range(4):
            self.mm(pfC[:, h * 128:(h + 1) * 128], kg[P, h * 128:(h + 1) * 128], vb[P, h * 128:(h + 1) * 128], True, True, [kg, vb], [pfC])
        pfn = R.rotF.get()
        for h in range(4):
            self.mm(pfn[:, h:h + 1], kg[P, h * 128:(h + 1) * 128], conb[0:T, 128:129], True, True, [kg, conb], [pfn])
        self.tt("dve", st.CT[:], st.CT[:], bch(gs[:, 0:4], 128, 128), ALU.mult, [st.CT, gs], [st.CT])
        self.tt("dve", st.CT[:], st.CT[:], v3(pfC[:, :], 4), ALU.add, [st.CT, pfC], [st.CT])
        Rot.rel(pfC)
        self.cp("act", st.CTb[:], st.CT[:], [st.CT], [st.CTb])
        self.tt("dve", st.nT[:], st.nT[:], gs[:, 0:4], ALU.mult, [st.nT, gs], [st.nT])
        self.tt("dve", st.nT[:], st.nT[:], pfn[:, 0:4], ALU.add, [st.nT, pfn], [st.nT])
        Rot.rel(pfn)
        self.cp("act", st.nTb[:], st.nT[:], [st.nT], [st.nTb])
        yield
        sig = g(R.F512)
        if K.OPT_SIGEXP:
            self.act(sig[P, :], mp[P, 1544:2056], AF.Exp, [mp], [sig], scale=-1.0)
            self.ts("pool", sig[P, :], sig[P, :], 1.0, ALU.add, [sig], [sig])
            self.recip(sig[P, :], sig[P, :], [sig], [sig])
        else:
            self.act(sig[P, :], mp[P, 1544:2056], AF.Sigmoid, [mp], [sig])
        szm = g(R.F512)
        self.act(szm[P, :], mp[P, 2056:2568], AF.Silu, [mp], [szm])
        self.tt("pool", szm[P, :], szm[P, :], R.mnorm[P, :], ALU.mult, [szm, R.mnorm], [szm])
        self.tt("dve", num[P, :], num[P, :], sig[P, :], ALU.mult, [num, sig], [num])
        self.tt("pool", sig[P, :], num[P, :], num[P, :], ALU.mult, [num], [sig])
        s4 = g(R.SM)
        self.red(s4[P, 0:4], v3(sig[P, :], 4), ALU.add, [sig], [s4])
        self.rstd(s4[P, 0:4], s4[P, 0:4], 128, [s4], [s4])
        yield
        self.tt("dve", v3(num[P, :], 4), v3(num[P, :], 4), bch(s4[P, 0:4], T, 128), ALU.mult, [num, s4], [num])
        omb = g(R.B512)
        self.tt("dve", omb[P, :], num[P, :], szm[P, :], ALU.mult, [num, szm], [omb])
        pb = self.rotB.get()
        for h in range(4):
            self.tr(pb[0:128, h * T:(h + 1) * T], omb[P, h * 128:(h + 1) * 128], idb, [omb, conb], [pb])
        self.cp("act", mixT[:, 4:8, coff:coff + T], v3(pb[:, 0:T4], 4), [pb], [mixT])
        Rot.rel(pb)
        Rot.rel(*L)
        yield

    def gdn_chunk(self, R, l, T, row0, st, mixT, coff):
        ctx = {}
        for _ in self.gdn_pre_g(R, l, T, row0, ctx):
            pass
        for _ in self.gdn_chain_g(R, l, T, st, mixT, coff, ctx):
            pass

    def gdn_pre_g(self, R, l, T, row0, ctx):
        con, conb = self.con, self.conb
        L = []

        def g(p):
            b = p.get()
            L.append(b)
            return b
        P = slice(0, T)
        T4 = 4 * T
        nst = {16: 3, 64: 5, 128: 6}[T]
        tri = con[0:T, C_TRI:C_TRI + T]
        SLm = con[0:T, C_SL:C_SL + T]
        INCL = con[0:T, C_INCL:C_INCL + T]
        idT = con[0:T, C_ID:C_ID + T]
        ones128 = con[0:T, C_ONE:C_ONE + 128]
        idb = self.ident(T, True)
        v3, bch, bcm = self.v3, self.bch, self.bcm
        ga = g(R.GA)
        self.dma(ga[P, :], self.proj.t[row0:row0 + T, 5288:5808], [], [ga])
        acc = None
        for j in range(4):
            gp = g(R.F1536)
            self.dma(gp[P, :], self.proj.t[row0 - 3 + j:row0 - 3 + j + T, 3752:5288], [], [gp])
            self.tt("pool", gp[P, :], gp[P, :], R.cw[P, j * 1536:(j + 1) * 1536], ALU.mult, [gp, R.cw], [gp])
            if acc is None:
                acc = gp
            else:
                self.tt("pool" if j == 1 else "dve", acc[P, :], acc[P, :], gp[P, :], ALU.add, [acc, gp], [acc])
        cs = acc
        self.act(cs[P, :], cs[P, :], AF.Silu, [cs], [cs])
        yield
        qkn = g(R.F1024)
        self.tt("pool", qkn[P, :], cs[P, 0:1024], cs[P, 0:1024], ALU.mult, [cs], [qkn])
        s8 = g(R.SM)
        self.red(s8[P, 0:8], v3(qkn[P, :], 8), ALU.add, [qkn], [s8])
        self.act(s8[P, 0:8], s8[P, 0:8], AF.Ln, [s8], [s8], bias=EPS)
        self.act(s8[P, 0:8], s8[P, 0:8], AF.Exp, [s8], [s8], scale=-0.5)
        self.ts("dve", s8[P, 0:4], s8[P, 0:4], 128.0 ** -0.5, ALU.mult, [s8], [s8])
        self.tt("dve", v3(qkn[P, :], 8), v3(cs[P, 0:1024], 8), s8[P, 0:8].unsqueeze(2).broadcast_to([T, 8, 128]), ALU.mult, [cs, s8], [qkn])
        yield
        y = g(R.SM)
        self.tt("dve", y[P, 0:4], ga[P, 0:4], R.dtb[P, 0:4], ALU.add, [ga, R.dtb], [y])
        s1 = g(R.SM)
        self.stt(s1[P, 0:4], y[P, 0:4], -1.0, y[P, 0:4], ALU.mult, ALU.max, [y], [s1])
        self.act(s1[P, 0:4], s1[P, 0:4], AF.Exp, [s1], [s1], scale=-1.0)
        self.act(s1[P, 0:4], s1[P, 0:4], AF.Ln, [s1], [s1], bias=1.0)
        gg = g(R.SM)
        self.ts("dve", gg[P, 0:4], y[P, 0:4], 0.0, ALU.max, [y], [gg])
        self.tt("dve", gg[P, 0:4], gg[P, 0:4], s1[P, 0:4], ALU.add, [gg, s1], [gg])
        self.tt("dve", gg[P, 0:4], gg[P, 0:4], R.nea[P, 0:4], ALU.mult, [gg, R.nea], [gg])
        bt = g(R.SM)
        if K.OPT_SIGEXP:
            self.act(bt[P, 0:4], ga[P, 4:8], AF.Exp, [ga], [bt], scale=-1.0)
            self.ts("dve", bt[P, 0:4], bt[P, 0:4], 1.0, ALU.add, [bt], [bt])
            self.recip(bt[P, 0:4], bt[P, 0:4], [bt], [bt])
        else:
            self.act(bt[P, 0:4], ga[P, 4:8], AF.Sigmoid, [ga], [bt])
        self.ts("dve", bt[P, 4:8], bt[P, 0:4], -1.0, ALU.mult, [bt], [bt])
        yield
        Rg = g(R.F256)
        self.tt("pool", v3(Rg[P, 0:T4], 4), bcm(SLm, T, T), bch(gg[P, 0:4], T, T), ALU.mult, [con, gg], [Rg])
        pf = R.rotF.get()
        self.mm(pf[0:T, 0:T4], tri, Rg[P, 0:T4], True, True, [con, Rg], [pf])
        pf128 = R.rotF.get()
        self.mm(pf128[0:128, 0:4], ones128, gg[P, 0:4], True, True, [con, gg], [pf128])
        self.mm(pf128[0:T, 8:12], tri, gg[P, 0:4], True, True, [con, gg], [pf128])
        gam = g(R.F256)
        self.act(gam[P, 0:T4], pf[0:T, 0:T4], AF.Exp, [pf], [gam])
        Gc = g(R.SM)
        self.cp("act", Gc[P, 0:4], pf128[0:T, 8:12], [pf128], [Gc])
        Rot.rel(pf)
        GL = g(R.SM)
        self.cp("act", GL[:, 0:4], pf128[:, 0:4], [pf128], [GL])
        Rot.rel(pf128)
        yield
        gam_s = g(R.F256)
        self.tt("dve", v3(gam_s[P, 0:T4], 4), v3(gam[P, 0:T4], 4), bcm(SLm, T, T), ALU.mult, [gam, con], [gam_s])
        self.tt("dve", v3(gam[P, 0:T4], 4), v3(gam[P, 0:T4], 4), bcm(INCL, T, T), ALU.mult, [gam, con], [gam])
        eG = g(R.SM)
        self.act(eG[P, 0:4], Gc[P, 0:4], AF.Exp, [Gc], [eG])
        eGl = g(R.SM)
        self.act(eGl[:, 0:4], GL[:, 0:4], AF.Exp, [GL], [eGl])
        self.tt("dve", eG[P, 4:8], GL[P, 0:4], Gc[P, 0:4], ALU.subtract, [GL, Gc], [eG])
        self.act(eG[P, 4:8], eG[P, 4:8], AF.Exp, [eG], [eG])
        self.tt("dve", eG[P, 8:12], bt[P, 0:4], eG[P, 0:4], ALU.mult, [bt, eG], [eG])
        yield
        qkb = g(R.B1024)
        self.cp("act", qkb[P, :], qkn[P, :], [qkn], [qkb])
        qgb = g(R.B512)
        self.tt("pool", v3(qgb[P, :], 4), v3(qkn[P, 0:512], 4), bch(eG[P, 0:4], T, 128), ALU.mult, [qkn, eG], [qgb])
        bk = g(R.F512)
        self.tt("dve", v3(bk[P, :], 4), v3(qkn[P, 512:1024], 4), bch(eG[P, 8:12], T, 128), ALU.mult, [qkn, eG], [bk])
        bv = g(R.F512)
        self.tt("pool", v3(bv[P, :], 4), v3(cs[P, 1024:1536], 4), bch(bt[P, 0:4], T, 128), ALU.mult, [cs, bt], [bv])
        kd = g(R.B512)
        self.tt("pool", v3(kd[P, :], 4), v3(qkn[P, 512:1024], 4), bch(eG[P, 4:8], T, 128), ALU.mult, [qkn, eG], [kd])
        yield
        qkT = g(R.BT)
        pb = self.rotB.get()
        for i in range(8):
            self.tr(pb[0:128, i * T:(i + 1) * T], qkb[P, i * 128:(i + 1) * 128], idb, [qkb, conb], [pb])
        self.cp("act" if K.OPT_EVACT else "dve", qkT[:, 0:8 * T], pb[:, 0:8 * T], [pb], [qkT])
        Rot.rel(pb)
        pb = self.rotB.get()
        for h in range(4):
            self.tr(pb[0:128, h * T:(h + 1) * T], qgb[P, h * 128:(h + 1) * 128], idb, [qgb, conb], [pb])
        self.cp("act" if K.OPT_EVACT else "dve", qkT[:, 8 * T:12 * T], pb[:, 0:4 * T], [pb], [qkT])
        Rot.rel(pb)
        yield
        qT = lambda h: qkT[:, h * T:(h + 1) * T]
        kT = lambda h: qkT[:, (4 + h) * T:(5 + h) * T]
        qgT = lambda h: qkT[:, (8 + h) * T:(9 + h) * T]
        pf = R.rotF.get()
        for h in range(4):
            self.mm(pf[0:T, h * T:(h + 1) * T], kT(h), kT(h), True, True, [qkT], [pf])
        pfq = R.rotF.get()
        for h in range(4):
            self.mm(pfq[0:T, h * T:(h + 1) * T], qT(h), kT(h), True, True, [qkT], [pfq])
        N = g(R.F256)
        self.tt("dve", N[P, 0:T4], pf[0:T, 0:T4], gam_s[P, 0:T4], ALU.mult, [pf, gam_s], [N])
        self.tt("dve", v3(N[P, 0:T4], 4), v3(N[P, 0:T4], 4), bch(bt[P, 4:8], T, T), ALU.mult, [N, bt], [N])
        QG = g(R.F256)
        self.tt("dve", QG[P, 0:T4], pfq[0:T, 0:T4], gam[P, 0:T4], ALU.mult, [pfq, gam], [QG])
        Rot.rel(pf, pfq)
        yield
        pf = R.rotF.get()
        for h in range(4):
            self.tr(pf[0:T, h * T:(h + 1) * T], N[P, h * T:(h + 1) * T], idT, [N, con], [pf])
        pfq = R.rotF.get()
        for h in range(4):
            self.tr(pfq[0:T, h * T:(h + 1) * T], QG[P, h * T:(h + 1) * T], idT, [QG, con], [pfq])
        Q = g(R.F256)
        self.cp("act", Q[P, 0:T4], pf[0:T, 0:T4], [pf], [Q])
        QGT = g(R.B256)
        self.cp("act" if K.OPT_EVACT else "dve", QGT[P, 0:T4], pfq[0:T, 0:T4], [pfq], [QGT])
        Rot.rel(pf, pfq)
        Y = g(R.F256)
        self.tt("dve", v3(Y[P, 0:T4], 4), v3(Q[P, 0:T4], 4), bcm(idT, T, T), ALU.add, [Q, con], [Y])
        yield
        Pm = N
        hs = lambda b_, h: b_[P, h * T:(h + 1) * T]
        for s in range(nst):
            last = (s == nst - 1)
            pfP = R.rotF.get()
            for h in range(4):
                self.mm(pfP[0:T, h * T:(h + 1) * T], hs(Q, h), hs(Pm, h), True, True, [Q, Pm], [pfP])
            if not last:
                pfQ = R.rotF.get()
                for h in range(4):
                    self.mm(pfQ[0:T, h * T:(h + 1) * T], hs(Pm, h), hs(Q, h), True, True, [Q, Pm], [pfQ])
            Pn = g(R.F256)
            self.cp("act", Pn[P, 0:T4], pfP[0:T, 0:T4], [pfP], [Pn])
            Rot.rel(pfP)
            if not last:
                Qn = g(R.F256)
                self.cp("act" if K.OPT_EVACT else "dve", Qn[P, 0:T4], pfQ[0:T, 0:T4], [pfQ], [Qn])
                Rot.rel(pfQ)
            pfY = R.rotF.get()
            for h in range(4):
                self.mm(pfY[0:T, h * T:(h + 1) * T], hs(Pn, h), hs(Y, h), True, True, [Pn, Y], [pfY])
            self.tt("dve", Y[P, 0:T4], Y[P, 0:T4], pfY[0:T, 0:T4], ALU.add, [Y, pfY], [Y])
            Rot.rel(pfY)
            yield
            for old in ((Pm, Q) if not last else (Pm, Q, Pn)):
                if old in L:
                    L.remove(old)
                    Rot.rel(old)
            Pm = Pn
            if not last:
                Q = Qn
        pfu = R.rotF.get()
        for h in range(4):
            self.mm(pfu[0:T, h * 128:(h + 1) * 128], hs(Y, h), bv[P, h * 128:(h + 1) * 128], True, True, [Y, bv], [pfu])
        usb = g(R.F512)
        self.cp("act", usb[P, :], pfu[0:T, :], [pfu], [usb])
        Rot.rel(pfu)
        yield
        pfw = R.rotF.get()
        for h in range(4):
            self.mm(pfw[0:128, h * T:(h + 1) * T], bk[P, h * 128:(h + 1) * 128], hs(Y, h), True, True, [bk, Y], [pfw])
        wTb = g(R.BT)
        self.cp("act" if K.OPT_EVACT else "dve", wTb[:, 0:T4], pfw[:, 0:T4], [pfw], [wTb])
        Rot.rel(pfw)
        keep = [usb, wTb, qkT, QGT, kd, eGl, ga]
        for b_ in list(L):
            if b_ not in keep:
                Rot.rel(b_)
        ctx.update(usb=usb, wTb=wTb, qkT=qkT, QGT=QGT, kd=kd, eGl=eGl, ga=ga, keep=keep)
        yield

    def gdn_chain_g(self, R, l, T, st, mixT, coff, ctx):
        con, conb = self.con, self.conb
        L = []

        def g(p):
            b = p.get()
            L.append(b)
            return b
        P = slice(0, T)
        T4 = 4 * T
        idb = self.ident(T, True)
        v3, bch, bcm = self.v3, self.bch, self.bcm
        usb, wTb, qkT, QGT, kd, eGl, ga = (ctx[k_] for k_ in ("usb", "wTb", "qkT", "QGT", "kd", "eGl", "ga"))
        qgT = lambda h: qkT[:, (8 + h) * T:(9 + h) * T]
        hs = lambda b_, h: b_[P, h * T:(h + 1) * T]
        pf = R.rotF.get()
        for h in range(4):
            self.mm(pf[0:T, h * 128:(h + 1) * 128], wTb[:, h * T:(h + 1) * T], st.Sb[:, h, :], True, True, [wTb, st.Sb], [pf])
        dl = g(R.B512)
        self.tt("dve", dl[P, :], usb[P, :], pf[0:T, :], ALU.subtract, [usb, pf], [dl])
        Rot.rel(pf)
        yield
        pfo = R.rotF.get()
        for h in range(4):
            o = pfo[0:T, h * 128:(h + 1) * 128]
            self.mm(o, qgT(h), st.Sb[:, h, :], True, False, [qkT, st.Sb], [pfo])
            self.mm(o, hs(QGT, h), dl[P, h * 128:(h + 1) * 128], False, True, [QGT, dl], [pfo])
        pfS = R.rotF.get()
        for h in range(4):
            self.mm(pfS[:, h * 128:(h + 1) * 128], kd[P, h * 128:(h + 1) * 128], dl[P, h * 128:(h + 1) * 128], True, True, [kd, dl], [pfS])
        self.tt("dve", st.S[:], st.S[:], bch(eGl[:, 0:4], 128, 128), ALU.mult, [st.S, eGl], [st.S])
        self.tt("dve", st.S[:], st.S[:], v3(pfS[:, :], 4), ALU.add, [st.S, pfS], [st.S])
        Rot.rel(pfS)
        self.cp("act", st.Sb[:], st.S[:], [st.S], [st.Sb])
        osb = g(R.F512)
        self.cp("act", osb[P, :], pfo[0:T, :], [pfo], [osb])
        Rot.rel(pfo)
        yield
        sq = g(R.F512)
        self.tt("pool", sq[P, :], osb[P, :], osb[P, :], ALU.mult, [osb], [sq])
        s4 = g(R.SM)
        self.red(s4[P, 0:4], v3(sq[P, :], 4), ALU.add, [sq], [s4])
        self.rstd(s4[P, 0:4], s4[P, 0:4], 128, [s4], [s4])
        yield
        self.tt("dve", v3(osb[P, :], 4), v3(osb[P, :], 4), bch(s4[P, 0:4], T, 128), ALU.mult, [osb, s4], [osb])
        self.act(sq[P, :], ga[P, 8:520], AF.Silu, [ga], [sq])
        self.tt("pool", v3(sq[P, :], 4), v3(sq[P, :], 4), bcm(R.gnorm[P, :], T, 128), ALU.mult, [sq, R.gnorm], [sq])
        ogb = g(R.B512)
        self.tt("dve", ogb[P, :], osb[P, :], sq[P, :], ALU.mult, [osb, sq], [ogb])
        pb = self.rotB.get()
        for h in range(4):
            self.tr(pb[0:128, h * T:(h + 1) * T], ogb[P, h * 128:(h + 1) * 128], idb, [ogb, conb], [pb])
        self.cp("act", mixT[:, 8:12, coff:coff + T], v3(pb[:, 0:T4], 4), [pb], [mixT])
        Rot.rel(pb)
        Rot.rel(*L)
        Rot.rel(*ctx["keep"])
        yield


_CACHE = {}


def _program(dbg=None, nlayers=DEPTH, phases="PAR"):
    key = (dbg, nlayers, phases)
    if key not in _CACHE:
        _CACHE[key] = K(dbg, nlayers, phases).build()
    return _CACHE[key]


def make_in_maps(inp):
    f = lambda a: np.ascontiguousarray(np.asarray(a, dtype=np.float32))
    consts = make_consts()
    rope = make_rope()
    shared = {
        "meta": f(inp["meta_tokens"]), "norm_w": f(inp["norm_w"]), "w_in": f(inp["w_in"]),
        "q_norm": f(inp["mla_q_norm"]), "w_uq": f(inp["mla_w_uq"]), "kv_norm": f(inp["mla_kv_norm"]),
        "w_uk": f(inp["mla_w_uk"]).reshape(DEPTH, 256, 512), "w_uv": f(inp["mla_w_uv"]).reshape(DEPTH, 256, 512),
        "gate_b": f(inp["mlstm_gate_b"]).reshape(DEPTH, 8), "m_norm": f(inp["mlstm_norm"]),
        "conv_w": f(inp["gdn_conv_w"]).reshape(DEPTH, 4 * 1536), "a_log": f(inp["gdn_a_log"]),
        "dt_bias": f(inp["gdn_dt_bias"]), "g_norm": f(inp["gdn_norm"]), "w_out": f(inp["w_out"]),
        "final_norm": f(inp["final_norm"]), "consts": consts, "rope": rope,
    }
    maps = []
    for c in range(8):
        sl = slice(NS * c, NS * c + NS)
        m = dict(shared)
        m["xp"] = f(inp["x_prompt"][c])
        m["xs"] = f(inp["x_sample"][sl]).reshape(NS * DS, D)
        m["cl"] = f(inp["cache_mla_latent"][:, sl])
        m["ck"] = f(inp["cache_mla_krope"][:, sl])
        m["imC"] = f(inp["state_mlstm_C"][:, sl])
        m["imn"] = f(inp["state_mlstm_n"][:, sl])
        m["imm"] = f(inp["state_mlstm_m"][:, sl])
        m["igS"] = f(inp["state_gdn_S"][:, sl])
        m["igc"] = f(inp["state_gdn_conv"][:, sl])
        maps.append(m)
    return maps


def kernel(**inputs):
    nc = _program()
    maps = make_in_maps(inputs)
    res = run_bass_kernel_spmd(nc, maps, core_ids=list(range(8)))
    R = res.results
    st = lambda name, axis: np.stack([np.asarray(r[name], dtype=np.float32) for r in R], axis=axis)
    cat = lambda name, axis: np.concatenate([np.asarray(r[name], dtype=np.float32) for r in R], axis=axis)
    y_prompt = st("y_p", 0)
    y_sample = st("y_s", 0).reshape(8 * NS, DS, D)
    return (
        y_prompt, y_sample,
        st("p_lat", 1), st("p_kr", 1), st("p_C", 1), st("p_n", 1), st("p_m", 1), st("p_S", 1), st("p_cv", 1),
        cat("s_lat", 1), cat("s_kr", 1), cat("s_C", 1), cat("s_n", 1), cat("s_m", 1), cat("s_S", 1), cat("s_cv", 1),
    )
```

```python
import math
import contextlib
import numpy as np
import concourse.bass as bass
import concourse.mybir as mybir
from concourse.bass_utils import run_bass_kernel_spmd

F32 = mybir.dt.float32
BF16 = mybir.dt.bfloat16
AF = mybir.ActivationFunctionType
ALU = mybir.AluOpType
AX = mybir.AxisListType

D = 1024
SEQ = 4096
NMETA = 16
DEPTH = 2
NS = 4
DS = 16
PAST = 4096
NCACHE = NMETA + PAST
EPS = 1e-6
IN_COLS = 5808
NROWS_X = 80 + SEQ
PROJ_ROWS = 3 + 16 + SEQ + NS * 19
SM_SCALE = 1.0 / math.sqrt(96.0)

C_ID, C_ONE, C_TRI, C_SL, C_INCL, C_NEG, C_S128, C_S16, C_SELR = 0, 128, 256, 384, 512, 640, 768, 896, 1024
NCON = 1120
RCHUNK = 128


def make_consts():
    c = np.zeros((128, NCON), np.float32)
    c[:, C_ID:C_ID + 128] = np.eye(128)
    c[:, C_ONE:C_ONE + 128] = 1.0
    k = np.arange(128)[:, None]
    t = np.arange(128)[None, :]
    c[:, C_TRI:C_TRI + 128] = (k <= t)
    c[:, C_SL:C_SL + 128] = (k > t)
    c[:, C_INCL:C_INCL + 128] = (t <= k)
    c[:, C_NEG:C_NEG + 128] = np.where(t <= k, 0.0, -1e30)
    c[127, C_S128:C_S128 + 128] = 1.0
    c[15, C_S16:C_S16 + 128] = 1.0
    c[:32, C_SELR + 64:C_SELR + 96] = np.eye(32)
    return c


def make_rope():
    half = 16
    freq = (np.float32(10000.0) ** (-np.arange(half, dtype=np.float32) / np.float32(half))).astype(np.float32)
    pos = np.arange(NCACHE + DS, dtype=np.float32)
    ang = (pos[:, None] * freq[None, :]).astype(np.float32)
    return np.concatenate([np.cos(ang), np.sin(ang)], axis=1).astype(np.float32)


class H:
    __slots__ = ("name", "w", "r", "excl")

    def __init__(self, name="", excl=False):
        self.name = name
        self.w = None
        self.r = []
        self.excl = excl


class Buf:
    __slots__ = ("t", "h", "busy")

    def __init__(self, t, name=""):
        self.t = t
        self.h = H(name)
        self.busy = False

    def __getitem__(self, k):
        return self.t[k]


def _hs(lst):
    out = []
    for x in lst:
        if x is None:
            continue
        out.append(x.h if isinstance(x, Buf) else x)
    return out


class Sched:
    ENGS = ("pe", "act", "dve", "pool", "sp")
    XLAT = 1.2

    def __init__(self, nc, n_dma_sems=20):
        self.nc = nc
        self.nodes = []
        self.aset = {}
        self.cur_set = 1
        self.seg_start = [0]
        self.n_dma = n_dma_sems
        self.sems = {}

    def _deps(self, reads, writes):
        d = set()
        for h in reads:
            if h.w is not None:
                d.add(h.w)
        for h in writes:
            if h.w is not None:
                d.add(h.w)
            d.update(h.r)
        return d

    def _record(self, idx, reads, writes):
        for h in reads:
            h.r.append(idx)
        for h in writes:
            h.w = idx
            h.r = []

    def emit(self, eng, fn, reads=(), writes=(), cost=0.3, aset=0):
        reads, writes = _hs(reads), _hs(writes)
        ex = [h for h in reads if h.excl]
        if ex:
            reads = [h for h in reads if not h.excl]
            writes = writes + [h for h in ex if h not in writes]
        deps = self._deps(reads, writes)
        idx = len(self.nodes)
        self.nodes.append((eng, fn, deps, False, cost))
        if aset:
            self.aset[idx] = aset
        self._record(idx, reads, writes)
        return idx

    def dma(self, q, fn, reads=(), writes=(), is_out=False, cost=2.5):
        reads, writes = _hs(reads), _hs(writes)
        deps = self._deps(reads, writes)
        idx = len(self.nodes)
        self.nodes.append((q, fn, deps, True, cost))
        self._record(idx, reads, writes)
        return idx

    def barrier(self):
        if self.seg_start[-1] != len(self.nodes):
            self.seg_start.append(len(self.nodes))

    def _schedule(self, lo, hi):
        import heapq
        nodes = self.nodes
        n = hi - lo
        succ = [[] for _ in range(n)]
        indeg = [0] * n
        for i in range(lo, hi):
            for d in nodes[i][2]:
                if d >= lo:
                    succ[d - lo].append(i - lo)
                    indeg[i - lo] += 1
        prio = [0.0] * n
        for i in range(n - 1, -1, -1):
            c = nodes[lo + i][4]
            m = 0.0
            for s in succ[i]:
                if prio[s] > m:
                    m = prio[s]
            prio[i] = c + m + self.XLAT
        future = {e: [] for e in self.ENGS}
        avail = {e: [] for e in self.ENGS}
        tfree = {e: 0.0 for e in self.ENGS}
        ready_t = [0.0] * n
        order = {e: [] for e in self.ENGS}
        for i in range(n):
            if indeg[i] == 0:
                heapq.heappush(future[nodes[lo + i][0]], (0.0, i))
        done = 0
        while done < n:
            best_e, best_t = None, None
            for e in self.ENGS:
                if avail[e]:
                    t = tfree[e]
                elif future[e]:
                    t = max(tfree[e], future[e][0][0])
                else:
                    continue
                if best_t is None or t < best_t:
                    best_e, best_t = e, t
            e, t = best_e, best_t
            fu = future[e]
            while fu and fu[0][0] <= t:
                rt, i = heapq.heappop(fu)
                heapq.heappush(avail[e], (-prio[i], i))
            if e == "act" and len(avail[e]) > 1:
                top = avail[e][0]
                ts_ = self.aset.get(lo + top[1], 0)
                if ts_ != 0 and ts_ != self.cur_set:
                    best = None
                    for cand in avail[e]:
                        cs_ = self.aset.get(lo + cand[1], 0)
                        if (cs_ == 0 or cs_ == self.cur_set) and (best is None or cand < best):
                            best = cand
                    if best is not None and (-best[0]) >= (-top[0]) - 12.0:
                        avail[e].remove(best)
                        heapq.heapify(avail[e])
                        i = best[1]
                    else:
                        _, i = heapq.heappop(avail[e])
                else:
                    _, i = heapq.heappop(avail[e])
            else:
                _, i = heapq.heappop(avail[e])
            node = nodes[lo + i]
            sw = 0.0
            if e == "act":
                s_ = self.aset.get(lo + i, 0)
                if s_ != 0 and s_ != self.cur_set:
                    self.cur_set = s_
                    sw = 1.3
            if node[3]:
                fin = t + node[4]
                tfree[e] = t + 0.15
            else:
                fin = t + node[4] + sw
                tfree[e] = fin
            order[e].append(lo + i)
            done += 1
            for s in succ[i]:
                se = nodes[lo + s][0]
                lat = 0.08 if (se == e and not node[3]) else self.XLAT
                if fin + lat > ready_t[s]:
                    ready_t[s] = fin + lat
                indeg[s] -= 1
                if indeg[s] == 0:
                    heapq.heappush(future[se], (ready_t[s], s))
        return order

    def finalize(self):
        nc = self.nc
        nodes = self.nodes
        self.barrier()
        segs = list(zip(self.seg_start[:-1], self.seg_start[1:]))
        eng_ops = {e: [] for e in self.ENGS}
        token = [None] * len(nodes)
        cnt = {e: 0 for e in self.ENGS}
        dma_i = {e: 0 for e in self.ENGS}
        dma_val = {e: [0] * self.n_dma for e in self.ENGS}
        dma_prev = {}
        for (lo, hi) in segs:
            order = self._schedule(lo, hi)
            for e in self.ENGS:
                for idx in order[e]:
                    if nodes[idx][3]:
                        i = dma_i[e]
                        dma_i[e] = (i + 1) % self.n_dma
                        key = ("dma", e, i)
                        if dma_val[e][i] > 0:
                            dma_prev[idx] = (key, dma_val[e][i])
                        dma_val[e][i] += 16
                        token[idx] = (key, dma_val[e][i], 16)
                    else:
                        cnt[e] += 1
                        token[idx] = (e, cnt[e], 1)
                    eng_ops[e].append(("node", idx))
            allw = {e: cnt[e] for e in self.ENGS if cnt[e] > 0}
            for e in self.ENGS:
                for i, v in enumerate(dma_val[e]):
                    if v > 0:
                        allw[("dma", e, i)] = v
            for e in self.ENGS:
                eng_ops[e].append(("bar", dict(allw)))
        keys = list(self.ENGS)
        for e in self.ENGS:
            for i, v in enumerate(dma_val[e]):
                if v > 0:
                    keys.append(("dma", e, i))
        with contextlib.ExitStack() as es:
            for k in keys:
                nm = k if isinstance(k, str) else "d_%s_%d" % (k[1], k[2])
                self.sems[k] = es.enter_context(nc.semaphore("s_" + nm))
            block = es.enter_context(nc.Block())
            sems = self.sems

            def run(eng_name):
                def body(e):
                    waited = {}
                    for kind, x in eng_ops[eng_name]:
                        if kind == "bar":
                            for k, v in x.items():
                                if k == eng_name:
                                    continue
                                if waited.get(k, 0) < v:
                                    waited[k] = v
                                    e.wait_ge(sems[k], v)
                            continue
                        idx = x
                        _, fn, deps, is_dma, _ = nodes[idx]
                        need = {}
                        if idx in dma_prev:
                            k, v = dma_prev[idx]
                            need[k] = v
                        for d in deps:
                            k, v, _ = token[d]
                            if k == eng_name and eng_name == "pe":
                                continue
                            if need.get(k, 0) < v:
                                need[k] = v
                        for k, v in need.items():
                            if waited.get(k, 0) < v:
                                waited[k] = v
                                e.wait_ge(sems[k], v)
                        k, v, inc = token[idx]
                        fn(e).then_inc(sems[k], inc)
                return body

            block.tensor(run("pe"))
            block.scalar(run("act"))
            block.vector(run("dve"))
            block.gpsimd(run("pool"))
            block.sync(run("sp"))


class Rot:
    def __init__(self, bufs):
        self.bufs = bufs
        self.i = 0

    def get(self):
        for _ in range(len(self.bufs)):
            b = self.bufs[self.i]
            self.i = (self.i + 1) % len(self.bufs)
            if not b.busy:
                b.busy = True
                return b
        raise AssertionError("rotating pool exhausted: all buffers leased")

    @staticmethod
    def rel(*bs):
        for b in bs:
            b.busy = False


class K:
    PRE_DEPTH = 2
    OPT_K32 = False
    OPT_SIGEXP = False
    OPT_WSPLIT = True
    OPT_X1 = False
    OPT_EVACT = True

    def __init__(self, dbg=None, nlayers=DEPTH, phases="PAR"):
        self.dbg = dbg
        self.nlayers = nlayers
        self.phases = phases
        self.nc = bass.Bass("TRN2", target_bir_lowering=False)
        self.S = Sched(self.nc)
        self.es = contextlib.ExitStack()
        self.dq = 0

    def dram(self, name, shape, dt=F32, kind=None):
        if kind is None:
            t = self.nc.dram_tensor(name, list(shape), dt)
        else:
            t = self.nc.dram_tensor(name, list(shape), dt, kind=kind)
        return Buf(t.ap(), name)

    def sb(self, es, name, shape, dt=F32):
        self.dq += 1
        name = "%s_u%d" % (name, self.dq)
        return Buf(es.enter_context(self.nc.sbuf_tensor(name, list(shape), dt)), name)

    def ps(self, es, name, shape, dt=F32):
        b = Buf(es.enter_context(self.nc.psum_tensor(name, list(shape), dt)), name)
        b.h.excl = True
        return b

    def mm(self, out, lhsT, rhs, start, stop, r, w):
        n = max(64, rhs.free_size()) * (4 if rhs.dtype == F32 else 1)
        self.S.emit("pe", lambda e: e.matmul(out, lhsT=lhsT, rhs=rhs, start=start, stop=stop), r, w, cost=0.04 + n / 1400.0)

    def tr(self, out, in_, ident, r, w):
        self.S.emit("pe", lambda e: e.transpose(out, in_, ident), r, w, cost=0.04 + max(64, in_.partition_size()) / 1400.0)

    @staticmethod
    def _c(eng, ap):
        n = ap.free_size()
        if eng == "dve":
            return 0.12 + n / 960.0
        if eng == "act":
            return 0.2 + n / 1400.0
        return 0.35 + n / 700.0

    def act(self, out, in_, func, r, w, scale=None, bias=None, accum=None):
        kw = {}
        if scale is not None:
            kw["scale"] = scale
        if bias is not None:
            kw["bias"] = bias
        if accum is not None:
            kw["accum_out"] = accum
        aset = 1 if func in (AF.Exp, AF.Ln) else (2 if func == AF.Silu else (3 if func in (AF.Sigmoid, AF.Sqrt) else 0))
        self.S.emit("act", lambda e: e.activation(out=out, in_=in_, func=func, **kw), r, w, cost=self._c("act", out) + (0.1 if accum is not None else 0), aset=aset)

    def tt(self, eng, out, in0, in1, op, r, w):
        self.S.emit(eng, lambda e: e.tensor_tensor(out=out, in0=in0, in1=in1, op=op), r, w, cost=self._c(eng, out))

    def ts(self, eng, out, in0, s1, op0, r, w, s2=None, op1=None):
        if op1 is None:
            self.S.emit(eng, lambda e: e.tensor_scalar(out=out, in0=in0, scalar1=s1, scalar2=None, op0=op0), r, w, cost=self._c(eng, out))
        else:
            self.S.emit(eng, lambda e: e.tensor_scalar(out=out, in0=in0, scalar1=s1, scalar2=s2, op0=op0, op1=op1), r, w, cost=self._c(eng, out))

    def stt(self, out, in0, scalar, in1, op0, op1, r, w):
        self.S.emit("dve", lambda e: e.scalar_tensor_tensor(out=out, in0=in0, scalar=scalar, in1=in1, op0=op0, op1=op1), r, w, cost=self._c("dve", out))

    def red(self, out, in_, op, r, w):
        self.S.emit("dve", lambda e: e.tensor_reduce(out=out, in_=in_, axis=AX.X, op=op), r, w, cost=self._c("dve", in_))

    def cp(self, eng, out, in_, r, w):
        if eng == "act":
            self.S.emit("act", lambda e: e.activation(out=out, in_=in_, func=AF.Copy), r, w, cost=self._c("act", out))
        else:
            self.S.emit(eng, lambda e: e.tensor_copy(out=out, in_=in_), r, w, cost=self._c(eng, out))

    def recip(self, out, in_, r, w):
        self.S.emit("dve", lambda e: e.reciprocal(out=out, in_=in_), r, w, cost=self._c("dve", out))

    def memset(self, eng, ap, val, w):
        self.S.emit(eng, lambda e: e.memset(ap, val), [], w, cost=self._c(eng, ap))

    def dma(self, out, in_, r, w, q=None, nonc=False, is_out=False):
        if q is None:
            q = ("sp", "act")[self.dq % 2] if False else "sp"
        c = 2.0 + in_.nbytes() / 150e3 * (2.0 if q == "pool" else 1.0)
        if nonc:
            self.S.dma(q, lambda e: e.dma_start(out=out, in_=in_, allow_slow_non_contiguous=True), r, w, is_out, cost=c)
        else:
            self.S.dma(q, lambda e: e.dma_start(out=out, in_=in_), r, w, is_out, cost=c)

    @staticmethod
    def interleave(gens):
        gens = [g for g in gens if g is not None]
        while gens:
            alive = []
            for g in gens:
                try:
                    next(g)
                    alive.append(g)
                except StopIteration:
                    pass
            gens = alive

    def rstd(self, out, ssq, n, r, w):
        self.act(out, ssq, AF.Ln, r, w, scale=1.0 / n, bias=EPS)
        self.act(out, out, AF.Exp, w, w, scale=-0.5)

    def build(self):
        nc = self.nc
        es = self.es
        dbg = self.dbg
        EI, EO = "ExternalInput", "ExternalOutput"
        self.xp = self.dram("xp", [SEQ, D], kind=EI)
        self.xs = self.dram("xs", [NS * DS, D], kind=EI)
        self.meta = self.dram("meta", [NMETA, D], kind=EI)
        self.cl = self.dram("cl", [DEPTH, NS, NCACHE, 256], kind=EI)
        self.ck = self.dram("ck", [DEPTH, NS, NCACHE, 32], kind=EI)
        self.imC = self.dram("imC", [DEPTH, NS, 4, 128, 128], kind=EI)
        self.imn = self.dram("imn", [DEPTH, NS, 4, 128], kind=EI)
        self.imm = self.dram("imm", [DEPTH, NS, 4], kind=EI)
        self.igS = self.dram("igS", [DEPTH, NS, 4, 128, 128], kind=EI)
        self.igc = self.dram("igc", [DEPTH, NS, 3, 1536], kind=EI)
        self.norm_w = self.dram("norm_w", [DEPTH, D], kind=EI)
        self.w_in = self.dram("w_in", [DEPTH, D, IN_COLS], kind=EI)
        self.q_norm = self.dram("q_norm", [DEPTH, 384], kind=EI)
        self.w_uq = self.dram("w_uq", [DEPTH, 384, 768], kind=EI)
        self.kv_norm = self.dram("kv_norm", [DEPTH, 256], kind=EI)
        self.w_uk = self.dram("w_uk", [DEPTH, 256, 512], kind=EI)
        self.w_uv = self.dram("w_uv", [DEPTH, 256, 512], kind=EI)
        self.gate_b = self.dram("gate_b", [DEPTH, 8], kind=EI)
        self.m_norm = self.dram("m_norm", [DEPTH, 512], kind=EI)
        self.conv_w = self.dram("conv_w", [DEPTH, 4 * 1536], kind=EI)
        self.a_log = self.dram("a_log", [DEPTH, 4], kind=EI)
        self.dt_bias = self.dram("dt_bias", [DEPTH, 4], kind=EI)
        self.g_norm = self.dram("g_norm", [DEPTH, 128], kind=EI)
        self.w_out = self.dram("w_out", [DEPTH, 1536, D], kind=EI)
        self.final_norm = self.dram("final_norm", [D], kind=EI)
        self.consts = self.dram("consts", [128, NCON], kind=EI)
        self.rope = self.dram("rope", [NCACHE + DS, 32], kind=EI)

        self.y_p = self.dram("y_p", [SEQ, D], kind=EO)
        self.y_s = self.dram("y_s", [NS * DS, D], kind=EO)
        self.p_lat = self.dram("p_lat", [DEPTH, NMETA + SEQ, 256], kind=EO)
        self.p_kr = self.dram("p_kr", [DEPTH, NMETA + SEQ, 32], kind=EO)
        self.p_C = self.dram("p_C", [DEPTH, 4, 128, 128], kind=EO)
        self.p_n = self.dram("p_n", [DEPTH, 4, 128], kind=EO)
        self.p_m = self.dram("p_m", [DEPTH, 4], kind=EO)
        self.p_S = self.dram("p_S", [DEPTH, 4, 128, 128], kind=EO)
        self.p_cv = self.dram("p_cv", [DEPTH, 3, 1536], kind=EO)
        self.s_lat = self.dram("s_lat", [DEPTH, NS, DS, 256], kind=EO)
        self.s_kr = self.dram("s_kr", [DEPTH, NS, DS, 32], kind=EO)
        self.s_C = self.dram("s_C", [DEPTH, NS, 4, 128, 128], kind=EO)
        self.s_n = self.dram("s_n", [DEPTH, NS, 4, 128], kind=EO)
        self.s_m = self.dram("s_m", [DEPTH, NS, 4], kind=EO)
        self.s_S = self.dram("s_S", [DEPTH, NS, 4, 128, 128], kind=EO)
        self.s_cv = self.dram("s_cv", [DEPTH, NS, 3, 1536], kind=EO)

        dk = EO if dbg else None
        self.proj = self.dram("proj_scr", [PROJ_ROWS, IN_COLS], kind=dk)
        self.mixa = self.dram("mixa_scr", [NROWS_X, 512], BF16, kind=dk)
        self.xscr = self.dram("x_scr", [NROWS_X, D], kind=dk)
        self.proj_h = {}
        self.mixa_h = {}
        self.x_h = {}

        self.con = self.sb(es, "con", [128, NCON])
        self.conb = self.sb(es, "conb", [128, 256 + 96], BF16)
        self.dma(self.con[:], self.consts.t, [], [self.con])
        self.dma(self.conb[:, 0:256], self.consts.t[:, 0:256], [], [self.conb], q="pool")
        self.dma(self.conb[:, 256:352], self.consts.t[:, C_SELR:C_SELR + 96], [], [self.conb], q="pool")
        self.psF = [self.ps(es, "psF%d" % i, [128, 512]) for i in range(6)]
        self.psB = [self.ps(es, "psB%d" % i, [128, 1024], BF16) for i in range(2)]
        self.rotB = Rot(self.psB)
        self.fin_bc = self.sb(es, "fin_bc", [128, D])
        self.dma(self.fin_bc[:], self.final_norm.t.partition_broadcast(128), [], [self.fin_bc])

        for l in range(self.nlayers):
            if "P" in self.phases:
                self.phase_P(l)
                self.S.barrier()
            if "A" in self.phases:
                self.phase_A(l)
                self.S.barrier()
            if "R" in self.phases:
                self.phase_R(l)
                self.S.barrier()
        self.S.finalize()
        es.close()
        return nc

    def ph(self, key):
        if key not in self.proj_h:
            self.proj_h[key] = H("proj%s" % (key,))
        return self.proj_h[key]

    def mh(self, key):
        if key not in self.mixa_h:
            self.mixa_h[key] = H("mixa%s" % (key,))
        return self.mixa_h[key]

    def xh(self, key):
        if key not in self.x_h:
            self.x_h[key] = H("x%s" % (key,))
        return self.x_h[key]

    def ident(self, n, bf=False):
        if bf:
            return self.conb[0:n, 0:n]
        return self.con[0:n, C_ID:C_ID + n]

    def load_x(self, l, tile, xt):
        if tile == "aux":
            if l == 0:
                self.dma(xt[0:16, :], self.meta.t, [], [xt])
                self.dma(xt[16:80, :], self.xs.t, [], [xt])
            else:
                self.dma(xt[0:80, :], self.xscr.t[0:80, :], [self.xh("aux")], [xt])
            return 80
        j = tile
        if l == 0:
            self.dma(xt[:, :], self.xp.t[j * 128:(j + 1) * 128, :], [], [xt])
        else:
            self.dma(xt[:, :], self.xscr.t[80 + j * 128:80 + (j + 1) * 128, :], [self.xh(j)], [xt])
        return 128

    def phase_P(self, l):
        con, conb = self.con, self.conb
        with contextlib.ExitStack() as es:
            w_bf = self.sb(es, "P_w", [128, 8, IN_COLS], BF16)
            wv = self.w_in.t[l].rearrange("(k p) c -> p k c", p=128)
            ngw = (IN_COLS + 511) // 512
            w_h = [H("w_in_g%d" % g_) for g_ in range(ngw)]
            if K.OPT_WSPLIT:
                for g_ in range(ngw):
                    c0_ = g_ * 512
                    cw_ = min(512, IN_COLS - c0_)
                    self.dma(w_bf[:, :, c0_:c0_ + cw_], wv[:, :, c0_:c0_ + cw_], [], [w_h[g_]], q="pool")
            else:
                for k in range(8):
                    self.dma(w_bf[:, k, :], wv[:, k, :], [], w_h, q="pool")
            nw = self.sb(es, "P_nw", [128, D])
            self.dma(nw[:], self.norm_w.t[l].partition_broadcast(128), [], [nw])
            xts = Rot([self.sb(es, "P_x%d" % i, [128, D]) for i in range(2)])
            junk = self.sb(es, "P_junk", [128, D], BF16)
            ssq = Rot([self.sb(es, "P_ssq%d" % i, [128, 1]) for i in range(2)])
            rs = Rot([self.sb(es, "P_rs%d" % i, [128, 1]) for i in range(2)])
            xns = Rot([self.sb(es, "P_xn%d" % i, [128, D], BF16) for i in range(2)])
            xTs = Rot([self.sb(es, "P_xT%d" % i, [128, 8, 128], BF16) for i in range(2)])
            prs = Rot([self.sb(es, "P_pr%d" % i, [128, IN_COLS]) for i in range(2)])
            rotF = Rot(self.psF)
            zt = self.sb(es, "P_z", [4, 1536])
            self.memset("pool", zt[:], 0.0, [zt])
            self.dma(self.proj.t[0:3, 3752:5288], zt[0:3, :], [zt], [self.ph("hist")])
            for b in range(NS):
                r0 = 4115 + 19 * b
                self.dma(self.proj.t[r0:r0 + 3, 3752:5288], self.igc.t[l, b], [], [self.ph("hist")])
            for tile in ["aux"] + list(range(SEQ // 128)):
                xt = xts.get()
                T = self.load_x(l, tile, xt)
                sq, r_ = ssq.get(), rs.get()
                self.act(junk[0:T, :], xt[0:T, :], AF.Square, [xt], [junk, sq], accum=sq[0:T, :])
                self.rstd(r_[0:T, :], sq[0:T, :], D, [sq], [r_])
                xn = xns.get()
                self.stt(xn[0:T, :], xt[0:T, :], r_[0:T, 0:1], nw[0:T, :], ALU.mult, ALU.mult, [xt, r_, nw], [xn])
                pb = self.rotB.get()
                for k in range(8):
                    self.tr(pb[0:128, k * 128:k * 128 + T], xn[0:T, k * 128:(k + 1) * 128], self.ident(T, True), [xn, conb], [pb])
                xT = xTs.get()
                pbv = pb[:, :].rearrange("p (k t) -> p k t", k=8)
                self.cp("dve", xT[:, :, 0:T], pbv[:, :, 0:T], [pb], [xT])
                Rot.rel(pb, xt, sq, r_, xn)
                pr = prs.get()
                ng = (IN_COLS + 511) // 512
                for g in range(ng):
                    c0 = g * 512
                    cw = min(512, IN_COLS - c0)
                    pf = rotF.get()
                    for k in range(8):
                        self.mm(pf[0:T, 0:cw], xT[:, k, 0:T], w_bf[:, k, c0:c0 + cw], k == 0, k == 7, [xT, w_h[g]], [pf])
                    if g % 2 == 0:
                        self.cp("act", pr[0:T, c0:c0 + cw], pf[0:T, 0:cw], [pf], [pr])
                    else:
                        self.cp("dve", pr[0:T, c0:c0 + cw], pf[0:T, 0:cw], [pf], [pr])
                    Rot.rel(pf)
                Rot.rel(xT)
                if tile == "aux":
                    self.dma(self.proj.t[3:19, :], pr[0:16, :], [pr], [self.ph("aux")])
                    for b in range(NS):
                        r0 = 4118 + 19 * b
                        self.dma(self.proj.t[r0:r0 + 16, :], pr[16 + 16 * b:32 + 16 * b, :], [pr], [self.ph("aux")])
                else:
                    r0 = 19 + tile * 128
                    self.dma(self.proj.t[r0:r0 + 128, :], pr[:, :], [pr], [self.ph(tile)])
                Rot.rel(pr)

    def phase_A(self, l):
        con, conb = self.con, self.conb
        with contextlib.ExitStack() as es:
            A = type("NS", (), {})()
            wuq = self.sb(es, "A_wuq", [128, 3, 768], BF16)
            self.dma(wuq[:], self.w_uq.t[l].rearrange("(k p) c -> p k c", p=128), [], [wuq], q="pool")
            wuk = self.sb(es, "A_wuk", [128, 2, 8, 96], BF16)
            self.memset("pool", wuk[:], 0.0, [wuk])
            for k in range(2):
                self.dma(wuk[:, k, :, 0:64], self.w_uk.t[l, k * 128:(k + 1) * 128, :].rearrange("p (h d) -> p h d", h=8), [], [wuk], q="pool")
            wuv = self.sb(es, "A_wuv", [128, 2, 512], BF16)
            self.dma(wuv[:], self.w_uv.t[l].rearrange("(k p) c -> p k c", p=128), [], [wuv], q="pool")
            qn_bc = self.sb(es, "A_qn", [128, 384])
            self.dma(qn_bc[:], self.q_norm.t[l].partition_broadcast(128), [], [qn_bc])
            kvn_bc = self.sb(es, "A_kvn", [128, 256])
            self.dma(kvn_bc[:], self.kv_norm.t[l].partition_broadcast(128), [], [kvn_bc])
            KT = self.sb(es, "A_KT", [96, 8, NCACHE + DS], BF16)
            V = self.sb(es, "A_V", [128, 34, 8, 65], BF16)
            self.memset("pool", V[:], 1.0, [V])
            slot_h = [H("slot%d" % i) for i in range(34)]
            for i in range(34):
                slot_h[i].w = V.h.w
            A.wuq, A.wuk, A.wuv, A.qn_bc, A.kvn_bc, A.KT, A.V, A.slot_h = wuq, wuk, wuv, qn_bc, kvn_bc, KT, V, slot_h
            A.junk = self.sb(es, "A_junk", [128, 384], BF16)
            A.ssq = Rot([self.sb(es, "A_ssq%d" % i, [128, 1]) for i in range(4)])
            A.rs = Rot([self.sb(es, "A_rs%d" % i, [128, 1]) for i in range(4)])
            A.cqn = Rot([self.sb(es, "A_cqn%d" % i, [128, 384], BF16) for i in range(2)])
            A.cqT = Rot([self.sb(es, "A_cqT%d" % i, [128, 3, 128], BF16) for i in range(2)])
            A.qsb = Rot([self.sb(es, "A_qsb%d" % i, [128, 8, 96], BF16) for i in range(2)])
            A.tA = self.sb(es, "A_tA", [128, 4, 16])
            A.tB = self.sb(es, "A_tB", [128, 4, 16])
            A.cn = Rot([self.sb(es, "A_cn%d" % i, [128, 256]) for i in range(2)])
            A.kr = Rot([self.sb(es, "A_kr%d" % i, [128, 32]) for i in range(2)])
            A.cbf = Rot([self.sb(es, "A_cbf%d" % i, [128, 288], BF16) for i in range(2)])
            A.QT = Rot([self.sb(es, "A_QT%d" % i, [96, 8, 128], BF16) for i in range(2)])
            cTs = [self.sb(es, "A_cT%d" % i, [128, 3, 128], BF16) for i in range(3)]
            for c_ in cTs:
                self.memset("pool", c_[:], 0.0, [c_])
            A.cT = Rot(cTs)
            A.PT = Rot([self.sb(es, "A_PT%d" % i, [128, 4, 128], BF16) for i in range(3)])
            A.rc = Rot([self.sb(es, "A_rc%d" % i, [128, 8]) for i in range(2)])
            A.oa = Rot([self.sb(es, "A_oa%d" % i, [128, 8, 64]) for i in range(2)])
            A.sz = Rot([self.sb(es, "A_sz%d" % i, [128, 512]) for i in range(2)])
            A.mx = Rot([self.sb(es, "A_mx%d" % i, [128, 512], BF16) for i in range(2)])
            A.pr = Rot([self.sb(es, "A_pr%d" % i, [128, 1184]) for i in range(2)])
            A.tab = Rot([self.sb(es, "A_tab%d" % i, [128, 32]) for i in range(2)])
            A.zseg = Rot([self.sb(es, "A_zs%d" % i, [16, 512]) for i in range(2)])
            cbs = [self.sb(es, "A_cbig%d" % i, [128, 4, 289], BF16) for i in range(3)]
            for c_ in cbs:
                self.memset("pool", c_[:, :, 256:257], 1.0, [c_])
            A.cbig = Rot(cbs)
            A.rotF = Rot(self.psF[0:4])
            A.O = (self.psF[4], self.psF[5])
            pr_aux = self.sb(es, "A_praux", [80, 1184])
            tab_aux = self.sb(es, "A_tabaux", [80, 32])
            QT_aux = self.sb(es, "A_QTaux", [96, 8, 80], BF16)
            cT_aux = self.sb(es, "A_cTaux", [128, 3, 80], BF16)
            self.memset("pool", cT_aux[:], 0.0, [cT_aux])
            self.dma(pr_aux[0:16, :], self.proj.t[3:19, 0:1184], [self.ph("aux")], [pr_aux])
            self.dma(tab_aux[0:16, :], self.rope.t[0:16, :], [], [tab_aux])
            for b in range(NS):
                r0 = 4118 + 19 * b
                self.dma(pr_aux[16 + 16 * b:32 + 16 * b, :], self.proj.t[r0:r0 + 16, 0:1184], [self.ph("aux")], [pr_aux])
                self.dma(tab_aux[16 + 16 * b:32 + 16 * b, :], self.rope.t[NCACHE:NCACHE + DS, :], [], [tab_aux])
            sl_h = H("s_lat_out")
            cn, kr = self.mla_proj(A, 80, pr_aux, tab_aux, QT_aux, cT_aux)
            wukT = self.sb(es, "A_wukT", [64, 8, 256], BF16)
            for k in range(2):
                pb = self.rotB.get()
                for h in range(8):
                    self.tr(pb[0:64, h * 128:(h + 1) * 128], wuk[:, k, h, 0:64], self.ident(128, True), [wuk, conb], [pb])
                self.cp("dve", wukT[:, :, k * 128:(k + 1) * 128], pb[0:64, :].rearrange("p (h r) -> p h r", h=8), [pb], [wukT])
                Rot.rel(pb)
            qr_pad = self.sb(es, "A_qrpad", [128, 8, 80], BF16)
            self.memset("pool", qr_pad[:], 0.0, [qr_pad])
            self.dma(qr_pad[0:32, :, :], QT_aux[64:96, :, :], [QT_aux], [qr_pad])
            self.dma(self.p_lat.t[l, 0:16, :], cn[0:16, :], [cn], [], is_out=True)
            self.dma(self.p_kr.t[l, 0:16, :], kr[0:16, :], [kr], [], is_out=True)
            for b in range(NS):
                self.dma(self.s_lat.t[l, b], cn[16 + 16 * b:32 + 16 * b, :], [cn], [sl_h], is_out=True)
                self.dma(self.s_kr.t[l, b], kr[16 + 16 * b:32 + 16 * b, :], [kr], [sl_h], is_out=True)
            Rot.rel(cn, kr)
            self.kv_build(A, cT_aux, 0, 16, 0)
            zs = A.zseg.get()
            self.dma(zs[:, :], self.proj.t[3:19, 672:1184], [self.ph("aux")], [zs])
            self.attend(A, 16, QT_aux, 0, [(0, 16)], None, zs, 0, self.mixa.t[0:16, :], self.mh("aux"))
            Rot.rel(zs)
            prepped = {}

            def prep(j):
                pr = A.pr.get()
                tab = A.tab.get()
                r0 = 19 + 128 * j
                self.dma(pr[:, :], self.proj.t[r0:r0 + 128, 0:1184], [self.ph(j)], [pr])
                self.dma(tab[:, :], self.rope.t[16 + 128 * j:16 + 128 * (j + 1), :], [], [tab])
                QT = A.QT.get()
                cT = A.cT.get()
                yield
                for _ in self.mla_proj_g(A, 128, pr, tab, QT, cT, prepped, j):
                    yield
                cn, kr = prepped[("ck", j)]
                self.dma(self.p_lat.t[l, 16 + 128 * j:16 + 128 * (j + 1), :], cn[:, :], [cn], [], is_out=True)
                self.dma(self.p_kr.t[l, 16 + 128 * j:16 + 128 * (j + 1), :], kr[:, :], [kr], [], is_out=True)
                Rot.rel(cn, kr, tab)
                yield
                for _ in self.kv_build_g(A, cT, 0, 128, 1 + j):
                    yield
                Rot.rel(cT)
                prepped[j] = (QT, pr)

            for _ in prep(0):
                pass
            NT = SEQ // 128
            for j in range(NT):
                QT, pr = prepped.pop(j)
                blocks = [(0, 16)] + [(1 + i, 128) for i in range(j + 1)]
                att = self.attend_g(A, 128, QT, 0, blocks, 1 + j, pr, 672, self.mixa.t[80 + 128 * j:80 + 128 * (j + 1), :], self.mh(j))
                self.interleave([att, prep(j + 1) if j + 1 < NT else None])
                Rot.rel(QT, pr)
            qlT = Rot([self.sb(es, "A_qlT%d" % i, [128, 2, 128], BF16) for i in range(2)])
            oln = Rot([self.sb(es, "A_oln%d" % i, [128, 256], BF16) for i in range(2)])
            olT = Rot([self.sb(es, "A_olT%d" % i, [128, 2, 128], BF16) for i in range(2)])
            rcl = Rot([self.sb(es, "A_rcl%d" % i, [128, 1]) for i in range(2)])
            Olat = A.O[0]
            for b in range(NS):
                off = 16 + 16 * b
                ql = qlT.get()
                for k in range(2):
                    pf = A.rotF.get()
                    for h in range(8):
                        self.mm(pf[0:128, h * 16:(h + 1) * 16], wukT[0:64, h, k * 128:(k + 1) * 128], QT_aux[0:64, h, off:off + 16],
                                True, True, [wukT, QT_aux], [pf])
                    self.cp("dve", ql[:, k, :], pf[:, 0:128], [pf], [ql])
                    Rot.rel(pf)
                qrv = qr_pad[:, :, off:off + 16]
                blist = []
                cb0 = A.cbig.get()
                self.dma(cb0[0:16, 0, 0:256], self.cl.t[l, b, 0:16, :], [], [cb0], q="pool")
                self.dma(cb0[0:16, 0, 257:289], self.ck.t[l, b, 0:16, :], [], [cb0], q="pool")
                self.dma(cb0[0:16, 1, 0:256], self.s_lat.t[l, b], [sl_h], [cb0], q="pool")
                self.dma(cb0[0:16, 1, 257:289], self.s_kr.t[l, b], [sl_h], [cb0], q="pool")
                groups = [[(cb0, 0, 16), (cb0, 1, 16)]]
                nblk = 34
                done = 0

                def run_group(grp, done):
                    n = grp[0][2]
                    g = len(grp)
                    pf = A.rotF.get()
                    for i, (cb, jj, _) in enumerate(grp):
                        cT = A.cT.get()
                        self.c_transpose289(A, cb[0:n, jj, :], n, cT, [cb])
                        o = pf[0:n, i * 128:(i + 1) * 128]
                        self.mm(o, cT[:, 0, 0:n], ql[:, 0, :], True, False, [cT, ql], [pf])
                        self.mm(o, cT[:, 1, 0:n], ql[:, 1, :], False, False, [cT, ql], [pf])
                        self.mm(o, cT[:, 2, 0:n], qrv, False, True, [cT, qr_pad], [pf])
                        Rot.rel(cT)
                    PT = A.PT.get()
                    self.act(PT[0:n, 0:g, :], pf[0:n, 0:g * 128].rearrange("p (g t) -> p g t", g=g), AF.Exp, [pf], [PT], scale=SM_SCALE)
                    Rot.rel(pf)
                    for i, (cb, jj, _) in enumerate(grp):
                        self.mm(Olat[0:128, 0:257], PT[0:n, i, :], cb[0:n, jj, 0:257], done + i == 0, done + i == nblk - 1, [PT, cb], [Olat])
                    Rot.rel(PT)
                    return done + g

                done = run_group(groups[0], done)
                Rot.rel(cb0)
                for g_ in range(8):
                    cb = A.cbig.get()
                    r0 = 16 + 512 * g_
                    self.dma(cb[:, :, 0:256], self.cl.t[l, b, r0:r0 + 512, :].rearrange("(j p) c -> p j c", p=128), [], [cb], q="pool")
                    self.dma(cb[:, :, 257:289], self.ck.t[l, b, r0:r0 + 512, :].rearrange("(j p) c -> p j c", p=128), [], [cb], q="pool")
                    done = run_group([(cb, jj, 128) for jj in range(4)], done)
                    Rot.rel(cb)
                rc1 = rcl.get()
                self.recip(rc1[:, 0:1], Olat[:, 256:257], [Olat], [rc1])
                on = oln.get()
                self.ts("dve", on[:, :], Olat[:, 0:256], rc1[:, 0:1], ALU.mult, [Olat, rc1], [on])
                pb = self.rotB.get()
                for k in range(2):
                    self.tr(pb[0:128, k * 128:(k + 1) * 128], on[:, k * 128:(k + 1) * 128], self.ident(128, True), [on, conb], [pb])
                oT = olT.get()
                self.cp("dve", oT[:, :, :], pb[:, 0:256].rearrange("p (k t) -> p k t", k=2), [pb], [oT])
                Rot.rel(pb, on, rc1)
                pf = A.rotF.get()
                for h in range(8):
                    for k in range(2):
                        self.mm(pf[0:16, h * 64:(h + 1) * 64], oT[:, k, h * 16:(h + 1) * 16], wuv[:, k, h * 64:(h + 1) * 64],
                                k == 0, k == 1, [oT, wuv], [pf])
                zs = A.zseg.get()
                r0 = 4118 + 19 * b
                self.dma(zs[:, :], self.proj.t[r0:r0 + 16, 672:1184], [self.ph("aux")], [zs])
                sz = A.sz.get()
                self.act(sz[0:16, :], zs[0:16, :], AF.Silu, [zs], [sz])
                mx = A.mx.get()
                self.tt("dve", mx[0:16, :], pf[0:16, 0:512], sz[0:16, :], ALU.mult, [pf, sz], [mx])
                Rot.rel(pf)
                self.dma(self.mixa.t[off:off + 16, :], mx[0:16, :], [mx], [self.mh("aux")])
                Rot.rel(zs, sz, mx, oT, ql)

    def c_transpose289(self, A, src, n, cT, r):
        conb = self.conb
        idb = self.ident(n, True)
        pb = self.rotB.get()
        self.tr(pb[0:128, 0:n], src[:, 0:128], idb, r + [conb], [pb])
        self.tr(pb[0:128, 128:128 + n], src[:, 128:256], idb, r + [conb], [pb])
        self.tr(pb[0:32, 256:256 + n], src[:, 257:289], idb, r + [conb], [pb])
        self.cp("dve", cT[:, 0:2, 0:n], pb[:, 0:256].rearrange("p (k t) -> p k t", k=2)[:, :, 0:n], [pb], [cT])
        self.cp("act", cT[0:32, 2, 0:n], pb[0:32, 256:256 + n], [pb], [cT])
        Rot.rel(pb)

    def rope_apply(self, A, T, o1, o2, x1, x2, cos, sin, tA, tB, r, w):
        self.tt("dve", tA, x1, cos, ALU.mult, r, [A.tA])
        self.tt("dve", tB, x2, sin, ALU.mult, r, [A.tB])
        self.tt("dve", o1, tA, tB, ALU.subtract, [A.tA, A.tB], w)
        self.tt("dve", tA, x2, cos, ALU.mult, r, [A.tA])
        self.tt("dve", tB, x1, sin, ALU.mult, r, [A.tB])
        self.tt("dve", o2, tA, tB, ALU.add, [A.tA, A.tB], w)

    def mla_proj(self, A, T, pr, tab, QT, cT):
        d = {}
        for _ in self.mla_proj_g(A, T, pr, tab, QT, cT, d, 0):
            pass
        return d[("ck", 0)]

    def mla_proj_g(self, A, T, pr, tab, QT, cT, outd, key):
        conb = self.conb
        idb = self.ident(T, True)
        sq, r_ = A.ssq.get(), A.rs.get()
        self.act(A.junk[0:T, 0:384], pr[0:T, 0:384], AF.Square, [pr], [A.junk, sq], accum=sq[0:T, :])
        self.rstd(r_[0:T, :], sq[0:T, :], 384, [sq], [r_])
        cqn = A.cqn.get()
        self.stt(cqn[0:T, :], pr[0:T, 0:384], r_[0:T, 0:1], A.qn_bc[0:T, :], ALU.mult, ALU.mult, [pr, r_, A.qn_bc], [cqn])
        pb = self.rotB.get()
        for k in range(3):
            self.tr(pb[0:128, k * 128:k * 128 + T], cqn[0:T, k * 128:(k + 1) * 128], idb, [cqn, conb], [pb])
        cqT = A.cqT.get()
        self.cp("dve", cqT[:, :, 0:T], pb[:, 0:384].rearrange("p (k t) -> p k t", k=3)[:, :, 0:T], [pb], [cqT])
        Rot.rel(pb, sq, r_, cqn)
        yield
        q_sb = A.qsb.get()
        cos4 = tab[0:T, 0:16].unsqueeze(1).broadcast_to([T, 4, 16])
        sin4 = tab[0:T, 16:32].unsqueeze(1).broadcast_to([T, 4, 16])
        for half in range(2):
            pf = A.rotF.get()
            for k in range(3):
                self.mm(pf[0:T, 0:384], cqT[:, k, 0:T], A.wuq[:, k, half * 384:(half + 1) * 384], k == 0, k == 2, [cqT, A.wuq], [pf])
            pv = pf[0:T, 0:384].rearrange("p (h d) -> p h d", h=4)
            hs = slice(half * 4, half * 4 + 4)
            self.cp("dve", q_sb[0:T, hs, 0:64], pv[:, :, 0:64], [pf], [q_sb])
            self.rope_apply(A, T, q_sb[0:T, hs, 64:80], q_sb[0:T, hs, 80:96], pv[:, :, 64:80], pv[:, :, 80:96],
                            cos4, sin4, A.tA[0:T, :, :], A.tB[0:T, :, :], [pf, tab], [q_sb])
            Rot.rel(pf)
            yield
        pb = self.rotB.get()
        for h in range(8):
            self.tr(pb[0:96, h * 128:h * 128 + T], q_sb[0:T, h, :], idb, [q_sb, conb], [pb])
        self.cp("dve", QT[0:96, :, 0:T], pb[0:96, :].rearrange("p (h t) -> p h t", h=8)[:, :, 0:T], [pb], [QT])
        Rot.rel(pb, q_sb, cqT)
        yield
        sq, r_ = A.ssq.get(), A.rs.get()
        self.act(A.junk[0:T, 0:256], pr[0:T, 384:640], AF.Square, [pr], [A.junk, sq], accum=sq[0:T, :])
        self.rstd(r_[0:T, :], sq[0:T, :], 256, [sq], [r_])
        cn = A.cn.get()
        self.stt(cn[0:T, :], pr[0:T, 384:640], r_[0:T, 0:1], A.kvn_bc[0:T, :], ALU.mult, ALU.mult, [pr, r_, A.kvn_bc], [cn])
        kr = A.kr.get()
        self.rope_apply(A, T, kr[0:T, 0:16], kr[0:T, 16:32], pr[0:T, 640:656], pr[0:T, 656:672],
                        tab[0:T, 0:16], tab[0:T, 16:32], A.tA[0:T, 0, :], A.tB[0:T, 0, :], [pr, tab], [kr])
        cbf = A.cbf.get()
        self.cp("pool", cbf[0:T, 0:256], cn[0:T, :], [cn], [cbf])
        self.cp("pool", cbf[0:T, 256:288], kr[0:T, :], [kr], [cbf])
        self.c_transpose(A, cbf[0:T, :], T, cT, [cbf])
        Rot.rel(sq, r_, cbf)
        outd[("ck", key)] = (cn, kr)
        yield

    def c_transpose(self, A, src, n, cT, r):
        conb = self.conb
        idb = self.ident(n, True)
        pb = self.rotB.get()
        self.tr(pb[0:128, 0:n], src[:, 0:128], idb, r + [conb], [pb])
        self.tr(pb[0:128, 128:128 + n], src[:, 128:256], idb, r + [conb], [pb])
        self.tr(pb[0:32, 256:256 + n], src[:, 256:288], idb, r + [conb], [pb])
        self.cp("dve", cT[:, 0:2, 0:n], pb[:, 0:256].rearrange("p (k t) -> p k t", k=2)[:, :, 0:n], [pb], [cT])
        self.cp("dve", cT[0:32, 2, 0:n], pb[0:32, 256:256 + n], [pb], [cT])
        Rot.rel(pb)

    def cache_block(self, A, cb, jj, n, slot):
        cT = A.cT.get()
        self.c_transpose(A, cb[0:n, jj, :], n, cT, [cb])
        self.kv_build(A, cT, 0, n, slot)
        Rot.rel(cT)

    def kv_build(self, A, cT, coff, n, slot):
        for _ in self.kv_build_g(A, cT, coff, n, slot):
            pass

    def kv_build_g(self, A, cT, coff, n, slot):
        conb = self.conb
        kcol = 0 if slot == 0 else 16 + 128 * (slot - 1)
        sh = A.slot_h[slot]
        for half in range(2):
            pf = A.rotF.get()
            for hh in range(4):
                h = half * 4 + hh
                o = pf[0:96, hh * 128:hh * 128 + n]
                self.mm(o, A.wuk[:, 0, h, :], cT[:, 0, coff:coff + n], True, False, [A.wuk, cT], [pf])
                self.mm(o, A.wuk[:, 1, h, :], cT[:, 1, coff:coff + n], False, False, [A.wuk, cT], [pf])
                self.mm(o, conb[:, 256:352], cT[:, 2, coff:coff + n], False, True, [conb, cT], [pf])
            src = pf[0:96, :].rearrange("p (h t) -> p h t", h=4)[:, :, 0:n]
            self.cp("dve", A.KT[0:96, half * 4:half * 4 + 4, kcol:kcol + n], src, [pf], [sh])
            Rot.rel(pf)
            yield
        pf = A.rotF.get()
        self.mm(pf[0:n, 0:512], cT[:, 0, coff:coff + n], A.wuv[:, 0, :], True, False, [cT, A.wuv], [pf])
        self.mm(pf[0:n, 0:512], cT[:, 1, coff:coff + n], A.wuv[:, 1, :], False, True, [cT, A.wuv], [pf])
        self.cp("dve", A.V[0:n, slot, :, 0:64], pf[0:n, 0:512].rearrange("p (h d) -> p h d", h=8), [pf], [sh])
        Rot.rel(pf)
        yield

    def attend(self, A, T, QT, qoff, blocks, diag_slot, zsrc, zoff, mix_out, mix_h):
        for _ in self.attend_g(A, T, QT, qoff, blocks, diag_slot, zsrc, zoff, mix_out, mix_h):
            pass

    def attend_g(self, A, T, QT, qoff, blocks, diag_slot, zsrc, zoff, mix_out, mix_h):
        groups = []
        for bl in blocks:
            if groups and groups[-1][0][1] == bl[1] and len(groups[-1]) < 4:
                groups[-1].append(bl)
            else:
                groups.append([bl])
        nblk = len(blocks)
        work = []
        for h in range(8):
            nb = 0
            for grp in groups:
                work.append((h, grp, nb))
                nb += len(grp)

        def finish(item):
            h, grp, nb0, pf = item
            Ob = A.O[h // 4]
            hh = h % 4
            n = grp[0][1]
            g = len(grp)
            PT = A.PT.get()
            self.act(PT[0:n, 0:g, 0:T], pf[0:n, 0:g * 128].rearrange("p (g t) -> p g t", g=g)[:, :, 0:T], AF.Exp,
                     [pf], [PT], scale=SM_SCALE)
            Rot.rel(pf)
            for i, (slot, _) in enumerate(grp):
                if diag_slot is not None and slot == diag_slot:
                    self.memset("pool", PT[64:128, i, 0:64], 0.0, [PT])
            for i, (slot, _) in enumerate(grp):
                self.mm(Ob[0:T, hh * 65:hh * 65 + 65], PT[0:n, i, 0:T], A.V[0:n, slot, h, :],
                        nb0 + i == 0, nb0 + i == nblk - 1, [PT, A.slot_h[slot]], [Ob])
            Rot.rel(PT)

        pend = None
        for (h, grp, nb0) in work:
            n = grp[0][1]
            pf = A.rotF.get()
            for i, (slot, _) in enumerate(grp):
                kcol = 0 if slot == 0 else 16 + 128 * (slot - 1)
                self.mm(pf[0:n, i * 128:i * 128 + T], A.KT[0:96, h, kcol:kcol + n], QT[0:96, h, qoff:qoff + T],
                        True, True, [A.slot_h[slot], QT], [pf])
            if pend is not None:
                finish(pend)
            pend = (h, grp, nb0, pf)
            yield
        finish(pend)
        yield
        rc = A.rc.get()
        oa = A.oa.get()
        for half in range(2):
            Ov = A.O[half][0:T, 0:260].rearrange("p (h d) -> p h d", h=4)
            rcv = rc[0:T, half * 4:half * 4 + 4].unsqueeze(2)
            self.recip(rcv, Ov[:, :, 64:65], [A.O[half]], [rc])
            self.tt("dve", oa[0:T, half * 4:half * 4 + 4, :], Ov[:, :, 0:64], rcv.broadcast_to([T, 4, 64]), ALU.mult,
                    [A.O[half], rc], [oa])
        sz = A.sz.get()
        self.act(sz[0:T, :], zsrc[0:T, zoff:zoff + 512], AF.Silu, [zsrc], [sz])
        mx = A.mx.get()
        self.tt("dve", mx[0:T, :], oa[0:T, :, :].rearrange("p h d -> p (h d)"), sz[0:T, :], ALU.mult, [oa, sz], [mx])
        self.dma(mix_out, mx[0:T, :], [mx], [mix_h])
        Rot.rel(rc, oa, sz, mx)

    def phase_R(self, l):
        con, conb = self.con, self.conb
        with contextlib.ExitStack() as es:
            R = type("NS", (), {})()
            wout = self.sb(es, "R_wout", [128, 12, D], BF16)
            wov = self.w_out.t[l].rearrange("(k p) c -> p k c", p=128)
            for k in range(0, 12, 4):
                self.dma(wout[:, k:k + 4, :], wov[:, k:k + 4, :], [], [wout], q="pool")
            cw = self.sb(es, "R_cw", [128, 4 * 1536])
            self.dma(cw[:], self.conv_w.t[l].partition_broadcast(128), [], [cw])
            mnorm = self.sb(es, "R_mnorm", [128, 512])
            self.dma(mnorm[:], self.m_norm.t[l].partition_broadcast(128), [], [mnorm])
            gnorm = self.sb(es, "R_gnorm", [128, 128])
            self.dma(gnorm[:], self.g_norm.t[l].partition_broadcast(128), [], [gnorm])
            gb = self.sb(es, "R_gb", [128, 8])
            self.dma(gb[:], self.gate_b.t[l].partition_broadcast(128), [], [gb])
            dtb = self.sb(es, "R_dtb", [128, 4])
            self.dma(dtb[:], self.dt_bias.t[l].partition_broadcast(128), [], [dtb])
            nea = self.sb(es, "R_nea", [128, 4])
            self.dma(nea[:], self.a_log.t[l].partition_broadcast(128), [], [nea])
            self.act(nea[:], nea[:], AF.Exp, [nea], [nea])
            self.ts("dve", nea[:], nea[:], -1.0, ALU.mult, [nea], [nea])
            R.wout, R.cw, R.mnorm, R.gnorm, R.gb, R.dtb, R.nea = wout, cw, mnorm, gnorm, gb, dtb, nea

            def pool(name, shape, dt, n):
                return Rot([self.sb(es, "R_%s%d" % (name, i), shape, dt) for i in range(n)])
            R.MP = pool("mp", [128, 2568], F32, 1)
            R.GA = pool("ga", [128, 520], F32, 1 + K.PRE_DEPTH)
            R.F1536 = pool("f1536", [128, 1536], F32, 3)
            R.F1024 = pool("f1024", [128, 1024], F32, 1)
            R.F512 = pool("f512", [128, 512], F32, 22)
            R.F256 = R.F512
            R.B1024 = pool("b1024", [128, 1024], BF16, 2)
            R.B512 = pool("b512", [128, 512], BF16, 12)
            R.B256 = R.B512
            R.BT = pool("bt", [128, 1536], BF16, 5)
            R.SM = pool("sm", [128, 12], F32, 56)
            R.X = pool("x", [128, D], F32, 1)
            R.XN = pool("xn", [128, D], F32, 1)
            R.MIXT = pool("mixt", [128, 12, 128], BF16, 2)
            R.MA = pool("ma", [128, 512], BF16, 1)
            R.rotF = Rot(self.psF)

            def state(nm):
                st = type("NS", (), {})()
                st.CT = self.sb(es, "R_CT" + nm, [128, 4, 128])
                st.CTb = self.sb(es, "R_CTb" + nm, [128, 4, 128], BF16)
                st.nT = self.sb(es, "R_nT" + nm, [128, 4])
                st.nTb = self.sb(es, "R_nTb" + nm, [128, 4], BF16)
                st.mbc = self.sb(es, "R_mbc" + nm, [128, 4])
                st.S = self.sb(es, "R_S" + nm, [128, 4, 128])
                st.Sb = self.sb(es, "R_Sb" + nm, [128, 4, 128], BF16)
                return st
            stp = state("p")
            sts = state("s")
            R.tmpC = self.sb(es, "R_tmpC", [128, 4, 128])
            for t_ in (stp.CT, stp.CTb, stp.nT, stp.nTb, stp.mbc, stp.S, stp.Sb):
                self.memset("pool", t_[:], 0.0, [t_])

            items = []
            mixA = R.MIXT.get()
            items.append(dict(T=16, row0=3, st=stp, mixT=mixA, coff=0, before=None, after=None))
            for b in range(NS):
                r0 = 4118 + 19 * b

                def bef(b=b):
                    self.load_state(R, l, b, sts)

                def aft(b=b, r0=r0, last=(b == NS - 1)):
                    self.store_state(R, sts, self.s_C.t[l, b], self.s_n.t[l, b], self.s_m.t[l, b:b + 1, :], self.s_S.t[l, b])
                    self.dma(self.s_cv.t[l, b], self.proj.t[r0 + 13:r0 + 16, 3752:5288], [], [], is_out=True)
                    if last:
                        self.out_proj(R, l, "aux", 80, mixA)
                        Rot.rel(mixA)
                items.append(dict(T=16, row0=r0, st=sts, mixT=mixA, coff=16 + 16 * b, before=bef, after=aft))
            cur = {}
            NCH = 128 // RCHUNK
            for j in range(SEQ // 128):
                for c in range(NCH):
                    def bef(j=j, c=c):
                        if c == 0:
                            cur["mixT"] = R.MIXT.get()

                    def aft(j=j, c=c):
                        if c == NCH - 1:
                            self.out_proj(R, l, j, 128, cur["mixT"])
                            Rot.rel(cur["mixT"])
                    items.append(dict(T=RCHUNK, row0=19 + 128 * j + RCHUNK * c, st=stp, mixT=None, coff=RCHUNK * c, before=bef, after=aft))
            ctxs = [dict() for _ in items]
            DEPTH_PRE = K.PRE_DEPTH
            for i0_ in range(min(DEPTH_PRE, len(items))):
                for _ in self.gdn_pre_g(R, l, items[i0_]["T"], items[i0_]["row0"], ctxs[i0_]):
                    pass
            for i, it in enumerate(items):
                if it["before"] is not None:
                    it["before"]()
                mixT = it["mixT"] if it["mixT"] is not None else cur["mixT"]
                gens = [self.mlstm_g(R, l, it["T"], it["row0"], it["st"], mixT, it["coff"]),
                        self.gdn_chain_g(R, l, it["T"], it["st"], mixT, it["coff"], ctxs[i])]
                if i + DEPTH_PRE < len(items):
                    nx = items[i + DEPTH_PRE]
                    gens.append(self.gdn_pre_g(R, l, nx["T"], nx["row0"], ctxs[i + DEPTH_PRE]))
                self.interleave(gens)
                if it["after"] is not None:
                    it["after"]()
            self.store_state(R, stp, self.p_C.t[l], self.p_n.t[l], self.p_m.t[l:l + 1, :], self.p_S.t[l])
            self.dma(self.p_cv.t[l], self.proj.t[19 + SEQ - 3:19 + SEQ, 3752:5288], [], [], is_out=True)

    def load_state(self, R, l, b, st):
        self.dma(R.tmpC[:], self.imC.t[l, b].rearrange("h e d -> e h d"), [], [R.tmpC])
        pf = R.rotF.get()
        for h in range(4):
            self.tr(pf[:, h * 128:(h + 1) * 128], R.tmpC[:, h, :], self.ident(128), [R.tmpC, self.con], [pf])
        self.cp("dve", st.CT[:], pf[:, :].rearrange("p (h e) -> p h e", h=4), [pf], [st.CT])
        Rot.rel(pf)
        self.cp("act", st.CTb[:], st.CT[:], [st.CT], [st.CTb])
        self.dma(st.nT[:], self.imn.t[l, b].rearrange("h d -> d h"), [], [st.nT], nonc=True)
        self.cp("act", st.nTb[:], st.nT[:], [st.nT], [st.nTb])
        self.dma(st.mbc[:], self.imm.t[l, b].partition_broadcast(128), [], [st.mbc])
        self.dma(st.S[:], self.igS.t[l, b].rearrange("h k v -> k h v"), [], [st.S])
        self.cp("act", st.Sb[:], st.S[:], [st.S], [st.Sb])

    def store_state(self, R, st, oC, on, om, oS):
        pf = R.rotF.get()
        for h in range(4):
            self.tr(pf[:, h * 128:(h + 1) * 128], st.CT[:, h, :], self.ident(128), [st.CT, self.con], [pf])
        self.cp("dve", R.tmpC[:], pf[:, :].rearrange("p (h e) -> p h e", h=4), [pf], [R.tmpC])
        Rot.rel(pf)
        self.dma(oC.rearrange("h e d -> e h d"), R.tmpC[:], [R.tmpC], [], is_out=True)
        self.dma(on.rearrange("h d -> d h"), st.nT[:], [st.nT], [], nonc=True, is_out=True)
        self.dma(om, st.mbc[0:1, :], [st.mbc], [], is_out=True)
        self.dma(oS.rearrange("h k v -> k h v"), st.S[:], [st.S], [], is_out=True)

    def out_proj(self, R, l, tile, T, mixT):
        conb = self.conb
        ma = R.MA.get()
        if tile == "aux":
            self.dma(ma[0:T, :], self.mixa.t[0:80, :], [], [ma])
        else:
            self.dma(ma[0:T, :], self.mixa.t[80 + 128 * tile:80 + 128 * (tile + 1), :], [], [ma])
        pb = self.rotB.get()
        for k in range(4):
            self.tr(pb[0:128, k * 128:k * 128 + T], ma[0:T, k * 128:(k + 1) * 128], self.ident(T, True), [ma, conb], [pb])
        self.cp("act", mixT[:, 0:4, 0:T], pb[:, 0:512].rearrange("p (k t) -> p k t", k=4)[:, :, 0:T], [pb], [mixT])
        Rot.rel(pb, ma)
        xt = R.X.get()
        self.load_x(l, tile, xt)
        xn = R.XN.get()
        for half in range(2):
            pf = R.rotF.get()
            for k in range(12):
                self.mm(pf[0:T, 0:512], mixT[:, k, 0:T], R.wout[:, k, half * 512:(half + 1) * 512], k == 0, k == 11, [mixT, R.wout], [pf])
            self.tt("dve", xn[0:T, half * 512:(half + 1) * 512], pf[0:T, 0:512], xt[0:T, half * 512:(half + 1) * 512], ALU.add, [pf, xt], [xn])
            Rot.rel(pf)
        Rot.rel(xt)
        if l < DEPTH - 1:
            if tile == "aux":
                self.dma(self.xscr.t[0:80, :], xn[0:80, :], [xn], [self.xh("aux")])
            else:
                self.dma(self.xscr.t[80 + 128 * tile:80 + 128 * (tile + 1), :], xn[:, :], [xn], [self.xh(tile)])
        else:
            sq = R.SM.get()
            yo = R.X.get()
            self.act(yo[0:T, :], xn[0:T, :], AF.Square, [xn], [yo, sq], accum=sq[0:T, 0:1])
            self.rstd(sq[0:T, 0:1], sq[0:T, 0:1], D, [sq], [sq])
            self.stt(yo[0:T, :], xn[0:T, :], sq[0:T, 0:1], self.fin_bc[0:T, :], ALU.mult, ALU.mult, [xn, sq, self.fin_bc], [yo])
            if tile == "aux":
                self.dma(self.y_s.t[:, :], yo[16:80, :], [yo], [], is_out=True)
            else:
                self.dma(self.y_p.t[128 * tile:128 * (tile + 1), :], yo[:, :], [yo], [], is_out=True)
            Rot.rel(sq, yo)
        Rot.rel(xn)

    @staticmethod
    def v3(ap, a):
        return ap.rearrange("p (a b) -> p a b", a=a)

    @staticmethod
    def bch(ap, T, n):
        return ap.unsqueeze(2).broadcast_to([T, 4, n])

    @staticmethod
    def bcm(ap, T, n):
        return ap.unsqueeze(1).broadcast_to([T, 4, n])

    def mlstm_chunk(self, R, l, T, row0, st, mixT, coff):
        for _ in self.mlstm_g(R, l, T, row0, st, mixT, coff):
            pass

    def mlstm_g(self, R, l, T, row0, st, mixT, coff):
        con, conb = self.con, self.conb
        L = []

        def g(p):
            b = p.get()
            L.append(b)
            return b
        P = slice(0, T)
        T4 = 4 * T
        tri = con[0:T, C_TRI:C_TRI + T]
        SLm = con[0:T, C_SL:C_SL + T]
        onesTT = con[0:T, C_ONE:C_ONE + T]
        idT = con[0:T, C_ID:C_ID + T]
        NEG = con[0:T, C_NEG:C_NEG + T]
        cs_ = C_S128 if T == 128 else C_S16
        sel = con[0:T, cs_:cs_ + 128]
        idb = self.ident(T, True)
        v3, bch, bcm = self.v3, self.bch, self.bcm
        KS = 128.0 ** -0.5
        mp = g(R.MP)
        self.dma(mp[P, :], self.proj.t[row0:row0 + T, 1184:3752], [], [mp])
        if K.OPT_K32:
            k32 = g(R.F512)
            self.dma(k32[P, :], self.proj.t[row0:row0 + T, 1696:2208], [], [k32])
        qkb = g(R.B1024)
        self.cp("act", qkb[P, 0:512], mp[P, 0:512], [mp], [qkb])
        self.act(qkb[P, 512:1024], mp[P, 512:1024], AF.Copy, [mp], [qkb], scale=KS)
        pb = self.rotB.get()
        for i in range(8):
            self.tr(pb[0:128, i * T:(i + 1) * T], qkb[P, i * 128:(i + 1) * 128], idb, [qkb, conb], [pb])
        qkT = g(R.BT)
        self.cp("act" if K.OPT_EVACT else "dve", qkT[:, 0:8 * T], pb[:, 0:8 * T], [pb], [qkT])
        Rot.rel(pb)
        vb = g(R.B512)
        self.cp("pool", vb[P, :], mp[P, 1024:1536], [mp], [vb])
        yield
        g8 = g(R.SM)
        self.tt("dve", g8[P, 0:8], mp[P, 1536:1544], R.gb[P, 0:8], ALU.add, [mp, R.gb], [g8])
        ipre, xf = g8[P, 0:4], g8[P, 4:8]
        s1 = g(R.SM)
        self.stt(s1[P, 0:4], xf, -1.0, xf, ALU.mult, ALU.max, [g8], [s1])
        self.act(s1[P, 0:4], s1[P, 0:4], AF.Exp, [s1], [s1], scale=-1.0)
        self.act(s1[P, 0:4], s1[P, 0:4], AF.Ln, [s1], [s1], bias=1.0)
        lf = g(R.SM)
        self.ts("dve", lf[P, 0:4], xf, 0.0, ALU.min, [g8], [lf])
        self.tt("dve", lf[P, 0:4], lf[P, 0:4], s1[P, 0:4], ALU.subtract, [lf, s1], [lf])
        yield
        R1 = g(R.F256)
        self.tt("pool", v3(R1[P, 0:T4], 4), bcm(SLm, T, T), bch(lf[P, 0:4], T, T), ALU.mult, [con, lf], [R1])
        R2 = g(R.F256)
        self.tt("pool", v3(R2[P, 0:T4], 4), bcm(idT, T, T), bch(ipre, T, T), ALU.mult, [con, g8], [R2])
        pf = R.rotF.get()
        self.mm(pf[0:T, 0:T4], tri, R1[P, 0:T4], True, False, [con, R1], [pf])
        self.mm(pf[0:T, 0:T4], onesTT, R2[P, 0:T4], False, True, [con, R2], [pf])
        pfb = R.rotF.get()
        self.mm(pfb[0:T, 0:4], tri, lf[P, 0:4], True, True, [con, lf], [pfb])
        Dm = g(R.F256)
        self.tt("dve", v3(Dm[P, 0:T4], 4), v3(pf[0:T, 0:T4], 4), bcm(NEG, T, T), ALU.add, [pf, con], [Dm])
        MIB = g(R.SM)
        self.cp("act", MIB[P, 8:12], pfb[0:T, 0:4], [pfb], [MIB])
        Rot.rel(pf, pfb)
        rmx = g(R.SM)
        self.red(rmx[P, 0:4], v3(Dm[P, 0:T4], 4), ALU.max, [Dm], [rmx])
        yield
        self.tt("dve", MIB[P, 4:8], MIB[P, 8:12], st.mbc[P, 0:4], ALU.add, [MIB, st.mbc], [MIB])
        self.tt("dve", MIB[P, 0:4], MIB[P, 4:8], rmx[P, 0:4], ALU.max, [MIB, rmx], [MIB])
        E = g(R.F256)
        self.tt("dve", v3(E[P, 0:T4], 4), v3(Dm[P, 0:T4], 4), bch(MIB[P, 0:4], T, T), ALU.subtract, [Dm, MIB], [E])
        self.act(E[P, 0:T4], E[P, 0:T4], AF.Exp, [E], [E])
        wi = g(R.SM)
        self.tt("dve", wi[P, 0:4], MIB[P, 4:8], MIB[P, 0:4], ALU.subtract, [MIB], [wi])
        self.act(wi[P, 0:4], wi[P, 0:4], AF.Exp, [wi], [wi])
        emt = g(R.SM)
        self.act(emt[P, 0:4], MIB[P, 0:4], AF.Exp, [MIB], [emt], scale=-1.0)
        yield
        pf = R.rotF.get()
        for h in range(4):
            self.mm(pf[0:T, h * T:(h + 1) * T], qkT[:, h * T:(h + 1) * T], qkT[:, (4 + h) * T:(5 + h) * T], True, True, [qkT], [pf])
        qkE = g(R.F256)
        self.tt("dve", qkE[P, 0:T4], pf[0:T, 0:T4], E[P, 0:T4], ALU.mult, [pf, E], [qkE])
        Rot.rel(pf)
        den1 = g(R.SM)
        self.red(den1[P, 0:4], v3(qkE[P, 0:T4], 4), ALU.add, [qkE], [den1])
        yield
        pf = R.rotF.get()
        for h in range(4):
            self.tr(pf[0:T, h * T:(h + 1) * T], qkE[P, h * T:(h + 1) * T], idT, [qkE, con], [pf])
        qkET = g(R.B256)
        self.cp("act", qkET[P, 0:T4], pf[0:T, 0:T4], [pf], [qkET])
        Rot.rel(pf)
        yield
        pf1 = R.rotF.get()
        for h in range(4):
            self.mm(pf1[0:T, h * 128:(h + 1) * 128], qkET[P, h * T:(h + 1) * T], vb[P, h * 128:(h + 1) * 128], True, True, [qkET, vb], [pf1])
        num1 = g(R.F512)
        self.cp("act", num1[P, :], pf1[0:T, :], [pf1], [num1])
        Rot.rel(pf1)
        yield
        pf2 = R.rotF.get()
        for h in range(4):
            self.mm(pf2[0:T, h * 128:(h + 1) * 128], qkT[:, h * T:(h + 1) * T], st.CTb[:, h, :], True, True, [qkT, st.CTb], [pf2])
        pf3 = R.rotF.get()
        for h in range(4):
            self.mm(pf3[0:T, h:h + 1], qkT[:, h * T:(h + 1) * T], st.nTb[:, h:h + 1], True, True, [qkT, st.nTb], [pf3])
        num = g(R.F512)
        self.tt("dve", v3(num[P, :], 4), v3(pf2[0:T, :], 4), bch(wi[P, 0:4], T, 128), ALU.mult, [pf2, wi], [num])
        Rot.rel(pf2)
        self.tt("dve", num[P, :], num[P, :], num1[P, :], ALU.add, [num, num1], [num])
        den = g(R.SM)
        self.tt("dve", den[P, 0:4], pf3[0:T, 0:4], wi[P, 0:4], ALU.mult, [pf3, wi], [den])
        Rot.rel(pf3)
        self.tt("dve", den[P, 0:4], den[P, 0:4], den1[P, 0:4], ALU.add, [den, den1], [den])
        self.stt(den[P, 0:4], den[P, 0:4], -1.0, den[P, 0:4], ALU.mult, ALU.max, [den], [den])
        self.tt("dve", den[P, 0:4], den[P, 0:4], emt[P, 0:4], ALU.max, [den, emt], [den])
        self.recip(den[P, 0:4], den[P, 0:4], [den], [den])
        self.tt("dve", v3(num[P, :], 4), v3(num[P, :], 4), bch(den[P, 0:4], T, 128), ALU.mult, [num, den], [num])
        yield
        pf = R.rotF.get()
        self.mm(pf[0:128, 0:12], sel, MIB[P, 0:12], True, True, [con, MIB], [pf])
        LB = g(R.SM)
        self.cp("act", LB[:, 0:12], pf[:, 0:12], [pf], [LB])
        Rot.rel(pf)
        yield
        self.cp("dve", st.mbc[:, 0:4], LB[:, 0:4], [LB], [st.mbc])
        gs = g(R.SM)
        self.tt("dve", gs[:, 0:4], LB[:, 4:8], LB[:, 0:4], ALU.subtract, [LB], [gs])
        self.act(gs[:, 0:4], gs[:, 0:4], AF.Exp, [gs], [gs])
        gt = g(R.SM)
        self.tt("dve", gt[P, 0:4], LB[P, 8:12], MIB[P, 8:12], ALU.subtract, [LB, MIB], [gt])
        self.tt("dve", gt[P, 0:4], gt[P, 0:4], ipre, ALU.add, [gt, g8], [gt])
        self.tt("dve", gt[P, 0:4], gt[P, 0:4], LB[P, 0:4], ALU.subtract, [gt, LB], [gt])
        self.act(gt[P, 0:4], gt[P, 0:4], AF.Exp, [gt], [gt])
        yield
        kg = g(R.B512)
        if K.OPT_K32:
            self.stt(v3(kg[P, :], 4), v3(k32[P, :], 4), KS, bch(gt[P, 0:4], T, 128), ALU.mult, ALU.mult, [k32, gt], [kg])
        else:
            self.stt(v3(kg[P, :], 4), v3(mp[P, 512:1024], 4), KS, bch(gt[P, 0:4], T, 128), ALU.mult, ALU.mult, [mp, gt], [kg])
        pfC = R.rotF.get()
        for h in range(4):
            self.mm(pfC[:, h * 128:(h + 1) * 128], kg[P, h * 128:(h + 1) * 128], vb[P, h * 128:(h + 1) * 128], True, True, [kg, vb], [pfC])
        pfn = R.rotF.get()
        for h in range(4):
            self.mm(pfn[:, h:h + 1], kg[P, h * 128:(h + 1) * 128], conb[0:T, 128:129], True, True, [kg, conb], [pfn])
        self.tt("dve", st.CT[:], st.CT[:], bch(gs[:, 0:4], 128, 128), ALU.mult, [st.CT, gs], [st.CT])
        self.tt("dve", st.CT[:], st.CT[:], v3(pfC[:, :], 4), ALU.add, [st.CT, pfC], [st.CT])
        Rot.rel(pfC)
        self.cp("act", st.CTb[:], st.CT[:], [st.CT], [st.CTb])
        self.tt("dve", st.nT[:], st.nT[:], gs[:, 0:4], ALU.mult, [st.nT, gs], [st.nT])
        self.tt("dve", st.nT[:], st.nT[:], pfn[:, 0:4], ALU.add, [st.nT, pfn], [st.nT])
        Rot.rel(pfn)
        self.cp("act", st.nTb[:], st.nT[:], [st.nT], [st.nTb])
        yield
        sig = g(R.F512)
        if K.OPT_SIGEXP:
            self.act(sig[P, :], mp[P, 1544:2056], AF.Exp, [mp], [sig], scale=-1.0)
            self.ts("pool", sig[P, :], sig[P, :], 1.0, ALU.add, [sig], [sig])
            self.recip(sig[P, :], sig[P, :], [sig], [sig])
        else:
            self.act(sig[P, :], mp[P, 1544:2056], AF.Sigmoid, [mp], [sig])
        szm = g(R.F512)
        self.act(szm[P, :], mp[P, 2056:2568], AF.Silu, [mp], [szm])
        self.tt("pool", szm[P, :], szm[P, :], R.mnorm[P, :], ALU.mult, [szm, R.mnorm], [szm])
        self.tt("dve", num[P, :], num[P, :], sig[P, :], ALU.mult, [num, sig], [num])
        self.tt("pool", sig[P, :], num[P, :], num[P, :], ALU.mult, [num], [sig])
        s4 = g(R.SM)
        self.red(s4[P, 0:4], v3(sig[P, :], 4), ALU.add, [sig], [s4])
        self.rstd(s4[P, 0:4], s4[P, 0:4], 128, [s4], [s4])
        yield
        self.tt("dve", v3(num[P, :], 4), v3(num[P, :], 4), bch(s4[P, 0:4], T, 128), ALU.mult, [num, s4], [num])
        omb = g(R.B512)
        self.tt("dve", omb[P, :], num[P, :], szm[P, :], ALU.mult, [num, szm], [omb])
        pb = self.rotB.get()
        for h in range(4):
            self.tr(pb[0:128, h * T:(h + 1) * T], omb[P, h * 128:(h + 1) * 128], idb, [omb, conb], [pb])
        self.cp("act", mixT[:, 4:8, coff:coff + T], v3(pb[:, 0:T4], 4), [pb], [mixT])
        Rot.rel(pb)
        Rot.rel(*L)
        yield

    def gdn_chunk(self, R, l, T, row0, st, mixT, coff):
        ctx = {}
        for _ in self.gdn_pre_g(R, l, T, row0, ctx):
            pass
        for _ in self.gdn_chain_g(R, l, T, st, mixT, coff, ctx):
            pass

    def gdn_pre_g(self, R, l, T, row0, ctx):
        con, conb = self.con, self.conb
        L = []

        def g(p):
            b = p.get()
            L.append(b)
            return b
        P = slice(0, T)
        T4 = 4 * T
        nst = {16: 3, 64: 5, 128: 6}[T]
        tri = con[0:T, C_TRI:C_TRI + T]
        SLm = con[0:T, C_SL:C_SL + T]
        INCL = con[0:T, C_INCL:C_INCL + T]
        idT = con[0:T, C_ID:C_ID + T]
        ones128 = con[0:T, C_ONE:C_ONE + 128]
        idb = self.ident(T, True)
        v3, bch, bcm = self.v3, self.bch, self.bcm
        ga = g(R.GA)
        self.dma(ga[P, :], self.proj.t[row0:row0 + T, 5288:5808], [], [ga])
        acc = None
        for j in range(4):
            gp = g(R.F1536)
            self.dma(gp[P, :], self.proj.t[row0 - 3 + j:row0 - 3 + j + T, 3752:5288], [], [gp])
            self.tt("pool", gp[P, :], gp[P, :], R.cw[P, j * 1536:(j + 1) * 1536], ALU.mult, [gp, R.cw], [gp])
            if acc is None:
                acc = gp
            else:
                self.tt("pool" if j == 1 else "dve", acc[P, :], acc[P, :], gp[P, :], ALU.add, [acc, gp], [acc])
                L.remove(gp)
                Rot.rel(gp)
        cs = acc
        self.act(cs[P, :], cs[P, :], AF.Silu, [cs], [cs])
        yield
        qkn = g(R.F1024)
        self.tt("pool", qkn[P, :], cs[P, 0:1024], cs[P, 0:1024], ALU.mult, [cs], [qkn])
        s8 = g(R.SM)
        self.red(s8[P, 0:8], v3(qkn[P, :], 8), ALU.add, [qkn], [s8])
        self.act(s8[P, 0:8], s8[P, 0:8], AF.Ln, [s8], [s8], bias=EPS)
        self.act(s8[P, 0:8], s8[P, 0:8], AF.Exp, [s8], [s8], scale=-0.5)
        self.ts("dve", s8[P, 0:4], s8[P, 0:4], 128.0 ** -0.5, ALU.mult, [s8], [s8])
        self.tt("dve", v3(qkn[P, :], 8), v3(cs[P, 0:1024], 8), s8[P, 0:8].unsqueeze(2).broadcast_to([T, 8, 128]), ALU.mult, [cs, s8], [qkn])
        yield
        y = g(R.SM)
        self.tt("dve", y[P, 0:4], ga[P, 0:4], R.dtb[P, 0:4], ALU.add, [ga, R.dtb], [y])
        s1 = g(R.SM)
        self.stt(s1[P, 0:4], y[P, 0:4], -1.0, y[P, 0:4], ALU.mult, ALU.max, [y], [s1])
        self.act(s1[P, 0:4], s1[P, 0:4], AF.Exp, [s1], [s1], scale=-1.0)
        self.act(s1[P, 0:4], s1[P, 0:4], AF.Ln, [s1], [s1], bias=1.0)
        gg = g(R.SM)
        self.ts("dve", gg[P, 0:4], y[P, 0:4], 0.0, ALU.max, [y], [gg])
        self.tt("dve", gg[P, 0:4], gg[P, 0:4], s1[P, 0:4], ALU.add, [gg, s1], [gg])
        self.tt("dve", gg[P, 0:4], gg[P, 0:4], R.nea[P, 0:4], ALU.mult, [gg, R.nea], [gg])
        bt = g(R.SM)
        if K.OPT_SIGEXP:
            self.act(bt[P, 0:4], ga[P, 4:8], AF.Exp, [ga], [bt], scale=-1.0)
            self.ts("dve", bt[P, 0:4], bt[P, 0:4], 1.0, ALU.add, [bt], [bt])
            self.recip(bt[P, 0:4], bt[P, 0:4], [bt], [bt])
        else:
            self.act(bt[P, 0:4], ga[P, 4:8], AF.Sigmoid, [ga], [bt])
        self.ts("dve", bt[P, 4:8], bt[P, 0:4], -1.0, ALU.mult, [bt], [bt])
        yield
        Rg = g(R.F256)
        self.tt("pool", v3(Rg[P, 0:T4], 4), bcm(SLm, T, T), bch(gg[P, 0:4], T, T), ALU.mult, [con, gg], [Rg])
        pf = R.rotF.get()
        self.mm(pf[0:T, 0:T4], tri, Rg[P, 0:T4], True, True, [con, Rg], [pf])
        pf128 = R.rotF.get()
        self.mm(pf128[0:128, 0:4], ones128, gg[P, 0:4], True, True, [con, gg], [pf128])
        self.mm(pf128[0:T, 8:12], tri, gg[P, 0:4], True, True, [con, gg], [pf128])
        gam = g(R.F256)
        self.act(gam[P, 0:T4], pf[0:T, 0:T4], AF.Exp, [pf], [gam])
        Gc = g(R.SM)
        self.cp("act", Gc[P, 0:4], pf128[0:T, 8:12], [pf128], [Gc])
        Rot.rel(pf)
        GL = g(R.SM)
        self.cp("act", GL[:, 0:4], pf128[:, 0:4], [pf128], [GL])
        Rot.rel(pf128)
        yield
        gam_s = g(R.F256)
        self.tt("dve", v3(gam_s[P, 0:T4], 4), v3(gam[P, 0:T4], 4), bcm(SLm, T, T), ALU.mult, [gam, con], [gam_s])
        self.tt("dve", v3(gam[P, 0:T4], 4), v3(gam[P, 0:T4], 4), bcm(INCL, T, T), ALU.mult, [gam, con], [gam])
        eG = g(R.SM)
        self.act(eG[P, 0:4], Gc[P, 0:4], AF.Exp, [Gc], [eG])
        eGl = g(R.SM)
        self.act(eGl[:, 0:4], GL[:, 0:4], AF.Exp, [GL], [eGl])
        self.tt("dve", eG[P, 4:8], GL[P, 0:4], Gc[P, 0:4], ALU.subtract, [GL, Gc], [eG])
        self.act(eG[P, 4:8], eG[P, 4:8], AF.Exp, [eG], [eG])
        self.tt("dve", eG[P, 8:12], bt[P, 0:4], eG[P, 0:4], ALU.mult, [bt, eG], [eG])
        yield
        qkb = g(R.B1024)
        self.cp("act", qkb[P, :], qkn[P, :], [qkn], [qkb])
        qgb = g(R.B512)
        self.tt("pool", v3(qgb[P, :], 4), v3(qkn[P, 0:512], 4), bch(eG[P, 0:4], T, 128), ALU.mult, [qkn, eG], [qgb])
        bk = g(R.F512)
        self.tt("dve", v3(bk[P, :], 4), v3(qkn[P, 512:1024], 4), bch(eG[P, 8:12], T, 128), ALU.mult, [qkn, eG], [bk])
        bv = g(R.F512)
        self.tt("pool", v3(bv[P, :], 4), v3(cs[P, 1024:1536], 4), bch(bt[P, 0:4], T, 128), ALU.mult, [cs, bt], [bv])
        kd = g(R.B512)
        self.tt("pool", v3(kd[P, :], 4), v3(qkn[P, 512:1024], 4), bch(eG[P, 4:8], T, 128), ALU.mult, [qkn, eG], [kd])
        yield
        qkT = g(R.BT)
        pb = self.rotB.get()
        for i in range(8):
            self.tr(pb[0:128, i * T:(i + 1) * T], qkb[P, i * 128:(i + 1) * 128], idb, [qkb, conb], [pb])
        self.cp("act" if K.OPT_EVACT else "dve", qkT[:, 0:8 * T], pb[:, 0:8 * T], [pb], [qkT])
        Rot.rel(pb)
        pb = self.rotB.get()
        for h in range(4):
            self.tr(pb[0:128, h * T:(h + 1) * T], qgb[P, h * 128:(h + 1) * 128], idb, [qgb, conb], [pb])
        self.cp("act" if K.OPT_EVACT else "dve", qkT[:, 8 * T:12 * T], pb[:, 0:4 * T], [pb], [qkT])
        Rot.rel(pb)
        yield
        qT = lambda h: qkT[:, h * T:(h + 1) * T]
        kT = lambda h: qkT[:, (4 + h) * T:(5 + h) * T]
        qgT = lambda h: qkT[:, (8 + h) * T:(9 + h) * T]
        pf = R.rotF.get()
        for h in range(4):
            self.mm(pf[0:T, h * T:(h + 1) * T], kT(h), kT(h), True, True, [qkT], [pf])
        pfq = R.rotF.get()
        for h in range(4):
            self.mm(pfq[0:T, h * T:(h + 1) * T], qT(h), kT(h), True, True, [qkT], [pfq])
        N = g(R.F256)
        self.tt("dve", N[P, 0:T4], pf[0:T, 0:T4], gam_s[P, 0:T4], ALU.mult, [pf, gam_s], [N])
        self.tt("dve", v3(N[P, 0:T4], 4), v3(N[P, 0:T4], 4), bch(bt[P, 4:8], T, T), ALU.mult, [N, bt], [N])
        QG = g(R.F256)
        self.tt("dve", QG[P, 0:T4], pfq[0:T, 0:T4], gam[P, 0:T4], ALU.mult, [pfq, gam], [QG])
        Rot.rel(pf, pfq)
        yield
        pf = R.rotF.get()
        for h in range(4):
            self.tr(pf[0:T, h * T:(h + 1) * T], N[P, h * T:(h + 1) * T], idT, [N, con], [pf])
        pfq = R.rotF.get()
        for h in range(4):
            self.tr(pfq[0:T, h * T:(h + 1) * T], QG[P, h * T:(h + 1) * T], idT, [QG, con], [pfq])
        Q = g(R.F256)
        self.cp("act", Q[P, 0:T4], pf[0:T, 0:T4], [pf], [Q])
        QGT = g(R.B256)
        self.cp("act" if K.OPT_EVACT else "dve", QGT[P, 0:T4], pfq[0:T, 0:T4], [pfq], [QGT])
        Rot.rel(pf, pfq)
        Y = g(R.F256)
        self.tt("dve", v3(Y[P, 0:T4], 4), v3(Q[P, 0:T4], 4), bcm(idT, T, T), ALU.add, [Q, con], [Y])
        yield
        Pm = N
        hs = lambda b_, h: b_[P, h * T:(h + 1) * T]
        for s in range(nst):
            last = (s == nst - 1)
            pfP = R.rotF.get()
            for h in range(4):
                self.mm(pfP[0:T, h * T:(h + 1) * T], hs(Q, h), hs(Pm, h), True, True, [Q, Pm], [pfP])
            if not last:
                pfQ = R.rotF.get()
                for h in range(4):
                    self.mm(pfQ[0:T, h * T:(h + 1) * T], hs(Pm, h), hs(Q, h), True, True, [Q, Pm], [pfQ])
            Pn = g(R.F256)
            self.cp("act", Pn[P, 0:T4], pfP[0:T, 0:T4], [pfP], [Pn])
            Rot.rel(pfP)
            if not last:
                Qn = g(R.F256)
                self.cp("act" if K.OPT_EVACT else "dve", Qn[P, 0:T4], pfQ[0:T, 0:T4], [pfQ], [Qn])
                Rot.rel(pfQ)
            pfY = R.rotF.get()
            for h in range(4):
                self.mm(pfY[0:T, h * T:(h + 1) * T], hs(Pn, h), hs(Y, h), True, True, [Pn, Y], [pfY])
            self.tt("dve", Y[P, 0:T4], Y[P, 0:T4], pfY[0:T, 0:T4], ALU.add, [Y, pfY], [Y])
            Rot.rel(pfY)
            yield
            for old in ((Pm, Q) if not last else (Pm, Q, Pn)):
                if old in L:
                    L.remove(old)
                    Rot.rel(old)
            Pm = Pn
            if not last:
                Q = Qn
        pfu = R.rotF.get()
        for h in range(4):
            self.mm(pfu[0:T, h * 128:(h + 1) * 128], hs(Y, h), bv[P, h * 128:(h + 1) * 128], True, True, [Y, bv], [pfu])
        usb = g(R.F512)
        self.cp("act", usb[P, :], pfu[0:T, :], [pfu], [usb])
        Rot.rel(pfu)
        yield
        pfw = R.rotF.get()
        for h in range(4):
            self.mm(pfw[0:128, h * T:(h + 1) * T], bk[P, h * 128:(h + 1) * 128], hs(Y, h), True, True, [bk, Y], [pfw])
        wTb = g(R.BT)
        self.cp("act" if K.OPT_EVACT else "dve", wTb[:, 0:T4], pfw[:, 0:T4], [pfw], [wTb])
        Rot.rel(pfw)
        keep = [usb, wTb, qkT, QGT, kd, eGl, ga]
        for b_ in list(L):
            if b_ not in keep:
                Rot.rel(b_)
        ctx.update(usb=usb, wTb=wTb, qkT=qkT, QGT=QGT, kd=kd, eGl=eGl, ga=ga, keep=keep)
        yield

    def gdn_chain_g(self, R, l, T, st, mixT, coff, ctx):
        con, conb = self.con, self.conb
        L = []

        def g(p):
            b = p.get()
            L.append(b)
            return b
        P = slice(0, T)
        T4 = 4 * T
        idb = self.ident(T, True)
        v3, bch, bcm = self.v3, self.bch, self.bcm
        usb, wTb, qkT, QGT, kd, eGl, ga = (ctx[k_] for k_ in ("usb", "wTb", "qkT", "QGT", "kd", "eGl", "ga"))
        qgT = lambda h: qkT[:, (8 + h) * T:(9 + h) * T]
        hs = lambda b_, h: b_[P, h * T:(h + 1) * T]
        pf = R.rotF.get()
        for h in range(4):
            self.mm(pf[0:T, h * 128:(h + 1) * 128], wTb[:, h * T:(h + 1) * T], st.Sb[:, h, :], True, True, [wTb, st.Sb], [pf])
        dl = g(R.B512)
        self.tt("dve", dl[P, :], usb[P, :], pf[0:T, :], ALU.subtract, [usb, pf], [dl])
        Rot.rel(pf)
        yield
        pfo = R.rotF.get()
        for h in range(4):
            o = pfo[0:T, h * 128:(h + 1) * 128]
            self.mm(o, qgT(h), st.Sb[:, h, :], True, False, [qkT, st.Sb], [pfo])
            self.mm(o, hs(QGT, h), dl[P, h * 128:(h + 1) * 128], False, True, [QGT, dl], [pfo])
        pfS = R.rotF.get()
        for h in range(4):
            self.mm(pfS[:, h * 128:(h + 1) * 128], kd[P, h * 128:(h + 1) * 128], dl[P, h * 128:(h + 1) * 128], True, True, [kd, dl], [pfS])
        self.tt("dve", st.S[:], st.S[:], bch(eGl[:, 0:4], 128, 128), ALU.mult, [st.S, eGl], [st.S])
        self.tt("dve", st.S[:], st.S[:], v3(pfS[:, :], 4), ALU.add, [st.S, pfS], [st.S])
        Rot.rel(pfS)
        self.cp("act", st.Sb[:], st.S[:], [st.S], [st.Sb])
        osb = g(R.F512)
        self.cp("act", osb[P, :], pfo[0:T, :], [pfo], [osb])
        Rot.rel(pfo)
        yield
        sq = g(R.F512)
        self.tt("pool", sq[P, :], osb[P, :], osb[P, :], ALU.mult, [osb], [sq])
        s4 = g(R.SM)
        self.red(s4[P, 0:4], v3(sq[P, :], 4), ALU.add, [sq], [s4])
        self.rstd(s4[P, 0:4], s4[P, 0:4], 128, [s4], [s4])
        yield
        self.tt("dve", v3(osb[P, :], 4), v3(osb[P, :], 4), bch(s4[P, 0:4], T, 128), ALU.mult, [osb, s4], [osb])
        self.act(sq[P, :], ga[P, 8:520], AF.Silu, [ga], [sq])
        self.tt("pool", v3(sq[P, :], 4), v3(sq[P, :], 4), bcm(R.gnorm[P, :], T, 128), ALU.mult, [sq, R.gnorm], [sq])
        ogb = g(R.B512)
        self.tt("dve", ogb[P, :], osb[P, :], sq[P, :], ALU.mult, [osb, sq], [ogb])
        pb = self.rotB.get()
        for h in range(4):
            self.tr(pb[0:128, h * T:(h + 1) * T], ogb[P, h * 128:(h + 1) * 128], idb, [ogb, conb], [pb])
        self.cp("act", mixT[:, 8:12, coff:coff + T], v3(pb[:, 0:T4], 4), [pb], [mixT])
        Rot.rel(pb)
        Rot.rel(*L)
        Rot.rel(*ctx["keep"])
        yield


_CACHE = {}


def _program(dbg=None, nlayers=DEPTH, phases="PAR"):
    key = (dbg, nlayers, phases)
    if key not in _CACHE:
        _CACHE[key] = K(dbg, nlayers, phases).build()
    return _CACHE[key]


def make_in_maps(inp):
    f = lambda a: np.ascontiguousarray(np.asarray(a, dtype=np.float32))
    consts = make_consts()
    rope = make_rope()
    shared = {
        "meta": f(inp["meta_tokens"]), "norm_w": f(inp["norm_w"]), "w_in": f(inp["w_in"]),
        "q_norm": f(inp["mla_q_norm"]), "w_uq": f(inp["mla_w_uq"]), "kv_norm": f(inp["mla_kv_norm"]),
        "w_uk": f(inp["mla_w_uk"]).reshape(DEPTH, 256, 512), "w_uv": f(inp["mla_w_uv"]).reshape(DEPTH, 256, 512),
        "gate_b": f(inp["mlstm_gate_b"]).reshape(DEPTH, 8), "m_norm": f(inp["mlstm_norm"]),
        "conv_w": f(inp["gdn_conv_w"]).reshape(DEPTH, 4 * 1536), "a_log": f(inp["gdn_a_log"]),
        "dt_bias": f(inp["gdn_dt_bias"]), "g_norm": f(inp["gdn_norm"]), "w_out": f(inp["w_out"]),
        "final_norm": f(inp["final_norm"]), "consts": consts, "rope": rope,
    }
    maps = []
    for c in range(8):
        sl = slice(NS * c, NS * c + NS)
        m = dict(shared)
        m["xp"] = f(inp["x_prompt"][c])
        m["xs"] = f(inp["x_sample"][sl]).reshape(NS * DS, D)
        m["cl"] = f(inp["cache_mla_latent"][:, sl])
        m["ck"] = f(inp["cache_mla_krope"][:, sl])
        m["imC"] = f(inp["state_mlstm_C"][:, sl])
        m["imn"] = f(inp["state_mlstm_n"][:, sl])
        m["imm"] = f(inp["state_mlstm_m"][:, sl])
        m["igS"] = f(inp["state_gdn_S"][:, sl])
        m["igc"] = f(inp["state_gdn_conv"][:, sl])
        maps.append(m)
    return maps


def kernel(**inputs):
    nc = _program()
    maps = make_in_maps(inputs)
    res = run_bass_kernel_spmd(nc, maps, core_ids=list(range(8)))
    R = res.results
    st = lambda name, axis: np.stack([np.asarray(r[name], dtype=np.float32) for r in R], axis=axis)
    cat = lambda name, axis: np.concatenate([np.asarray(r[name], dtype=np.float32) for r in R], axis=axis)
    y_prompt = st("y_p", 0)
    y_sample = st("y_s", 0).reshape(8 * NS, DS, D)
    return (
        y_prompt, y_sample,
        st("p_lat", 1), st("p_kr", 1), st("p_C", 1), st("p_n", 1), st("p_m", 1), st("p_S", 1), st("p_cv", 1),
        cat("s_lat", 1), cat("s_kr", 1), cat("s_C", 1), cat("s_n", 1), cat("s_m", 1), cat("s_S", 1), cat("s_cv", 1),
    )
```
